# Optimizing a Trainium2 kernel written in Bass

```python
import jax, jax.numpy as jnp
from jax import lax
import numpy as np

D_MODEL = 1024
BATCH = 8
SEQ = 2048
DEPTH = 1
DEC_BATCH = 128
DEC_SEQ = 8
PAST_LEN = 16384
PAGE_SIZE = 128

N_META = 16
GDN_HEADS = 8
GDN_DK = 64
GDN_DV = 64
GDN_CONV = 4
GLA_HEADS = 4
GLA_DK = 64
GLA_DV = 128
GLA_GATE_RANK = 16
GLA_TAU = 16.0
CHUNK = 64
D_FF = 2816
FFN_CONV = 3
GDN_QKV = GDN_HEADS * (2 * GDN_DK + GDN_DV)
MIX_WIDTH = GDN_HEADS * GDN_DV + GLA_HEADS * GLA_DV
IN_SPLITS = (GDN_QKV, GDN_HEADS, GDN_HEADS, GDN_HEADS * GDN_DV,
             GLA_HEADS * GLA_DK, GLA_HEADS * GLA_DK, GLA_HEADS * GLA_DV,
             GLA_GATE_RANK, GLA_HEADS * GLA_DV)
IN_WIDTH = sum(IN_SPLITS)
DEEPNORM_ALPHA = (2 * DEPTH) ** 0.25
DEEPNORM_BETA = (8 * DEPTH) ** -0.25

kernel_name = "hymba_gdn_gla_convffn_step"


def split_cols(p, sizes):
    idx = [int(i) for i in np.cumsum(sizes)[:-1]]
    return jnp.split(p, idx, axis=-1)


def layer_norm(x, g, b, eps=1e-5):
    xf = x.astype(jnp.float32)
    mu = jnp.mean(xf, axis=-1, keepdims=True)
    var = jnp.mean(jnp.square(xf - mu), axis=-1, keepdims=True)
    return ((xf - mu) * lax.rsqrt(var + eps) * g + b).astype(x.dtype)


def rms_norm(x, g, eps=1e-6):
    xf = x.astype(jnp.float32)
    return xf * lax.rsqrt(jnp.mean(jnp.square(xf), axis=-1, keepdims=True) + eps) * g


def l2norm(x, eps=1e-6):
    xf = x.astype(jnp.float32)
    return xf * lax.rsqrt(jnp.sum(jnp.square(xf), axis=-1, keepdims=True) + eps)


def causal_conv(x, buf, w):
    width = w.shape[0]
    L = x.shape[1]
    ext = jnp.concatenate([buf.astype(x.dtype), x], axis=1)
    out = ext[:, 0:L] * w[0]
    for i in range(1, width):
        out = out + ext[:, i:i + L] * w[i]
    return out, ext[:, L:]


def pad_chunks(x, C):
    pad = (-x.shape[1]) % C
    return jnp.pad(x, [(0, 0), (0, pad)] + [(0, 0)] * (x.ndim - 2))


def to_chunks(x, C):
    b, lp, h = x.shape[:3]
    x = x.reshape((b, lp // C, C, h) + x.shape[3:])
    return jnp.moveaxis(x, (1, 3), (0, 2))


def from_chunks(o):
    o = jnp.moveaxis(o, (0, 2), (1, 3))
    n_b, n, c, h, d = o.shape
    return o.reshape(n_b, n * c, h, d)


def gdn_chunked(q, k, v, g, beta, S0):
    f32 = jnp.float32
    L = q.shape[1]
    C = min(CHUNK, L)
    dv = v.shape[-1]
    q, k, v, g, beta = (to_chunks(pad_chunks(t.astype(f32), C), C) for t in (q, k, v, g, beta))
    G = jnp.cumsum(g, axis=-1)
    causal = jnp.tril(jnp.ones((C, C), dtype=bool))
    strict = jnp.tril(jnp.ones((C, C), dtype=bool), -1)
    decay = jnp.exp(jnp.where(causal, G[..., :, None] - G[..., None, :], -jnp.inf))
    kk = jnp.einsum('nbhcd,nbhjd->nbhcj', k, k)
    tri = jnp.eye(C, dtype=f32) + jnp.where(strict, beta[..., :, None] * kk * decay, 0.0)
    rhs = jnp.concatenate([beta[..., None] * v, (beta * jnp.exp(G))[..., None] * k], axis=-1)
    sol = lax.linalg.triangular_solve(tri, rhs, left_side=True, lower=True, unit_diagonal=True)
    w_v, w_k = sol[..., :dv], sol[..., dv:]
    qk = jnp.einsum('nbhcd,nbhjd->nbhcj', q, k) * decay
    q_dec = q * jnp.exp(G)[..., None]
    k_dec = k * jnp.exp(G[..., -1:] - G)[..., None]
    g_last = jnp.exp(G[..., -1])

    def step(S, xs):
        wv, wk, qkc, qd, kd, gl = xs
        U = wv - jnp.einsum('bhcd,bhdv->bhcv', wk, S)
        o = jnp.einsum('bhcd,bhdv->bhcv', qd, S) + jnp.einsum('bhcj,bhjv->bhcv', qkc, U)
        S = gl[..., None, None] * S + jnp.einsum('bhcd,bhcv->bhdv', kd, U)
        return S, o

    S, o = lax.scan(step, S0.astype(f32), (w_v, w_k, qk, q_dec, k_dec, g_last))
    return from_chunks(o)[:, :L], S


def gla_chunked(q, k, v, log_a, S0):
    f32 = jnp.float32
    L = q.shape[1]
    C = min(CHUNK, L)
    q, k, v, log_a = (to_chunks(pad_chunks(t.astype(f32), C), C) for t in (q, k, v, log_a))
    bcum = jnp.cumsum(log_a, axis=-2)
    causal = jnp.tril(jnp.ones((C, C), dtype=bool))[..., None]

    def step(S, xs):
        qc, kc, vc, bc = xs
        dec = jnp.exp(jnp.where(causal, bc[..., :, None, :] - bc[..., None, :, :], -jnp.inf))
        A = jnp.einsum('bhcd,bhjd,bhcjd->bhcj', qc, kc, dec)
        o = jnp.einsum('bhcd,bhdv->bhcv', qc * jnp.exp(bc), S) + jnp.einsum('bhcj,bhjv->bhcv', A, vc)
        S = jnp.exp(bc[..., -1, :])[..., None] * S + jnp.einsum(
            'bhcd,bhcv->bhdv', kc * jnp.exp(bc[..., -1:, :] - bc), vc)
        return S, o

    S, o = lax.scan(step, S0.astype(f32), (q, k, v, bcum))
    return from_chunks(o)[:, :L], S


def token_mixer(h, conv_buf, S_gdn, S_gla, segments, w_in, gdn_conv_w, gdn_A_log, gdn_dt_bias,
                gdn_norm_g, gla_wgk2, gla_bgk, gla_norm_g, w_out):
    f32 = jnp.float32
    bsz, L, _ = h.shape
    p = h @ w_in
    qkv_pre, a_in, b_in, gdn_gate, gq, gk, gv, gk_low, gla_gate = split_cols(p, IN_SPLITS)
    qkv, new_conv = causal_conv(qkv_pre, conv_buf, gdn_conv_w)
    qkv = jax.nn.silu(qkv)
    q, k, v = split_cols(qkv, (GDN_HEADS * GDN_DK, GDN_HEADS * GDN_DK, GDN_HEADS * GDN_DV))
    q = l2norm(q.reshape(bsz, L, GDN_HEADS, GDN_DK)) * (GDN_DK ** -0.5)
    k = l2norm(k.reshape(bsz, L, GDN_HEADS, GDN_DK))
    v = v.reshape(bsz, L, GDN_HEADS, GDN_DV)
    g = -jnp.exp(gdn_A_log.astype(f32)) * jax.nn.softplus(a_in.astype(f32) + gdn_dt_bias.astype(f32))
    beta = jax.nn.sigmoid(b_in.astype(f32))
    log_a = jax.nn.log_sigmoid((gk_low @ gla_wgk2 + gla_bgk).astype(f32)) / GLA_TAU
    log_a = log_a.reshape(bsz, L, GLA_HEADS, GLA_DK)
    gq = gq.reshape(bsz, L, GLA_HEADS, GLA_DK) * (GLA_DK ** -0.5)
    gk = gk.reshape(bsz, L, GLA_HEADS, GLA_DK)
    gv = gv.reshape(bsz, L, GLA_HEADS, GLA_DV)
    outs_a, outs_b = [], []
    start = 0
    for seg in segments:
        sl = slice(start, start + seg)
        o_a, S_gdn = gdn_chunked(q[:, sl], k[:, sl], v[:, sl], g[:, sl], beta[:, sl], S_gdn)
        o_b, S_gla = gla_chunked(gq[:, sl], gk[:, sl], gv[:, sl], log_a[:, sl], S_gla)
        outs_a.append(o_a)
        outs_b.append(o_b)
        start += seg
    o_gdn = rms_norm(jnp.concatenate(outs_a, axis=1), gdn_norm_g).reshape(bsz, L, -1) * jax.nn.silu(gdn_gate)
    o_gla = rms_norm(jnp.concatenate(outs_b, axis=1), gla_norm_g).reshape(bsz, L, -1) * jax.nn.silu(gla_gate)
    out = jnp.concatenate([o_gdn, o_gla], axis=-1).astype(h.dtype) @ w_out
    return out, new_conv, S_gdn, S_gla


def conv_ffn(h, buf, w_up, ffn_conv_w, ffn_conv_b, w_down):
    u = h @ w_up
    uc, new_buf = causal_conv(u, buf, ffn_conv_w)
    gate, val = jnp.split(uc + ffn_conv_b, 2, axis=-1)
    return (jax.nn.silu(gate) * val) @ w_down, new_buf


def decoder_layer(h, conv_buf, S_gdn, S_gla, ffn_buf, segments, w_in, gdn_conv_w, gdn_A_log,
                  gdn_dt_bias, gdn_norm_g, gla_wgk2, gla_bgk, gla_norm_g, w_out, ln1_g, ln1_b,
                  w_up, ffn_conv_w, ffn_conv_b, w_down, ln2_g, ln2_b):
    m, new_conv, S_gdn, S_gla = token_mixer(h, conv_buf, S_gdn, S_gla, segments, w_in, gdn_conv_w,
                                            gdn_A_log, gdn_dt_bias, gdn_norm_g, gla_wgk2, gla_bgk,
                                            gla_norm_g, w_out)
    h = layer_norm(DEEPNORM_ALPHA * h + m, ln1_g, ln1_b)
    f, new_ffn = conv_ffn(h, ffn_buf, w_up, ffn_conv_w, ffn_conv_b, w_down)
    h = layer_norm(DEEPNORM_ALPHA * h + f, ln2_g, ln2_b)
    return h, S_gdn, new_conv, S_gla, new_ffn


def setup_inputs(seed: int = 0) -> dict:
    key = jax.random.key(seed)
    ks = jax.random.split(key, 26)
    f32 = jnp.float32

    def nrm(k, shape, s):
        return s * jax.random.normal(k, shape, f32)

    beta = DEEPNORM_BETA
    col_scale = jnp.concatenate([
        jnp.ones((2 * GDN_HEADS * GDN_DK,), f32), jnp.full((GDN_HEADS * GDN_DV,), beta, f32),
        jnp.ones((2 * GDN_HEADS + GDN_HEADS * GDN_DV + 2 * GLA_HEADS * GLA_DK,), f32),
        jnp.full((GLA_HEADS * GLA_DV,), beta, f32),
        jnp.ones((GLA_GATE_RANK + GLA_HEADS * GLA_DV,), f32)])
    dt = jnp.exp(jax.random.uniform(ks[10], (DEPTH, GDN_HEADS), f32, np.log(0.001), np.log(0.1)))
    return {
        "x_prompt": nrm(ks[0], (BATCH, SEQ, D_MODEL), 1.0),
        "x_sample": nrm(ks[1], (DEC_BATCH, DEC_SEQ, D_MODEL), 1.0),
        "state_gdn": nrm(ks[2], (DEPTH, DEC_BATCH, GDN_HEADS, GDN_DK, GDN_DV), 0.1),
        "state_gdn_conv": nrm(ks[3], (DEPTH, DEC_BATCH, GDN_CONV - 1, GDN_QKV), 1.0),
        "state_gla": nrm(ks[4], (DEPTH, DEC_BATCH, GLA_HEADS, GLA_DK, GLA_DV), 1.0),
        "state_ffn_conv": nrm(ks[5], (DEPTH, DEC_BATCH, FFN_CONV - 1, 2 * D_FF), 0.6),
        "meta_tokens": nrm(ks[6], (N_META, D_MODEL), 1.0),
        "ln_in_g": 1.0 + nrm(ks[7], (D_MODEL,), 0.02),
        "ln_in_b": nrm(ks[8], (D_MODEL,), 0.02),
        "w_in": nrm(ks[9], (DEPTH, D_MODEL, IN_WIDTH), D_MODEL ** -0.5) * col_scale,
        "gdn_conv_w": nrm(ks[11], (DEPTH, GDN_CONV, GDN_QKV), GDN_CONV ** -0.5),
        "gdn_A_log": jnp.log(jax.random.uniform(ks[12], (DEPTH, GDN_HEADS), f32, 1.0, 16.0)),
        "gdn_dt_bias": dt + jnp.log(-jnp.expm1(-dt)),
        "gdn_norm_g": 1.0 + nrm(ks[13], (DEPTH, GDN_DV), 0.02),
        "gla_wgk2": nrm(ks[14], (DEPTH, GLA_GATE_RANK, GLA_HEADS * GLA_DK), GLA_GATE_RANK ** -0.5),
        "gla_bgk": nrm(ks[15], (DEPTH, GLA_HEADS * GLA_DK), 0.1),
        "gla_norm_g": 1.0 + nrm(ks[16], (DEPTH, GLA_DV), 0.02),
        "w_out": nrm(ks[17], (DEPTH, MIX_WIDTH, D_MODEL), MIX_WIDTH ** -0.5 * beta),
        "ln1_g": 1.0 + nrm(ks[18], (DEPTH, D_MODEL), 0.02),
        "ln1_b": nrm(ks[19], (DEPTH, D_MODEL), 0.02),
        "w_up": nrm(ks[20], (DEPTH, D_MODEL, 2 * D_FF), D_MODEL ** -0.5 * beta),
        "ffn_conv_w": nrm(ks[21], (DEPTH, FFN_CONV, 2 * D_FF), FFN_CONV ** -0.5),
        "ffn_conv_b": nrm(ks[22], (DEPTH, 2 * D_FF), 0.02),
        "w_down": nrm(ks[23], (DEPTH, D_FF, D_MODEL), D_FF ** -0.5 * beta),
        "ln2_g": 1.0 + nrm(ks[24], (DEPTH, D_MODEL), 0.02),
        "ln2_b": nrm(ks[25], (DEPTH, D_MODEL), 0.02),
    }


def reference(x_prompt, x_sample, state_gdn, state_gdn_conv, state_gla, state_ffn_conv, meta_tokens,
              ln_in_g, ln_in_b, w_in, gdn_conv_w, gdn_A_log, gdn_dt_bias, gdn_norm_g, gla_wgk2,
              gla_bgk, gla_norm_g, w_out, ln1_g, ln1_b, w_up, ffn_conv_w, ffn_conv_b, w_down,
              ln2_g, ln2_b):
    f32 = jnp.float32
    dt = x_prompt.dtype
    bp, seq = x_prompt.shape[0], x_prompt.shape[1]
    dec_seq = x_sample.shape[1]
    meta = jnp.broadcast_to(meta_tokens.astype(dt)[None], (bp, N_META, D_MODEL))
    h_p = layer_norm(jnp.concatenate([meta, x_prompt], axis=1), ln_in_g, ln_in_b)
    h_s = layer_norm(x_sample, ln_in_g, ln_in_b)
    new_p = [[], [], [], []]
    new_s = [[], [], [], []]
    for l in range(DEPTH):
        lp = dict(w_in=w_in[l], gdn_conv_w=gdn_conv_w[l], gdn_A_log=gdn_A_log[l],
                  gdn_dt_bias=gdn_dt_bias[l], gdn_norm_g=gdn_norm_g[l], gla_wgk2=gla_wgk2[l],
                  gla_bgk=gla_bgk[l], gla_norm_g=gla_norm_g[l], w_out=w_out[l], ln1_g=ln1_g[l],
                  ln1_b=ln1_b[l], w_up=w_up[l], ffn_conv_w=ffn_conv_w[l], ffn_conv_b=ffn_conv_b[l],
                  w_down=w_down[l], ln2_g=ln2_g[l], ln2_b=ln2_b[l])
        h_p, *st_p = decoder_layer(
            h_p, jnp.zeros((bp, GDN_CONV - 1, GDN_QKV), dt),
            jnp.zeros((bp, GDN_HEADS, GDN_DK, GDN_DV), f32),
            jnp.zeros((bp, GLA_HEADS, GLA_DK, GLA_DV), f32),
            jnp.zeros((bp, FFN_CONV - 1, 2 * D_FF), dt), (N_META, seq), **lp)
        h_s, *st_s = decoder_layer(h_s, state_gdn_conv[l], state_gdn[l], state_gla[l],
                                   state_ffn_conv[l], (dec_seq,), **lp)
        for i in range(4):
            new_p[i].append(st_p[i].astype(dt))
            new_s[i].append(st_s[i].astype(dt))
    y_prompt = h_p[:, N_META:]
    y_sample = h_s
    gdn_p, gdn_conv_p, gla_p, ffn_conv_p = (jnp.stack(s, axis=0) for s in new_p)
    gdn_s, gdn_conv_s, gla_s, ffn_conv_s = (jnp.stack(s, axis=0) for s in new_s)
    return (y_prompt, y_sample, gdn_p, gdn_conv_p, gla_p, ffn_conv_p, gdn_s, gdn_conv_s, gla_s, ffn_conv_s)
```

```python
from contextlib import ExitStack

import numpy as np
import concourse.bass as bass
import concourse.mybir as mybir
from concourse.bass_utils import run_bass_kernel_spmd

F32 = mybir.dt.float32
BF16 = mybir.dt.bfloat16
F32R = mybir.dt.float32r
AF = mybir.ActivationFunctionType
ALU = mybir.AluOpType
AX = mybir.AxisListType


class TB:
    def __init__(self, name, t):
        self.name = name
        self.t = t
        self.last_w = None
        self.readers = {}
        self.parts = []


class Sched:
    def __init__(self, nc, nslots=40):
        self.nc = nc
        self.stack = ExitStack()
        self.engs = {"pe": nc.tensor, "dve": nc.vector, "act": nc.scalar, "pool": nc.gpsimd, "sp": nc.sync}
        self.nslots = nslots

    def __enter__(self):
        self.stack.__enter__()
        nc = self.nc
        self.sem = {k: self.stack.enter_context(nc.semaphore(f"s_{k}")) for k in self.engs}
        self.cnt = {k: 0 for k in self.engs}
        self.waited = {k: {} for k in self.engs}
        self.slot_sem = [self.stack.enter_context(nc.semaphore(f"d_{i}")) for i in range(self.nslots)]
        self.slot_cnt = [0] * self.nslots
        self.next_slot = {"sp": 0, "pool": 0}
        self.out_deps = []
        self.nbuf = 0
        return self

    def __exit__(self, *a):
        return self.stack.__exit__(*a)

    def sb(self, name, shape, dtype):
        t = self.stack.enter_context(self.nc.sbuf_tensor(name, list(shape), dtype))
        return TB(name, t)

    def ps(self, name, shape, dtype):
        t = self.stack.enter_context(self.nc.psum_tensor(name, list(shape), dtype))
        return TB(name, t)

    def alias(self, name, tb):
        return TB(name, tb.t)

    def _semof(self, key):
        if isinstance(key, tuple):
            return self.slot_sem[key[1]]
        return self.sem[key]

    def _wait(self, eng, dep):
        key, val = dep
        if eng == "pe" and key == "pe":
            return
        w = self.waited[eng]
        if w.get(key, 0) >= val:
            return
        self.engs[eng].wait_ge(self._semof(key), val)
        w[key] = val

    @staticmethod
    def _expand(bufs):
        out = []
        for b in bufs:
            out.append(b)
            out.extend(b.parts)
        return out

    def _deps(self, reads, writes):
        reads, writes = self._expand(reads), self._expand(writes)
        deps = set()
        for b in reads:
            if b.last_w is not None:
                deps.add(b.last_w)
        for b in writes:
            if b.last_w is not None:
                deps.add(b.last_w)
            for k, v in b.readers.items():
                deps.add((k, v))
        return deps

    def _commit(self, me, reads, writes):
        reads, writes = self._expand(reads), self._expand(writes)
        for b in writes:
            b.last_w = me
            b.readers = {}
        for b in reads:
            if b not in writes:
                b.readers[me[0]] = max(b.readers.get(me[0], 0), me[1])

    def op(self, eng, emit, reads=(), writes=()):
        for d in sorted(self._deps(reads, writes), key=str):
            self._wait(eng, d)
        inst = emit(self.engs[eng])
        self.cnt[eng] += 1
        inst.then_inc(self.sem[eng], 1)
        self._commit((eng, self.cnt[eng]), reads, writes)

    def dma(self, out, in_, reads=(), writes=(), cast=False, is_out=False, q=None):
        eng = q or ("pool" if cast else "sp")
        nsp = (self.nslots * 5) // 8
        lo, n = (0, nsp) if eng == "sp" else (nsp, self.nslots - nsp)
        i = lo + self.next_slot[eng]
        self.next_slot[eng] = (self.next_slot[eng] + 1) % n
        if self.slot_cnt[i] > 0:
            self._wait(eng, (("slot", i), 16 * self.slot_cnt[i]))
        for d in sorted(self._deps(reads, writes), key=str):
            self._wait(eng, d)
        inst = self.engs[eng].dma_start(out=out, in_=in_)
        inst.then_inc(self.slot_sem[i], 16)
        self.slot_cnt[i] += 1
        me = (("slot", i), 16 * self.slot_cnt[i])
        self._commit(me, reads, writes)
        if is_out:
            self.out_deps.append(me)

    def finish(self):
        for d in self.out_deps:
            self._wait("sp", d)
        for k in ("pe", "dve", "act", "pool"):
            if self.cnt[k] > 0:
                self._wait("sp", (k, self.cnt[k]))

    def barrier(self):
        deps = [(k, self.cnt[k]) for k in ("pe", "dve", "act", "pool") if self.cnt[k] > 0]
        deps += [(("slot", i), 16 * c) for i, c in enumerate(self.slot_cnt) if c > 0]
        for e in ("pe", "dve", "act", "pool", "sp"):
            for d in deps:
                if d[0] != e:
                    self._wait(e, d)


DBG = {}
D = 1024
SEQ = 2048
NMETA = 16
TP = SEQ + NMETA
NS = 16
LS = 8
DFF = 2816
NFF = 44
ALPHA = 2.0 ** 0.25
NFM = 2064
NTM = 1808
NEG = -30000.0

C_ID, C_ONE, C_BO64 = 0, 1, 2
C_PU, C_PSU, C_PMBT, C_PMBS, C_PM01T = 3, 4, 5, 6, 7
C_SU, C_SSU, C_SBO, C_SMBT, C_SMBS, C_SM01T, C_SBM = 8, 9, 10, 11, 12, 13, 14
NCM = 15


def _const_mats():
    m = np.zeros((NCM, 128, 128), np.float32)
    k = np.arange(128)[:, None]
    c = np.arange(128)[None, :]
    m[C_ID] = (k == c)
    m[C_ONE] = 1.0
    m[C_BO64] = (k // 64 == c // 64)
    m[C_PU] = (k <= c)
    m[C_PSU] = (k > c)
    m[C_PMBT] = np.where(c >= k, 0.0, NEG)
    m[C_PMBS] = np.where(c < k, 0.0, NEG)
    m[C_PM01T] = (c >= k)
    sb = (k // LS == c // LS)
    m[C_SU] = (k <= c) & sb
    m[C_SSU] = (k > c) & sb
    m[C_SBO] = sb
    m[C_SMBT] = np.where((c >= k) & sb, 0.0, NEG)
    m[C_SMBS] = np.where((c < k) & sb, 0.0, NEG)
    m[C_SM01T] = (c >= k) & sb
    m[C_SBM][:, :NS] = (k // LS == np.arange(NS)[None, :])
    return np.ascontiguousarray(m.transpose(1, 0, 2))


def _bc(ap, axis, shape):
    return ap.unsqueeze(axis).to_broadcast(list(shape))


class K:
    def __init__(self):
        nc = self.nc = bass.Bass("TRN2", target_bir_lowering=False)
        di = lambda n, s: nc.dram_tensor(n, list(s), F32, kind="ExternalInput").ap()
        do = lambda n, s: nc.dram_tensor(n, list(s), F32, kind="ExternalOutput").ap()
        self.xp = di("xp", [SEQ, D]); self.xs = di("xs", [NS * LS, D]); self.meta = di("meta", [NMETA, D])
        self.sgdn = di("sgdn", [NS, 8, 64, 64]); self.sgconv = di("sgconv", [NS * 3, 1536])
        self.sgla = di("sgla", [NS, 4, 64, 128]); self.sfconv = di("sfconv", [NS * 2, 2 * DFF])
        self.cmat_d = di("cmat", [128, NCM, 128])
        self.w_in_d = di("w_in_r", [D, NFM + NTM]); self.w_out_d = di("w_out", [D, D])
        self.w_up_d = di("w_up", [D, 2 * DFF]); self.w_down_d = di("w_down", [DFF, D])
        self.lnv_d = di("lnv", [6, D]); self.cwg_d = di("cwg", [128, 12, 4]); self.cwf_d = di("cwf", [128, NFF, 4])
        self.pvec_d = di("pvec", [1, 464]); self.wgk2_d = di("wgk2", [16, 256])
        self.y_p = do("y_p", [SEQ, D]); self.y_s = do("y_s", [NS * LS, D])
        self.gdn_p = do("gdn_p", [8, 64, 64]); self.gconv_p = do("gconv_p", [3, 1536])
        self.gla_p = do("gla_p", [4, 64, 128]); self.fconv_p = do("fconv_p", [2, 2 * DFF])
        self.gdn_s = do("gdn_s", [NS, 8, 64, 64]); self.gconv_s = do("gconv_s", [NS * 3, 1536])
        self.gla_s = do("gla_s", [NS, 4, 64, 128]); self.fconv_s = do("fconv_s", [NS * 2, 2 * DFF])
        self.h1s = nc.dram_tensor("h1s", [TP + NS * LS, D], F32, kind="Internal").ap()
        self.S = Sched(nc)

    def cm(self, idx, r=128, c=128):
        return self.cmat.t[0:r, idx, 0:c]

    def bank(self, b, n=1):
        if n == 1:
            return self.pst.t[:, b, :]
        return self.pst.t[:, b:b + n, :].rearrange("p b f -> p (b f)")

    def layer_norm(self, buf, C, g_ap, b_ap, eps_tile, tag):
        S = self.S
        st, mv = self.ln_st, self.ln_mv
        x = buf.t

        def stats(e):
            e.bn_stats(out=st.t[0:C, 0, :], in_=x[0:C, 0:512])
            return e.bn_stats(out=st.t[0:C, 1, :], in_=x[0:C, 512:1024])
        S.op("dve", stats, [buf], [st])
        S.op("dve", lambda e: e.bn_aggr(out=mv.t[0:C, 0:2], in_=st.t[0:C, :, :].rearrange("p a b -> p (a b)")), [st], [mv])
        S.op("act", lambda e: e.activation(out=mv.t[0:C, 2:3], in_=mv.t[0:C, 1:2], func=AF.Ln, bias=eps_tile, scale=1.0), [mv, self.epsc], [mv])
        S.op("act", lambda e: e.activation(out=mv.t[0:C, 3:4], in_=mv.t[0:C, 2:3], func=AF.Exp, scale=-0.5), [mv], [mv])
        S.op("dve", lambda e: e.tensor_scalar(out=x[0:C, :], in0=x[0:C, :], scalar1=mv.t[0:C, 0:1], scalar2=mv.t[0:C, 3:4],
                                              op0=ALU.subtract, op1=ALU.mult), [buf, mv], [buf])
        S.op("dve", lambda e: e.tensor_tensor(out=x[0:C, :], in0=x[0:C, :], in1=g_ap[0:C, :], op=ALU.mult), [buf, self.lnbc], [buf])
        S.op("dve", lambda e: e.tensor_tensor(out=x[0:C, :], in0=x[0:C, :], in1=b_ap[0:C, :], op=ALU.add), [buf, self.lnbc], [buf])

    def stage1_alloc(self):
        S = self.S
        self.lnbc = S.sb("lnbc", [128, 4, D], F32)
        self.pvec = S.sb("pvec_sb", [128, 464], F32)
        self.negA = S.sb("negA", [128, 8], F32)
        self.wgk2 = S.sb("wgk2_sb", [16, 256], F32)
        self.cwg = S.sb("cwg_sb", [128, 12, 4], F32)
        self.w_in = S.sb("w_in_sb", [128, 8, NFM + NTM], BF16)
        self.w_out = S.sb("w_out_sb", [128, 8, D], BF16)
        self.xhs = [S.sb(f"xh{i}", [128, D], F32) for i in range(2)]
        self.hT = S.sb("hT", [128, 8, 128], BF16)
        self.qkvx = S.sb("qkvx", [128, 12, 176], F32)
        self.fmx = S.sb("fmx", [128, 5, 128], F32)
        self.tm0 = S.sb("tm0", [128, 272], F32)
        self.gv_tok = S.sb("gv_tok", [128, 512], F32)
        self.sg_gdn = S.sb("sg_gdn", [128, 512], F32)
        self.sg_gla = S.sb("sg_gla", [128, 512], F32)
        self.bigs = S.sb("bigs", [128, 6, 1024], F32)
        self.big = [S.alias(f"big{i}", self.bigs) for i in range(6)]
        self.Pc = S.sb("Pc", [128, 1024], F32)
        self.PTc = S.sb("PTc", [128, 1024], F32)
        self.TTc = S.sb("TTc", [128, 1024], F32)
        for tb in (self.Pc, self.PTc, self.TTc):
            tb.parts = [S.alias(f"{tb.name}_hg{g}", tb) for g in range(2)]
        self.kq = S.sb("kq", [128, 4, 2, 128], F32)
        self.wkT = S.sb("wkT", [128, 8, 128], F32)
        self.KTm = self.wkT
        self.QTm = S.sb("QTm", [128, 8, 128], F32)
        self.keTm = S.sb("keTm", [128, 4, 128], F32)
        self.qeTm = S.sb("qeTm", [128, 4, 128], F32)
        self.wv = S.sb("wv", [128, 512], F32)
        self.RK = S.sb("RK", [128, 512], F32)
        self.RV = S.sb("RV", [128, 512], F32)
        self.kdec = S.sb("kdec", [128, 512], F32)
        self.U = self.RV
        self.ogdn = self.RK
        self.Sg = S.sb("Sg", [128, 4, 64], F32)
        self.sc = S.sb("sc", [128, 96], F32)
        self.lt = TB("lt", self.wv.t[:, 0:256])
        self.lt.parts = [self.wv]
        self.ebT = S.sb("ebT", [128, 2, 128], F32)
        self.enbT = S.sb("enbT", [128, 2, 128], F32)
        self.qeT = S.sb("qeT", [128, 2, 128], F32)
        self.PTg = S.sb("PTg", [128, 4, 128], F32)
        self.kd = TB("kd", self.enbT.t[:, :, :].rearrange("p a c -> p (a c)"))
        self.kd.parts = [self.enbT]
        self.ogla = S.sb("ogla", [128, 512], F32)
        self.Sl = S.sb("Sl", [128, 2, 128], F32)
        self.mix = S.sb("mix", [128, D], F32)
        self.mixT = S.sb("mixT", [128, 8, 128], BF16)
        self.otmp = TB("otmp", self.bigs.t[:, 3, 0:512])
        self.otmp.parts = [self.big[3]]

    def bg(self, i, n=1):
        if n == 1:
            return self.bigs.t[:, i, :]
        return self.bigs.t[:, i:i + n, :].rearrange("p b f -> p (b f)")

    def stage1_setup(self):
        S = self.S
        nc = self.nc
        S.dma(self.cmat.t[:], self.cmat_d, [], [self.cmat])
        for i in range(4):
            S.dma(self.lnbc.t[:, i, :], self.lnv_d[i:i + 1, :].partition_broadcast(128), [], [self.lnbc])
        S.dma(self.pvec.t[:], self.pvec_d[0:1, :].partition_broadcast(128), [], [self.pvec])
        S.dma(self.wgk2.t[:], self.wgk2_d, [], [self.wgk2])
        S.dma(self.cwg.t[:], self.cwg_d, [], [self.cwg])
        S.op("pool", lambda e: e.memset(self.epsc.t[:, 0:1], 1e-5), [], [self.epsc])
        S.op("pool", lambda e: e.memset(self.epsc.t[:, 1:2], 1e-6), [self.epsc], [self.epsc])
        S.op("pool", lambda e: e.memset(self.epsc.t[:, 2:3], 1.0), [self.epsc], [self.epsc])
        S.op("pool", lambda e: e.memset(self.epsc.t[:, 3:4], 0.0), [self.epsc], [self.epsc])
        wv_ = self.w_in_d.rearrange("(kc p) n -> p kc n", p=128)
        for kc in range(8):
            S.dma(self.w_in.t[:, kc, :], wv_[:, kc, :], [], [self.w_in], cast=True)
        wo_ = self.w_out_d.rearrange("(kc p) n -> p kc n", p=128)
        for kc in range(0, 8, 4):
            S.dma(self.w_out.t[:, kc:kc + 4, :], wo_[:, kc:kc + 4, :], [], [self.w_out], cast=True)
        S.op("act", lambda e: e.activation(out=self.negA.t[:], in_=self.pvec.t[:, 0:8], func=AF.Exp), [self.pvec], [self.negA])
        S.op("dve", lambda e: e.tensor_scalar(out=self.negA.t[:], in0=self.negA.t[:], scalar1=-1.0, scalar2=None, op0=ALU.mult),
             [self.negA], [self.negA])
        S.op("pool", lambda e: e.memset(self.Sg.t[:], 0.0), [], [self.Sg])
        S.op("pool", lambda e: e.memset(self.Sl.t[:], 0.0), [], [self.Sl])
        S.op("pool", lambda e: e.memset(self.qkvx.t[:], 0.0), [], [self.qkvx])
        for tb in (self.wkT, self.QTm, self.keTm, self.qeTm):
            S.op("pool", lambda e, tb=tb: e.memset(tb.t[:], 0.0), [], [tb])

    def front0(self, e0, C, kind, xh):
        self.front0_load(e0, C, kind, xh)
        self.front0_compute(e0, C, kind, xh)

    def front0_load(self, e0, C, kind, xh):
        S = self.S
        smp = kind == "s"
        if smp:
            S.dma(xh.t[0:C, :], self.xs, [], [xh])
        elif e0 == 0:
            S.dma(xh.t[0:NMETA, :], self.meta, [], [xh])
            S.dma(xh.t[NMETA:128, :], self.xp[0:128 - NMETA, :], [], [xh])
        else:
            S.dma(xh.t[0:C, :], self.xp[e0 - NMETA:e0 - NMETA + C, :], [], [xh])

    def front0_compute(self, e0, C, kind, xh):
        S, cm, pb, bank, hT = self.S, self.cm, self.pb, self.bank, self.hT
        self.layer_norm(xh, C, self.lnbc.t[:, 0, :], self.lnbc.t[:, 1, :], self.epsc.t[0:C, 0:1], "in")
        for half in range(2):
            def tr(e, half=half):
                for j in range(4):
                    kc = half * 4 + j
                    r = e.transpose(bank(half)[:, j * 128:j * 128 + C], xh.t[0:C, kc * 128:(kc + 1) * 128], cm(C_ID, C, C))
                return r
            S.op("pe", tr, [xh, self.cmat], [pb[half]])
            S.op("act", lambda e, half=half: e.activation(
                out=hT.t[:, half * 4:half * 4 + 4, 0:C],
                in_=bank(half).rearrange("p (j c) -> p j c", j=4)[:, :, 0:C], func=AF.Copy), [pb[half]], [hT])

    def chunk(self, e0, C, kind, xh, nxt):
        S = self.S
        self.xh = xh
        cm = self.cm
        pb = self.pb
        bank = self.bank
        big = self.big
        bg = self.bg
        smp = kind == "s"
        if smp:
            iU, iSU, iBO, iMBT, iMBS, iM01 = C_SU, C_SSU, C_SBO, C_SMBT, C_SMBS, C_SM01T
            G, L, nlev = NS, LS, 3
        else:
            iU, iSU, iBO, iMBT, iMBS, iM01 = C_PU, C_PSU, C_ONE, C_PMBT, C_PMBS, C_PM01T
            G, L, nlev = 1, C, {128: 7, 16: 4}[C]
        hT, qkvx, fmx, tm0, kq, sc = self.hT, self.qkvx, self.fmx, self.tm0, self.kq, self.sc
        ident = cm(C_ID)

        if DBG.get("step", 99) < 4:
            return
        qv = qkvx.t[:, :, 0:G * (L + 3)].rearrange("p a (g l) -> p a g l", g=G)
        for grp in range(5):
            b = 2 + (grp % 4)
            ccs = list(range(grp * 4, min(grp * 4 + 4, 17)))

            def mmf(e, ccs=ccs, b=b):
                for j, cc in enumerate(ccs):
                    M = 128 if cc < 16 else 16
                    for kc in range(8):
                        r = e.matmul(bank(b)[0:M, j * 128:j * 128 + C], lhsT=self.w_in.t[:, kc, cc * 128:cc * 128 + M],
                                     rhs=hT.t[:, kc, 0:C], start=(kc == 0), stop=(kc == 7))
                return r
            S.op("pe", mmf, [self.w_in, hT], [pb[b]])
            src = bank(b).rearrange("p (j c) -> p j c", j=4)
            if grp < 3:
                S.op("act", lambda e, grp=grp, src=src: e.activation(
                    out=qv[:, grp * 4:grp * 4 + 4, :, 3:3 + L],
                    in_=src[:, :, 0:C].rearrange("p j (g l) -> p j g l", g=G), func=AF.Copy), [pb[b]], [qkvx])
            elif grp == 3:
                S.op("act", lambda e, src=src: e.activation(out=fmx.t[:, 0:4, 0:C], in_=src[:, :, 0:C], func=AF.Copy), [pb[b]], [fmx])
            else:
                S.op("act", lambda e, src=src: e.activation(out=fmx.t[0:16, 4, 0:C], in_=src[0:16, 0, 0:C], func=AF.Copy), [pb[b]], [fmx])
        if DBG.get("step", 99) < 5:
            return
        tmoff = [NFM, NFM + 272, NFM + 784, NFM + 1296]
        tmn = [272, 512, 512, 512]
        for gi in range(4):
            b = 6 + (gi % 2)

            def mmt(e, gi=gi, b=b):
                for kc in range(8):
                    r = e.matmul(bank(b)[0:C, 0:tmn[gi]], lhsT=hT.t[:, kc, 0:C], rhs=self.w_in.t[:, kc, tmoff[gi]:tmoff[gi] + tmn[gi]],
                                 start=(kc == 0), stop=(kc == 7))
                return r
            S.op("pe", mmt, [self.w_in, hT], [pb[b]])
            if gi == 0:
                S.op("dve", lambda e, b=b: e.tensor_copy(out=tm0.t[0:C, :], in_=bank(b)[0:C, 0:272]), [pb[b]], [tm0])
            elif gi == 1:
                S.op("act", lambda e, b=b: e.activation(out=self.sg_gdn.t[0:C, :], in_=bank(b)[0:C, :], func=AF.Copy), [pb[b]], [self.sg_gdn])
            elif gi == 2:
                S.op("dve", lambda e, b=b: e.tensor_copy(out=self.gv_tok.t[0:C, :], in_=bank(b)[0:C, :]), [pb[b]], [self.gv_tok])
            else:
                S.op("act", lambda e, b=b: e.activation(out=self.sg_gla.t[0:C, :], in_=bank(b)[0:C, :], func=AF.Copy), [pb[b]], [self.sg_gla])
        if nxt is not None:
            self.front0_load(*nxt)
        if DBG.get("step", 99) < 6:
            return
        acc = bg(0, 2)[:, 0:12 * C].rearrange("p (a g l) -> p a g l", a=12, g=G)
        tmp = bg(2, 2)[:, 0:12 * C].rearrange("p (a g l) -> p a g l", a=12, g=G)
        accT, tmpT = [big[0], big[1]], [big[2], big[3]]

        def cwb(i):
            return self.cwg.t[:, :, i:i + 1].unsqueeze(3).to_broadcast([128, 12, G, L])
        S.op("dve", lambda e: e.tensor_tensor(out=acc, in0=qv[:, :, :, 0:L], in1=cwb(0), op=ALU.mult), [qkvx, self.cwg], accT)
        for i in range(1, 4):
            S.op("pool" if i == 1 else "dve", lambda e, i=i: e.tensor_tensor(out=tmp, in0=qv[:, :, :, i:i + L], in1=cwb(i), op=ALU.mult), [qkvx, self.cwg], tmpT)
            S.op("dve", lambda e: e.tensor_tensor(out=acc, in0=acc, in1=tmp, op=ALU.add), accT + tmpT, accT)
        qa = bg(0, 2)[:, 0:12 * C].rearrange("p (a c) -> p a c", a=12)
        S.op("act", lambda e: e.activation(out=qa, in_=qa, func=AF.Silu), accT, accT)
        if DBG.get("step", 99) < 7:
            return
        sq = bg(4)[:, 0:8 * C].rearrange("p (a c) -> p a c", a=8)
        rn = bg(5)[:, 0:8 * C].rearrange("p (a c) -> p a c", a=8)
        for sg in (self.sg_gdn, self.sg_gla):
            S.op("act", lambda e, sg=sg: e.activation(out=sg.t[0:C, :], in_=sg.t[0:C, :], func=AF.Silu), [sg], [sg])
        S.op("act", lambda e: e.activation(out=sq, in_=qa[:, 0:8, :], func=AF.Square), accT, [big[4]])
        S.op("act", lambda e: e.activation(out=sc.t[0:C, 16:24], in_=tm0.t[0:C, 8:16], func=AF.Sigmoid), [tm0], [sc])
        for half in range(2):
            S.op("pe", lambda e, half=half: e.matmul(bank(half)[:, 0:4 * C], lhsT=cm(C_BO64),
                                                     rhs=bg(4)[:, half * 4 * C:(half + 1) * 4 * C], start=True, stop=True),
                 [big[4], self.cmat], [pb[half]])
            S.op("act", lambda e, half=half: e.activation(out=bg(5)[:, half * 4 * C:(half + 1) * 4 * C], in_=bank(half)[:, 0:4 * C],
                                                          func=AF.Ln, bias=self.epsc.t[:, 1:2], scale=1.0), [pb[half], self.epsc], [big[5]])
        S.op("act", lambda e: e.activation(out=bg(5)[:, 0:8 * C], in_=bg(5)[:, 0:8 * C], func=AF.Exp, scale=-0.5), [big[5]], [big[5]])
        S.op("dve", lambda e: e.scalar_tensor_tensor(out=kq.t[:, :, 1, 0:C], in0=qa[:, 0:4, :], scalar=0.125, in1=rn[:, 0:4, :],
                                                      op0=ALU.mult, op1=ALU.mult), accT + [big[5]], [kq])
        S.op("pool", lambda e: e.tensor_tensor(out=kq.t[:, :, 0, 0:C], in0=qa[:, 4:8, :], in1=rn[:, 4:8, :], op=ALU.mult), accT + [big[5]], [kq])
        for h2 in range(2):
            rows = slice(64 * h2, 64 * h2 + 64)
            pad = lambda tb: tb.t[rows, :, 0:C].rearrange("p (a two) c -> p a two c", two=2)[:, :, h2, :]
            S.op("act", lambda e, rows=rows, pad=pad: e.activation(out=pad(self.KTm), in_=kq.t[rows, :, 0, 0:C], func=AF.Copy), [kq], [self.KTm])
            S.op("dve", lambda e, rows=rows, pad=pad: e.tensor_copy(out=pad(self.QTm), in_=kq.t[rows, :, 1, 0:C]), [kq], [self.QTm])
        if DBG.get("step", 99) < 8:
            return
        s_ = lambda a, b_: sc.t[0:C, a:b_]
        S.op("dve", lambda e: e.tensor_tensor(out=s_(0, 8), in0=tm0.t[0:C, 0:8], in1=self.pvec.t[0:C, 8:16], op=ALU.add), [tm0, self.pvec], [sc])
        S.op("act", lambda e: e.activation(out=s_(0, 8), in_=s_(0, 8), func=AF.Exp), [sc], [sc])
        S.op("act", lambda e: e.activation(out=s_(0, 8), in_=s_(0, 8), func=AF.Ln, bias=self.epsc.t[0:C, 2:3], scale=1.0), [sc, self.epsc], [sc])
        S.op("dve", lambda e: e.tensor_tensor(out=s_(8, 16), in0=s_(0, 8), in1=self.negA.t[0:C, :], op=ALU.mult), [sc, self.negA], [sc])

        def mmG(e):
            e.matmul(bank(0)[0:C, 0:8], lhsT=cm(iU, C, C), rhs=s_(8, 16), start=True, stop=True)
            return e.matmul(bank(0)[0:C, 8:16], lhsT=cm(iBO, C, C), rhs=s_(8, 16), start=True, stop=True)
        S.op("pe", mmG, [sc, self.cmat], [pb[0]])
        S.op("dve", lambda e: e.tensor_copy(out=s_(24, 40), in_=bank(0)[0:C, 0:16]), [pb[0]], [sc])
        S.op("act", lambda e: e.activation(out=s_(40, 48), in_=s_(24, 32), func=AF.Exp), [sc], [sc])
        S.op("dve", lambda e: e.tensor_tensor(out=s_(48, 56), in0=s_(32, 40), in1=s_(24, 32), op=ALU.subtract), [sc], [sc])
        S.op("act", lambda e: e.activation(out=s_(48, 56), in_=s_(48, 56), func=AF.Exp), [sc], [sc])
        S.op("act", lambda e: e.activation(out=s_(56, 64), in_=s_(32, 40), func=AF.Exp), [sc], [sc])
        S.op("dve", lambda e: e.tensor_tensor(out=s_(64, 72), in0=s_(16, 24), in1=s_(40, 48), op=ALU.mult), [sc], [sc])
        if DBG.get("step", 99) < 9:
            return
        def trk(e):
            for p in range(4):
                r = e.transpose(bank(6)[0:C, p * 128:(p + 1) * 128], kq.t[:, p, 0, 0:C], ident)
            return r
        S.op("pe", trk, [kq, self.cmat], [pb[6]])

        def trv(e):
            for p in range(4):
                r = e.transpose(bank(7)[0:C, p * 128:(p + 1) * 128], qa[:, 8 + p, :], ident)
            return r
        S.op("pe", trv, accT + [self.cmat], [pb[7]])
        h3 = lambda ap: ap.rearrange("p (h d) -> p h d", h=8)
        S.op("dve", lambda e: e.tensor_tensor(out=h3(self.RK.t[0:C, :]), in0=h3(bank(6)[0:C, :]), in1=_bc(s_(64, 72), 2, [C, 8, 64]), op=ALU.mult),
             [pb[6], sc], [self.RK])
        S.op("dve", lambda e: e.tensor_tensor(out=h3(self.kdec.t[0:C, :]), in0=h3(bank(6)[0:C, :]), in1=_bc(s_(48, 56), 2, [C, 8, 64]), op=ALU.mult),
             [pb[6], sc], [self.kdec])
        S.op("dve", lambda e: e.tensor_tensor(out=h3(self.RV.t[0:C, :]), in0=h3(bank(7)[0:C, :]), in1=_bc(s_(16, 24), 2, [C, 8, 64]), op=ALU.mult),
             [pb[7], sc], [self.RV])
        if DBG.get("step", 99) < 10:
            return
        v3 = lambda i: bg(i)[0:C, 0:8 * C].rearrange("p (h c) -> p h c", h=8)
        S.op("dve", lambda e: e.tensor_tensor(out=v3(2), in0=_bc(cm(iU, C, C), 1, [C, 8, C]), in1=_bc(s_(8, 16), 2, [C, 8, C]), op=ALU.mult),
             [self.cmat, sc], [big[2]])
        for half in range(2):
            S.op("pe", lambda e, half=half: e.matmul(bank(half)[0:C, 0:4 * C], lhsT=cm(C_ONE, C, C),
                                                     rhs=bg(2)[0:C, half * 4 * C:(half + 1) * 4 * C], start=True, stop=True),
                 [big[2], self.cmat], [pb[half]])
        gbc = bank(0, 2)

        def gview(r):
            return self.pst.t[0:r, 0:2, 0:4 * C].rearrange("p b (h c) -> p b h c", h=4)
        v4 = lambda i: bg(i)[0:C, 0:8 * C].rearrange("p (b h c) -> p b h c", b=2, h=4)
        S.op("pool", lambda e: e.tensor_tensor(out=v3(3), in0=_bc(cm(iMBT, C, C), 1, [C, 8, C]), in1=_bc(s_(24, 32), 2, [C, 8, C]), op=ALU.subtract),
             [self.cmat, sc], [big[3]])
        S.op("pool", lambda e: e.tensor_tensor(out=v3(4), in0=_bc(cm(iMBS, C, C), 1, [C, 8, C]), in1=_bc(s_(24, 32), 2, [C, 8, C]), op=ALU.add),
             [self.cmat, sc], [big[4]])
        S.op("dve", lambda e: e.tensor_tensor(out=v4(5), in0=gview(C), in1=v4(3), op=ALU.add), [pb[0], pb[1], big[3]], [big[5]])
        S.op("act", lambda e: e.activation(out=bg(5)[0:C, 0:8 * C], in_=bg(5)[0:C, 0:8 * C], func=AF.Exp), [big[5]], [big[5]])
        S.op("dve", lambda e: e.tensor_tensor(out=v4(1), in0=v4(4), in1=gview(C), op=ALU.subtract), [pb[0], pb[1], big[4]], [big[1]])
        S.op("act", lambda e: e.activation(out=bg(1)[0:C, 0:8 * C], in_=bg(1)[0:C, 0:8 * C], func=AF.Exp), [big[1]], [big[1]])
        if DBG.get("step", 99) < 11:
            return
        def mmkk(e):
            for h in range(8):
                p, h2 = h // 2, h % 2
                ov = bank(2 + h // 2).rearrange("p (hh two c) -> p hh two c", hh=2, two=2)
                if C == 128:
                    r = e.matmul(bank(2 + h // 2)[0:C, (h % 2) * 256:(h % 2) * 256 + 256], lhsT=self.KTm.t[:, h, 0:C],
                                 rhs=kq.t[:, p, :, :].rearrange("p a c -> p (a c)"), start=True, stop=True)
                else:
                    for two in range(2):
                        r = e.matmul(ov[0:C, h % 2, two, 0:C], lhsT=self.KTm.t[:, h, 0:C],
                                     rhs=kq.t[:, p, two, 0:C], start=True, stop=True)
            return r
        S.op("pe", mmkk, [kq, self.KTm], [pb[2], pb[3], pb[4], pb[5]])
        kkv = self.pst.t[0:C, 2:6, :].rearrange("p b (hh two c) -> p b hh two c", hh=2, two=2)
        v5 = lambda i: bg(i)[0:C, 0:8 * C].rearrange("p (b hh c) -> p b hh c", b=4, hh=2)
        S.op("dve", lambda e: e.tensor_tensor(out=v5(2), in0=kkv[:, :, :, 0, 0:C], in1=v5(1), op=ALU.mult), [pb[2], pb[3], pb[4], pb[5], big[1]], [big[2]])
        use_r = (C == 128) and bool(DBG.get("f32r"))
        ro = (lambda ap: ap.bitcast(F32R)) if use_r else (lambda ap: ap)
        ri = (lambda ap: ap.bitcast(F32R)) if use_r else (lambda ap: ap)
        Pc, PTc, TTc = self.Pc, self.PTc, self.TTc
        c3 = lambda tb: tb.t[0:C, 0:8 * C].rearrange("p (h c) -> p h c", h=8)
        c4 = lambda tb: tb.t[0:C, 0:8 * C].rearrange("p (b h c) -> p b h c", b=2, h=4)
        S.op("dve", lambda e: e.scalar_tensor_tensor(out=ro(c3(Pc)), in0=v3(2), scalar=-1.0, in1=_bc(s_(16, 24), 2, [C, 8, C]),
                                                      op0=ALU.mult, op1=ALU.mult), [big[2], sc], [Pc])
        S.op("dve", lambda e: e.tensor_tensor(out=v5(0), in0=kkv[:, :, :, 1, 0:C], in1=v5(5), op=ALU.mult), [pb[2], pb[3], pb[4], pb[5], big[5]] + accT, [big[0]])
        for half in range(2):
            def trn(e, half=half):
                for j in range(4):
                    r = e.transpose(bank(half)[0:C, j * C:(j + 1) * C], c3(Pc)[:, half * 4 + j, :], cm(C_ID, C, C))
                return r
            S.op("pe", trn, [Pc, self.cmat], [pb[half]])
        S.op("act", lambda e: e.activation(out=ro(c4(PTc)), in_=gview(C), func=AF.Copy), [pb[0], pb[1]], [PTc])
        S.op("dve", lambda e: e.tensor_tensor(out=ro(c3(TTc)), in0=c3(PTc), in1=_bc(cm(C_ID, C, C), 1, [C, 8, C]), op=ALU.add), [PTc, self.cmat], [TTc])
        if DBG.get("step", 99) < 12:
            return
        if nxt is not None:
            self.front0_compute(*nxt)
        gla_gen = self.gla_prep(C, iU, iSU, iM01)
        for lev in range(nlev):
            doA, doC, doB = lev >= 1, lev <= nlev - 2, lev <= nlev - 3
            for hg in range(2):
                bA, bB, bC = (2, 3, 4) if hg == 0 else (5, 6, 7)

                def mminv(e, hg=hg, bA=bA, bB=bB, bC=bC, doA=doA, doB=doB, doC=doC):
                    r = None
                    for j in range(4):
                        h = hg * 4 + j
                        o = lambda b_: bank(b_)[0:C, j * C:(j + 1) * C]
                        if doA:
                            r = e.matmul(o(bA), lhsT=ri(c3(Pc)[:, h, :]), rhs=ri(c3(TTc)[:, h, :]), start=True, stop=True)
                        if doC:
                            r = e.matmul(o(bC), lhsT=ri(c3(PTc)[:, h, :]), rhs=ri(c3(Pc)[:, h, :]), start=True, stop=True)
                        if doB:
                            r = e.matmul(o(bB), lhsT=ri(c3(Pc)[:, h, :]), rhs=ri(c3(PTc)[:, h, :]), start=True, stop=True)
                    return r
                wr = ([pb[bA]] if doA else []) + ([pb[bB]] if doB else []) + ([pb[bC]] if doC else [])
                S.op("pe", mminv, [Pc.parts[hg], PTc.parts[hg], TTc.parts[hg]], wr)
                hs = slice(hg * 4 * C, (hg + 1) * 4 * C)
                if doA:
                    S.op("dve", lambda e, bA=bA, hs=hs: e.tensor_tensor(out=ro(TTc.t[0:C, hs]), in0=bank(bA)[0:C, 0:4 * C], in1=TTc.t[0:C, hs], op=ALU.add),
                         [pb[bA], TTc.parts[hg]], [TTc.parts[hg]])
                if doC:
                    S.op("act", lambda e, bC=bC, hs=hs: e.activation(out=ro(Pc.t[0:C, hs]), in_=bank(bC)[0:C, 0:4 * C], func=AF.Copy), [pb[bC]], [Pc.parts[hg]])
                if doB:
                    S.op("act", lambda e, bB=bB, hs=hs: e.activation(out=ro(PTc.t[0:C, hs]), in_=bank(bB)[0:C, 0:4 * C], func=AF.Copy), [pb[bB]], [PTc.parts[hg]])
            for _ in range(4):
                next(gla_gen, None)
        for _ in gla_gen:
            pass
        if DBG.get("step", 99) < 13:
            return
        def mmwv(e):
            for h in range(8):
                r = e.matmul(bank(0)[0:C, h * 64:(h + 1) * 64], lhsT=c3(TTc)[:, h, :], rhs=self.RV.t[0:C, h * 64:(h + 1) * 64], start=True, stop=True)
            return r
        S.op("pe", mmwv, [TTc, self.RV], [pb[0]])
        S.op("act", lambda e: e.activation(out=self.wv.t[0:C, :], in_=bank(0)[0:C, :], func=AF.Copy), [pb[0]], [self.wv])

        def mmwk(e):
            for h in range(8):
                p = h // 2
                r = e.matmul(bank(2 + h // 4)[:, (h % 4) * 128:(h % 4) * 128 + C], lhsT=self.RK.t[0:C, p * 128:(p + 1) * 128], rhs=c3(TTc)[:, h, :],
                             start=True, stop=True)
            return r
        S.op("pe", mmwk, [TTc, self.RK], [pb[2], pb[3]])
        wkv = self.pst.t[:, 2:4, :].rearrange("p b (hh two c) -> p (b hh) two c", hh=2, two=2)
        for h2 in range(2):
            rows = slice(64 * h2, 64 * h2 + 64)
            S.op("dve" if h2 == 0 else "act",
                 (lambda e, rows=rows, h2=h2: e.tensor_copy(out=self.wkT.t[rows, :, 0:C].rearrange("p (a two) c -> p a two c", two=2)[:, :, h2, :], in_=wkv[rows, :, h2, 0:C])) if h2 == 0 else
                 (lambda e, rows=rows, h2=h2: e.activation(out=self.wkT.t[rows, :, 0:C].rearrange("p (a two) c -> p a two c", two=2)[:, :, h2, :], in_=wkv[rows, :, h2, 0:C], func=AF.Copy)),
                 [pb[2], pb[3]], [self.wkT])
        if DBG.get("step", 99) < 14:
            return
        for _ in gla_gen:
            pass
        if DBG.get("step", 99) < 15:
            return
        if smp:
            self.state_sample(C)
        else:
            self.state_prompt(C)
        if DBG.get("step", 99) < 16:
            return
        self.post_mix(e0, C, smp)
        if DBG.get("step", 99) < 17:
            return
        if not smp:
            S.op("pool", lambda e: e.tensor_copy(out=qkvx.t[:, :, 0:3], in_=qkvx.t[:, :, L:L + 3]), [qkvx], [qkvx])

    def gla_prep(self, C, iU, iSU, iM01):
        S, cm, pb, bank = self.S, self.cm, self.pb, self.bank
        fmx, lt = self.fmx, self.lt
        yield
        S.op("pe", lambda e: e.matmul(bank(0)[0:C, 0:256], lhsT=fmx.t[0:16, 4, 0:C], rhs=self.wgk2.t[:, :], start=True, stop=True),
             [fmx, self.wgk2], [pb[0]])
        yield
        S.op("dve", lambda e: e.tensor_tensor(out=lt.t[0:C, :], in0=bank(0)[0:C, 0:256], in1=self.pvec.t[0:C, 208:464], op=ALU.add), [pb[0], self.pvec], [lt])
        yield
        S.op("act", lambda e: e.activation(out=lt.t[0:C, :], in_=lt.t[0:C, :], func=AF.Exp, scale=-1.0), [lt], [lt])
        yield
        S.op("act", lambda e: e.activation(out=lt.t[0:C, :], in_=lt.t[0:C, :], func=AF.Ln, bias=self.epsc.t[0:C, 2:3], scale=1.0), [lt, self.epsc], [lt])

        yield
        def mmbc(e):
            for p in range(2):
                r = e.matmul(bank(1)[:, p * 128:p * 128 + C], lhsT=lt.t[0:C, p * 128:(p + 1) * 128], rhs=cm(iU, C, C), start=True, stop=True)
            return r
        yield
        S.op("pe", mmbc, [lt, self.cmat], [pb[1]])
        bcv = bank(1)[:, 0:256].rearrange("p (a c) -> p a c", a=2)[:, :, 0:C]
        yield
        S.op("act", lambda e: e.activation(out=self.ebT.t[:, :, 0:C], in_=bcv, func=AF.Exp, scale=-1.0 / 16.0), [pb[1]], [self.ebT])
        yield
        S.op("act", lambda e: e.activation(out=self.enbT.t[:, :, 0:C], in_=bcv, func=AF.Exp, scale=1.0 / 16.0), [pb[1]], [self.enbT])
        yield
        S.op("dve", lambda e: e.scalar_tensor_tensor(out=self.qeT.t[:, :, 0:C], in0=fmx.t[:, 0:2, 0:C], scalar=0.125, in1=self.ebT.t[:, :, 0:C],
                                                      op0=ALU.mult, op1=ALU.mult), [fmx, self.ebT], [self.qeT])
        yield
        for h2 in range(2):
            rows = slice(64 * h2, 64 * h2 + 64)
            pad = lambda tb: tb.t[rows, :, 0:C].rearrange("p (a two) c -> p a two c", two=2)[:, :, h2, :]
            S.op("pool", lambda e, rows=rows, pad=pad: e.tensor_tensor(out=pad(self.keTm), in0=fmx.t[rows, 2:4, 0:C], in1=self.enbT.t[rows, :, 0:C], op=ALU.mult),
                 [fmx, self.enbT], [self.keTm])
            S.op("act", lambda e, rows=rows, pad=pad: e.activation(out=pad(self.qeTm), in_=self.qeT.t[rows, :, 0:C], func=AF.Copy), [self.qeT], [self.qeTm])

        yield
        def mmA(e):
            for h in range(4):
                p, h2 = h // 2, h % 2
                rows = slice(64 * h2, 64 * h2 + 64)
                r = e.matmul(bank(0)[0:C, h * 128:h * 128 + C], lhsT=self.keTm.t[:, h, 0:C], rhs=self.qeT.t[:, p, 0:C], start=True, stop=True)
            return r
        yield
        S.op("pe", mmA, [self.keTm, self.qeT], [pb[0]])
        yield
        S.op("dve", lambda e: e.tensor_tensor(out=self.PTg.t[0:C, :, 0:C], in0=bank(0).rearrange("p (h c) -> p h c", h=4)[0:C, :, 0:C],
                                              in1=_bc(cm(iM01, C, C), 1, [C, 4, C]), op=ALU.mult), [pb[0], self.cmat], [self.PTg])
        yield
        S.op("pe", lambda e: e.matmul(bank(1)[0:C, 0:256], lhsT=cm(iSU, C, C), rhs=lt.t[0:C, :], start=True, stop=True), [lt, self.cmat], [pb[1]])
        yield
        S.op("act", lambda e: e.activation(out=self.kd.t[0:C, :], in_=bank(1)[0:C, 0:256], func=AF.Exp, scale=-1.0 / 16.0), [pb[1]], [self.kd])
        yield
        S.op("pool", lambda e: e.tensor_tensor(out=self.kd.t[0:C, :], in0=self.kd.t[0:C, :], in1=self.tm0.t[0:C, 16:272], op=ALU.mult), [self.kd, self.tm0], [self.kd])

    def state_prompt(self, C):
        S, cm, pb, bank, big, bg = self.S, self.cm, self.pb, self.bank, self.big, self.bg
        sc, kq = self.sc, self.kq
        s_ = lambda a, b_: sc.t[0:C, a:b_]
        v3 = lambda i: bg(i)[0:C, 0:8 * C].rearrange("p (h c) -> p h c", h=8)
        Sg, Sl, U = self.Sg, self.Sl, self.U
        R = lambda h2: slice(64 * h2, 64 * h2 + 64)
        h3 = lambda ap: ap.rearrange("p (h d) -> p h d", h=8)
        S.op("pe", lambda e: e.matmul(bank(1)[:, 0:8], lhsT=cm(C_ONE, C, 128), rhs=s_(8, 16), start=True, stop=True), [sc, self.cmat], [pb[1]])
        S.op("act", lambda e: e.activation(out=self.glb.t[:, 0:8], in_=bank(1)[:, 0:8], func=AF.Exp), [pb[1]], [self.glb])

        def mm1(e):
            for h in range(8):
                p, h2 = h // 2, h % 2
                r = e.matmul(bank(6)[0:C, h * 64:(h + 1) * 64], lhsT=self.wkT.t[:, h, 0:C], rhs=Sg.t[:, p, :], start=True, stop=True)
            return r
        S.op("pe", mm1, [self.wkT, Sg], [pb[6]])
        def mg1(e):
            for h in range(4):
                p, h2 = h // 2, h % 2
                r = e.matmul(bank(2)[0:C, h * 128:(h + 1) * 128], lhsT=self.qeTm.t[:, h, 0:C], rhs=Sl.t[:, p, :], start=True, stop=True)
            return r
        S.op("pe", mg1, [self.qeTm, Sl], [pb[2]])
        def mg2(e):
            for h in range(4):
                r = e.matmul(bank(3)[0:C, h * 128:(h + 1) * 128], lhsT=self.PTg.t[0:C, h, 0:C], rhs=self.gv_tok.t[0:C, h * 128:(h + 1) * 128], start=True, stop=True)
            return r
        S.op("pe", mg2, [self.PTg, self.gv_tok], [pb[3]])
        def mg3(e):
            for h in range(4):
                p = h // 2
                r = e.matmul(bank(4)[:, h * 128:(h + 1) * 128], lhsT=self.kd.t[0:C, p * 128:(p + 1) * 128], rhs=self.gv_tok.t[0:C, h * 128:(h + 1) * 128], start=True, stop=True)
            return r
        S.op("pe", mg3, [self.kd, self.gv_tok], [pb[4]])
        S.op("dve", lambda e: e.tensor_tensor(out=U.t[0:C, :], in0=self.wv.t[0:C, :], in1=bank(6)[0:C, :], op=ALU.subtract), [self.wv, pb[6]], [U])

        def mm2(e):
            for h in range(8):
                p, h2 = h // 2, h % 2
                r = e.matmul(bank(7)[0:C, h * 64:(h + 1) * 64], lhsT=self.QTm.t[:, h, 0:C], rhs=Sg.t[:, p, :], start=True, stop=True)
            return r
        S.op("pe", mm2, [self.QTm, Sg], [pb[7]])

        def mm3(e):
            for h in range(8):
                r = e.matmul(bank(0)[0:C, h * 64:(h + 1) * 64], lhsT=v3(0)[:, h, :], rhs=U.t[0:C, h * 64:(h + 1) * 64], start=True, stop=True)
            return r
        S.op("pe", mm3, [big[0], U], [pb[0]])
        S.op("act", lambda e: e.activation(out=self.ogla.t[0:C, :], in_=bank(2)[0:C, :], func=AF.Copy), [pb[2]], [self.ogla])
        S.op("dve", lambda e: e.tensor_tensor(out=self.ogla.t[0:C, :], in0=self.ogla.t[0:C, :], in1=bank(3)[0:C, :], op=ALU.add), [self.ogla, pb[3]], [self.ogla])
        S.op("dve", lambda e: e.tensor_tensor(out=h3(self.otmp.t[0:C, :]), in0=h3(bank(7)[0:C, :]), in1=_bc(s_(40, 48), 2, [C, 8, 64]), op=ALU.mult),
             [pb[7], sc], [self.otmp])
        S.op("dve", lambda e: e.tensor_tensor(out=self.ogdn.t[0:C, :], in0=self.otmp.t[0:C, :], in1=bank(0)[0:C, :], op=ALU.add), [self.otmp, pb[0]], [self.ogdn])

        def mm4(e):
            for h in range(8):
                p = h // 2
                r = e.matmul(bank(1)[:, h * 64:(h + 1) * 64], lhsT=self.kdec.t[0:C, p * 128:(p + 1) * 128], rhs=U.t[0:C, h * 64:(h + 1) * 64], start=True, stop=True)
            return r
        S.op("pe", mm4, [self.kdec, U, self.glb], [pb[1]])
        for h2 in range(2):
            glv = self.glb.t[R(h2), 0:8].rearrange("p (a two) -> p a two", two=2)[:, :, h2]
            psv = bank(1).rearrange("p (a two v) -> p a two v", two=2, v=64)[R(h2), :, h2, :]
            S.op("pool", lambda e, h2=h2, glv=glv: e.tensor_tensor(out=Sg.t[R(h2), :, :], in0=Sg.t[R(h2), :, :], in1=_bc(glv, 2, [64, 4, 64]), op=ALU.mult),
                 [Sg, self.glb], [Sg])
            S.op("dve", lambda e, h2=h2, psv=psv: e.tensor_tensor(out=Sg.t[R(h2), :, :], in0=Sg.t[R(h2), :, :], in1=psv, op=ALU.add), [Sg, pb[1]], [Sg])


        for h in range(4):
            p, h2 = h // 2, h % 2
            S.op("dve", lambda e, p=p, h2=h2, h=h: e.scalar_tensor_tensor(
                out=Sl.t[R(h2), p, :], in0=Sl.t[R(h2), p, :], scalar=self.ebT.t[R(h2), p, C - 1:C], in1=bank(4)[R(h2), h * 128:(h + 1) * 128],
                op0=ALU.mult, op1=ALU.add), [Sl, self.ebT, pb[4]], [Sl])

    def state_sample(self, C):
        S, cm, pb, bank, big, bg = self.S, self.cm, self.pb, self.bank, self.big, self.bg
        sc, kq = self.sc, self.kq
        s_ = lambda a, b_: sc.t[0:C, a:b_]
        v3 = lambda i: bg(i)[0:C, 0:8 * C].rearrange("p (h c) -> p h c", h=8)
        U = self.U
        R = lambda h2: slice(64 * h2, 64 * h2 + 64)
        h3 = lambda ap: ap.rearrange("p (h d) -> p h d", h=8)
        bm = cm(C_SBM, C, NS)
        gsel = bg(5)[0:C, 0:128].rearrange("p (h s) -> p h s", h=8)
        S.op("pool", lambda e: e.tensor_tensor(out=gsel, in0=_bc(s_(8, 16), 2, [C, 8, NS]), in1=_bc(bm, 1, [C, 8, NS]), op=ALU.mult), [sc, self.cmat], [big[5]])
        S.op("pe", lambda e: e.matmul(bank(1)[:, 0:128], lhsT=cm(C_ONE), rhs=bg(5)[0:C, 0:128], start=True, stop=True), [big[5], self.cmat], [pb[1]])
        S.op("act", lambda e: e.activation(out=self.glb.t[:, 0:128], in_=bank(1)[:, 0:128], func=AF.Exp), [pb[1]], [self.glb])
        glbs = self.glb.t[:, 0:128].rearrange("p (h s) -> p h s", h=8)
        S0 = bg(4).rearrange("p (s v) -> p s v", s=NS)
        tmp = bg(5)[0:C, :].rearrange("p (s v) -> p s v", s=NS)
        tmpT = bg(5)[0:C, :].rearrange("p (s v) -> p v s", s=NS)
        Ub = bg(1)[0:C, :].rearrange("p (s v) -> p s v", s=NS)
        ps67 = self.pst.t[:, 6:8, :].rearrange("p b (s v) -> p (b s) v", v=64)
        wks, o1s = self.otmp, self.ogdn
        S0v = lambda ap: ap.rearrange("p (s v) -> p s v", s=NS)
        S0s = [(S0v(bg(4)), big[4]), (S0v(bg(2)), big[2])]
        scr = [dict(tmp=S0v(bg(5)[0:C, :]), tmpT=bg(5)[0:C, :].rearrange("p (s v) -> p v s", s=NS), tmpB=big[5],
                    Ub=S0v(bg(1)[0:C, :]), Ubf=bg(1), UbB=big[1], bk=6),
               dict(tmp=S0v(self.Pc.t[0:C, :]), tmpT=self.Pc.t[0:C, :].rearrange("p (s v) -> p v s", s=NS), tmpB=self.Pc,
                    Ub=S0v(self.PTc.t[0:C, :]), Ubf=self.PTc.t, UbB=self.PTc, bk=4)]

        def load(p):
            S0, S0t = S0s[p % 2]
            for h2 in range(2):
                S.dma(S0[R(h2), :, :], self.sgdn[:, 2 * p + h2, :, :].rearrange("s d v -> d s v"), [], [S0t])

        def head_chain(p, h2):
            h = 2 * p + h2
            S0, S0t = S0s[p % 2]
            q = scr[h2]
            bk = q["bk"]
            psx = self.pst.t[:, bk:bk + 2, :].rearrange("p b (s v) -> p (b s) v", v=64)
            for (lhs, lhsb, dst) in ((self.wkT.t[:, h, 0:C], self.wkT, wks), (self.QTm.t[:, h, 0:C], self.QTm, o1s)):
                def mma(e, lhs=lhs):
                    for i in range(2):
                        r = e.matmul(bank(bk + i)[0:C, :], lhsT=lhs, rhs=S0[:, i * 8:(i + 1) * 8, :].rearrange("p s v -> p (s v)"), start=True, stop=True)
                    return r
                S.op("pe", mma, [lhsb, S0t], [pb[bk], pb[bk + 1]])
                yield
                S.op("dve", lambda e: e.tensor_tensor(out=q["tmp"], in0=psx[0:C], in1=_bc(bm, 2, [C, NS, 64]), op=ALU.mult), [pb[bk], pb[bk + 1], self.cmat], [q["tmpB"]])
                yield
                S.op("dve", lambda e, dst=dst: e.tensor_reduce(out=dst.t[0:C, h * 64:(h + 1) * 64], in_=q["tmpT"], op=ALU.add, axis=AX.X), [q["tmpB"]], [dst])
                yield
            cs = slice(h * 64, (h + 1) * 64)
            S.op("dve", lambda e: e.tensor_tensor(out=U.t[0:C, cs], in0=self.wv.t[0:C, cs], in1=wks.t[0:C, cs], op=ALU.subtract), [self.wv, wks], [U])
            yield
            S.op("pe", lambda e: e.matmul(bank(0)[0:C, cs], lhsT=v3(0)[:, h, :], rhs=U.t[0:C, cs], start=True, stop=True), [big[0], U], [pb[0]])
            yield
            S.op("pool", lambda e: e.tensor_tensor(out=q["Ub"], in0=_bc(U.t[0:C, cs], 1, [C, NS, 64]), in1=_bc(bm, 2, [C, NS, 64]), op=ALU.mult),
                 [U, self.cmat], [q["UbB"]])
            yield

            def mmb(e):
                for i in range(2):
                    r = e.matmul(bank(bk + i)[:, :], lhsT=self.kdec.t[0:C, p * 128:(p + 1) * 128], rhs=q["Ubf"][0:C, i * 512:(i + 1) * 512], start=True, stop=True)
                return r
            S.op("pe", mmb, [self.kdec, q["UbB"]], [pb[bk], pb[bk + 1]])
            yield
            S.op("pool", lambda e: e.tensor_tensor(out=S0[R(h2), :, :], in0=S0[R(h2), :, :], in1=_bc(glbs[R(h2), h, :], 2, [64, NS, 64]), op=ALU.mult),
                 [S0t, self.glb], [S0t])
            yield
            S.op("dve", lambda e: e.tensor_tensor(out=S0[R(h2), :, :], in0=S0[R(h2), :, :], in1=psx[R(h2)], op=ALU.add), [S0t, pb[bk], pb[bk + 1]], [S0t])
            yield
            S.dma(self.gdn_s[:, h, :, :].rearrange("s d v -> d s v"), S0[R(h2), :, :], [S0t], [], is_out=True)

        load(0)
        for p in range(4):
            if p + 1 < 4:
                load(p + 1)
            gens = [head_chain(p, 0), head_chain(p, 1)]
            while gens:
                for g_ in list(gens):
                    if next(g_, "done") == "done":
                        gens.remove(g_)
        S.op("dve", lambda e: e.tensor_tensor(out=h3(o1s.t[0:C, :]), in0=h3(o1s.t[0:C, :]), in1=_bc(s_(40, 48), 2, [C, 8, 64]), op=ALU.mult), [o1s, sc], [o1s])
        S.op("dve", lambda e: e.tensor_tensor(out=self.ogdn.t[0:C, :], in0=o1s.t[0:C, :], in1=bank(0)[0:C, :], op=ALU.add), [o1s, pb[0]], [self.ogdn])
        S0g = bg(0, 2).rearrange("p (s v) -> p s v", s=NS)
        tg = bg(2, 2)[0:C, :].rearrange("p (s v) -> p s v", s=NS)
        tgT = bg(2, 2)[0:C, :].rearrange("p (s v) -> p v s", s=NS)
        Vb = bg(4, 2)[0:C, :].rearrange("p (s v) -> p s v", s=NS)
        ps25 = self.pst.t[:, 2:6, :].rearrange("p b (s v) -> p (b s) v", v=128)
        S0gT, tgB, VbB = [big[0], big[1]], [big[2], big[3]], [big[4], big[5]]
        pbs = [pb[2], pb[3], pb[4], pb[5]]
        for p in range(2):
            for h2 in range(2):
                S.dma(S0g[R(h2), :, :], self.sgla[:, 2 * p + h2, :, :].rearrange("s d v -> d s v"), [], S0gT)
            for h2 in range(2):
                h = 2 * p + h2
                cs = slice(h * 128, (h + 1) * 128)

                def mmq(e, p=p, h2=h2, h=h):
                    for i in range(4):
                        r = e.matmul(bank(2 + i)[0:C, :], lhsT=self.qeTm.t[:, h, 0:C], rhs=S0g[:, i * 4:(i + 1) * 4, :].rearrange("p s v -> p (s v)"), start=True, stop=True)
                    return r
                S.op("pe", mmq, [self.qeTm] + S0gT, pbs)
                S.op("dve", lambda e: e.tensor_tensor(out=tg, in0=ps25[0:C], in1=_bc(bm, 2, [C, NS, 128]), op=ALU.mult), pbs + [self.cmat], tgB)
                S.op("dve", lambda e, cs=cs: e.tensor_reduce(out=self.ogla.t[0:C, cs], in_=tgT, op=ALU.add, axis=AX.X), tgB, [self.ogla])
                S.op("pool", lambda e, cs=cs: e.tensor_tensor(out=Vb, in0=_bc(self.gv_tok.t[0:C, cs], 1, [C, NS, 128]), in1=_bc(bm, 2, [C, NS, 128]), op=ALU.mult),
                     [self.gv_tok, self.cmat], VbB)

                def mmv(e, p=p):
                    for i in range(4):
                        r = e.matmul(bank(2 + i)[:, :], lhsT=self.kd.t[0:C, p * 128:(p + 1) * 128], rhs=bg(4, 2)[0:C, i * 512:(i + 1) * 512], start=True, stop=True)
                    return r
                S.op("pe", mmv, [self.kd] + VbB, pbs)
                ebl = self.ebT.t[R(h2), p, :].rearrange("p (s l) -> p s l", l=LS)[:, :, LS - 1]
                S.op("pool", lambda e, h2=h2, ebl=ebl: e.tensor_tensor(out=S0g[R(h2), :, :], in0=S0g[R(h2), :, :], in1=_bc(ebl, 2, [64, NS, 128]), op=ALU.mult),
                     S0gT + [self.ebT], S0gT)
                S.op("dve", lambda e, h2=h2: e.tensor_tensor(out=S0g[R(h2), :, :], in0=S0g[R(h2), :, :], in1=ps25[R(h2)], op=ALU.add), S0gT + pbs, S0gT)
                S.dma(self.gla_s[:, h, :, :].rearrange("s d v -> d s v"), S0g[R(h2), :, :], S0gT, [], is_out=True)

        def mg2(e):
            for h in range(4):
                r = e.matmul(bank(6)[0:C, h * 128:(h + 1) * 128], lhsT=self.PTg.t[0:C, h, 0:C], rhs=self.gv_tok.t[0:C, h * 128:(h + 1) * 128], start=True, stop=True)
            return r
        S.op("pe", mg2, [self.PTg, self.gv_tok], [pb[6]])
        S.op("dve", lambda e: e.tensor_tensor(out=self.ogla.t[0:C, :], in0=self.ogla.t[0:C, :], in1=bank(6)[0:C, :], op=ALU.add), [self.ogla, pb[6]], [self.ogla])

    def post_mix(self, e0, C, smp):
        S, cm, pb, bank = self.S, self.cm, self.pb, self.bank
        sc, mix, xh = self.sc, self.mix, self.xh
        s_ = lambda a, b_: sc.t[0:C, a:b_]
        if not mix.parts:
            mix.parts = [S.alias("mixA", mix), S.alias("mixB", mix)]
            self.scn = [S.alias("scA", sc), S.alias("scB", sc)]

        def norm_chain(o, nh, dv, col0, gcol, sg, sco, sqb, mixp, scp):
            v = lambda ap: ap.rearrange("p (h d) -> p h d", h=nh)
            yield
            S.op("dve", lambda e, o=o: e.tensor_tensor(out=sqb.t[0:C, :], in0=o.t[0:C, :], in1=o.t[0:C, :], op=ALU.mult), [o], [sqb])
            yield
            S.op("dve", lambda e, v=v, sco=sco, nh=nh: e.tensor_reduce(out=s_(sco, sco + nh), in_=v(sqb.t[0:C, :]), op=ALU.add, axis=AX.X), [sqb], [scp])
            yield
            S.op("act", lambda e, sco=sco, nh=nh, dv=dv: e.activation(out=s_(sco, sco + nh), in_=s_(sco, sco + nh), func=AF.Ln,
                                                                      bias=self.epsc.t[0:C, 1:2], scale=1.0 / dv), [scp, self.epsc], [scp])
            yield
            S.op("act", lambda e, sco=sco, nh=nh: e.activation(out=s_(sco, sco + nh), in_=s_(sco, sco + nh), func=AF.Exp, scale=-0.5), [scp], [scp])
            mv_ = v(mix.t[0:C, col0:col0 + 512])
            yield
            S.op("dve", lambda e, o=o, v=v, mv_=mv_, sco=sco, nh=nh, dv=dv: e.tensor_tensor(out=mv_, in0=v(o.t[0:C, :]), in1=_bc(s_(sco, sco + nh), 2, [C, nh, dv]), op=ALU.mult),
                 [o, scp], [mixp])
            yield
            S.op("pool", lambda e, mv_=mv_, gcol=gcol, nh=nh, dv=dv: e.tensor_tensor(out=mv_, in0=mv_, in1=_bc(self.pvec.t[0:C, gcol:gcol + dv], 1, [C, nh, dv]), op=ALU.mult),
                 [mixp, self.pvec], [mixp])
            yield
            S.op("dve", lambda e, col0=col0, sg=sg: e.tensor_tensor(out=mix.t[0:C, col0:col0 + 512], in0=mix.t[0:C, col0:col0 + 512], in1=sg.t[0:C, :], op=ALU.mult),
                 [mixp, sg], [mixp])
        gens = [norm_chain(self.ogdn, 8, 64, 0, 16, self.sg_gdn, 72, self.otmp, mix.parts[0], self.scn[0]),
                norm_chain(self.ogla, 4, 128, 512, 80, self.sg_gla, 80, self.kdec, mix.parts[1], self.scn[1])]
        while gens:
            for g_ in list(gens):
                if next(g_, 'done') == 'done':
                    gens.remove(g_)

        for half in range(2):
            def tr(e, half=half):
                for j in range(4):
                    kc = half * 4 + j
                    r = e.transpose(bank(half)[:, j * 128:j * 128 + C], mix.t[0:C, kc * 128:(kc + 1) * 128], cm(C_ID, C, C))
                return r
            S.op("pe", tr, [mix, self.cmat], [pb[half]])
            S.op("act", lambda e, half=half: e.activation(out=self.mixT.t[:, half * 4:half * 4 + 4, 0:C],
                                                          in_=bank(half).rearrange("p (j c) -> p j c", j=4)[:, :, 0:C], func=AF.Copy), [pb[half]], [self.mixT])
        for half in range(2):
            def mmo(e, half=half):
                for kc in range(8):
                    r = e.matmul(bank(6 + half)[0:C, :], lhsT=self.mixT.t[:, kc, 0:C], rhs=self.w_out.t[:, kc, half * 512:(half + 1) * 512],
                                 start=(kc == 0), stop=(kc == 7))
                return r
            S.op("pe", mmo, [self.mixT, self.w_out], [pb[6 + half]])
            S.op("dve", lambda e, half=half: e.scalar_tensor_tensor(out=xh.t[0:C, half * 512:(half + 1) * 512], in0=xh.t[0:C, half * 512:(half + 1) * 512],
                                                                     scalar=ALPHA, in1=bank(6 + half)[0:C, :], op0=ALU.mult, op1=ALU.add), [xh, pb[6 + half]], [xh])
        self.layer_norm(xh, C, self.lnbc.t[:, 2, :], self.lnbc.t[:, 3, :], self.epsc.t[0:C, 0:1], "ln1")
        row0 = TP if smp else e0
        S.dma(self.h1s[row0:row0 + C, :], xh.t[0:C, :], [xh], [self.h1scr])

    def conv_state_out(self, C, smp):
        S, cm, pb, bank = self.S, self.cm, self.pb, self.bank
        if smp:
            n = NS * 3
            cst = self.bg(3)[:, 0:12 * n].rearrange("p (a c) -> p a c", a=12)
            S.op("pool", lambda e: e.tensor_copy(out=cst.rearrange("p a (s l) -> p a s l", l=3),
                                                 in_=self.qkvx.t[:, :, 0:NS * 11].rearrange("p a (s l) -> p a s l", l=11)[:, :, :, 8:11]),
                 [self.qkvx], [self.big[3]])
            src = lambda cc: cst[:, cc, :]
            dst = self.gconv_s
        else:
            n = 3
            src = lambda cc: self.qkvx.t[:, cc, 0:3]
            dst = self.gconv_p
        stage = self.bg(1, 2)[0:n, 0:1536]
        for g in range(3):
            def tr(e, g=g):
                for j in range(4):
                    r = e.transpose(bank(2 + g)[0:n, j * 128:(j + 1) * 128], src(g * 4 + j), cm(C_ID))
                return r
            S.op("pe", tr, [self.qkvx, self.big[3], self.cmat], [pb[2 + g]])
            S.op("act", lambda e, g=g: e.activation(out=stage[:, g * 512:(g + 1) * 512], in_=bank(2 + g)[0:n, :], func=AF.Copy), [pb[2 + g]], [self.big[1], self.big[2]])
        S.dma(dst, stage, [self.big[1], self.big[2]], [], is_out=True)

    def stage1(self):
        S = self.S
        self.stage1_alloc()
        self.stage1_setup()
        R = lambda h2: slice(64 * h2, 64 * h2 + 64)
        plist = []
        e0 = 0
        while e0 < TP:
            C = min(128, TP - e0)
            skip = (DBG.get("maxchunks") is not None and e0 // 128 >= DBG["maxchunks"] and C == 128) or (DBG.get("no16") and C == 16)
            if not skip:
                plist.append((e0, C, "p"))
            e0 += C
        plist = [(a, b, c, self.xhs[i % 2]) for i, (a, b, c) in enumerate(plist)]
        self.sample_item = (0, NS * LS, "s", self.xhs[len(plist) % 2])
        self.front0(*plist[0])
        for i, it in enumerate(plist):
            self.chunk(*it, plist[i + 1] if i + 1 < len(plist) else None)
        if DBG.get("stop_early"):
            return
        if DBG.get("stop_after_prompt"):
            return
        for h2 in range(2):
            S.dma(self.gdn_p.rearrange("(a two) d v -> two d a v", two=2)[h2], self.Sg.t[R(h2), :, :], [self.Sg], [], is_out=True)
            S.dma(self.gla_p.rearrange("(a two) d v -> two d a v", two=2)[h2], self.Sl.t[R(h2), :, :], [self.Sl], [], is_out=True)
        if not DBG.get("no_cso"):
            self.conv_state_out(16, False)
        if DBG.get("stop_after_outputs"):
            return
        stc = self.bg(3, 2)[0:NS * 3, 0:1536]
        S.dma(stc, self.sgconv, [], [self.big[3], self.big[4]])
        qs = self.qkvx.t[:, :, 0:NS * 11].rearrange("p a (s l) -> p a s l", l=11)
        for g in range(3):
            def tr(e, g=g):
                for j in range(4):
                    cc = g * 4 + j
                    r = e.transpose(self.bank(2 + g)[:, j * 128:j * 128 + NS * 3], stc[:, cc * 128:(cc + 1) * 128], self.cm(C_ID, NS * 3, NS * 3))
                return r
            S.op("pe", tr, [self.big[3], self.big[4], self.cmat], [self.pb[2 + g]])
            S.op("act", lambda e, g=g: e.activation(out=qs[:, g * 4:(g + 1) * 4, :, 0:3],
                                                    in_=self.bank(2 + g).rearrange("p (j c) -> p j c", j=4)[:, :, 0:NS * 3].rearrange("p j (s l) -> p j s l", l=3),
                                                    func=AF.Copy), [self.pb[2 + g]], [self.qkvx])
        self.front0(*self.sample_item)
        self.chunk(*self.sample_item, None)
        self.conv_state_out(NS * LS, True)

    def stage2(self):
        S, cm, pb, bank = self.S, self.cm, self.pb, self.bank
        self.lnbc = S.sb("lnbc2", [128, 2, D], F32)
        w_up = S.sb("w_up_sb", [128, 8, 2 * DFF], BF16)
        w_dn = S.sb("w_dn", [128, 22, D], BF16)
        cwf = S.sb("cwf_sb", [128, NFF, 4], F32)
        h1t = [S.sb(f"h1t{i}", [128, D], F32) for i in range(2)]
        h1T = S.sb("h1T", [128, 8, 512], BF16)
        ue = [S.sb(f"ue{i}", [128, 160], F32) for i in range(2)]
        accs = [[S.sb(f"acc{w}{i}", [128, 512], F32) for i in range(2)] for w in range(2)]
        sgt1 = S.sb("sgt", [128, 512], F32)
        sgt = [sgt1, sgt1]
        hc = S.sb("hc", [128, NFF, 4], F32)
        uh = [S.sb(f"uh{i}", [128, NFF, 2], F32) for i in range(2)]
        actT = S.sb("actT", [128, 22, 512], BF16)
        usave = S.sb("usave", [128, NFF, 32], F32)
        st32 = S.sb("st32", [128, 512], F32)
        for i in range(2):
            S.dma(self.lnbc.t[:, i, :], self.lnv_d[4 + i:5 + i, :].partition_broadcast(128), [], [self.lnbc])
        S.dma(cwf.t[:], self.cwf_d, [], [cwf])
        wu_ = self.w_up_d.rearrange("(kc p) n -> p kc n", p=128)
        wd_ = self.w_down_d.rearrange("(j p) n -> p j n", p=128)
        w_up_tb = {}
        wdma = []
        for g in range(3):
            for which in range(2):
                c0 = which * DFF + g * 1024
                c1 = min(c0 + 1024, (which + 1) * DFF)
                tb = S.alias(f"w_up_{which}_{g}", w_up)
                for gg in (2 * g, 2 * g + 1):
                    w_up_tb[(which, gg)] = tb
                for kc in range(0, 8, 4):
                    wdma.append((lambda kc=kc, c0=c0, c1=c1, tb=tb: S.dma(w_up.t[:, kc:kc + 4, c0:c1], wu_[:, kc:kc + 4, c0:c1], [], [tb], cast=True)))
        w_dn_tb = {}
        wddma = []
        for j in range(0, 22, 2):
            tb = S.alias(f"w_dn_{j}", w_dn)
            w_dn_tb[j] = tb
            w_dn_tb[j + 1] = tb
            wddma.append((lambda j=j, tb=tb: S.dma(w_dn.t[:, j:j + 2, :], wd_[:, j:j + 2, :], [], [tb], cast=True)))
        for _ in range(4):
            wdma.pop(0)()
        S.op("pool", lambda e: e.memset(uh[0].t[:], 0.0), [], [uh[0]])

        def conv_out(n, dst, src):
            for g in range(11):
                def tr(e, g=g):
                    for j in range(4):
                        r = e.transpose(bank(g % 4)[0:n, j * 128:(j + 1) * 128], src.t[:, g * 4 + j, 0:n], cm(C_ID))
                    return r
                S.op("pe", tr, [src, self.cmat], [pb[g % 4]])
                S.op("act", lambda e, g=g: e.activation(out=st32.t[0:n, :], in_=bank(g % 4)[0:n, :], func=AF.Copy), [pb[g % 4]], [st32])
                S.dma(dst[:, g * 512:(g + 1) * 512], st32.t[0:n, :], [st32], [], is_out=True)

        tiles = []
        e0 = 0
        while e0 < TP:
            W = min(512, TP - e0)
            tiles.append((e0, W, False))
            e0 += W
        tiles.append((TP, NS * LS, True))
        upb = 0
        for ti, (r0, W, smp) in enumerate(tiles):
            G, L = (NS, LS) if smp else (1, W)
            uprev, ucur = uh[ti % 2], uh[(ti + 1) % 2]
            if not smp and not DBG.get("nohc"):
                S.op("pool", lambda e, uprev=uprev: e.tensor_tensor(out=hc.t[:, :, 0:2], in0=uprev.t[:, :, :], in1=cwf.t[:, :, 0:1].to_broadcast([128, NFF, 2]), op=ALU.mult),
                     [uprev, cwf], [hc])
                S.op("pool", lambda e, uprev=uprev: e.tensor_tensor(out=hc.t[:, :, 2:3], in0=uprev.t[:, :, 1:2], in1=cwf.t[:, :, 1:2], op=ALU.mult),
                     [uprev, cwf, hc], [hc])
            if smp:
                for g in range(11):
                    S.dma(st32.t[0:NS * 2, :], self.sfconv[:, g * 512:(g + 1) * 512], [], [st32])

                    def tr(e, g=g):
                        for j in range(4):
                            r = e.transpose(bank(4 + g % 2)[:, j * 32:(j + 1) * 32], st32.t[0:NS * 2, j * 128:(j + 1) * 128], cm(C_ID, NS * 2, NS * 2))
                        return r
                    S.op("pe", tr, [st32, self.cmat], [pb[4 + g % 2]])
                    S.op("act", lambda e, g=g: e.activation(out=usave.t[:, g * 4:(g + 1) * 4, :],
                                                            in_=bank(4 + g % 2)[:, 0:128].rearrange("p (j c) -> p j c", j=4), func=AF.Copy), [pb[4 + g % 2]], [usave])
            nsub = (W + 127) // 128
            for j in range(nsub):
                C = min(128, W - j * 128)
                ht = h1t[j % 2]
                S.dma(ht.t[0:C, :], self.h1s[r0 + j * 128:r0 + j * 128 + C, :], [self.h1scr], [ht])
                for half in range(2):
                    def tr(e, half=half, ht=ht, C=C):
                        for q in range(4):
                            r = e.transpose(bank(6 + half)[:, q * 128:q * 128 + C], ht.t[0:C, (half * 4 + q) * 128:(half * 4 + q + 1) * 128], cm(C_ID, C, C))
                        return r
                    S.op("pe", tr, [ht, self.cmat], [pb[6 + half]])
                    S.op("act", lambda e, half=half, j=j, C=C: e.activation(out=h1T.t[:, half * 4:half * 4 + 4, j * 128:j * 128 + C],
                                                                            in_=bank(6 + half).rearrange("p (q c) -> p q c", q=4)[:, :, 0:C], func=AF.Copy),
                         [pb[6 + half]], [h1T])
            pend = None
            for jg in range(22):
                par = jg % 2
                if jg % 8 == 1:
                    for _ in range(4):
                        if wdma:
                            wdma.pop(0)()
                if wddma and jg % 2 == 0:
                    wddma.pop(0)()
                if pend is not None:
                    pend()
                for which in (0, 1):
                    acc = accs[which][par]
                    cc = jg + 22 * which
                    b = upb % 4
                    upb += 1
                    wtb = w_up_tb[(which, jg // 4)]

                    def mmu(e, cc=cc, b=b):
                        for kc in range(8):
                            r = e.matmul(bank(b)[:, 0:W], lhsT=w_up.t[:, kc, cc * 128:(cc + 1) * 128], rhs=h1T.t[:, kc, 0:W], start=(kc == 0), stop=(kc == 7))
                        return r
                    S.op("pe", mmu, [wtb, h1T], [pb[b]])
                    if smp:
                        u = ue[cc % 2]
                        uv = u.t[:, 0:G * (L + 2)].rearrange("p (g l) -> p g l", g=G)
                        S.op("act", lambda e, b=b, uv=uv: e.activation(out=uv[:, :, 2:2 + L], in_=bank(b)[:, 0:W].rearrange("p (g l) -> p g l", g=G), func=AF.Copy),
                             [pb[b]], [u])
                        hsrc = usave.t[:, cc, :].rearrange("p (s i) -> p s i", i=2)
                        S.op("pool", lambda e, uv=uv, hsrc=hsrc: e.tensor_copy(out=uv[:, :, 0:2], in_=hsrc), [usave, u], [u])
                        av = acc.t[:, 0:W].rearrange("p (g l) -> p g l", g=G)
                        S.op("act", lambda e, uv=uv, av=av, cc=cc: e.activation(out=av, in_=uv[:, :, 2:2 + L], func=AF.Identity,
                                                                                scale=cwf.t[:, cc, 2:3], bias=cwf.t[:, cc, 3:4]), [u, cwf], [acc])
                        S.op("dve", lambda e, uv=uv, av=av, cc=cc: e.scalar_tensor_tensor(out=av, in0=uv[:, :, 1:1 + L], scalar=cwf.t[:, cc, 1:2], in1=av,
                                                                                           op0=ALU.mult, op1=ALU.add), [u, cwf, acc], [acc])
                        S.op("dve", lambda e, uv=uv, av=av, cc=cc: e.scalar_tensor_tensor(out=av, in0=uv[:, :, 0:L], scalar=cwf.t[:, cc, 0:1], in1=av,
                                                                                           op0=ALU.mult, op1=ALU.add), [u, cwf, acc], [acc])
                        hdst = usave.t[:, cc, :].rearrange("p (s i) -> p s i", i=2)
                        S.op("pool", lambda e, uv=uv, hdst=hdst: e.tensor_copy(out=hdst, in_=uv[:, :, L:L + 2]), [u, usave], [usave])
                    else:
                        S.op("act", lambda e, cc=cc, b=b: e.activation(out=acc.t[:, 0:W], in_=bank(b)[:, 0:W], func=AF.Identity,
                                                                       scale=cwf.t[:, cc, 2:3], bias=cwf.t[:, cc, 3:4]), [pb[b], cwf], [acc])
                        if not DBG.get("noact2"):
                            S.op("dve", lambda e, cc=cc, b=b, ucur=ucur: e.tensor_copy(out=ucur.t[:, cc, 0:2], in_=bank(b)[:, W - 2:W]), [pb[b], acc], [ucur])
                        if not DBG.get("nostt"):
                            S.op("dve", lambda e, cc=cc, b=b, acc=acc: e.scalar_tensor_tensor(out=acc.t[:, 1:W], in0=bank(b)[:, 0:W - 1], scalar=cwf.t[:, cc, 1:2], in1=acc.t[:, 1:W],
                                                                                     op0=ALU.mult, op1=ALU.add), [pb[b], cwf, acc], [acc])
                            S.op("dve", lambda e, cc=cc, b=b, acc=acc: e.scalar_tensor_tensor(out=acc.t[:, 2:W], in0=bank(b)[:, 0:W - 2], scalar=cwf.t[:, cc, 0:1], in1=acc.t[:, 2:W],
                                                                                     op0=ALU.mult, op1=ALU.add), [pb[b], cwf, acc], [acc])
                        if not DBG.get("nopool"):
                            S.op("pool", lambda e, cc=cc, acc=acc: e.tensor_tensor(out=acc.t[:, 0:2], in0=acc.t[:, 0:2], in1=hc.t[:, cc, 0:2], op=ALU.add), [acc, hc], [acc])
                            S.op("pool", lambda e, cc=cc, acc=acc: e.tensor_tensor(out=acc.t[:, 0:1], in0=acc.t[:, 0:1], in1=hc.t[:, cc, 2:3], op=ALU.add), [acc, hc], [acc])
                def fin(jg=jg, par=par):
                    S.op("act", lambda e: e.activation(out=sgt[par].t[:, 0:W], in_=accs[0][par].t[:, 0:W], func=AF.Silu), [accs[0][par]], [sgt[par]])
                    S.op("pool", lambda e: e.tensor_tensor(out=actT.t[:, jg, 0:W], in0=sgt[par].t[:, 0:W], in1=accs[1][par].t[:, 0:W], op=ALU.mult),
                         [sgt[par], accs[1][par]], [actT])
                pend = fin
            pend()
            for j in range(nsub):
                C = min(128, W - j * 128)
                ht = h1t[j % 2]
                S.dma(ht.t[0:C, :], self.h1s[r0 + j * 128:r0 + j * 128 + C, :], [self.h1scr], [ht])
                for half in range(2):
                    def mmd(e, half=half, j=j, C=C):
                        for jg in range(22):
                            r = e.matmul(bank(4 + 2 * (j % 2) + half)[0:C, :], lhsT=actT.t[:, jg, j * 128:j * 128 + C], rhs=w_dn.t[:, jg, half * 512:(half + 1) * 512],
                                         start=(jg == 0), stop=(jg == 21))
                        return r
                    S.op("pe", mmd, [actT] + [w_dn_tb[j_] for j_ in range(0, 22, 2)], [pb[4 + 2 * (j % 2) + half]])
                    S.op("dve", lambda e, half=half, ht=ht, C=C: e.scalar_tensor_tensor(out=ht.t[0:C, half * 512:(half + 1) * 512], in0=ht.t[0:C, half * 512:(half + 1) * 512],
                                                                                        scalar=ALPHA, in1=bank(4 + 2 * (j % 2) + half)[0:C, :], op0=ALU.mult, op1=ALU.add),
                         [ht, pb[4 + 2 * (j % 2) + half]], [ht])
                self.layer_norm(ht, C, self.lnbc.t[:, 0, :], self.lnbc.t[:, 1, :], self.epsc.t[0:C, 0:1], "ln2")
                if smp:
                    S.dma(self.y_s, ht.t[0:C, :], [ht], [], is_out=True)
                else:
                    e = r0 + j * 128
                    if e == 0:
                        S.dma(self.y_p[0:128 - NMETA, :], ht.t[NMETA:128, :], [ht], [], is_out=True)
                    else:
                        S.dma(self.y_p[e - NMETA:e - NMETA + C, :], ht.t[0:C, :], [ht], [], is_out=True)
            if (not smp) and r0 + W == TP:
                conv_out(2, self.fconv_p, ucur)
        conv_out(NS * 2, self.fconv_s, usave)

    def build(self):
        S = self.S
        with S:
            self.cmat = S.sb("cmat_sb", [128, NCM, 128], F32)
            self.epsc = S.sb("epsc", [128, 4], F32)
            self.ln_st = S.sb("ln_st", [128, 2, 6], F32)
            self.ln_mv = S.sb("ln_mv", [128, 4], F32)
            self.glb = S.sb("glb", [128, 128], F32)
            self.pst = S.ps("pst", [128, 8, 512], F32)
            self.pb = [S.alias(f"pb{i}", self.pst) for i in range(8)]
            self.h1scr = TB("h1scr", None)
            main_stack = S.stack
            S.stack = ExitStack()
            with S.stack:
                if not DBG.get("skip1"):
                    self.stage1()
                S.barrier()
            S.stack = ExitStack()
            with S.stack:
                if not DBG.get("skip2"):
                    self.stage2()
                S.finish()
            S.stack = main_stack
        return self.nc


_PROG = None


def _program():
    global _PROG
    if _PROG is None:
        _PROG = K().build()
    return _PROG


def kernel(x_prompt, x_sample, state_gdn, state_gdn_conv, state_gla, state_ffn_conv, meta_tokens,
           ln_in_g, ln_in_b, w_in, gdn_conv_w, gdn_A_log, gdn_dt_bias, gdn_norm_g, gla_wgk2,
           gla_bgk, gla_norm_g, w_out, ln1_g, ln1_b, w_up, ffn_conv_w, ffn_conv_b, w_down,
           ln2_g, ln2_b):
    f = lambda a: np.ascontiguousarray(np.asarray(a, dtype=np.float32))
    x_prompt, x_sample = f(x_prompt), f(x_sample)
    w_in0 = f(w_in)[0]
    fm_cols = np.r_[0:1536, 2064:2320, 2320:2576, 3088:3104]
    tm_cols = np.r_[1536:1552, 2320:2576, 1552:2064, 2576:3088, 3104:3616]
    w_in_r = np.ascontiguousarray(w_in0[:, np.r_[fm_cols, tm_cols]])
    lnv = np.stack([f(ln_in_g), f(ln_in_b), f(ln1_g)[0], f(ln1_b)[0], f(ln2_g)[0], f(ln2_b)[0]])
    cwg = np.ascontiguousarray(f(gdn_conv_w)[0].T.reshape(12, 128, 4).transpose(1, 0, 2))
    cwf4 = np.concatenate([f(ffn_conv_w)[0], f(ffn_conv_b)], axis=0)
    cwf = np.ascontiguousarray(cwf4.T.reshape(NFF, 128, 4).transpose(1, 0, 2))
    pvec = np.concatenate([f(gdn_A_log)[0], f(gdn_dt_bias)[0], f(gdn_norm_g)[0], f(gla_norm_g)[0], f(gla_bgk)[0]])[None, :]
    shared = dict(meta=f(meta_tokens), cmat=_const_mats(), w_in_r=w_in_r, w_out=f(w_out)[0], w_up=f(w_up)[0], w_down=f(w_down)[0],
                  lnv=np.ascontiguousarray(lnv), cwg=cwg, cwf=cwf, pvec=np.ascontiguousarray(pvec), wgk2=f(gla_wgk2)[0])
    sg, sgc, sl, sfc = f(state_gdn)[0], f(state_gdn_conv)[0], f(state_gla)[0], f(state_ffn_conv)[0]
    in_maps = []
    for c in range(8):
        sl_ = slice(c * NS, (c + 1) * NS)
        m = dict(shared)
        m.update(xp=x_prompt[c], xs=np.ascontiguousarray(x_sample[sl_].reshape(NS * LS, D)), sgdn=sg[sl_],
                 sgconv=np.ascontiguousarray(sgc[sl_].reshape(NS * 3, 1536)), sgla=sl[sl_],
                 sfconv=np.ascontiguousarray(sfc[sl_].reshape(NS * 2, 2 * DFF)))
        in_maps.append(m)
    ncr = DBG.get("ncores", 8)
    res = run_bass_kernel_spmd(_program(), in_maps[:ncr], core_ids=list(range(ncr)))
    r = res.results
    cat = lambda k: np.stack([np.asarray(r[min(c, ncr - 1)][k]) for c in range(8)])
    y_prompt = cat("y_p")
    y_sample = cat("y_s").reshape(128, LS, D)
    gdn_p = cat("gdn_p")[None]
    gconv_p = cat("gconv_p")[None]
    gla_p = cat("gla_p")[None]
    fconv_p = cat("fconv_p")[None]
    gdn_s = cat("gdn_s").reshape(1, 128, 8, 64, 64)
    gconv_s = cat("gconv_s").reshape(1, 128, 3, 1536)
    gla_s = cat("gla_s").reshape(1, 128, 4, 64, 128)
    fconv_s = cat("fconv_s").reshape(1, 128, 2, 2 * DFF)
    outs = (y_prompt, y_sample, gdn_p, gconv_p, gla_p, fconv_p, gdn_s, gconv_s, gla_s, fconv_s)
    return tuple(np.ascontiguousarray(o, dtype=np.float32) for o in outs)
```

```python
from contextlib import ExitStack

import numpy as np
import concourse.bass as bass
import concourse.mybir as mybir
from concourse.bass_utils import run_bass_kernel_spmd

F32 = mybir.dt.float32
BF16 = mybir.dt.bfloat16
F32R = mybir.dt.float32r
AF = mybir.ActivationFunctionType
ALU = mybir.AluOpType
AX = mybir.AxisListType


class TB:
    def __init__(self, name, t):
        self.name = name
        self.t = t
        self.last_w = None
        self.readers = {}
        self.parts = []


class Sched:
    def __init__(self, nc, nslots=40):
        self.nc = nc
        self.stack = ExitStack()
        self.engs = {"pe": nc.tensor, "dve": nc.vector, "act": nc.scalar, "pool": nc.gpsimd, "sp": nc.sync}
        self.nslots = nslots

    def __enter__(self):
        self.stack.__enter__()
        nc = self.nc
        self.sem = {k: self.stack.enter_context(nc.semaphore(f"s_{k}")) for k in self.engs}
        self.cnt = {k: 0 for k in self.engs}
        self.waited = {k: {} for k in self.engs}
        self.slot_sem = [self.stack.enter_context(nc.semaphore(f"d_{i}")) for i in range(self.nslots)]
        self.slot_cnt = [0] * self.nslots
        self.next_slot = {"sp": 0, "pool": 0}
        self.out_deps = []
        self.nbuf = 0
        return self

    def __exit__(self, *a):
        return self.stack.__exit__(*a)

    def sb(self, name, shape, dtype):
        t = self.stack.enter_context(self.nc.sbuf_tensor(name, list(shape), dtype))
        return TB(name, t)

    def ps(self, name, shape, dtype):
        t = self.stack.enter_context(self.nc.psum_tensor(name, list(shape), dtype))
        return TB(name, t)

    def alias(self, name, tb):
        return TB(name, tb.t)

    def _semof(self, key):
        if isinstance(key, tuple):
            return self.slot_sem[key[1]]
        return self.sem[key]

    def _wait(self, eng, dep):
        key, val = dep
        if eng == "pe" and key == "pe":
            return
        w = self.waited[eng]
        if w.get(key, 0) >= val:
            return
        self.engs[eng].wait_ge(self._semof(key), val)
        w[key] = val

    @staticmethod
    def _expand(bufs):
        out = []
        for b in bufs:
            out.append(b)
            out.extend(b.parts)
        return out

    def _deps(self, reads, writes):
        reads, writes = self._expand(reads), self._expand(writes)
        deps = set()
        for b in reads:
            if b.last_w is not None:
                deps.add(b.last_w)
        for b in writes:
            if b.last_w is not None:
                deps.add(b.last_w)
            for k, v in b.readers.items():
                deps.add((k, v))
        return deps

    def _commit(self, me, reads, writes):
        reads, writes = self._expand(reads), self._expand(writes)
        for b in writes:
            b.last_w = me
            b.readers = {}
        for b in reads:
            if b not in writes:
                b.readers[me[0]] = max(b.readers.get(me[0], 0), me[1])

    def op(self, eng, emit, reads=(), writes=()):
        for d in sorted(self._deps(reads, writes), key=str):
            self._wait(eng, d)
        inst = emit(self.engs[eng])
        self.cnt[eng] += 1
        inst.then_inc(self.sem[eng], 1)
        self._commit((eng, self.cnt[eng]), reads, writes)

    def dma(self, out, in_, reads=(), writes=(), cast=False, is_out=False, q=None):
        eng = q or ("pool" if cast else "sp")
        nsp = (self.nslots * 5) // 8
        lo, n = (0, nsp) if eng == "sp" else (nsp, self.nslots - nsp)
        i = lo + self.next_slot[eng]
        self.next_slot[eng] = (self.next_slot[eng] + 1) % n
        if self.slot_cnt[i] > 0:
            self._wait(eng, (("slot", i), 16 * self.slot_cnt[i]))
        for d in sorted(self._deps(reads, writes), key=str):
            self._wait(eng, d)
        inst = self.engs[eng].dma_start(out=out, in_=in_)
        inst.then_inc(self.slot_sem[i], 16)
        self.slot_cnt[i] += 1
        me = (("slot", i), 16 * self.slot_cnt[i])
        self._commit(me, reads, writes)
        if is_out:
            self.out_deps.append(me)

    def finish(self):
        for d in self.out_deps:
            self._wait("sp", d)
        for k in ("pe", "dve", "act", "pool"):
            if self.cnt[k] > 0:
                self._wait("sp", (k, self.cnt[k]))

    def barrier(self):
        deps = [(k, self.cnt[k]) for k in ("pe", "dve", "act", "pool") if self.cnt[k] > 0]
        deps += [(("slot", i), 16 * c) for i, c in enumerate(self.slot_cnt) if c > 0]
        for e in ("pe", "dve", "act", "pool", "sp"):
            for d in deps:
                if d[0] != e:
                    self._wait(e, d)


DBG = {}
D = 1024
SEQ = 2048
NMETA = 16
TP = SEQ + NMETA
NS = 16
LS = 8
DFF = 2816
NFF = 44
ALPHA = 2.0 ** 0.25
NFM = 2064
NTM = 1808
NEG = -30000.0

C_ID, C_ONE, C_BO64 = 0, 1, 2
C_PU, C_PSU, C_PMBT, C_PMBS, C_PM01T = 3, 4, 5, 6, 7
C_SU, C_SSU, C_SBO, C_SMBT, C_SMBS, C_SM01T, C_SBM = 8, 9, 10, 11, 12, 13, 14
NCM = 15


def _const_mats():
    m = np.zeros((NCM, 128, 128), np.float32)
    k = np.arange(128)[:, None]
    c = np.arange(128)[None, :]
    m[C_ID] = (k == c)
    m[C_ONE] = 1.0
    m[C_BO64] = (k // 64 == c // 64)
    m[C_PU] = (k <= c)
    m[C_PSU] = (k > c)
    m[C_PMBT] = np.where(c >= k, 0.0, NEG)
    m[C_PMBS] = np.where(c < k, 0.0, NEG)
    m[C_PM01T] = (c >= k)
    sb = (k // LS == c // LS)
    m[C_SU] = (k <= c) & sb
    m[C_SSU] = (k > c) & sb
    m[C_SBO] = sb
    m[C_SMBT] = np.where((c >= k) & sb, 0.0, NEG)
    m[C_SMBS] = np.where((c < k) & sb, 0.0, NEG)
    m[C_SM01T] = (c >= k) & sb
    m[C_SBM][:, :NS] = (k // LS == np.arange(NS)[None, :])
    return np.ascontiguousarray(m.transpose(1, 0, 2))


def _bc(ap, axis, shape):
    return ap.unsqueeze(axis).to_broadcast(list(shape))


class K:
    def __init__(self):
        nc = self.nc = bass.Bass("TRN2", target_bir_lowering=False)
        di = lambda n, s: nc.dram_tensor(n, list(s), F32, kind="ExternalInput").ap()
        do = lambda n, s: nc.dram_tensor(n, list(s), F32, kind="ExternalOutput").ap()
        self.xp = di("xp", [SEQ, D]); self.xs = di("xs", [NS * LS, D]); self.meta = di("meta", [NMETA, D])
        self.sgdn = di("sgdn", [NS, 8, 64, 64]); self.sgconv = di("sgconv", [NS * 3, 1536])
        self.sgla = di("sgla", [NS, 4, 64, 128]); self.sfconv = di("sfconv", [NS * 2, 2 * DFF])
        self.cmat_d = di("cmat", [128, NCM, 128])
        self.w_in_d = di("w_in_r", [D, NFM + NTM]); self.w_out_d = di("w_out", [D, D])
        self.w_up_d = di("w_up", [D, 2 * DFF]); self.w_down_d = di("w_down", [DFF, D])
        self.lnv_d = di("lnv", [6, D]); self.cwg_d = di("cwg", [128, 12, 4]); self.cwf_d = di("cwf", [128, NFF, 4])
        self.pvec_d = di("pvec", [1, 464]); self.wgk2_d = di("wgk2", [16, 256])
        self.y_p = do("y_p", [SEQ, D]); self.y_s = do("y_s", [NS * LS, D])
        self.gdn_p = do("gdn_p", [8, 64, 64]); self.gconv_p = do("gconv_p", [3, 1536])
        self.gla_p = do("gla_p", [4, 64, 128]); self.fconv_p = do("fconv_p", [2, 2 * DFF])
        self.gdn_s = do("gdn_s", [NS, 8, 64, 64]); self.gconv_s = do("gconv_s", [NS * 3, 1536])
        self.gla_s = do("gla_s", [NS, 4, 64, 128]); self.fconv_s = do("fconv_s", [NS * 2, 2 * DFF])
        self.h1s = nc.dram_tensor("h1s", [TP + NS * LS, D], F32, kind="Internal").ap()
        self.S = Sched(nc)

    def cm(self, idx, r=128, c=128):
        return self.cmat.t[0:r, idx, 0:c]

    def bank(self, b, n=1):
        if n == 1:
            return self.pst.t[:, b, :]
        return self.pst.t[:, b:b + n, :].rearrange("p b f -> p (b f)")

    def layer_norm(self, buf, C, g_ap, b_ap, eps_tile, tag):
        S = self.S
        st, mv = self.ln_st, self.ln_mv
        x = buf.t

        def stats(e):
            e.bn_stats(out=st.t[0:C, 0, :], in_=x[0:C, 0:512])
            return e.bn_stats(out=st.t[0:C, 1, :], in_=x[0:C, 512:1024])
        S.op("dve", stats, [buf], [st])
        S.op("dve", lambda e: e.bn_aggr(out=mv.t[0:C, 0:2], in_=st.t[0:C, :, :].rearrange("p a b -> p (a b)")), [st], [mv])
        S.op("act", lambda e: e.activation(out=mv.t[0:C, 2:3], in_=mv.t[0:C, 1:2], func=AF.Ln, bias=eps_tile, scale=1.0), [mv, self.epsc], [mv])
        S.op("act", lambda e: e.activation(out=mv.t[0:C, 3:4], in_=mv.t[0:C, 2:3], func=AF.Exp, scale=-0.5), [mv], [mv])
        S.op("dve", lambda e: e.tensor_scalar(out=x[0:C, :], in0=x[0:C, :], scalar1=mv.t[0:C, 0:1], scalar2=mv.t[0:C, 3:4],
                                              op0=ALU.subtract, op1=ALU.mult), [buf, mv], [buf])
        S.op("dve", lambda e: e.tensor_tensor(out=x[0:C, :], in0=x[0:C, :], in1=g_ap[0:C, :], op=ALU.mult), [buf, self.lnbc], [buf])
        S.op("dve", lambda e: e.tensor_tensor(out=x[0:C, :], in0=x[0:C, :], in1=b_ap[0:C, :], op=ALU.add), [buf, self.lnbc], [buf])

    def stage1_alloc(self):
        S = self.S
        self.lnbc = S.sb("lnbc", [128, 4, D], F32)
        self.pvec = S.sb("pvec_sb", [128, 464], F32)
        self.negA = S.sb("negA", [128, 8], F32)
        self.wgk2 = S.sb("wgk2_sb", [16, 256], F32)
        self.cwg = S.sb("cwg_sb", [128, 12, 4], F32)
        self.w_in = S.sb("w_in_sb", [128, 8, NFM + NTM], BF16)
        self.w_out = S.sb("w_out_sb", [128, 8, D], BF16)
        self.xhs = [S.sb(f"xh{i}", [128, D], F32) for i in range(2)]
        self.hT = S.sb("hT", [128, 8, 128], BF16)
        self.qkvx = S.sb("qkvx", [128, 12, 176], F32)
        self.fmx = S.sb("fmx", [128, 5, 128], F32)
        self.tm0 = S.sb("tm0", [128, 272], F32)
        self.gv_tok = S.sb("gv_tok", [128, 512], F32)
        self.sg_gdn = S.sb("sg_gdn", [128, 512], F32)
        self.sg_gla = S.sb("sg_gla", [128, 512], F32)
        self.bigs = S.sb("bigs", [128, 6, 1024], F32)
        self.big = [S.alias(f"big{i}", self.bigs) for i in range(6)]
        self.Pc = S.sb("Pc", [128, 1024], F32)
        self.PTc = S.sb("PTc", [128, 1024], F32)
        self.TTc = S.sb("TTc", [128, 1024], F32)
        for tb in (self.Pc, self.PTc, self.TTc):
            tb.parts = [S.alias(f"{tb.name}_hg{g}", tb) for g in range(2)]
        self.kq = S.sb("kq", [128, 4, 2, 128], F32)
        self.wkT = S.sb("wkT", [128, 8, 128], F32)
        self.KTm = self.wkT
        self.QTm = S.sb("QTm", [128, 8, 128], F32)
        self.keTm = S.sb("keTm", [128, 4, 128], F32)
        self.qeTm = S.sb("qeTm", [128, 4, 128], F32)
        self.wv = S.sb("wv", [128, 512], F32)
        self.RK = S.sb("RK", [128, 512], F32)
        self.RV = S.sb("RV", [128, 512], F32)
        self.kdec = S.sb("kdec", [128, 512], F32)
        self.U = self.RV
        self.ogdn = self.RK
        self.Sg = S.sb("Sg", [128, 4, 64], F32)
        self.sc = S.sb("sc", [128, 96], F32)
        self.lt = TB("lt", self.wv.t[:, 0:256])
        self.lt.parts = [self.wv]
        self.ebT = S.sb("ebT", [128, 2, 128], F32)
        self.enbT = S.sb("enbT", [128, 2, 128], F32)
        self.qeT = S.sb("qeT", [128, 2, 128], F32)
        self.PTg = S.sb("PTg", [128, 4, 128], F32)
        self.kd = TB("kd", self.enbT.t[:, :, :].rearrange("p a c -> p (a c)"))
        self.kd.parts = [self.enbT]
        self.ogla = S.sb("ogla", [128, 512], F32)
        self.Sl = S.sb("Sl", [128, 2, 128], F32)
        self.mix = S.sb("mix", [128, D], F32)
        self.mixT = S.sb("mixT", [128, 8, 128], BF16)
        self.otmp = TB("otmp", self.bigs.t[:, 3, 0:512])
        self.otmp.parts = [self.big[3]]

    def bg(self, i, n=1):
        if n == 1:
            return self.bigs.t[:, i, :]
        return self.bigs.t[:, i:i + n, :].rearrange("p b f -> p (b f)")

    def stage1_setup(self):
        S = self.S
        nc = self.nc
        S.dma(self.cmat.t[:], self.cmat_d, [], [self.cmat])
        for i in range(4):
            S.dma(self.lnbc.t[:, i, :], self.lnv_d[i:i + 1, :].partition_broadcast(128), [], [self.lnbc])
        S.dma(self.pvec.t[:], self.pvec_d[0:1, :].partition_broadcast(128), [], [self.pvec])
        S.dma(self.wgk2.t[:], self.wgk2_d, [], [self.wgk2])
        S.dma(self.cwg.t[:], self.cwg_d, [], [self.cwg])
        S.op("pool", lambda e: e.memset(self.epsc.t[:, 0:1], 1e-5), [], [self.epsc])
        S.op("pool", lambda e: e.memset(self.epsc.t[:, 1:2], 1e-6), [self.epsc], [self.epsc])
        S.op("pool", lambda e: e.memset(self.epsc.t[:, 2:3], 1.0), [self.epsc], [self.epsc])
        S.op("pool", lambda e: e.memset(self.epsc.t[:, 3:4], 0.0), [self.epsc], [self.epsc])
        wv_ = self.w_in_d.rearrange("(kc p) n -> p kc n", p=128)
        for kc in range(8):
            S.dma(self.w_in.t[:, kc, :], wv_[:, kc, :], [], [self.w_in], cast=True)
        wo_ = self.w_out_d.rearrange("(kc p) n -> p kc n", p=128)
        for kc in range(0, 8, 4):
            S.dma(self.w_out.t[:, kc:kc + 4, :], wo_[:, kc:kc + 4, :], [], [self.w_out], cast=True)
        S.op("act", lambda e: e.activation(out=self.negA.t[:], in_=self.pvec.t[:, 0:8], func=AF.Exp), [self.pvec], [self.negA])
        S.op("dve", lambda e: e.tensor_scalar(out=self.negA.t[:], in0=self.negA.t[:], scalar1=-1.0, scalar2=None, op0=ALU.mult),
             [self.negA], [self.negA])
        S.op("pool", lambda e: e.memset(self.Sg.t[:], 0.0), [], [self.Sg])
        S.op("pool", lambda e: e.memset(self.Sl.t[:], 0.0), [], [self.Sl])
        S.op("pool", lambda e: e.memset(self.qkvx.t[:], 0.0), [], [self.qkvx])
        for tb in (self.wkT, self.QTm, self.keTm, self.qeTm):
            S.op("pool", lambda e, tb=tb: e.memset(tb.t[:], 0.0), [], [tb])

    def front0(self, e0, C, kind, xh):
        self.front0_load(e0, C, kind, xh)
        self.front0_compute(e0, C, kind, xh)

    def front0_load(self, e0, C, kind, xh):
        S = self.S
        smp = kind == "s"
        if smp:
            S.dma(xh.t[0:C, :], self.xs, [], [xh])
        elif e0 == 0:
            S.dma(xh.t[0:NMETA, :], self.meta, [], [xh])
            S.dma(xh.t[NMETA:128, :], self.xp[0:128 - NMETA, :], [], [xh])
        else:
            S.dma(xh.t[0:C, :], self.xp[e0 - NMETA:e0 - NMETA + C, :], [], [xh])

    def front0_compute(self, e0, C, kind, xh):
        S, cm, pb, bank, hT = self.S, self.cm, self.pb, self.bank, self.hT
        self.layer_norm(xh, C, self.lnbc.t[:, 0, :], self.lnbc.t[:, 1, :], self.epsc.t[0:C, 0:1], "in")
        for half in range(2):
            def tr(e, half=half):
                for j in range(4):
                    kc = half * 4 + j
                    r = e.transpose(bank(half)[:, j * 128:j * 128 + C], xh.t[0:C, kc * 128:(kc + 1) * 128], cm(C_ID, C, C))
                return r
            S.op("pe", tr, [xh, self.cmat], [pb[half]])
            S.op("act", lambda e, half=half: e.activation(
                out=hT.t[:, half * 4:half * 4 + 4, 0:C],
                in_=bank(half).rearrange("p (j c) -> p j c", j=4)[:, :, 0:C], func=AF.Copy), [pb[half]], [hT])

    def chunk(self, e0, C, kind, xh, nxt):
        S = self.S
        self.xh = xh
        cm = self.cm
        pb = self.pb
        bank = self.bank
        big = self.big
        bg = self.bg
        smp = kind == "s"
        if smp:
            iU, iSU, iBO, iMBT, iMBS, iM01 = C_SU, C_SSU, C_SBO, C_SMBT, C_SMBS, C_SM01T
            G, L, nlev = NS, LS, 3
        else:
            iU, iSU, iBO, iMBT, iMBS, iM01 = C_PU, C_PSU, C_ONE, C_PMBT, C_PMBS, C_PM01T
            G, L, nlev = 1, C, {128: 7, 16: 4}[C]
        hT, qkvx, fmx, tm0, kq, sc = self.hT, self.qkvx, self.fmx, self.tm0, self.kq, self.sc
        ident = cm(C_ID)

        if DBG.get("step", 99) < 4:
            return
        qv = qkvx.t[:, :, 0:G * (L + 3)].rearrange("p a (g l) -> p a g l", g=G)
        for grp in range(5):
            b = 2 + (grp % 4)
            ccs = list(range(grp * 4, min(grp * 4 + 4, 17)))

            def mmf(e, ccs=ccs, b=b):
                for j, cc in enumerate(ccs):
                    M = 128 if cc < 16 else 16
                    for kc in range(8):
                        r = e.matmul(bank(b)[0:M, j * 128:j * 128 + C], lhsT=self.w_in.t[:, kc, cc * 128:cc * 128 + M],
                                     rhs=hT.t[:, kc, 0:C], start=(kc == 0), stop=(kc == 7))
                return r
            S.op("pe", mmf, [self.w_in, hT], [pb[b]])
            src = bank(b).rearrange("p (j c) -> p j c", j=4)
            if grp < 3:
                S.op("act", lambda e, grp=grp, src=src: e.activation(
                    out=qv[:, grp * 4:grp * 4 + 4, :, 3:3 + L],
                    in_=src[:, :, 0:C].rearrange("p j (g l) -> p j g l", g=G), func=AF.Copy), [pb[b]], [qkvx])
            elif grp == 3:
                S.op("act", lambda e, src=src: e.activation(out=fmx.t[:, 0:4, 0:C], in_=src[:, :, 0:C], func=AF.Copy), [pb[b]], [fmx])
            else:
                S.op("act", lambda e, src=src: e.activation(out=fmx.t[0:16, 4, 0:C], in_=src[0:16, 0, 0:C], func=AF.Copy), [pb[b]], [fmx])
        if DBG.get("step", 99) < 5:
            return
        tmoff = [NFM, NFM + 272, NFM + 784, NFM + 1296]
        tmn = [272, 512, 512, 512]
        for gi in range(4):
            b = 6 + (gi % 2)

            def mmt(e, gi=gi, b=b):
                for kc in range(8):
                    r = e.matmul(bank(b)[0:C, 0:tmn[gi]], lhsT=hT.t[:, kc, 0:C], rhs=self.w_in.t[:, kc, tmoff[gi]:tmoff[gi] + tmn[gi]],
                                 start=(kc == 0), stop=(kc == 7))
                return r
            S.op("pe", mmt, [self.w_in, hT], [pb[b]])
            if gi == 0:
                S.op("dve", lambda e, b=b: e.tensor_copy(out=tm0.t[0:C, :], in_=bank(b)[0:C, 0:272]), [pb[b]], [tm0])
            elif gi == 1:
                S.op("act", lambda e, b=b: e.activation(out=self.sg_gdn.t[0:C, :], in_=bank(b)[0:C, :], func=AF.Copy), [pb[b]], [self.sg_gdn])
            elif gi == 2:
                S.op("dve", lambda e, b=b: e.tensor_copy(out=self.gv_tok.t[0:C, :], in_=bank(b)[0:C, :]), [pb[b]], [self.gv_tok])
            else:
                S.op("act", lambda e, b=b: e.activation(out=self.sg_gla.t[0:C, :], in_=bank(b)[0:C, :], func=AF.Copy), [pb[b]], [self.sg_gla])
        if nxt is not None:
            self.front0_load(*nxt)
        if DBG.get("step", 99) < 6:
            return
        acc = bg(0, 2)[:, 0:12 * C].rearrange("p (a g l) -> p a g l", a=12, g=G)
        tmp = bg(2, 2)[:, 0:12 * C].rearrange("p (a g l) -> p a g l", a=12, g=G)
        accT, tmpT = [big[0], big[1]], [big[2], big[3]]

        def cwb(i):
            return self.cwg.t[:, :, i:i + 1].unsqueeze(3).to_broadcast([128, 12, G, L])
        S.op("dve", lambda e: e.tensor_tensor(out=acc, in0=qv[:, :, :, 0:L], in1=cwb(0), op=ALU.mult), [qkvx, self.cwg], accT)
        for i in range(1, 4):
            S.op("pool" if i == 1 else "dve", lambda e, i=i: e.tensor_tensor(out=tmp, in0=qv[:, :, :, i:i + L], in1=cwb(i), op=ALU.mult), [qkvx, self.cwg], tmpT)
            S.op("dve", lambda e: e.tensor_tensor(out=acc, in0=acc, in1=tmp, op=ALU.add), accT + tmpT, accT)
        qa = bg(0, 2)[:, 0:12 * C].rearrange("p (a c) -> p a c", a=12)
        S.op("act", lambda e: e.activation(out=qa, in_=qa, func=AF.Silu), accT, accT)
        if DBG.get("step", 99) < 7:
            return
        sq = bg(4)[:, 0:8 * C].rearrange("p (a c) -> p a c", a=8)
        rn = bg(5)[:, 0:8 * C].rearrange("p (a c) -> p a c", a=8)
        for sg in (self.sg_gdn, self.sg_gla):
            S.op("act", lambda e, sg=sg: e.activation(out=sg.t[0:C, :], in_=sg.t[0:C, :], func=AF.Silu), [sg], [sg])
        S.op("act", lambda e: e.activation(out=sq, in_=qa[:, 0:8, :], func=AF.Square), accT, [big[4]])
        S.op("act", lambda e: e.activation(out=sc.t[0:C, 16:24], in_=tm0.t[0:C, 8:16], func=AF.Sigmoid), [tm0], [sc])
        for half in range(2):
            S.op("pe", lambda e, half=half: e.matmul(bank(half)[:, 0:4 * C], lhsT=cm(C_BO64),
                                                     rhs=bg(4)[:, half * 4 * C:(half + 1) * 4 * C], start=True, stop=True),
                 [big[4], self.cmat], [pb[half]])
            S.op("act", lambda e, half=half: e.activation(out=bg(5)[:, half * 4 * C:(half + 1) * 4 * C], in_=bank(half)[:, 0:4 * C],
                                                          func=AF.Ln, bias=self.epsc.t[:, 1:2], scale=1.0), [pb[half], self.epsc], [big[5]])
        S.op("act", lambda e: e.activation(out=bg(5)[:, 0:8 * C], in_=bg(5)[:, 0:8 * C], func=AF.Exp, scale=-0.5), [big[5]], [big[5]])
        S.op("dve", lambda e: e.scalar_tensor_tensor(out=kq.t[:, :, 1, 0:C], in0=qa[:, 0:4, :], scalar=0.125, in1=rn[:, 0:4, :],
                                                      op0=ALU.mult, op1=ALU.mult), accT + [big[5]], [kq])
        S.op("pool", lambda e: e.tensor_tensor(out=kq.t[:, :, 0, 0:C], in0=qa[:, 4:8, :], in1=rn[:, 4:8, :], op=ALU.mult), accT + [big[5]], [kq])
        for h2 in range(2):
            rows = slice(64 * h2, 64 * h2 + 64)
            pad = lambda tb: tb.t[rows, :, 0:C].rearrange("p (a two) c -> p a two c", two=2)[:, :, h2, :]
            S.op("act", lambda e, rows=rows, pad=pad: e.activation(out=pad(self.KTm), in_=kq.t[rows, :, 0, 0:C], func=AF.Copy), [kq], [self.KTm])
            S.op("dve", lambda e, rows=rows, pad=pad: e.tensor_copy(out=pad(self.QTm), in_=kq.t[rows, :, 1, 0:C]), [kq], [self.QTm])
        if DBG.get("step", 99) < 8:
            return
        s_ = lambda a, b_: sc.t[0:C, a:b_]
        S.op("dve", lambda e: e.tensor_tensor(out=s_(0, 8), in0=tm0.t[0:C, 0:8], in1=self.pvec.t[0:C, 8:16], op=ALU.add), [tm0, self.pvec], [sc])
        S.op("act", lambda e: e.activation(out=s_(0, 8), in_=s_(0, 8), func=AF.Exp), [sc], [sc])
        S.op("act", lambda e: e.activation(out=s_(0, 8), in_=s_(0, 8), func=AF.Ln, bias=self.epsc.t[0:C, 2:3], scale=1.0), [sc, self.epsc], [sc])
        S.op("dve", lambda e: e.tensor_tensor(out=s_(8, 16), in0=s_(0, 8), in1=self.negA.t[0:C, :], op=ALU.mult), [sc, self.negA], [sc])

        def mmG(e):
            e.matmul(bank(0)[0:C, 0:8], lhsT=cm(iU, C, C), rhs=s_(8, 16), start=True, stop=True)
            return e.matmul(bank(0)[0:C, 8:16], lhsT=cm(iBO, C, C), rhs=s_(8, 16), start=True, stop=True)
        S.op("pe", mmG, [sc, self.cmat], [pb[0]])
        S.op("dve", lambda e: e.tensor_copy(out=s_(24, 40), in_=bank(0)[0:C, 0:16]), [pb[0]], [sc])
        S.op("act", lambda e: e.activation(out=s_(40, 48), in_=s_(24, 32), func=AF.Exp), [sc], [sc])
        S.op("dve", lambda e: e.tensor_tensor(out=s_(48, 56), in0=s_(32, 40), in1=s_(24, 32), op=ALU.subtract), [sc], [sc])
        S.op("act", lambda e: e.activation(out=s_(48, 56), in_=s_(48, 56), func=AF.Exp), [sc], [sc])
        S.op("act", lambda e: e.activation(out=s_(56, 64), in_=s_(32, 40), func=AF.Exp), [sc], [sc])
        S.op("dve", lambda e: e.tensor_tensor(out=s_(64, 72), in0=s_(16, 24), in1=s_(40, 48), op=ALU.mult), [sc], [sc])
        if DBG.get("step", 99) < 9:
            return
        def trk(e):
            for p in range(4):
                r = e.transpose(bank(6)[0:C, p * 128:(p + 1) * 128], kq.t[:, p, 0, 0:C], ident)
            return r
        S.op("pe", trk, [kq, self.cmat], [pb[6]])

        def trv(e):
            for p in range(4):
                r = e.transpose(bank(7)[0:C, p * 128:(p + 1) * 128], qa[:, 8 + p, :], ident)
            return r
        S.op("pe", trv, accT + [self.cmat], [pb[7]])
        h3 = lambda ap: ap.rearrange("p (h d) -> p h d", h=8)
        S.op("dve", lambda e: e.tensor_tensor(out=h3(self.RK.t[0:C, :]), in0=h3(bank(6)[0:C, :]), in1=_bc(s_(64, 72), 2, [C, 8, 64]), op=ALU.mult),
             [pb[6], sc], [self.RK])
        S.op("dve", lambda e: e.tensor_tensor(out=h3(self.kdec.t[0:C, :]), in0=h3(bank(6)[0:C, :]), in1=_bc(s_(48, 56), 2, [C, 8, 64]), op=ALU.mult),
             [pb[6], sc], [self.kdec])
        S.op("dve", lambda e: e.tensor_tensor(out=h3(self.RV.t[0:C, :]), in0=h3(bank(7)[0:C, :]), in1=_bc(s_(16, 24), 2, [C, 8, 64]), op=ALU.mult),
             [pb[7], sc], [self.RV])
        if DBG.get("step", 99) < 10:
            return
        v3 = lambda i: bg(i)[0:C, 0:8 * C].rearrange("p (h c) -> p h c", h=8)
        S.op("dve", lambda e: e.tensor_tensor(out=v3(2), in0=_bc(cm(iU, C, C), 1, [C, 8, C]), in1=_bc(s_(8, 16), 2, [C, 8, C]), op=ALU.mult),
             [self.cmat, sc], [big[2]])
        for half in range(2):
            S.op("pe", lambda e, half=half: e.matmul(bank(half)[0:C, 0:4 * C], lhsT=cm(C_ONE, C, C),
                                                     rhs=bg(2)[0:C, half * 4 * C:(half + 1) * 4 * C], start=True, stop=True),
                 [big[2], self.cmat], [pb[half]])
        gbc = bank(0, 2)

        def gview(r):
            return self.pst.t[0:r, 0:2, 0:4 * C].rearrange("p b (h c) -> p b h c", h=4)
        v4 = lambda i: bg(i)[0:C, 0:8 * C].rearrange("p (b h c) -> p b h c", b=2, h=4)
        S.op("pool", lambda e: e.tensor_tensor(out=v3(3), in0=_bc(cm(iMBT, C, C), 1, [C, 8, C]), in1=_bc(s_(24, 32), 2, [C, 8, C]), op=ALU.subtract),
             [self.cmat, sc], [big[3]])
        S.op("pool", lambda e: e.tensor_tensor(out=v3(4), in0=_bc(cm(iMBS, C, C), 1, [C, 8, C]), in1=_bc(s_(24, 32), 2, [C, 8, C]), op=ALU.add),
             [self.cmat, sc], [big[4]])
        S.op("dve", lambda e: e.tensor_tensor(out=v4(5), in0=gview(C), in1=v4(3), op=ALU.add), [pb[0], pb[1], big[3]], [big[5]])
        S.op("act", lambda e: e.activation(out=bg(5)[0:C, 0:8 * C], in_=bg(5)[0:C, 0:8 * C], func=AF.Exp), [big[5]], [big[5]])
        S.op("dve", lambda e: e.tensor_tensor(out=v4(1), in0=v4(4), in1=gview(C), op=ALU.subtract), [pb[0], pb[1], big[4]], [big[1]])
        S.op("act", lambda e: e.activation(out=bg(1)[0:C, 0:8 * C], in_=bg(1)[0:C, 0:8 * C], func=AF.Exp), [big[1]], [big[1]])
        if DBG.get("step", 99) < 11:
            return
        def mmkk(e):
            for h in range(8):
                p, h2 = h // 2, h % 2
                ov = bank(2 + h // 2).rearrange("p (hh two c) -> p hh two c", hh=2, two=2)
                if C == 128:
                    r = e.matmul(bank(2 + h // 2)[0:C, (h % 2) * 256:(h % 2) * 256 + 256], lhsT=self.KTm.t[:, h, 0:C],
                                 rhs=kq.t[:, p, :, :].rearrange("p a c -> p (a c)"), start=True, stop=True)
                else:
                    for two in range(2):
                        r = e.matmul(ov[0:C, h % 2, two, 0:C], lhsT=self.KTm.t[:, h, 0:C],
                                     rhs=kq.t[:, p, two, 0:C], start=True, stop=True)
            return r
        S.op("pe", mmkk, [kq, self.KTm], [pb[2], pb[3], pb[4], pb[5]])
        kkv = self.pst.t[0:C, 2:6, :].rearrange("p b (hh two c) -> p b hh two c", hh=2, two=2)
        v5 = lambda i: bg(i)[0:C, 0:8 * C].rearrange("p (b hh c) -> p b hh c", b=4, hh=2)
        S.op("dve", lambda e: e.tensor_tensor(out=v5(2), in0=kkv[:, :, :, 0, 0:C], in1=v5(1), op=ALU.mult), [pb[2], pb[3], pb[4], pb[5], big[1]], [big[2]])
        use_r = (C == 128) and bool(DBG.get("f32r"))
        ro = (lambda ap: ap.bitcast(F32R)) if use_r else (lambda ap: ap)
        ri = (lambda ap: ap.bitcast(F32R)) if use_r else (lambda ap: ap)
        Pc, PTc, TTc = self.Pc, self.PTc, self.TTc
        c3 = lambda tb: tb.t[0:C, 0:8 * C].rearrange("p (h c) -> p h c", h=8)
        c4 = lambda tb: tb.t[0:C, 0:8 * C].rearrange("p (b h c) -> p b h c", b=2, h=4)
        S.op("dve", lambda e: e.scalar_tensor_tensor(out=ro(c3(Pc)), in0=v3(2), scalar=-1.0, in1=_bc(s_(16, 24), 2, [C, 8, C]),
                                                      op0=ALU.mult, op1=ALU.mult), [big[2], sc], [Pc])
        S.op("dve", lambda e: e.tensor_tensor(out=v5(0), in0=kkv[:, :, :, 1, 0:C], in1=v5(5), op=ALU.mult), [pb[2], pb[3], pb[4], pb[5], big[5]] + accT, [big[0]])
        for half in range(2):
            def trn(e, half=half):
                for j in range(4):
                    r = e.transpose(bank(half)[0:C, j * C:(j + 1) * C], c3(Pc)[:, half * 4 + j, :], cm(C_ID, C, C))
                return r
            S.op("pe", trn, [Pc, self.cmat], [pb[half]])
        S.op("act", lambda e: e.activation(out=ro(c4(PTc)), in_=gview(C), func=AF.Copy), [pb[0], pb[1]], [PTc])
        S.op("dve", lambda e: e.tensor_tensor(out=ro(c3(TTc)), in0=c3(PTc), in1=_bc(cm(C_ID, C, C), 1, [C, 8, C]), op=ALU.add), [PTc, self.cmat], [TTc])
        if DBG.get("step", 99) < 12:
            return
        if nxt is not None:
            self.front0_compute(*nxt)
        gla_gen = self.gla_prep(C, iU, iSU, iM01)
        for lev in range(nlev):
            doA, doC, doB = lev >= 1, lev <= nlev - 2, lev <= nlev - 3
            for hg in range(2):
                bA, bB, bC = (2, 3, 4) if hg == 0 else (5, 6, 7)

                def mminv(e, hg=hg, bA=bA, bB=bB, bC=bC, doA=doA, doB=doB, doC=doC):
                    r = None
                    for j in range(4):
                        h = hg * 4 + j
                        o = lambda b_: bank(b_)[0:C, j * C:(j + 1) * C]
                        if doA:
                            r = e.matmul(o(bA), lhsT=ri(c3(Pc)[:, h, :]), rhs=ri(c3(TTc)[:, h, :]), start=True, stop=True)
                        if doC:
                            r = e.matmul(o(bC), lhsT=ri(c3(PTc)[:, h, :]), rhs=ri(c3(Pc)[:, h, :]), start=True, stop=True)
                        if doB:
                            r = e.matmul(o(bB), lhsT=ri(c3(Pc)[:, h, :]), rhs=ri(c3(PTc)[:, h, :]), start=True, stop=True)
                    return r
                wr = ([pb[bA]] if doA else []) + ([pb[bB]] if doB else []) + ([pb[bC]] if doC else [])
                S.op("pe", mminv, [Pc.parts[hg], PTc.parts[hg], TTc.parts[hg]], wr)
                hs = slice(hg * 4 * C, (hg + 1) * 4 * C)
                if doA:
                    S.op("dve", lambda e, bA=bA, hs=hs: e.tensor_tensor(out=ro(TTc.t[0:C, hs]), in0=bank(bA)[0:C, 0:4 * C], in1=TTc.t[0:C, hs], op=ALU.add),
                         [pb[bA], TTc.parts[hg]], [TTc.parts[hg]])
                if doC:
                    S.op("act", lambda e, bC=bC, hs=hs: e.activation(out=ro(Pc.t[0:C, hs]), in_=bank(bC)[0:C, 0:4 * C], func=AF.Copy), [pb[bC]], [Pc.parts[hg]])
                if doB:
                    S.op("act", lambda e, bB=bB, hs=hs: e.activation(out=ro(PTc.t[0:C, hs]), in_=bank(bB)[0:C, 0:4 * C], func=AF.Copy), [pb[bB]], [PTc.parts[hg]])
            for _ in range(4):
                next(gla_gen, None)
        for _ in gla_gen:
            pass
        if DBG.get("step", 99) < 13:
            return
        def mmwv(e):
            for h in range(8):
                r = e.matmul(bank(0)[0:C, h * 64:(h + 1) * 64], lhsT=c3(TTc)[:, h, :], rhs=self.RV.t[0:C, h * 64:(h + 1) * 64], start=True, stop=True)
            return r
        S.op("pe", mmwv, [TTc, self.RV], [pb[0]])
        S.op("act", lambda e: e.activation(out=self.wv.t[0:C, :], in_=bank(0)[0:C, :], func=AF.Copy), [pb[0]], [self.wv])

        def mmwk(e):
            for h in range(8):
                p = h // 2
                r = e.matmul(bank(2 + h // 4)[:, (h % 4) * 128:(h % 4) * 128 + C], lhsT=self.RK.t[0:C, p * 128:(p + 1) * 128], rhs=c3(TTc)[:, h, :],
                             start=True, stop=True)
            return r
        S.op("pe", mmwk, [TTc, self.RK], [pb[2], pb[3]])
        wkv = self.pst.t[:, 2:4, :].rearrange("p b (hh two c) -> p (b hh) two c", hh=2, two=2)
        for h2 in range(2):
            rows = slice(64 * h2, 64 * h2 + 64)
            S.op("dve" if h2 == 0 else "act",
                 (lambda e, rows=rows, h2=h2: e.tensor_copy(out=self.wkT.t[rows, :, 0:C].rearrange("p (a two) c -> p a two c", two=2)[:, :, h2, :], in_=wkv[rows, :, h2, 0:C])) if h2 == 0 else
                 (lambda e, rows=rows, h2=h2: e.activation(out=self.wkT.t[rows, :, 0:C].rearrange("p (a two) c -> p a two c", two=2)[:, :, h2, :], in_=wkv[rows, :, h2, 0:C], func=AF.Copy)),
                 [pb[2], pb[3]], [self.wkT])
        if DBG.get("step", 99) < 14:
            return
        for _ in gla_gen:
            pass
        if DBG.get("step", 99) < 15:
            return
        if smp:
            self.state_sample(C)
        else:
            self.state_prompt(C)
        if DBG.get("step", 99) < 16:
            return
        self.post_mix(e0, C, smp)
        if DBG.get("step", 99) < 17:
            return
        if not smp:
            S.op("pool", lambda e: e.tensor_copy(out=qkvx.t[:, :, 0:3], in_=qkvx.t[:, :, L:L + 3]), [qkvx], [qkvx])

    def gla_prep(self, C, iU, iSU, iM01):
        S, cm, pb, bank = self.S, self.cm, self.pb, self.bank
        fmx, lt = self.fmx, self.lt
        yield
        S.op("pe", lambda e: e.matmul(bank(0)[0:C, 0:256], lhsT=fmx.t[0:16, 4, 0:C], rhs=self.wgk2.t[:, :], start=True, stop=True),
             [fmx, self.wgk2], [pb[0]])
        yield
        S.op("dve", lambda e: e.tensor_tensor(out=lt.t[0:C, :], in0=bank(0)[0:C, 0:256], in1=self.pvec.t[0:C, 208:464], op=ALU.add), [pb[0], self.pvec], [lt])
        yield
        S.op("act", lambda e: e.activation(out=lt.t[0:C, :], in_=lt.t[0:C, :], func=AF.Exp, scale=-1.0), [lt], [lt])
        yield
        S.op("act", lambda e: e.activation(out=lt.t[0:C, :], in_=lt.t[0:C, :], func=AF.Ln, bias=self.epsc.t[0:C, 2:3], scale=1.0), [lt, self.epsc], [lt])

        yield
        def mmbc(e):
            for p in range(2):
                r = e.matmul(bank(1)[:, p * 128:p * 128 + C], lhsT=lt.t[0:C, p * 128:(p + 1) * 128], rhs=cm(iU, C, C), start=True, stop=True)
            return r
        yield
        S.op("pe", mmbc, [lt, self.cmat], [pb[1]])
        bcv = bank(1)[:, 0:256].rearrange("p (a c) -> p a c", a=2)[:, :, 0:C]
        yield
        S.op("act", lambda e: e.activation(out=self.ebT.t[:, :, 0:C], in_=bcv, func=AF.Exp, scale=-1.0 / 16.0), [pb[1]], [self.ebT])
        yield
        S.op("act", lambda e: e.activation(out=self.enbT.t[:, :, 0:C], in_=bcv, func=AF.Exp, scale=1.0 / 16.0), [pb[1]], [self.enbT])
        yield
        S.op("dve", lambda e: e.scalar_tensor_tensor(out=self.qeT.t[:, :, 0:C], in0=fmx.t[:, 0:2, 0:C], scalar=0.125, in1=self.ebT.t[:, :, 0:C],
                                                      op0=ALU.mult, op1=ALU.mult), [fmx, self.ebT], [self.qeT])
        yield
        for h2 in range(2):
            rows = slice(64 * h2, 64 * h2 + 64)
            pad = lambda tb: tb.t[rows, :, 0:C].rearrange("p (a two) c -> p a two c", two=2)[:, :, h2, :]
            S.op("pool", lambda e, rows=rows, pad=pad: e.tensor_tensor(out=pad(self.keTm), in0=fmx.t[rows, 2:4, 0:C], in1=self.enbT.t[rows, :, 0:C], op=ALU.mult),
                 [fmx, self.enbT], [self.keTm])
            S.op("act", lambda e, rows=rows, pad=pad: e.activation(out=pad(self.qeTm), in_=self.qeT.t[rows, :, 0:C], func=AF.Copy), [self.qeT], [self.qeTm])

        yield
        def mmA(e):
            for h in range(4):
                p, h2 = h // 2, h % 2
                rows = slice(64 * h2, 64 * h2 + 64)
                r = e.matmul(bank(0)[0:C, h * 128:h * 128 + C], lhsT=self.keTm.t[:, h, 0:C], rhs=self.qeT.t[:, p, 0:C], start=True, stop=True)
            return r
        yield
        S.op("pe", mmA, [self.keTm, self.qeT], [pb[0]])
        yield
        S.op("dve", lambda e: e.tensor_tensor(out=self.PTg.t[0:C, :, 0:C], in0=bank(0).rearrange("p (h c) -> p h c", h=4)[0:C, :, 0:C],
                                              in1=_bc(cm(iM01, C, C), 1, [C, 4, C]), op=ALU.mult), [pb[0], self.cmat], [self.PTg])
        yield
        S.op("pe", lambda e: e.matmul(bank(1)[0:C, 0:256], lhsT=cm(iSU, C, C), rhs=lt.t[0:C, :], start=True, stop=True), [lt, self.cmat], [pb[1]])
        yield
        S.op("act", lambda e: e.activation(out=self.kd.t[0:C, :], in_=bank(1)[0:C, 0:256], func=AF.Exp, scale=-1.0 / 16.0), [pb[1]], [self.kd])
        yield
        S.op("pool", lambda e: e.tensor_tensor(out=self.kd.t[0:C, :], in0=self.kd.t[0:C, :], in1=self.tm0.t[0:C, 16:272], op=ALU.mult), [self.kd, self.tm0], [self.kd])

    def state_prompt(self, C):
        S, cm, pb, bank, big, bg = self.S, self.cm, self.pb, self.bank, self.big, self.bg
        sc, kq = self.sc, self.kq
        s_ = lambda a, b_: sc.t[0:C, a:b_]
        v3 = lambda i: bg(i)[0:C, 0:8 * C].rearrange("p (h c) -> p h c", h=8)
        Sg, Sl, U = self.Sg, self.Sl, self.U
        R = lambda h2: slice(64 * h2, 64 * h2 + 64)
        h3 = lambda ap: ap.rearrange("p (h d) -> p h d", h=8)
        S.op("pe", lambda e: e.matmul(bank(1)[:, 0:8], lhsT=cm(C_ONE, C, 128), rhs=s_(8, 16), start=True, stop=True), [sc, self.cmat], [pb[1]])
        S.op("act", lambda e: e.activation(out=self.glb.t[:, 0:8], in_=bank(1)[:, 0:8], func=AF.Exp), [pb[1]], [self.glb])

        def mm1(e):
            for h in range(8):
                p, h2 = h // 2, h % 2
                r = e.matmul(bank(6)[0:C, h * 64:(h + 1) * 64], lhsT=self.wkT.t[:, h, 0:C], rhs=Sg.t[:, p, :], start=True, stop=True)
            return r
        S.op("pe", mm1, [self.wkT, Sg], [pb[6]])
        def mg1(e):
            for h in range(4):
                p, h2 = h // 2, h % 2
                r = e.matmul(bank(2)[0:C, h * 128:(h + 1) * 128], lhsT=self.qeTm.t[:, h, 0:C], rhs=Sl.t[:, p, :], start=True, stop=True)
            return r
        S.op("pe", mg1, [self.qeTm, Sl], [pb[2]])
        def mg2(e):
            for h in range(4):
                r = e.matmul(bank(3)[0:C, h * 128:(h + 1) * 128], lhsT=self.PTg.t[0:C, h, 0:C], rhs=self.gv_tok.t[0:C, h * 128:(h + 1) * 128], start=True, stop=True)
            return r
        S.op("pe", mg2, [self.PTg, self.gv_tok], [pb[3]])
        def mg3(e):
            for h in range(4):
                p = h // 2
                r = e.matmul(bank(4)[:, h * 128:(h + 1) * 128], lhsT=self.kd.t[0:C, p * 128:(p + 1) * 128], rhs=self.gv_tok.t[0:C, h * 128:(h + 1) * 128], start=True, stop=True)
            return r
        S.op("pe", mg3, [self.kd, self.gv_tok], [pb[4]])
        S.op("dve", lambda e: e.tensor_tensor(out=U.t[0:C, :], in0=self.wv.t[0:C, :], in1=bank(6)[0:C, :], op=ALU.subtract), [self.wv, pb[6]], [U])

        def mm2(e):
            for h in range(8):
                p, h2 = h // 2, h % 2
                r = e.matmul(bank(7)[0:C, h * 64:(h + 1) * 64], lhsT=self.QTm.t[:, h, 0:C], rhs=Sg.t[:, p, :], start=True, stop=True)
            return r
        S.op("pe", mm2, [self.QTm, Sg], [pb[7]])

        def mm3(e):
            for h in range(8):
                r = e.matmul(bank(0)[0:C, h * 64:(h + 1) * 64], lhsT=v3(0)[:, h, :], rhs=U.t[0:C, h * 64:(h + 1) * 64], start=True, stop=True)
            return r
        S.op("pe", mm3, [big[0], U], [pb[0]])
        S.op("act", lambda e: e.activation(out=self.ogla.t[0:C, :], in_=bank(2)[0:C, :], func=AF.Copy), [pb[2]], [self.ogla])
        S.op("dve", lambda e: e.tensor_tensor(out=self.ogla.t[0:C, :], in0=self.ogla.t[0:C, :], in1=bank(3)[0:C, :], op=ALU.add), [self.ogla, pb[3]], [self.ogla])
        S.op("dve", lambda e: e.tensor_tensor(out=h3(self.otmp.t[0:C, :]), in0=h3(bank(7)[0:C, :]), in1=_bc(s_(40, 48), 2, [C, 8, 64]), op=ALU.mult),
             [pb[7], sc], [self.otmp])
        S.op("dve", lambda e: e.tensor_tensor(out=self.ogdn.t[0:C, :], in0=self.otmp.t[0:C, :], in1=bank(0)[0:C, :], op=ALU.add), [self.otmp, pb[0]], [self.ogdn])

        def mm4(e):
            for h in range(8):
                p = h // 2
                r = e.matmul(bank(1)[:, h * 64:(h + 1) * 64], lhsT=self.kdec.t[0:C, p * 128:(p + 1) * 128], rhs=U.t[0:C, h * 64:(h + 1) * 64], start=True, stop=True)
            return r
        S.op("pe", mm4, [self.kdec, U, self.glb], [pb[1]])
        for h2 in range(2):
            glv = self.glb.t[R(h2), 0:8].rearrange("p (a two) -> p a two", two=2)[:, :, h2]
            psv = bank(1).rearrange("p (a two v) -> p a two v", two=2, v=64)[R(h2), :, h2, :]
            S.op("pool", lambda e, h2=h2, glv=glv: e.tensor_tensor(out=Sg.t[R(h2), :, :], in0=Sg.t[R(h2), :, :], in1=_bc(glv, 2, [64, 4, 64]), op=ALU.mult),
                 [Sg, self.glb], [Sg])
            S.op("dve", lambda e, h2=h2, psv=psv: e.tensor_tensor(out=Sg.t[R(h2), :, :], in0=Sg.t[R(h2), :, :], in1=psv, op=ALU.add), [Sg, pb[1]], [Sg])


        for h in range(4):
            p, h2 = h // 2, h % 2
            S.op("dve", lambda e, p=p, h2=h2, h=h: e.scalar_tensor_tensor(
                out=Sl.t[R(h2), p, :], in0=Sl.t[R(h2), p, :], scalar=self.ebT.t[R(h2), p, C - 1:C], in1=bank(4)[R(h2), h * 128:(h + 1) * 128],
                op0=ALU.mult, op1=ALU.add), [Sl, self.ebT, pb[4]], [Sl])

    def state_sample(self, C):
        S, cm, pb, bank, big, bg = self.S, self.cm, self.pb, self.bank, self.big, self.bg
        sc, kq = self.sc, self.kq
        s_ = lambda a, b_: sc.t[0:C, a:b_]
        v3 = lambda i: bg(i)[0:C, 0:8 * C].rearrange("p (h c) -> p h c", h=8)
        U = self.U
        R = lambda h2: slice(64 * h2, 64 * h2 + 64)
        h3 = lambda ap: ap.rearrange("p (h d) -> p h d", h=8)
        bm = cm(C_SBM, C, NS)
        gsel = bg(5)[0:C, 0:128].rearrange("p (h s) -> p h s", h=8)
        S.op("pool", lambda e: e.tensor_tensor(out=gsel, in0=_bc(s_(8, 16), 2, [C, 8, NS]), in1=_bc(bm, 1, [C, 8, NS]), op=ALU.mult), [sc, self.cmat], [big[5]])
        S.op("pe", lambda e: e.matmul(bank(1)[:, 0:128], lhsT=cm(C_ONE), rhs=bg(5)[0:C, 0:128], start=True, stop=True), [big[5], self.cmat], [pb[1]])
        S.op("act", lambda e: e.activation(out=self.glb.t[:, 0:128], in_=bank(1)[:, 0:128], func=AF.Exp), [pb[1]], [self.glb])
        glbs = self.glb.t[:, 0:128].rearrange("p (h s) -> p h s", h=8)
        S0 = bg(4).rearrange("p (s v) -> p s v", s=NS)
        tmp = bg(5)[0:C, :].rearrange("p (s v) -> p s v", s=NS)
        tmpT = bg(5)[0:C, :].rearrange("p (s v) -> p v s", s=NS)
        Ub = bg(1)[0:C, :].rearrange("p (s v) -> p s v", s=NS)
        ps67 = self.pst.t[:, 6:8, :].rearrange("p b (s v) -> p (b s) v", v=64)
        wks, o1s = self.otmp, self.ogdn
        S0v = lambda ap: ap.rearrange("p (s v) -> p s v", s=NS)
        S0s = [(S0v(bg(4)), big[4]), (S0v(bg(2)), big[2])]
        scr = [dict(tmp=S0v(bg(5)[0:C, :]), tmpT=bg(5)[0:C, :].rearrange("p (s v) -> p v s", s=NS), tmpB=big[5],
                    Ub=S0v(bg(1)[0:C, :]), Ubf=bg(1), UbB=big[1], bk=6),
               dict(tmp=S0v(self.Pc.t[0:C, :]), tmpT=self.Pc.t[0:C, :].rearrange("p (s v) -> p v s", s=NS), tmpB=self.Pc,
                    Ub=S0v(self.PTc.t[0:C, :]), Ubf=self.PTc.t, UbB=self.PTc, bk=4)]

        def load(p):
            S0, S0t = S0s[p % 2]
            for h2 in range(2):
                S.dma(S0[R(h2), :, :], self.sgdn[:, 2 * p + h2, :, :].rearrange("s d v -> d s v"), [], [S0t])

        def head_chain(p, h2):
            h = 2 * p + h2
            S0, S0t = S0s[p % 2]
            q = scr[h2]
            bk = q["bk"]
            psx = self.pst.t[:, bk:bk + 2, :].rearrange("p b (s v) -> p (b s) v", v=64)
            for (lhs, lhsb, dst) in ((self.wkT.t[:, h, 0:C], self.wkT, wks), (self.QTm.t[:, h, 0:C], self.QTm, o1s)):
                def mma(e, lhs=lhs):
                    for i in range(2):
                        r = e.matmul(bank(bk + i)[0:C, :], lhsT=lhs, rhs=S0[:, i * 8:(i + 1) * 8, :].rearrange("p s v -> p (s v)"), start=True, stop=True)
                    return r
                S.op("pe", mma, [lhsb, S0t], [pb[bk], pb[bk + 1]])
                yield
                S.op("dve", lambda e: e.tensor_tensor(out=q["tmp"], in0=psx[0:C], in1=_bc(bm, 2, [C, NS, 64]), op=ALU.mult), [pb[bk], pb[bk + 1], self.cmat], [q["tmpB"]])
                yield
                S.op("dve", lambda e, dst=dst: e.tensor_reduce(out=dst.t[0:C, h * 64:(h + 1) * 64], in_=q["tmpT"], op=ALU.add, axis=AX.X), [q["tmpB"]], [dst])
                yield
            cs = slice(h * 64, (h + 1) * 64)
            S.op("dve", lambda e: e.tensor_tensor(out=U.t[0:C, cs], in0=self.wv.t[0:C, cs], in1=wks.t[0:C, cs], op=ALU.subtract), [self.wv, wks], [U])
            yield
            S.op("pe", lambda e: e.matmul(bank(0)[0:C, cs], lhsT=v3(0)[:, h, :], rhs=U.t[0:C, cs], start=True, stop=True), [big[0], U], [pb[0]])
            yield
            S.op("pool", lambda e: e.tensor_tensor(out=q["Ub"], in0=_bc(U.t[0:C, cs], 1, [C, NS, 64]), in1=_bc(bm, 2, [C, NS, 64]), op=ALU.mult),
                 [U, self.cmat], [q["UbB"]])
            yield

            def mmb(e):
                for i in range(2):
                    r = e.matmul(bank(bk + i)[:, :], lhsT=self.kdec.t[0:C, p * 128:(p + 1) * 128], rhs=q["Ubf"][0:C, i * 512:(i + 1) * 512], start=True, stop=True)
                return r
            S.op("pe", mmb, [self.kdec, q["UbB"]], [pb[bk], pb[bk + 1]])
            yield
            S.op("pool", lambda e: e.tensor_tensor(out=S0[R(h2), :, :], in0=S0[R(h2), :, :], in1=_bc(glbs[R(h2), h, :], 2, [64, NS, 64]), op=ALU.mult),
                 [S0t, self.glb], [S0t])
            yield
            S.op("dve", lambda e: e.tensor_tensor(out=S0[R(h2), :, :], in0=S0[R(h2), :, :], in1=psx[R(h2)], op=ALU.add), [S0t, pb[bk], pb[bk + 1]], [S0t])
            yield
            S.dma(self.gdn_s[:, h, :, :].rearrange("s d v -> d s v"), S0[R(h2), :, :], [S0t], [], is_out=True)

        load(0)
        for p in range(4):
            if p + 1 < 4:
                load(p + 1)
            gens = [head_chain(p, 0), head_chain(p, 1)]
            while gens:
                for g_ in list(gens):
                    if next(g_, "done") == "done":
                        gens.remove(g_)
        S.op("dve", lambda e: e.tensor_tensor(out=h3(o1s.t[0:C, :]), in0=h3(o1s.t[0:C, :]), in1=_bc(s_(40, 48), 2, [C, 8, 64]), op=ALU.mult), [o1s, sc], [o1s])
        S.op("dve", lambda e: e.tensor_tensor(out=self.ogdn.t[0:C, :], in0=o1s.t[0:C, :], in1=bank(0)[0:C, :], op=ALU.add), [o1s, pb[0]], [self.ogdn])
        S0g = bg(0, 2).rearrange("p (s v) -> p s v", s=NS)
        tg = bg(2, 2)[0:C, :].rearrange("p (s v) -> p s v", s=NS)
        tgT = bg(2, 2)[0:C, :].rearrange("p (s v) -> p v s", s=NS)
        Vb = bg(4, 2)[0:C, :].rearrange("p (s v) -> p s v", s=NS)
        ps25 = self.pst.t[:, 2:6, :].rearrange("p b (s v) -> p (b s) v", v=128)
        S0gT, tgB, VbB = [big[0], big[1]], [big[2], big[3]], [big[4], big[5]]
        pbs = [pb[2], pb[3], pb[4], pb[5]]
        for p in range(2):
            for h2 in range(2):
                S.dma(S0g[R(h2), :, :], self.sgla[:, 2 * p + h2, :, :].rearrange("s d v -> d s v"), [], S0gT)
            for h2 in range(2):
                h = 2 * p + h2
                cs = slice(h * 128, (h + 1) * 128)

                def mmq(e, p=p, h2=h2, h=h):
                    for i in range(4):
                        r = e.matmul(bank(2 + i)[0:C, :], lhsT=self.qeTm.t[:, h, 0:C], rhs=S0g[:, i * 4:(i + 1) * 4, :].rearrange("p s v -> p (s v)"), start=True, stop=True)
                    return r
                S.op("pe", mmq, [self.qeTm] + S0gT, pbs)
                S.op("dve", lambda e: e.tensor_tensor(out=tg, in0=ps25[0:C], in1=_bc(bm, 2, [C, NS, 128]), op=ALU.mult), pbs + [self.cmat], tgB)
                S.op("dve", lambda e, cs=cs: e.tensor_reduce(out=self.ogla.t[0:C, cs], in_=tgT, op=ALU.add, axis=AX.X), tgB, [self.ogla])
                S.op("pool", lambda e, cs=cs: e.tensor_tensor(out=Vb, in0=_bc(self.gv_tok.t[0:C, cs], 1, [C, NS, 128]), in1=_bc(bm, 2, [C, NS, 128]), op=ALU.mult),
                     [self.gv_tok, self.cmat], VbB)

                def mmv(e, p=p):
                    for i in range(4):
                        r = e.matmul(bank(2 + i)[:, :], lhsT=self.kd.t[0:C, p * 128:(p + 1) * 128], rhs=bg(4, 2)[0:C, i * 512:(i + 1) * 512], start=True, stop=True)
                    return r
                S.op("pe", mmv, [self.kd] + VbB, pbs)
                ebl = self.ebT.t[R(h2), p, :].rearrange("p (s l) -> p s l", l=LS)[:, :, LS - 1]
                S.op("pool", lambda e, h2=h2, ebl=ebl: e.tensor_tensor(out=S0g[R(h2), :, :], in0=S0g[R(h2), :, :], in1=_bc(ebl, 2, [64, NS, 128]), op=ALU.mult),
                     S0gT + [self.ebT], S0gT)
                S.op("dve", lambda e, h2=h2: e.tensor_tensor(out=S0g[R(h2), :, :], in0=S0g[R(h2), :, :], in1=ps25[R(h2)], op=ALU.add), S0gT + pbs, S0gT)
                S.dma(self.gla_s[:, h, :, :].rearrange("s d v -> d s v"), S0g[R(h2), :, :], S0gT, [], is_out=True)

        def mg2(e):
            for h in range(4):
                r = e.matmul(bank(6)[0:C, h * 128:(h + 1) * 128], lhsT=self.PTg.t[0:C, h, 0:C], rhs=self.gv_tok.t[0:C, h * 128:(h + 1) * 128], start=True, stop=True)
            return r
        S.op("pe", mg2, [self.PTg, self.gv_tok], [pb[6]])
        S.op("dve", lambda e: e.tensor_tensor(out=self.ogla.t[0:C, :], in0=self.ogla.t[0:C, :], in1=bank(6)[0:C, :], op=ALU.add), [self.ogla, pb[6]], [self.ogla])

    def post_mix(self, e0, C, smp):
        S, cm, pb, bank = self.S, self.cm, self.pb, self.bank
        sc, mix, xh = self.sc, self.mix, self.xh
        s_ = lambda a, b_: sc.t[0:C, a:b_]
        if not mix.parts:
            mix.parts = [S.alias("mixA", mix), S.alias("mixB", mix)]
            self.scn = [S.alias("scA", sc), S.alias("scB", sc)]

        def norm_chain(o, nh, dv, col0, gcol, sg, sco, sqb, mixp, scp):
            v = lambda ap: ap.rearrange("p (h d) -> p h d", h=nh)
            yield
            S.op("dve", lambda e, o=o: e.tensor_tensor(out=sqb.t[0:C, :], in0=o.t[0:C, :], in1=o.t[0:C, :], op=ALU.mult), [o], [sqb])
            yield
            S.op("dve", lambda e, v=v, sco=sco, nh=nh: e.tensor_reduce(out=s_(sco, sco + nh), in_=v(sqb.t[0:C, :]), op=ALU.add, axis=AX.X), [sqb], [scp])
            yield
            S.op("act", lambda e, sco=sco, nh=nh, dv=dv: e.activation(out=s_(sco, sco + nh), in_=s_(sco, sco + nh), func=AF.Ln,
                                                                      bias=self.epsc.t[0:C, 1:2], scale=1.0 / dv), [scp, self.epsc], [scp])
            yield
            S.op("act", lambda e, sco=sco, nh=nh: e.activation(out=s_(sco, sco + nh), in_=s_(sco, sco + nh), func=AF.Exp, scale=-0.5), [scp], [scp])
            mv_ = v(mix.t[0:C, col0:col0 + 512])
            yield
            S.op("dve", lambda e, o=o, v=v, mv_=mv_, sco=sco, nh=nh, dv=dv: e.tensor_tensor(out=mv_, in0=v(o.t[0:C, :]), in1=_bc(s_(sco, sco + nh), 2, [C, nh, dv]), op=ALU.mult),
                 [o, scp], [mixp])
            yield
            S.op("pool", lambda e, mv_=mv_, gcol=gcol, nh=nh, dv=dv: e.tensor_tensor(out=mv_, in0=mv_, in1=_bc(self.pvec.t[0:C, gcol:gcol + dv], 1, [C, nh, dv]), op=ALU.mult),
                 [mixp, self.pvec], [mixp])
            yield
            S.op("dve", lambda e, col0=col0, sg=sg: e.tensor_tensor(out=mix.t[0:C, col0:col0 + 512], in0=mix.t[0:C, col0:col0 + 512], in1=sg.t[0:C, :], op=ALU.mult),
                 [mixp, sg], [mixp])
        gens = [norm_chain(self.ogdn, 8, 64, 0, 16, self.sg_gdn, 72, self.otmp, mix.parts[0], self.scn[0]),
                norm_chain(self.ogla, 4, 128, 512, 80, self.sg_gla, 80, self.kdec, mix.parts[1], self.scn[1])]
        while gens:
            for g_ in list(gens):
                if next(g_, 'done') == 'done':
                    gens.remove(g_)

        for half in range(2):
            def tr(e, half=half):
                for j in range(4):
                    kc = half * 4 + j
                    r = e.transpose(bank(half)[:, j * 128:j * 128 + C], mix.t[0:C, kc * 128:(kc + 1) * 128], cm(C_ID, C, C))
                return r
            S.op("pe", tr, [mix, self.cmat], [pb[half]])
            S.op("act", lambda e, half=half: e.activation(out=self.mixT.t[:, half * 4:half * 4 + 4, 0:C],
                                                          in_=bank(half).rearrange("p (j c) -> p j c", j=4)[:, :, 0:C], func=AF.Copy), [pb[half]], [self.mixT])
        for half in range(2):
            def mmo(e, half=half):
                for kc in range(8):
                    r = e.matmul(bank(6 + half)[0:C, :], lhsT=self.mixT.t[:, kc, 0:C], rhs=self.w_out.t[:, kc, half * 512:(half + 1) * 512],
                                 start=(kc == 0), stop=(kc == 7))
                return r
            S.op("pe", mmo, [self.mixT, self.w_out], [pb[6 + half]])
            S.op("dve", lambda e, half=half: e.scalar_tensor_tensor(out=xh.t[0:C, half * 512:(half + 1) * 512], in0=xh.t[0:C, half * 512:(half + 1) * 512],
                                                                     scalar=ALPHA, in1=bank(6 + half)[0:C, :], op0=ALU.mult, op1=ALU.add), [xh, pb[6 + half]], [xh])
        self.layer_norm(xh, C, self.lnbc.t[:, 2, :], self.lnbc.t[:, 3, :], self.epsc.t[0:C, 0:1], "ln1")
        row0 = TP if smp else e0
        S.dma(self.h1s[row0:row0 + C, :], xh.t[0:C, :], [xh], [self.h1scr])

    def conv_state_out(self, C, smp):
        S, cm, pb, bank = self.S, self.cm, self.pb, self.bank
        if smp:
            n = NS * 3
            cst = self.bg(3)[:, 0:12 * n].rearrange("p (a c) -> p a c", a=12)
            S.op("pool", lambda e: e.tensor_copy(out=cst.rearrange("p a (s l) -> p a s l", l=3),
                                                 in_=self.qkvx.t[:, :, 0:NS * 11].rearrange("p a (s l) -> p a s l", l=11)[:, :, :, 8:11]),
                 [self.qkvx], [self.big[3]])
            src = lambda cc: cst[:, cc, :]
            dst = self.gconv_s
        else:
            n = 3
            src = lambda cc: self.qkvx.t[:, cc, 0:3]
            dst = self.gconv_p
        stage = self.bg(1, 2)[0:n, 0:1536]
        for g in range(3):
            def tr(e, g=g):
                for j in range(4):
                    r = e.transpose(bank(2 + g)[0:n, j * 128:(j + 1) * 128], src(g * 4 + j), cm(C_ID))
                return r
            S.op("pe", tr, [self.qkvx, self.big[3], self.cmat], [pb[2 + g]])
            S.op("act", lambda e, g=g: e.activation(out=stage[:, g * 512:(g + 1) * 512], in_=bank(2 + g)[0:n, :], func=AF.Copy), [pb[2 + g]], [self.big[1], self.big[2]])
        S.dma(dst, stage, [self.big[1], self.big[2]], [], is_out=True)

    def stage1(self):
        S = self.S
        self.stage1_alloc()
        self.stage1_setup()
        R = lambda h2: slice(64 * h2, 64 * h2 + 64)
        plist = []
        e0 = 0
        while e0 < TP:
            C = min(128, TP - e0)
            skip = (DBG.get("maxchunks") is not None and e0 // 128 >= DBG["maxchunks"] and C == 128) or (DBG.get("no16") and C == 16)
            if not skip:
                plist.append((e0, C, "p"))
            e0 += C
        plist = [(a, b, c, self.xhs[i % 2]) for i, (a, b, c) in enumerate(plist)]
        self.sample_item = (0, NS * LS, "s", self.xhs[len(plist) % 2])
        self.front0(*plist[0])
        for i, it in enumerate(plist):
            self.chunk(*it, plist[i + 1] if i + 1 < len(plist) else None)
        if DBG.get("stop_early"):
            return
        if DBG.get("stop_after_prompt"):
            return
        for h2 in range(2):
            S.dma(self.gdn_p.rearrange("(a two) d v -> two d a v", two=2)[h2], self.Sg.t[R(h2), :, :], [self.Sg], [], is_out=True)
            S.dma(self.gla_p.rearrange("(a two) d v -> two d a v", two=2)[h2], self.Sl.t[R(h2), :, :], [self.Sl], [], is_out=True)
        if not DBG.get("no_cso"):
            self.conv_state_out(16, False)
        if DBG.get("stop_after_outputs"):
            return
        stc = self.bg(3, 2)[0:NS * 3, 0:1536]
        S.dma(stc, self.sgconv, [], [self.big[3], self.big[4]])
        qs = self.qkvx.t[:, :, 0:NS * 11].rearrange("p a (s l) -> p a s l", l=11)
        for g in range(3):
            def tr(e, g=g):
                for j in range(4):
                    cc = g * 4 + j
                    r = e.transpose(self.bank(2 + g)[:, j * 128:j * 128 + NS * 3], stc[:, cc * 128:(cc + 1) * 128], self.cm(C_ID, NS * 3, NS * 3))
                return r
            S.op("pe", tr, [self.big[3], self.big[4], self.cmat], [self.pb[2 + g]])
            S.op("act", lambda e, g=g: e.activation(out=qs[:, g * 4:(g + 1) * 4, :, 0:3],
                                                    in_=self.bank(2 + g).rearrange("p (j c) -> p j c", j=4)[:, :, 0:NS * 3].rearrange("p j (s l) -> p j s l", l=3),
                                                    func=AF.Copy), [self.pb[2 + g]], [self.qkvx])
        self.front0(*self.sample_item)
        self.chunk(*self.sample_item, None)
        self.conv_state_out(NS * LS, True)

    def stage2(self):
        S, cm, pb, bank = self.S, self.cm, self.pb, self.bank
        self.lnbc = S.sb("lnbc2", [128, 2, D], F32)
        w_up = S.sb("w_up_sb", [128, 8, 2 * DFF], BF16)
        w_dn = S.sb("w_dn", [128, 22, D], BF16)
        cwf = S.sb("cwf_sb", [128, NFF, 4], F32)
        h1t = [S.sb(f"h1t{i}", [128, D], F32) for i in range(2)]
        h1T = S.sb("h1T", [128, 8, 512], BF16)
        ue = [S.sb(f"ue{i}", [128, 160], F32) for i in range(2)]
        accs = [[S.sb(f"acc{w}{i}", [128, 512], F32) for i in range(2)] for w in range(2)]
        sgt1 = S.sb("sgt", [128, 512], F32)
        sgt = [sgt1, sgt1]
        hc = S.sb("hc", [128, NFF, 4], F32)
        uh = [S.sb(f"uh{i}", [128, NFF, 2], F32) for i in range(2)]
        actT = S.sb("actT", [128, 22, 512], BF16)
        usave = S.sb("usave", [128, NFF, 32], F32)
        st32 = S.sb("st32", [128, 512], F32)
        for i in range(2):
            S.dma(self.lnbc.t[:, i, :], self.lnv_d[4 + i:5 + i, :].partition_broadcast(128), [], [self.lnbc])
        S.dma(cwf.t[:], self.cwf_d, [], [cwf])
        wu_ = self.w_up_d.rearrange("(kc p) n -> p kc n", p=128)
        wd_ = self.w_down_d.rearrange("(j p) n -> p j n", p=128)
        w_up_tb = {}
        wdma = []
        for g in range(3):
            for which in range(2):
                c0 = which * DFF + g * 1024
                c1 = min(c0 + 1024, (which + 1) * DFF)
                tb = S.alias(f"w_up_{which}_{g}", w_up)
                for gg in (2 * g, 2 * g + 1):
                    w_up_tb[(which, gg)] = tb
                for kc in range(0, 8, 4):
                    wdma.append((lambda kc=kc, c0=c0, c1=c1, tb=tb: S.dma(w_up.t[:, kc:kc + 4, c0:c1], wu_[:, kc:kc + 4, c0:c1], [], [tb], cast=True)))
        w_dn_tb = {}
        wddma = []
        for j in range(0, 22, 2):
            tb = S.alias(f"w_dn_{j}", w_dn)
            w_dn_tb[j] = tb
            w_dn_tb[j + 1] = tb
            wddma.append((lambda j=j, tb=tb: S.dma(w_dn.t[:, j:j + 2, :], wd_[:, j:j + 2, :], [], [tb], cast=True)))
        for _ in range(4):
            wdma.pop(0)()
        S.op("pool", lambda e: e.memset(uh[0].t[:], 0.0), [], [uh[0]])

        def conv_out(n, dst, src):
            for g in range(11):
                def tr(e, g=g):
                    for j in range(4):
                        r = e.transpose(bank(g % 4)[0:n, j * 128:(j + 1) * 128], src.t[:, g * 4 + j, 0:n], cm(C_ID))
                    return r
                S.op("pe", tr, [src, self.cmat], [pb[g % 4]])
                S.op("act", lambda e, g=g: e.activation(out=st32.t[0:n, :], in_=bank(g % 4)[0:n, :], func=AF.Copy), [pb[g % 4]], [st32])
                S.dma(dst[:, g * 512:(g + 1) * 512], st32.t[0:n, :], [st32], [], is_out=True)

        tiles = []
        e0 = 0
        while e0 < TP:
            W = min(512, TP - e0)
            tiles.append((e0, W, False))
            e0 += W
        tiles.append((TP, NS * LS, True))
        upb = 0
        for ti, (r0, W, smp) in enumerate(tiles):
            G, L = (NS, LS) if smp else (1, W)
            uprev, ucur = uh[ti % 2], uh[(ti + 1) % 2]
            if not smp and not DBG.get("nohc"):
                S.op("pool", lambda e, uprev=uprev: e.tensor_tensor(out=hc.t[:, :, 0:2], in0=uprev.t[:, :, :], in1=cwf.t[:, :, 0:1].to_broadcast([128, NFF, 2]), op=ALU.mult),
                     [uprev, cwf], [hc])
                S.op("pool", lambda e, uprev=uprev: e.tensor_tensor(out=hc.t[:, :, 2:3], in0=uprev.t[:, :, 1:2], in1=cwf.t[:, :, 1:2], op=ALU.mult),
                     [uprev, cwf, hc], [hc])
            if smp:
                for g in range(11):
                    S.dma(st32.t[0:NS * 2, :], self.sfconv[:, g * 512:(g + 1) * 512], [], [st32])

                    def tr(e, g=g):
                        for j in range(4):
                            r = e.transpose(bank(4 + g % 2)[:, j * 32:(j + 1) * 32], st32.t[0:NS * 2, j * 128:(j + 1) * 128], cm(C_ID, NS * 2, NS * 2))
                        return r
                    S.op("pe", tr, [st32, self.cmat], [pb[4 + g % 2]])
                    S.op("act", lambda e, g=g: e.activation(out=usave.t[:, g * 4:(g + 1) * 4, :],
                                                            in_=bank(4 + g % 2)[:, 0:128].rearrange("p (j c) -> p j c", j=4), func=AF.Copy), [pb[4 + g % 2]], [usave])
            nsub = (W + 127) // 128
            for j in range(nsub):
                C = min(128, W - j * 128)
                ht = h1t[j % 2]
                S.dma(ht.t[0:C, :], self.h1s[r0 + j * 128:r0 + j * 128 + C, :], [self.h1scr], [ht])
                for half in range(2):
                    def tr(e, half=half, ht=ht, C=C):
                        for q in range(4):
                            r = e.transpose(bank(6 + half)[:, q * 128:q * 128 + C], ht.t[0:C, (half * 4 + q) * 128:(half * 4 + q + 1) * 128], cm(C_ID, C, C))
                        return r
                    S.op("pe", tr, [ht, self.cmat], [pb[6 + half]])
                    S.op("act", lambda e, half=half, j=j, C=C: e.activation(out=h1T.t[:, half * 4:half * 4 + 4, j * 128:j * 128 + C],
                                                                            in_=bank(6 + half).rearrange("p (q c) -> p q c", q=4)[:, :, 0:C], func=AF.Copy),
                         [pb[6 + half]], [h1T])
            pend = None
            for jg in range(22):
                par = jg % 2
                if jg % 8 == 1:
                    for _ in range(4):
                        if wdma:
                            wdma.pop(0)()
                if wddma and jg % 2 == 0:
                    wddma.pop(0)()
                if pend is not None:
                    pend()
                for which in (0, 1):
                    acc = accs[which][par]
                    cc = jg + 22 * which
                    b = upb % 4
                    upb += 1
                    wtb = w_up_tb[(which, jg // 4)]

                    def mmu(e, cc=cc, b=b):
                        for kc in range(8):
                            r = e.matmul(bank(b)[:, 0:W], lhsT=w_up.t[:, kc, cc * 128:(cc + 1) * 128], rhs=h1T.t[:, kc, 0:W], start=(kc == 0), stop=(kc == 7))
                        return r
                    S.op("pe", mmu, [wtb, h1T], [pb[b]])
                    if smp:
                        u = ue[cc % 2]
                        uv = u.t[:, 0:G * (L + 2)].rearrange("p (g l) -> p g l", g=G)
                        S.op("act", lambda e, b=b, uv=uv: e.activation(out=uv[:, :, 2:2 + L], in_=bank(b)[:, 0:W].rearrange("p (g l) -> p g l", g=G), func=AF.Copy),
                             [pb[b]], [u])
                        hsrc = usave.t[:, cc, :].rearrange("p (s i) -> p s i", i=2)
                        S.op("pool", lambda e, uv=uv, hsrc=hsrc: e.tensor_copy(out=uv[:, :, 0:2], in_=hsrc), [usave, u], [u])
                        av = acc.t[:, 0:W].rearrange("p (g l) -> p g l", g=G)
                        S.op("act", lambda e, uv=uv, av=av, cc=cc: e.activation(out=av, in_=uv[:, :, 2:2 + L], func=AF.Identity,
                                                                                scale=cwf.t[:, cc, 2:3], bias=cwf.t[:, cc, 3:4]), [u, cwf], [acc])
                        S.op("dve", lambda e, uv=uv, av=av, cc=cc: e.scalar_tensor_tensor(out=av, in0=uv[:, :, 1:1 + L], scalar=cwf.t[:, cc, 1:2], in1=av,
                                                                                           op0=ALU.mult, op1=ALU.add), [u, cwf, acc], [acc])
                        S.op("dve", lambda e, uv=uv, av=av, cc=cc: e.scalar_tensor_tensor(out=av, in0=uv[:, :, 0:L], scalar=cwf.t[:, cc, 0:1], in1=av,
                                                                                           op0=ALU.mult, op1=ALU.add), [u, cwf, acc], [acc])
                        hdst = usave.t[:, cc, :].rearrange("p (s i) -> p s i", i=2)
                        S.op("pool", lambda e, uv=uv, hdst=hdst: e.tensor_copy(out=hdst, in_=uv[:, :, L:L + 2]), [u, usave], [usave])
                    else:
                        S.op("act", lambda e, cc=cc, b=b: e.activation(out=acc.t[:, 0:W], in_=bank(b)[:, 0:W], func=AF.Identity,
                                                                       scale=cwf.t[:, cc, 2:3], bias=cwf.t[:, cc, 3:4]), [pb[b], cwf], [acc])
                        if not DBG.get("noact2"):
                            S.op("dve", lambda e, cc=cc, b=b, ucur=ucur: e.tensor_copy(out=ucur.t[:, cc, 0:2], in_=bank(b)[:, W - 2:W]), [pb[b], acc], [ucur])
                        if not DBG.get("nostt"):
                            S.op("dve", lambda e, cc=cc, b=b, acc=acc: e.scalar_tensor_tensor(out=acc.t[:, 1:W], in0=bank(b)[:, 0:W - 1], scalar=cwf.t[:, cc, 1:2], in1=acc.t[:, 1:W],
                                                                                     op0=ALU.mult, op1=ALU.add), [pb[b], cwf, acc], [acc])
                            S.op("dve", lambda e, cc=cc, b=b, acc=acc: e.scalar_tensor_tensor(out=acc.t[:, 2:W], in0=bank(b)[:, 0:W - 2], scalar=cwf.t[:, cc, 0:1], in1=acc.t[:, 2:W],
                                                                                     op0=ALU.mult, op1=ALU.add), [pb[b], cwf, acc], [acc])
                        if not DBG.get("nopool"):
                            S.op("pool", lambda e, cc=cc, acc=acc: e.tensor_tensor(out=acc.t[:, 0:2], in0=acc.t[:, 0:2], in1=hc.t[:, cc, 0:2], op=ALU.add), [acc, hc], [acc])
                            S.op("pool", lambda e, cc=cc, acc=acc: e.tensor_tensor(out=acc.t[:, 0:1], in0=acc.t[:, 0:1], in1=hc.t[:, cc, 2:3], op=ALU.add), [acc, hc], [acc])
                def fin(jg=jg, par=par):
                    S.op("act", lambda e: e.activation(out=sgt[par].t[:, 0:W], in_=accs[0][par].t[:, 0:W], func=AF.Silu), [accs[0][par]], [sgt[par]])
                    S.op("pool", lambda e: e.tensor_tensor(out=actT.t[:, jg, 0:W], in0=sgt[par].t[:, 0:W], in1=accs[1][par].t[:, 0:W], op=ALU.mult),
                         [sgt[par], accs[1][par]], [actT])
                pend = fin
            pend()
            def reload(j):
                Cj = min(128, W - j * 128)
                S.dma(h1t[j % 2].t[0:Cj, :], self.h1s[r0 + j * 128:r0 + j * 128 + Cj, :], [self.h1scr], [h1t[j % 2]])
            reload(0)
            for j in range(nsub):
                C = min(128, W - j * 128)
                ht = h1t[j % 2]
                if j + 1 < nsub:
                    reload(j + 1)
                for half in range(2):
                    def mmd(e, half=half, j=j, C=C):
                        for jg in range(22):
                            r = e.matmul(bank(4 + half)[0:C, :], lhsT=actT.t[:, jg, j * 128:j * 128 + C], rhs=w_dn.t[:, jg, half * 512:(half + 1) * 512],
                                         start=(jg == 0), stop=(jg == 21))
                        return r
                    S.op("pe", mmd, [actT] + [w_dn_tb[j_] for j_ in range(0, 22, 2)], [pb[4 + half]])
                    S.op("dve", lambda e, half=half, ht=ht, C=C: e.scalar_tensor_tensor(out=ht.t[0:C, half * 512:(half + 1) * 512], in0=ht.t[0:C, half * 512:(half + 1) * 512],
                                                                                        scalar=ALPHA, in1=bank(4 + half)[0:C, :], op0=ALU.mult, op1=ALU.add),
                         [ht, pb[4 + half]], [ht])
                self.layer_norm(ht, C, self.lnbc.t[:, 0, :], self.lnbc.t[:, 1, :], self.epsc.t[0:C, 0:1], "ln2")
                if smp:
                    S.dma(self.y_s, ht.t[0:C, :], [ht], [], is_out=True)
                else:
                    e = r0 + j * 128
                    if e == 0:
                        S.dma(self.y_p[0:128 - NMETA, :], ht.t[NMETA:128, :], [ht], [], is_out=True)
                    else:
                        S.dma(self.y_p[e - NMETA:e - NMETA + C, :], ht.t[0:C, :], [ht], [], is_out=True)
            if (not smp) and r0 + W == TP:
                conv_out(2, self.fconv_p, ucur)
        conv_out(NS * 2, self.fconv_s, usave)

    def build(self):
        S = self.S
        with S:
            self.cmat = S.sb("cmat_sb", [128, NCM, 128], F32)
            self.epsc = S.sb("epsc", [128, 4], F32)
            self.ln_st = S.sb("ln_st", [128, 2, 6], F32)
            self.ln_mv = S.sb("ln_mv", [128, 4], F32)
            self.glb = S.sb("glb", [128, 128], F32)
            self.pst = S.ps("pst", [128, 8, 512], F32)
            self.pb = [S.alias(f"pb{i}", self.pst) for i in range(8)]
            self.h1scr = TB("h1scr", None)
            main_stack = S.stack
            S.stack = ExitStack()
            with S.stack:
                if not DBG.get("skip1"):
                    self.stage1()
                S.barrier()
            S.stack = ExitStack()
            with S.stack:
                if not DBG.get("skip2"):
                    self.stage2()
                S.finish()
            S.stack = main_stack
        return self.nc


_PROG = None


def _program():
    global _PROG
    if _PROG is None:
        _PROG = K().build()
    return _PROG


def kernel(x_prompt, x_sample, state_gdn, state_gdn_conv, state_gla, state_ffn_conv, meta_tokens,
           ln_in_g, ln_in_b, w_in, gdn_conv_w, gdn_A_log, gdn_dt_bias, gdn_norm_g, gla_wgk2,
           gla_bgk, gla_norm_g, w_out, ln1_g, ln1_b, w_up, ffn_conv_w, ffn_conv_b, w_down,
           ln2_g, ln2_b):
    f = lambda a: np.ascontiguousarray(np.asarray(a, dtype=np.float32))
    x_prompt, x_sample = f(x_prompt), f(x_sample)
    w_in0 = f(w_in)[0]
    fm_cols = np.r_[0:1536, 2064:2320, 2320:2576, 3088:3104]
    tm_cols = np.r_[1536:1552, 2320:2576, 1552:2064, 2576:3088, 3104:3616]
    w_in_r = np.ascontiguousarray(w_in0[:, np.r_[fm_cols, tm_cols]])
    lnv = np.stack([f(ln_in_g), f(ln_in_b), f(ln1_g)[0], f(ln1_b)[0], f(ln2_g)[0], f(ln2_b)[0]])
    cwg = np.ascontiguousarray(f(gdn_conv_w)[0].T.reshape(12, 128, 4).transpose(1, 0, 2))
    cwf4 = np.concatenate([f(ffn_conv_w)[0], f(ffn_conv_b)], axis=0)
    cwf = np.ascontiguousarray(cwf4.T.reshape(NFF, 128, 4).transpose(1, 0, 2))
    pvec = np.concatenate([f(gdn_A_log)[0], f(gdn_dt_bias)[0], f(gdn_norm_g)[0], f(gla_norm_g)[0], f(gla_bgk)[0]])[None, :]
    shared = dict(meta=f(meta_tokens), cmat=_const_mats(), w_in_r=w_in_r, w_out=f(w_out)[0], w_up=f(w_up)[0], w_down=f(w_down)[0],
                  lnv=np.ascontiguousarray(lnv), cwg=cwg, cwf=cwf, pvec=np.ascontiguousarray(pvec), wgk2=f(gla_wgk2)[0])
    sg, sgc, sl, sfc = f(state_gdn)[0], f(state_gdn_conv)[0], f(state_gla)[0], f(state_ffn_conv)[0]
    in_maps = []
    for c in range(8):
        sl_ = slice(c * NS, (c + 1) * NS)
        m = dict(shared)
        m.update(xp=x_prompt[c], xs=np.ascontiguousarray(x_sample[sl_].reshape(NS * LS, D)), sgdn=sg[sl_],
                 sgconv=np.ascontiguousarray(sgc[sl_].reshape(NS * 3, 1536)), sgla=sl[sl_],
                 sfconv=np.ascontiguousarray(sfc[sl_].reshape(NS * 2, 2 * DFF)))
        in_maps.append(m)
    ncr = DBG.get("ncores", 8)
    res = run_bass_kernel_spmd(_program(), in_maps[:ncr], core_ids=list(range(ncr)))
    r = res.results
    cat = lambda k: np.stack([np.asarray(r[min(c, ncr - 1)][k]) for c in range(8)])
    y_prompt = cat("y_p")
    y_sample = cat("y_s").reshape(128, LS, D)
    gdn_p = cat("gdn_p")[None]
    gconv_p = cat("gconv_p")[None]
    gla_p = cat("gla_p")[None]
    fconv_p = cat("fconv_p")[None]
    gdn_s = cat("gdn_s").reshape(1, 128, 8, 64, 64)
    gconv_s = cat("gconv_s").reshape(1, 128, 3, 1536)
    gla_s = cat("gla_s").reshape(1, 128, 4, 64, 128)
    fconv_s = cat("fconv_s").reshape(1, 128, 2, 2 * DFF)
    outs = (y_prompt, y_sample, gdn_p, gconv_p, gla_p, fconv_p, gdn_s, gconv_s, gla_s, fconv_s)
    return tuple(np.ascontiguousarray(o, dtype=np.float32) for o in outs)
```

```python
from contextlib import ExitStack

import numpy as np
import concourse.bass as bass
import concourse.mybir as mybir
from concourse.bass_utils import run_bass_kernel_spmd

F32 = mybir.dt.float32
BF16 = mybir.dt.bfloat16
F32R = mybir.dt.float32r
AF = mybir.ActivationFunctionType
ALU = mybir.AluOpType
AX = mybir.AxisListType


class TB:
    def __init__(self, name, t):
        self.name = name
        self.t = t
        self.last_w = None
        self.readers = {}
        self.parts = []


class Sched:
    def __init__(self, nc, nslots=40):
        self.nc = nc
        self.stack = ExitStack()
        self.engs = {"pe": nc.tensor, "dve": nc.vector, "act": nc.scalar, "pool": nc.gpsimd, "sp": nc.sync}
        self.nslots = nslots

    def __enter__(self):
        self.stack.__enter__()
        nc = self.nc
        self.sem = {k: self.stack.enter_context(nc.semaphore(f"s_{k}")) for k in self.engs}
        self.cnt = {k: 0 for k in self.engs}
        self.waited = {k: {} for k in self.engs}
        self.slot_sem = [self.stack.enter_context(nc.semaphore(f"d_{i}")) for i in range(self.nslots)]
        self.slot_cnt = [0] * self.nslots
        self.next_slot = {"sp": 0, "pool": 0}
        self.out_deps = []
        self.nbuf = 0
        return self

    def __exit__(self, *a):
        return self.stack.__exit__(*a)

    def sb(self, name, shape, dtype):
        t = self.stack.enter_context(self.nc.sbuf_tensor(name, list(shape), dtype))
        return TB(name, t)

    def ps(self, name, shape, dtype):
        t = self.stack.enter_context(self.nc.psum_tensor(name, list(shape), dtype))
        return TB(name, t)

    def alias(self, name, tb):
        return TB(name, tb.t)

    def _semof(self, key):
        if isinstance(key, tuple):
            return self.slot_sem[key[1]]
        return self.sem[key]

    def _wait(self, eng, dep):
        key, val = dep
        if eng == "pe" and key == "pe":
            return
        w = self.waited[eng]
        if w.get(key, 0) >= val:
            return
        self.engs[eng].wait_ge(self._semof(key), val)
        w[key] = val

    @staticmethod
    def _expand(bufs):
        out = []
        for b in bufs:
            out.append(b)
            out.extend(b.parts)
        return out

    def _deps(self, reads, writes):
        reads, writes = self._expand(reads), self._expand(writes)
        deps = set()
        for b in reads:
            if b.last_w is not None:
                deps.add(b.last_w)
        for b in writes:
            if b.last_w is not None:
                deps.add(b.last_w)
            for k, v in b.readers.items():
                deps.add((k, v))
        return deps

    def _commit(self, me, reads, writes):
        reads, writes = self._expand(reads), self._expand(writes)
        for b in writes:
            b.last_w = me
            b.readers = {}
        for b in reads:
            if b not in writes:
                b.readers[me[0]] = max(b.readers.get(me[0], 0), me[1])

    def op(self, eng, emit, reads=(), writes=()):
        for d in sorted(self._deps(reads, writes), key=str):
            self._wait(eng, d)
        inst = emit(self.engs[eng])
        self.cnt[eng] += 1
        inst.then_inc(self.sem[eng], 1)
        self._commit((eng, self.cnt[eng]), reads, writes)

    def dma(self, out, in_, reads=(), writes=(), cast=False, is_out=False, q=None):
        eng = q or ("pool" if cast else "sp")
        nsp = (self.nslots * 5) // 8
        lo, n = (0, nsp) if eng == "sp" else (nsp, self.nslots - nsp)
        i = lo + self.next_slot[eng]
        self.next_slot[eng] = (self.next_slot[eng] + 1) % n
        if self.slot_cnt[i] > 0:
            self._wait(eng, (("slot", i), 16 * self.slot_cnt[i]))
        for d in sorted(self._deps(reads, writes), key=str):
            self._wait(eng, d)
        inst = self.engs[eng].dma_start(out=out, in_=in_)
        inst.then_inc(self.slot_sem[i], 16)
        self.slot_cnt[i] += 1
        me = (("slot", i), 16 * self.slot_cnt[i])
        self._commit(me, reads, writes)
        if is_out:
            self.out_deps.append(me)

    def finish(self):
        for d in self.out_deps:
            self._wait("sp", d)
        for k in ("pe", "dve", "act", "pool"):
            if self.cnt[k] > 0:
                self._wait("sp", (k, self.cnt[k]))

    def barrier(self):
        deps = [(k, self.cnt[k]) for k in ("pe", "dve", "act", "pool") if self.cnt[k] > 0]
        deps += [(("slot", i), 16 * c) for i, c in enumerate(self.slot_cnt) if c > 0]
        for e in ("pe", "dve", "act", "pool", "sp"):
            for d in deps:
                if d[0] != e:
                    self._wait(e, d)


DBG = {}
D = 1024
SEQ = 2048
NMETA = 16
TP = SEQ + NMETA
NS = 16
LS = 8
DFF = 2816
NFF = 44
ALPHA = 2.0 ** 0.25
NFM = 2064
NTM = 1808
NEG = -30000.0

C_ID, C_ONE, C_BO64 = 0, 1, 2
C_PU, C_PSU, C_PMBT, C_PMBS, C_PM01T = 3, 4, 5, 6, 7
C_SU, C_SSU, C_SBO, C_SMBT, C_SMBS, C_SM01T, C_SBM = 8, 9, 10, 11, 12, 13, 14
NCM = 15


def _const_mats():
    m = np.zeros((NCM, 128, 128), np.float32)
    k = np.arange(128)[:, None]
    c = np.arange(128)[None, :]
    m[C_ID] = (k == c)
    m[C_ONE] = 1.0
    m[C_BO64] = (k // 64 == c // 64)
    m[C_PU] = (k <= c)
    m[C_PSU] = (k > c)
    m[C_PMBT] = np.where(c >= k, 0.0, NEG)
    m[C_PMBS] = np.where(c < k, 0.0, NEG)
    m[C_PM01T] = (c >= k)
    sb = (k // LS == c // LS)
    m[C_SU] = (k <= c) & sb
    m[C_SSU] = (k > c) & sb
    m[C_SBO] = sb
    m[C_SMBT] = np.where((c >= k) & sb, 0.0, NEG)
    m[C_SMBS] = np.where((c < k) & sb, 0.0, NEG)
    m[C_SM01T] = (c >= k) & sb
    m[C_SBM][:, :NS] = (k // LS == np.arange(NS)[None, :])
    return np.ascontiguousarray(m.transpose(1, 0, 2))


def _bc(ap, axis, shape):
    return ap.unsqueeze(axis).to_broadcast(list(shape))


class K:
    def __init__(self):
        nc = self.nc = bass.Bass("TRN2", target_bir_lowering=False)
        di = lambda n, s: nc.dram_tensor(n, list(s), F32, kind="ExternalInput").ap()
        do = lambda n, s: nc.dram_tensor(n, list(s), F32, kind="ExternalOutput").ap()
        self.xp = di("xp", [SEQ, D]); self.xs = di("xs", [NS * LS, D]); self.meta = di("meta", [NMETA, D])
        self.sgdn = di("sgdn", [NS, 8, 64, 64]); self.sgconv = di("sgconv", [NS * 3, 1536])
        self.sgla = di("sgla", [NS, 4, 64, 128]); self.sfconv = di("sfconv", [NS * 2, 2 * DFF])
        self.cmat_d = di("cmat", [128, NCM, 128])
        self.w_in_d = di("w_in_r", [D, NFM + NTM]); self.w_out_d = di("w_out", [D, D])
        self.w_up_d = di("w_up", [D, 2 * DFF]); self.w_down_d = di("w_down", [DFF, D])
        self.lnv_d = di("lnv", [6, D]); self.cwg_d = di("cwg", [128, 12, 4]); self.cwf_d = di("cwf", [128, NFF, 4])
        self.pvec_d = di("pvec", [1, 464]); self.wgk2_d = di("wgk2", [16, 256])
        self.y_p = do("y_p", [SEQ, D]); self.y_s = do("y_s", [NS * LS, D])
        self.gdn_p = do("gdn_p", [8, 64, 64]); self.gconv_p = do("gconv_p", [3, 1536])
        self.gla_p = do("gla_p", [4, 64, 128]); self.fconv_p = do("fconv_p", [2, 2 * DFF])
        self.gdn_s = do("gdn_s", [NS, 8, 64, 64]); self.gconv_s = do("gconv_s", [NS * 3, 1536])
        self.gla_s = do("gla_s", [NS, 4, 64, 128]); self.fconv_s = do("fconv_s", [NS * 2, 2 * DFF])
        self.h1s = nc.dram_tensor("h1s", [TP + NS * LS, D], F32, kind="Internal").ap()
        self.S = Sched(nc)

    def cm(self, idx, r=128, c=128):
        return self.cmat.t[0:r, idx, 0:c]

    def bank(self, b, n=1):
        if n == 1:
            return self.pst.t[:, b, :]
        return self.pst.t[:, b:b + n, :].rearrange("p b f -> p (b f)")

    def layer_norm(self, buf, C, g_ap, b_ap, eps_tile, tag):
        S = self.S
        st, mv = self.ln_st, self.ln_mv
        x = buf.t

        def stats(e):
            e.bn_stats(out=st.t[0:C, 0, :], in_=x[0:C, 0:512])
            return e.bn_stats(out=st.t[0:C, 1, :], in_=x[0:C, 512:1024])
        S.op("dve", stats, [buf], [st])
        S.op("dve", lambda e: e.bn_aggr(out=mv.t[0:C, 0:2], in_=st.t[0:C, :, :].rearrange("p a b -> p (a b)")), [st], [mv])
        S.op("act", lambda e: e.activation(out=mv.t[0:C, 2:3], in_=mv.t[0:C, 1:2], func=AF.Ln, bias=eps_tile, scale=1.0), [mv, self.epsc], [mv])
        S.op("act", lambda e: e.activation(out=mv.t[0:C, 3:4], in_=mv.t[0:C, 2:3], func=AF.Exp, scale=-0.5), [mv], [mv])
        S.op("dve", lambda e: e.tensor_scalar(out=x[0:C, :], in0=x[0:C, :], scalar1=mv.t[0:C, 0:1], scalar2=mv.t[0:C, 3:4],
                                              op0=ALU.subtract, op1=ALU.mult), [buf, mv], [buf])
        S.op("dve", lambda e: e.tensor_tensor(out=x[0:C, :], in0=x[0:C, :], in1=g_ap[0:C, :], op=ALU.mult), [buf, self.lnbc], [buf])
        S.op("dve", lambda e: e.tensor_tensor(out=x[0:C, :], in0=x[0:C, :], in1=b_ap[0:C, :], op=ALU.add), [buf, self.lnbc], [buf])

    def stage1_alloc(self):
        S = self.S
        self.lnbc = S.sb("lnbc", [128, 4, D], F32)
        self.pvec = S.sb("pvec_sb", [128, 464], F32)
        self.negA = S.sb("negA", [128, 8], F32)
        self.wgk2 = S.sb("wgk2_sb", [16, 256], F32)
        self.cwg = S.sb("cwg_sb", [128, 12, 4], F32)
        self.w_in = S.sb("w_in_sb", [128, 8, NFM + NTM], BF16)
        self.w_out = S.sb("w_out_sb", [128, 8, D], BF16)
        self.xhs = [S.sb(f"xh{i}", [128, D], F32) for i in range(2)]
        self.hT = S.sb("hT", [128, 8, 128], BF16)
        self.qkvx = S.sb("qkvx", [128, 12, 176], F32)
        self.fmx = S.sb("fmx", [128, 5, 128], F32)
        self.tm0 = S.sb("tm0", [128, 272], F32)
        self.gv_tok = S.sb("gv_tok", [128, 512], F32)
        self.sg_gdn = S.sb("sg_gdn", [128, 512], F32)
        self.sg_gla = S.sb("sg_gla", [128, 512], F32)
        self.bigs = S.sb("bigs", [128, 6, 1024], F32)
        self.big = [S.alias(f"big{i}", self.bigs) for i in range(6)]
        self.Pc = S.sb("Pc", [128, 1024], F32)
        self.PTc = S.sb("PTc", [128, 1024], F32)
        self.TTc = S.sb("TTc", [128, 1024], F32)
        for tb in (self.Pc, self.PTc, self.TTc):
            tb.parts = [S.alias(f"{tb.name}_hg{g}", tb) for g in range(2)]
        self.kq = S.sb("kq", [128, 4, 2, 128], F32)
        self.wkT = S.sb("wkT", [128, 8, 128], F32)
        self.KTm = self.wkT
        self.QTm = S.sb("QTm", [128, 8, 128], F32)
        self.keTm = S.sb("keTm", [128, 4, 128], F32)
        self.qeTm = S.sb("qeTm", [128, 4, 128], F32)
        self.wv = S.sb("wv", [128, 512], F32)
        self.RK = S.sb("RK", [128, 512], F32)
        self.RV = S.sb("RV", [128, 512], F32)
        self.kdec = S.sb("kdec", [128, 512], F32)
        self.U = self.RV
        self.ogdn = self.RK
        self.Sg = S.sb("Sg", [128, 4, 64], F32)
        self.sc = S.sb("sc", [128, 96], F32)
        self.lt = TB("lt", self.wv.t[:, 0:256])
        self.lt.parts = [self.wv]
        self.ebT = S.sb("ebT", [128, 2, 128], F32)
        self.enbT = S.sb("enbT", [128, 2, 128], F32)
        self.qeT = S.sb("qeT", [128, 2, 128], F32)
        self.PTg = S.sb("PTg", [128, 4, 128], F32)
        self.kd = TB("kd", self.enbT.t[:, :, :].rearrange("p a c -> p (a c)"))
        self.kd.parts = [self.enbT]
        self.ogla = S.sb("ogla", [128, 512], F32)
        self.Sl = S.sb("Sl", [128, 2, 128], F32)
        self.mix = S.sb("mix", [128, D], F32)
        self.mixT = S.sb("mixT", [128, 8, 128], BF16)
        self.otmp = TB("otmp", self.bigs.t[:, 3, 0:512])
        self.otmp.parts = [self.big[3]]

    def bg(self, i, n=1):
        if n == 1:
            return self.bigs.t[:, i, :]
        return self.bigs.t[:, i:i + n, :].rearrange("p b f -> p (b f)")

    def stage1_setup(self):
        S = self.S
        nc = self.nc
        S.dma(self.cmat.t[:], self.cmat_d, [], [self.cmat])
        for i in range(4):
            S.dma(self.lnbc.t[:, i, :], self.lnv_d[i:i + 1, :].partition_broadcast(128), [], [self.lnbc])
        S.dma(self.pvec.t[:], self.pvec_d[0:1, :].partition_broadcast(128), [], [self.pvec])
        S.dma(self.wgk2.t[:], self.wgk2_d, [], [self.wgk2])
        S.dma(self.cwg.t[:], self.cwg_d, [], [self.cwg])
        S.op("pool", lambda e: e.memset(self.epsc.t[:, 0:1], 1e-5), [], [self.epsc])
        S.op("pool", lambda e: e.memset(self.epsc.t[:, 1:2], 1e-6), [self.epsc], [self.epsc])
        S.op("pool", lambda e: e.memset(self.epsc.t[:, 2:3], 1.0), [self.epsc], [self.epsc])
        S.op("pool", lambda e: e.memset(self.epsc.t[:, 3:4], 0.0), [self.epsc], [self.epsc])
        wv_ = self.w_in_d.rearrange("(kc p) n -> p kc n", p=128)
        for kc in range(8):
            S.dma(self.w_in.t[:, kc, :], wv_[:, kc, :], [], [self.w_in], cast=True)
        wo_ = self.w_out_d.rearrange("(kc p) n -> p kc n", p=128)
        for kc in range(0, 8, 4):
            S.dma(self.w_out.t[:, kc:kc + 4, :], wo_[:, kc:kc + 4, :], [], [self.w_out], cast=True)
        S.op("act", lambda e: e.activation(out=self.negA.t[:], in_=self.pvec.t[:, 0:8], func=AF.Exp), [self.pvec], [self.negA])
        S.op("dve", lambda e: e.tensor_scalar(out=self.negA.t[:], in0=self.negA.t[:], scalar1=-1.0, scalar2=None, op0=ALU.mult),
             [self.negA], [self.negA])
        S.op("pool", lambda e: e.memset(self.Sg.t[:], 0.0), [], [self.Sg])
        S.op("pool", lambda e: e.memset(self.Sl.t[:], 0.0), [], [self.Sl])
        S.op("pool", lambda e: e.memset(self.qkvx.t[:], 0.0), [], [self.qkvx])
        for tb in (self.wkT, self.QTm, self.keTm, self.qeTm):
            S.op("pool", lambda e, tb=tb: e.memset(tb.t[:], 0.0), [], [tb])

    def front0(self, e0, C, kind, xh):
        self.front0_load(e0, C, kind, xh)
        self.front0_compute(e0, C, kind, xh)

    def front0_load(self, e0, C, kind, xh):
        S = self.S
        smp = kind == "s"
        if smp:
            S.dma(xh.t[0:C, :], self.xs, [], [xh])
        elif e0 == 0:
            S.dma(xh.t[0:NMETA, :], self.meta, [], [xh])
            S.dma(xh.t[NMETA:128, :], self.xp[0:128 - NMETA, :], [], [xh])
        else:
            S.dma(xh.t[0:C, :], self.xp[e0 - NMETA:e0 - NMETA + C, :], [], [xh])

    def front0_compute(self, e0, C, kind, xh):
        S, cm, pb, bank, hT = self.S, self.cm, self.pb, self.bank, self.hT
        self.layer_norm(xh, C, self.lnbc.t[:, 0, :], self.lnbc.t[:, 1, :], self.epsc.t[0:C, 0:1], "in")
        for half in range(2):
            def tr(e, half=half):
                for j in range(4):
                    kc = half * 4 + j
                    r = e.transpose(bank(half)[:, j * 128:j * 128 + C], xh.t[0:C, kc * 128:(kc + 1) * 128], cm(C_ID, C, C))
                return r
            S.op("pe", tr, [xh, self.cmat], [pb[half]])
            S.op("act", lambda e, half=half: e.activation(
                out=hT.t[:, half * 4:half * 4 + 4, 0:C],
                in_=bank(half).rearrange("p (j c) -> p j c", j=4)[:, :, 0:C], func=AF.Copy), [pb[half]], [hT])

    def chunk(self, e0, C, kind, xh, nxt):
        S = self.S
        self.xh = xh
        cm = self.cm
        pb = self.pb
        bank = self.bank
        big = self.big
        bg = self.bg
        smp = kind == "s"
        if smp:
            iU, iSU, iBO, iMBT, iMBS, iM01 = C_SU, C_SSU, C_SBO, C_SMBT, C_SMBS, C_SM01T
            G, L, nlev = NS, LS, 3
        else:
            iU, iSU, iBO, iMBT, iMBS, iM01 = C_PU, C_PSU, C_ONE, C_PMBT, C_PMBS, C_PM01T
            G, L, nlev = 1, C, {128: 7, 16: 4}[C]
        hT, qkvx, fmx, tm0, kq, sc = self.hT, self.qkvx, self.fmx, self.tm0, self.kq, self.sc
        ident = cm(C_ID)

        if DBG.get("step", 99) < 4:
            return
        qv = qkvx.t[:, :, 0:G * (L + 3)].rearrange("p a (g l) -> p a g l", g=G)
        for grp in range(5):
            b = 2 + (grp % 4)
            ccs = list(range(grp * 4, min(grp * 4 + 4, 17)))

            def mmf(e, ccs=ccs, b=b):
                for j, cc in enumerate(ccs):
                    M = 128 if cc < 16 else 16
                    for kc in range(8):
                        r = e.matmul(bank(b)[0:M, j * 128:j * 128 + C], lhsT=self.w_in.t[:, kc, cc * 128:cc * 128 + M],
                                     rhs=hT.t[:, kc, 0:C], start=(kc == 0), stop=(kc == 7))
                return r
            S.op("pe", mmf, [self.w_in, hT], [pb[b]])
            src = bank(b).rearrange("p (j c) -> p j c", j=4)
            if grp < 3:
                S.op("act", lambda e, grp=grp, src=src: e.activation(
                    out=qv[:, grp * 4:grp * 4 + 4, :, 3:3 + L],
                    in_=src[:, :, 0:C].rearrange("p j (g l) -> p j g l", g=G), func=AF.Copy), [pb[b]], [qkvx])
            elif grp == 3:
                S.op("act", lambda e, src=src: e.activation(out=fmx.t[:, 0:4, 0:C], in_=src[:, :, 0:C], func=AF.Copy), [pb[b]], [fmx])
            else:
                S.op("act", lambda e, src=src: e.activation(out=fmx.t[0:16, 4, 0:C], in_=src[0:16, 0, 0:C], func=AF.Copy), [pb[b]], [fmx])
        if DBG.get("step", 99) < 5:
            return
        tmoff = [NFM, NFM + 272, NFM + 784, NFM + 1296]
        tmn = [272, 512, 512, 512]
        for gi in range(4):
            b = 6 + (gi % 2)

            def mmt(e, gi=gi, b=b):
                for kc in range(8):
                    r = e.matmul(bank(b)[0:C, 0:tmn[gi]], lhsT=hT.t[:, kc, 0:C], rhs=self.w_in.t[:, kc, tmoff[gi]:tmoff[gi] + tmn[gi]],
                                 start=(kc == 0), stop=(kc == 7))
                return r
            S.op("pe", mmt, [self.w_in, hT], [pb[b]])
            if gi == 0:
                S.op("dve", lambda e, b=b: e.tensor_copy(out=tm0.t[0:C, :], in_=bank(b)[0:C, 0:272]), [pb[b]], [tm0])
            elif gi == 1:
                S.op("act", lambda e, b=b: e.activation(out=self.sg_gdn.t[0:C, :], in_=bank(b)[0:C, :], func=AF.Copy), [pb[b]], [self.sg_gdn])
            elif gi == 2:
                S.op("dve", lambda e, b=b: e.tensor_copy(out=self.gv_tok.t[0:C, :], in_=bank(b)[0:C, :]), [pb[b]], [self.gv_tok])
            else:
                S.op("act", lambda e, b=b: e.activation(out=self.sg_gla.t[0:C, :], in_=bank(b)[0:C, :], func=AF.Copy), [pb[b]], [self.sg_gla])
        if nxt is not None:
            self.front0_load(*nxt)
        if DBG.get("step", 99) < 6:
            return
        acc = bg(0, 2)[:, 0:12 * C].rearrange("p (a g l) -> p a g l", a=12, g=G)
        tmp = bg(2, 2)[:, 0:12 * C].rearrange("p (a g l) -> p a g l", a=12, g=G)
        accT, tmpT = [big[0], big[1]], [big[2], big[3]]

        def cwb(i):
            return self.cwg.t[:, :, i:i + 1].unsqueeze(3).to_broadcast([128, 12, G, L])
        S.op("dve", lambda e: e.tensor_tensor(out=acc, in0=qv[:, :, :, 0:L], in1=cwb(0), op=ALU.mult), [qkvx, self.cwg], accT)
        for i in range(1, 4):
            S.op("pool" if i == 1 else "dve", lambda e, i=i: e.tensor_tensor(out=tmp, in0=qv[:, :, :, i:i + L], in1=cwb(i), op=ALU.mult), [qkvx, self.cwg], tmpT)
            S.op("dve", lambda e: e.tensor_tensor(out=acc, in0=acc, in1=tmp, op=ALU.add), accT + tmpT, accT)
        qa = bg(0, 2)[:, 0:12 * C].rearrange("p (a c) -> p a c", a=12)
        S.op("act", lambda e: e.activation(out=qa, in_=qa, func=AF.Silu), accT, accT)
        if DBG.get("step", 99) < 7:
            return
        sq = bg(4)[:, 0:8 * C].rearrange("p (a c) -> p a c", a=8)
        rn = bg(5)[:, 0:8 * C].rearrange("p (a c) -> p a c", a=8)
        for sg in (self.sg_gdn, self.sg_gla):
            S.op("act", lambda e, sg=sg: e.activation(out=sg.t[0:C, :], in_=sg.t[0:C, :], func=AF.Silu), [sg], [sg])
        S.op("act", lambda e: e.activation(out=sq, in_=qa[:, 0:8, :], func=AF.Square), accT, [big[4]])
        S.op("act", lambda e: e.activation(out=sc.t[0:C, 16:24], in_=tm0.t[0:C, 8:16], func=AF.Sigmoid), [tm0], [sc])
        for half in range(2):
            S.op("pe", lambda e, half=half: e.matmul(bank(half)[:, 0:4 * C], lhsT=cm(C_BO64),
                                                     rhs=bg(4)[:, half * 4 * C:(half + 1) * 4 * C], start=True, stop=True),
                 [big[4], self.cmat], [pb[half]])
            S.op("act", lambda e, half=half: e.activation(out=bg(5)[:, half * 4 * C:(half + 1) * 4 * C], in_=bank(half)[:, 0:4 * C],
                                                          func=AF.Ln, bias=self.epsc.t[:, 1:2], scale=1.0), [pb[half], self.epsc], [big[5]])
        S.op("act", lambda e: e.activation(out=bg(5)[:, 0:8 * C], in_=bg(5)[:, 0:8 * C], func=AF.Exp, scale=-0.5), [big[5]], [big[5]])
        S.op("dve", lambda e: e.scalar_tensor_tensor(out=kq.t[:, :, 1, 0:C], in0=qa[:, 0:4, :], scalar=0.125, in1=rn[:, 0:4, :],
                                                      op0=ALU.mult, op1=ALU.mult), accT + [big[5]], [kq])
        S.op("pool", lambda e: e.tensor_tensor(out=kq.t[:, :, 0, 0:C], in0=qa[:, 4:8, :], in1=rn[:, 4:8, :], op=ALU.mult), accT + [big[5]], [kq])
        for h2 in range(2):
            rows = slice(64 * h2, 64 * h2 + 64)
            pad = lambda tb: tb.t[rows, :, 0:C].rearrange("p (a two) c -> p a two c", two=2)[:, :, h2, :]
            S.op("act", lambda e, rows=rows, pad=pad: e.activation(out=pad(self.KTm), in_=kq.t[rows, :, 0, 0:C], func=AF.Copy), [kq], [self.KTm])
            S.op("dve", lambda e, rows=rows, pad=pad: e.tensor_copy(out=pad(self.QTm), in_=kq.t[rows, :, 1, 0:C]), [kq], [self.QTm])
        if DBG.get("step", 99) < 8:
            return
        s_ = lambda a, b_: sc.t[0:C, a:b_]
        S.op("dve", lambda e: e.tensor_tensor(out=s_(0, 8), in0=tm0.t[0:C, 0:8], in1=self.pvec.t[0:C, 8:16], op=ALU.add), [tm0, self.pvec], [sc])
        S.op("act", lambda e: e.activation(out=s_(0, 8), in_=s_(0, 8), func=AF.Exp), [sc], [sc])
        S.op("act", lambda e: e.activation(out=s_(0, 8), in_=s_(0, 8), func=AF.Ln, bias=self.epsc.t[0:C, 2:3], scale=1.0), [sc, self.epsc], [sc])
        S.op("dve", lambda e: e.tensor_tensor(out=s_(8, 16), in0=s_(0, 8), in1=self.negA.t[0:C, :], op=ALU.mult), [sc, self.negA], [sc])

        def mmG(e):
            e.matmul(bank(0)[0:C, 0:8], lhsT=cm(iU, C, C), rhs=s_(8, 16), start=True, stop=True)
            return e.matmul(bank(0)[0:C, 8:16], lhsT=cm(iBO, C, C), rhs=s_(8, 16), start=True, stop=True)
        S.op("pe", mmG, [sc, self.cmat], [pb[0]])
        S.op("dve", lambda e: e.tensor_copy(out=s_(24, 40), in_=bank(0)[0:C, 0:16]), [pb[0]], [sc])
        S.op("act", lambda e: e.activation(out=s_(40, 48), in_=s_(24, 32), func=AF.Exp), [sc], [sc])
        S.op("dve", lambda e: e.tensor_tensor(out=s_(48, 56), in0=s_(32, 40), in1=s_(24, 32), op=ALU.subtract), [sc], [sc])
        S.op("act", lambda e: e.activation(out=s_(48, 56), in_=s_(48, 56), func=AF.Exp), [sc], [sc])
        S.op("act", lambda e: e.activation(out=s_(56, 64), in_=s_(32, 40), func=AF.Exp), [sc], [sc])
        S.op("dve", lambda e: e.tensor_tensor(out=s_(64, 72), in0=s_(16, 24), in1=s_(40, 48), op=ALU.mult), [sc], [sc])
        if DBG.get("step", 99) < 9:
            return
        def trk(e):
            for p in range(4):
                r = e.transpose(bank(6)[0:C, p * 128:(p + 1) * 128], kq.t[:, p, 0, 0:C], ident)
            return r
        S.op("pe", trk, [kq, self.cmat], [pb[6]])

        def trv(e):
            for p in range(4):
                r = e.transpose(bank(7)[0:C, p * 128:(p + 1) * 128], qa[:, 8 + p, :], ident)
            return r
        S.op("pe", trv, accT + [self.cmat], [pb[7]])
        h3 = lambda ap: ap.rearrange("p (h d) -> p h d", h=8)
        S.op("dve", lambda e: e.tensor_tensor(out=h3(self.RK.t[0:C, :]), in0=h3(bank(6)[0:C, :]), in1=_bc(s_(64, 72), 2, [C, 8, 64]), op=ALU.mult),
             [pb[6], sc], [self.RK])
        S.op("dve", lambda e: e.tensor_tensor(out=h3(self.kdec.t[0:C, :]), in0=h3(bank(6)[0:C, :]), in1=_bc(s_(48, 56), 2, [C, 8, 64]), op=ALU.mult),
             [pb[6], sc], [self.kdec])
        S.op("dve", lambda e: e.tensor_tensor(out=h3(self.RV.t[0:C, :]), in0=h3(bank(7)[0:C, :]), in1=_bc(s_(16, 24), 2, [C, 8, 64]), op=ALU.mult),
             [pb[7], sc], [self.RV])
        if DBG.get("step", 99) < 10:
            return
        v3 = lambda i: bg(i)[0:C, 0:8 * C].rearrange("p (h c) -> p h c", h=8)
        S.op("dve", lambda e: e.tensor_tensor(out=v3(2), in0=_bc(cm(iU, C, C), 1, [C, 8, C]), in1=_bc(s_(8, 16), 2, [C, 8, C]), op=ALU.mult),
             [self.cmat, sc], [big[2]])
        for half in range(2):
            S.op("pe", lambda e, half=half: e.matmul(bank(half)[0:C, 0:4 * C], lhsT=cm(C_ONE, C, C),
                                                     rhs=bg(2)[0:C, half * 4 * C:(half + 1) * 4 * C], start=True, stop=True),
                 [big[2], self.cmat], [pb[half]])
        gbc = bank(0, 2)

        def gview(r):
            return self.pst.t[0:r, 0:2, 0:4 * C].rearrange("p b (h c) -> p b h c", h=4)
        v4 = lambda i: bg(i)[0:C, 0:8 * C].rearrange("p (b h c) -> p b h c", b=2, h=4)
        S.op("pool", lambda e: e.tensor_tensor(out=v3(3), in0=_bc(cm(iMBT, C, C), 1, [C, 8, C]), in1=_bc(s_(24, 32), 2, [C, 8, C]), op=ALU.subtract),
             [self.cmat, sc], [big[3]])
        S.op("pool", lambda e: e.tensor_tensor(out=v3(4), in0=_bc(cm(iMBS, C, C), 1, [C, 8, C]), in1=_bc(s_(24, 32), 2, [C, 8, C]), op=ALU.add),
             [self.cmat, sc], [big[4]])
        S.op("dve", lambda e: e.tensor_tensor(out=v4(5), in0=gview(C), in1=v4(3), op=ALU.add), [pb[0], pb[1], big[3]], [big[5]])
        S.op("act", lambda e: e.activation(out=bg(5)[0:C, 0:8 * C], in_=bg(5)[0:C, 0:8 * C], func=AF.Exp), [big[5]], [big[5]])
        S.op("dve", lambda e: e.tensor_tensor(out=v4(1), in0=v4(4), in1=gview(C), op=ALU.subtract), [pb[0], pb[1], big[4]], [big[1]])
        S.op("act", lambda e: e.activation(out=bg(1)[0:C, 0:8 * C], in_=bg(1)[0:C, 0:8 * C], func=AF.Exp), [big[1]], [big[1]])
        if DBG.get("step", 99) < 11:
            return
        def mmkk(e):
            for h in range(8):
                p, h2 = h // 2, h % 2
                ov = bank(2 + h // 2).rearrange("p (hh two c) -> p hh two c", hh=2, two=2)
                if C == 128:
                    r = e.matmul(bank(2 + h // 2)[0:C, (h % 2) * 256:(h % 2) * 256 + 256], lhsT=self.KTm.t[:, h, 0:C],
                                 rhs=kq.t[:, p, :, :].rearrange("p a c -> p (a c)"), start=True, stop=True)
                else:
                    for two in range(2):
                        r = e.matmul(ov[0:C, h % 2, two, 0:C], lhsT=self.KTm.t[:, h, 0:C],
                                     rhs=kq.t[:, p, two, 0:C], start=True, stop=True)
            return r
        S.op("pe", mmkk, [kq, self.KTm], [pb[2], pb[3], pb[4], pb[5]])
        kkv = self.pst.t[0:C, 2:6, :].rearrange("p b (hh two c) -> p b hh two c", hh=2, two=2)
        v5 = lambda i: bg(i)[0:C, 0:8 * C].rearrange("p (b hh c) -> p b hh c", b=4, hh=2)
        S.op("dve", lambda e: e.tensor_tensor(out=v5(2), in0=kkv[:, :, :, 0, 0:C], in1=v5(1), op=ALU.mult), [pb[2], pb[3], pb[4], pb[5], big[1]], [big[2]])
        use_r = (C == 128) and bool(DBG.get("f32r"))
        ro = (lambda ap: ap.bitcast(F32R)) if use_r else (lambda ap: ap)
        ri = (lambda ap: ap.bitcast(F32R)) if use_r else (lambda ap: ap)
        Pc, PTc, TTc = self.Pc, self.PTc, self.TTc
        c3 = lambda tb: tb.t[0:C, 0:8 * C].rearrange("p (h c) -> p h c", h=8)
        c4 = lambda tb: tb.t[0:C, 0:8 * C].rearrange("p (b h c) -> p b h c", b=2, h=4)
        S.op("dve", lambda e: e.scalar_tensor_tensor(out=ro(c3(Pc)), in0=v3(2), scalar=-1.0, in1=_bc(s_(16, 24), 2, [C, 8, C]),
                                                      op0=ALU.mult, op1=ALU.mult), [big[2], sc], [Pc])
        S.op("dve", lambda e: e.tensor_tensor(out=v5(0), in0=kkv[:, :, :, 1, 0:C], in1=v5(5), op=ALU.mult), [pb[2], pb[3], pb[4], pb[5], big[5]] + accT, [big[0]])
        for half in range(2):
            def trn(e, half=half):
                for j in range(4):
                    r = e.transpose(bank(half)[0:C, j * C:(j + 1) * C], c3(Pc)[:, half * 4 + j, :], cm(C_ID, C, C))
                return r
            S.op("pe", trn, [Pc, self.cmat], [pb[half]])
        S.op("act", lambda e: e.activation(out=ro(c4(PTc)), in_=gview(C), func=AF.Copy), [pb[0], pb[1]], [PTc])
        S.op("dve", lambda e: e.tensor_tensor(out=ro(c3(TTc)), in0=c3(PTc), in1=_bc(cm(C_ID, C, C), 1, [C, 8, C]), op=ALU.add), [PTc, self.cmat], [TTc])
        if DBG.get("step", 99) < 12:
            return
        if nxt is not None:
            self.front0_compute(*nxt)
        gla_gen = self.gla_prep(C, iU, iSU, iM01)
        for lev in range(nlev):
            doA, doC, doB = lev >= 1, lev <= nlev - 2, lev <= nlev - 3
            for hg in range(2):
                bA, bB, bC = (2, 3, 4) if hg == 0 else (5, 6, 7)

                def mminv(e, hg=hg, bA=bA, bB=bB, bC=bC, doA=doA, doB=doB, doC=doC):
                    r = None
                    for j in range(4):
                        h = hg * 4 + j
                        o = lambda b_: bank(b_)[0:C, j * C:(j + 1) * C]
                        if doA:
                            r = e.matmul(o(bA), lhsT=ri(c3(Pc)[:, h, :]), rhs=ri(c3(TTc)[:, h, :]), start=True, stop=True)
                        if doC:
                            r = e.matmul(o(bC), lhsT=ri(c3(PTc)[:, h, :]), rhs=ri(c3(Pc)[:, h, :]), start=True, stop=True)
                        if doB:
                            r = e.matmul(o(bB), lhsT=ri(c3(Pc)[:, h, :]), rhs=ri(c3(PTc)[:, h, :]), start=True, stop=True)
                    return r
                wr = ([pb[bA]] if doA else []) + ([pb[bB]] if doB else []) + ([pb[bC]] if doC else [])
                S.op("pe", mminv, [Pc.parts[hg], PTc.parts[hg], TTc.parts[hg]], wr)
                hs = slice(hg * 4 * C, (hg + 1) * 4 * C)
                if doA:
                    S.op("dve", lambda e, bA=bA, hs=hs: e.tensor_tensor(out=ro(TTc.t[0:C, hs]), in0=bank(bA)[0:C, 0:4 * C], in1=TTc.t[0:C, hs], op=ALU.add),
                         [pb[bA], TTc.parts[hg]], [TTc.parts[hg]])
                if doC:
                    S.op("act", lambda e, bC=bC, hs=hs: e.activation(out=ro(Pc.t[0:C, hs]), in_=bank(bC)[0:C, 0:4 * C], func=AF.Copy), [pb[bC]], [Pc.parts[hg]])
                if doB:
                    S.op("act", lambda e, bB=bB, hs=hs: e.activation(out=ro(PTc.t[0:C, hs]), in_=bank(bB)[0:C, 0:4 * C], func=AF.Copy), [pb[bB]], [PTc.parts[hg]])
            for _ in range(4):
                next(gla_gen, None)
        for _ in gla_gen:
            pass
        if DBG.get("step", 99) < 13:
            return
        def mmwv(e):
            for h in range(8):
                r = e.matmul(bank(0)[0:C, h * 64:(h + 1) * 64], lhsT=c3(TTc)[:, h, :], rhs=self.RV.t[0:C, h * 64:(h + 1) * 64], start=True, stop=True)
            return r
        S.op("pe", mmwv, [TTc, self.RV], [pb[0]])
        S.op("act", lambda e: e.activation(out=self.wv.t[0:C, :], in_=bank(0)[0:C, :], func=AF.Copy), [pb[0]], [self.wv])

        def mmwk(e):
            for h in range(8):
                p = h // 2
                r = e.matmul(bank(2 + h // 4)[:, (h % 4) * 128:(h % 4) * 128 + C], lhsT=self.RK.t[0:C, p * 128:(p + 1) * 128], rhs=c3(TTc)[:, h, :],
                             start=True, stop=True)
            return r
        S.op("pe", mmwk, [TTc, self.RK], [pb[2], pb[3]])
        wkv = self.pst.t[:, 2:4, :].rearrange("p b (hh two c) -> p (b hh) two c", hh=2, two=2)
        for h2 in range(2):
            rows = slice(64 * h2, 64 * h2 + 64)
            S.op("dve" if h2 == 0 else "act",
                 (lambda e, rows=rows, h2=h2: e.tensor_copy(out=self.wkT.t[rows, :, 0:C].rearrange("p (a two) c -> p a two c", two=2)[:, :, h2, :], in_=wkv[rows, :, h2, 0:C])) if h2 == 0 else
                 (lambda e, rows=rows, h2=h2: e.activation(out=self.wkT.t[rows, :, 0:C].rearrange("p (a two) c -> p a two c", two=2)[:, :, h2, :], in_=wkv[rows, :, h2, 0:C], func=AF.Copy)),
                 [pb[2], pb[3]], [self.wkT])
        if DBG.get("step", 99) < 14:
            return
        for _ in gla_gen:
            pass
        if DBG.get("step", 99) < 15:
            return
        if smp:
            self.state_sample(C)
        else:
            self.state_prompt(C)
        if DBG.get("step", 99) < 16:
            return
        self.post_mix(e0, C, smp)
        if DBG.get("step", 99) < 17:
            return
        if not smp:
            S.op("pool", lambda e: e.tensor_copy(out=qkvx.t[:, :, 0:3], in_=qkvx.t[:, :, L:L + 3]), [qkvx], [qkvx])

    def gla_prep(self, C, iU, iSU, iM01):
        S, cm, pb, bank = self.S, self.cm, self.pb, self.bank
        fmx, lt = self.fmx, self.lt
        yield
        S.op("pe", lambda e: e.matmul(bank(0)[0:C, 0:256], lhsT=fmx.t[0:16, 4, 0:C], rhs=self.wgk2.t[:, :], start=True, stop=True),
             [fmx, self.wgk2], [pb[0]])
        yield
        S.op("dve", lambda e: e.tensor_tensor(out=lt.t[0:C, :], in0=bank(0)[0:C, 0:256], in1=self.pvec.t[0:C, 208:464], op=ALU.add), [pb[0], self.pvec], [lt])
        yield
        S.op("act", lambda e: e.activation(out=lt.t[0:C, :], in_=lt.t[0:C, :], func=AF.Exp, scale=-1.0), [lt], [lt])
        yield
        S.op("act", lambda e: e.activation(out=lt.t[0:C, :], in_=lt.t[0:C, :], func=AF.Ln, bias=self.epsc.t[0:C, 2:3], scale=1.0), [lt, self.epsc], [lt])

        yield
        def mmbc(e):
            for p in range(2):
                r = e.matmul(bank(1)[:, p * 128:p * 128 + C], lhsT=lt.t[0:C, p * 128:(p + 1) * 128], rhs=cm(iU, C, C), start=True, stop=True)
            return r
        yield
        S.op("pe", mmbc, [lt, self.cmat], [pb[1]])
        bcv = bank(1)[:, 0:256].rearrange("p (a c) -> p a c", a=2)[:, :, 0:C]
        yield
        S.op("act", lambda e: e.activation(out=self.ebT.t[:, :, 0:C], in_=bcv, func=AF.Exp, scale=-1.0 / 16.0), [pb[1]], [self.ebT])
        yield
        S.op("act", lambda e: e.activation(out=self.enbT.t[:, :, 0:C], in_=bcv, func=AF.Exp, scale=1.0 / 16.0), [pb[1]], [self.enbT])
        yield
        S.op("dve", lambda e: e.scalar_tensor_tensor(out=self.qeT.t[:, :, 0:C], in0=fmx.t[:, 0:2, 0:C], scalar=0.125, in1=self.ebT.t[:, :, 0:C],
                                                      op0=ALU.mult, op1=ALU.mult), [fmx, self.ebT], [self.qeT])
        yield
        for h2 in range(2):
            rows = slice(64 * h2, 64 * h2 + 64)
            pad = lambda tb: tb.t[rows, :, 0:C].rearrange("p (a two) c -> p a two c", two=2)[:, :, h2, :]
            S.op("pool", lambda e, rows=rows, pad=pad: e.tensor_tensor(out=pad(self.keTm), in0=fmx.t[rows, 2:4, 0:C], in1=self.enbT.t[rows, :, 0:C], op=ALU.mult),
                 [fmx, self.enbT], [self.keTm])
            S.op("act", lambda e, rows=rows, pad=pad: e.activation(out=pad(self.qeTm), in_=self.qeT.t[rows, :, 0:C], func=AF.Copy), [self.qeT], [self.qeTm])

        yield
        def mmA(e):
            for h in range(4):
                p, h2 = h // 2, h % 2
                rows = slice(64 * h2, 64 * h2 + 64)
                r = e.matmul(bank(0)[0:C, h * 128:h * 128 + C], lhsT=self.keTm.t[:, h, 0:C], rhs=self.qeT.t[:, p, 0:C], start=True, stop=True)
            return r
        yield
        S.op("pe", mmA, [self.keTm, self.qeT], [pb[0]])
        yield
        S.op("dve", lambda e: e.tensor_tensor(out=self.PTg.t[0:C, :, 0:C], in0=bank(0).rearrange("p (h c) -> p h c", h=4)[0:C, :, 0:C],
                                              in1=_bc(cm(iM01, C, C), 1, [C, 4, C]), op=ALU.mult), [pb[0], self.cmat], [self.PTg])
        yield
        S.op("pe", lambda e: e.matmul(bank(1)[0:C, 0:256], lhsT=cm(iSU, C, C), rhs=lt.t[0:C, :], start=True, stop=True), [lt, self.cmat], [pb[1]])
        yield
        S.op("act", lambda e: e.activation(out=self.kd.t[0:C, :], in_=bank(1)[0:C, 0:256], func=AF.Exp, scale=-1.0 / 16.0), [pb[1]], [self.kd])
        yield
        S.op("pool", lambda e: e.tensor_tensor(out=self.kd.t[0:C, :], in0=self.kd.t[0:C, :], in1=self.tm0.t[0:C, 16:272], op=ALU.mult), [self.kd, self.tm0], [self.kd])

    def state_prompt(self, C):
        S, cm, pb, bank, big, bg = self.S, self.cm, self.pb, self.bank, self.big, self.bg
        sc, kq = self.sc, self.kq
        s_ = lambda a, b_: sc.t[0:C, a:b_]
        v3 = lambda i: bg(i)[0:C, 0:8 * C].rearrange("p (h c) -> p h c", h=8)
        Sg, Sl, U = self.Sg, self.Sl, self.U
        R = lambda h2: slice(64 * h2, 64 * h2 + 64)
        h3 = lambda ap: ap.rearrange("p (h d) -> p h d", h=8)
        S.op("pe", lambda e: e.matmul(bank(1)[:, 0:8], lhsT=cm(C_ONE, C, 128), rhs=s_(8, 16), start=True, stop=True), [sc, self.cmat], [pb[1]])
        S.op("act", lambda e: e.activation(out=self.glb.t[:, 0:8], in_=bank(1)[:, 0:8], func=AF.Exp), [pb[1]], [self.glb])

        def mm1(e):
            for h in range(8):
                p, h2 = h // 2, h % 2
                r = e.matmul(bank(6)[0:C, h * 64:(h + 1) * 64], lhsT=self.wkT.t[:, h, 0:C], rhs=Sg.t[:, p, :], start=True, stop=True)
            return r
        S.op("pe", mm1, [self.wkT, Sg], [pb[6]])
        def mg1(e):
            for h in range(4):
                p, h2 = h // 2, h % 2
                r = e.matmul(bank(2)[0:C, h * 128:(h + 1) * 128], lhsT=self.qeTm.t[:, h, 0:C], rhs=Sl.t[:, p, :], start=True, stop=True)
            return r
        S.op("pe", mg1, [self.qeTm, Sl], [pb[2]])
        def mg2(e):
            for h in range(4):
                r = e.matmul(bank(3)[0:C, h * 128:(h + 1) * 128], lhsT=self.PTg.t[0:C, h, 0:C], rhs=self.gv_tok.t[0:C, h * 128:(h + 1) * 128], start=True, stop=True)
            return r
        S.op("pe", mg2, [self.PTg, self.gv_tok], [pb[3]])
        def mg3(e):
            for h in range(4):
                p = h // 2
                r = e.matmul(bank(4)[:, h * 128:(h + 1) * 128], lhsT=self.kd.t[0:C, p * 128:(p + 1) * 128], rhs=self.gv_tok.t[0:C, h * 128:(h + 1) * 128], start=True, stop=True)
            return r
        S.op("pe", mg3, [self.kd, self.gv_tok], [pb[4]])
        S.op("dve", lambda e: e.tensor_tensor(out=U.t[0:C, :], in0=self.wv.t[0:C, :], in1=bank(6)[0:C, :], op=ALU.subtract), [self.wv, pb[6]], [U])

        def mm2(e):
            for h in range(8):
                p, h2 = h // 2, h % 2
                r = e.matmul(bank(7)[0:C, h * 64:(h + 1) * 64], lhsT=self.QTm.t[:, h, 0:C], rhs=Sg.t[:, p, :], start=True, stop=True)
            return r
        S.op("pe", mm2, [self.QTm, Sg], [pb[7]])

        def mm3(e):
            for h in range(8):
                r = e.matmul(bank(0)[0:C, h * 64:(h + 1) * 64], lhsT=v3(0)[:, h, :], rhs=U.t[0:C, h * 64:(h + 1) * 64], start=True, stop=True)
            return r
        S.op("pe", mm3, [big[0], U], [pb[0]])
        S.op("act", lambda e: e.activation(out=self.ogla.t[0:C, :], in_=bank(2)[0:C, :], func=AF.Copy), [pb[2]], [self.ogla])
        S.op("dve", lambda e: e.tensor_tensor(out=self.ogla.t[0:C, :], in0=self.ogla.t[0:C, :], in1=bank(3)[0:C, :], op=ALU.add), [self.ogla, pb[3]], [self.ogla])
        S.op("dve", lambda e: e.tensor_tensor(out=h3(self.otmp.t[0:C, :]), in0=h3(bank(7)[0:C, :]), in1=_bc(s_(40, 48), 2, [C, 8, 64]), op=ALU.mult),
             [pb[7], sc], [self.otmp])
        S.op("dve", lambda e: e.tensor_tensor(out=self.ogdn.t[0:C, :], in0=self.otmp.t[0:C, :], in1=bank(0)[0:C, :], op=ALU.add), [self.otmp, pb[0]], [self.ogdn])

        def mm4(e):
            for h in range(8):
                p = h // 2
                r = e.matmul(bank(1)[:, h * 64:(h + 1) * 64], lhsT=self.kdec.t[0:C, p * 128:(p + 1) * 128], rhs=U.t[0:C, h * 64:(h + 1) * 64], start=True, stop=True)
            return r
        S.op("pe", mm4, [self.kdec, U, self.glb], [pb[1]])
        for h2 in range(2):
            glv = self.glb.t[R(h2), 0:8].rearrange("p (a two) -> p a two", two=2)[:, :, h2]
            psv = bank(1).rearrange("p (a two v) -> p a two v", two=2, v=64)[R(h2), :, h2, :]
            S.op("pool", lambda e, h2=h2, glv=glv: e.tensor_tensor(out=Sg.t[R(h2), :, :], in0=Sg.t[R(h2), :, :], in1=_bc(glv, 2, [64, 4, 64]), op=ALU.mult),
                 [Sg, self.glb], [Sg])
            S.op("dve", lambda e, h2=h2, psv=psv: e.tensor_tensor(out=Sg.t[R(h2), :, :], in0=Sg.t[R(h2), :, :], in1=psv, op=ALU.add), [Sg, pb[1]], [Sg])


        for h in range(4):
            p, h2 = h // 2, h % 2
            S.op("dve", lambda e, p=p, h2=h2, h=h: e.scalar_tensor_tensor(
                out=Sl.t[R(h2), p, :], in0=Sl.t[R(h2), p, :], scalar=self.ebT.t[R(h2), p, C - 1:C], in1=bank(4)[R(h2), h * 128:(h + 1) * 128],
                op0=ALU.mult, op1=ALU.add), [Sl, self.ebT, pb[4]], [Sl])

    def state_sample(self, C):
        S, cm, pb, bank, big, bg = self.S, self.cm, self.pb, self.bank, self.big, self.bg
        sc, kq = self.sc, self.kq
        s_ = lambda a, b_: sc.t[0:C, a:b_]
        v3 = lambda i: bg(i)[0:C, 0:8 * C].rearrange("p (h c) -> p h c", h=8)
        U = self.U
        R = lambda h2: slice(64 * h2, 64 * h2 + 64)
        h3 = lambda ap: ap.rearrange("p (h d) -> p h d", h=8)
        bm = cm(C_SBM, C, NS)
        gsel = bg(5)[0:C, 0:128].rearrange("p (h s) -> p h s", h=8)
        S.op("pool", lambda e: e.tensor_tensor(out=gsel, in0=_bc(s_(8, 16), 2, [C, 8, NS]), in1=_bc(bm, 1, [C, 8, NS]), op=ALU.mult), [sc, self.cmat], [big[5]])
        S.op("pe", lambda e: e.matmul(bank(1)[:, 0:128], lhsT=cm(C_ONE), rhs=bg(5)[0:C, 0:128], start=True, stop=True), [big[5], self.cmat], [pb[1]])
        S.op("act", lambda e: e.activation(out=self.glb.t[:, 0:128], in_=bank(1)[:, 0:128], func=AF.Exp), [pb[1]], [self.glb])
        glbs = self.glb.t[:, 0:128].rearrange("p (h s) -> p h s", h=8)
        S0 = bg(4).rearrange("p (s v) -> p s v", s=NS)
        tmp = bg(5)[0:C, :].rearrange("p (s v) -> p s v", s=NS)
        tmpT = bg(5)[0:C, :].rearrange("p (s v) -> p v s", s=NS)
        Ub = bg(1)[0:C, :].rearrange("p (s v) -> p s v", s=NS)
        ps67 = self.pst.t[:, 6:8, :].rearrange("p b (s v) -> p (b s) v", v=64)
        wks, o1s = self.otmp, self.ogdn
        S0v = lambda ap: ap.rearrange("p (s v) -> p s v", s=NS)
        S0s = [(S0v(bg(4)), big[4]), (S0v(bg(2)), big[2])]
        scr = [dict(tmp=S0v(bg(5)[0:C, :]), tmpT=bg(5)[0:C, :].rearrange("p (s v) -> p v s", s=NS), tmpB=big[5],
                    Ub=S0v(bg(1)[0:C, :]), Ubf=bg(1), UbB=big[1], bk=6),
               dict(tmp=S0v(self.Pc.t[0:C, :]), tmpT=self.Pc.t[0:C, :].rearrange("p (s v) -> p v s", s=NS), tmpB=self.Pc,
                    Ub=S0v(self.PTc.t[0:C, :]), Ubf=self.PTc.t, UbB=self.PTc, bk=4)]

        def load(p):
            S0, S0t = S0s[p % 2]
            for h2 in range(2):
                S.dma(S0[R(h2), :, :], self.sgdn[:, 2 * p + h2, :, :].rearrange("s d v -> d s v"), [], [S0t])

        def head_chain(p, h2):
            h = 2 * p + h2
            S0, S0t = S0s[p % 2]
            q = scr[h2]
            bk = q["bk"]
            psx = self.pst.t[:, bk:bk + 2, :].rearrange("p b (s v) -> p (b s) v", v=64)
            for (lhs, lhsb, dst) in ((self.wkT.t[:, h, 0:C], self.wkT, wks), (self.QTm.t[:, h, 0:C], self.QTm, o1s)):
                def mma(e, lhs=lhs):
                    for i in range(2):
                        r = e.matmul(bank(bk + i)[0:C, :], lhsT=lhs, rhs=S0[:, i * 8:(i + 1) * 8, :].rearrange("p s v -> p (s v)"), start=True, stop=True)
                    return r
                S.op("pe", mma, [lhsb, S0t], [pb[bk], pb[bk + 1]])
                yield
                S.op("dve", lambda e: e.tensor_tensor(out=q["tmp"], in0=psx[0:C], in1=_bc(bm, 2, [C, NS, 64]), op=ALU.mult), [pb[bk], pb[bk + 1], self.cmat], [q["tmpB"]])
                yield
                S.op("dve", lambda e, dst=dst: e.tensor_reduce(out=dst.t[0:C, h * 64:(h + 1) * 64], in_=q["tmpT"], op=ALU.add, axis=AX.X), [q["tmpB"]], [dst])
                yield
            cs = slice(h * 64, (h + 1) * 64)
            S.op("dve", lambda e: e.tensor_tensor(out=U.t[0:C, cs], in0=self.wv.t[0:C, cs], in1=wks.t[0:C, cs], op=ALU.subtract), [self.wv, wks], [U])
            yield
            S.op("pe", lambda e: e.matmul(bank(0)[0:C, cs], lhsT=v3(0)[:, h, :], rhs=U.t[0:C, cs], start=True, stop=True), [big[0], U], [pb[0]])
            yield
            S.op("pool", lambda e: e.tensor_tensor(out=q["Ub"], in0=_bc(U.t[0:C, cs], 1, [C, NS, 64]), in1=_bc(bm, 2, [C, NS, 64]), op=ALU.mult),
                 [U, self.cmat], [q["UbB"]])
            yield

            def mmb(e):
                for i in range(2):
                    r = e.matmul(bank(bk + i)[:, :], lhsT=self.kdec.t[0:C, p * 128:(p + 1) * 128], rhs=q["Ubf"][0:C, i * 512:(i + 1) * 512], start=True, stop=True)
                return r
            S.op("pe", mmb, [self.kdec, q["UbB"]], [pb[bk], pb[bk + 1]])
            yield
            S.op("pool", lambda e: e.tensor_tensor(out=S0[R(h2), :, :], in0=S0[R(h2), :, :], in1=_bc(glbs[R(h2), h, :], 2, [64, NS, 64]), op=ALU.mult),
                 [S0t, self.glb], [S0t])
            yield
            S.op("dve", lambda e: e.tensor_tensor(out=S0[R(h2), :, :], in0=S0[R(h2), :, :], in1=psx[R(h2)], op=ALU.add), [S0t, pb[bk], pb[bk + 1]], [S0t])
            yield
            S.dma(self.gdn_s[:, h, :, :].rearrange("s d v -> d s v"), S0[R(h2), :, :], [S0t], [], is_out=True)

        load(0)
        for p in range(4):
            if p + 1 < 4:
                load(p + 1)
            gens = [head_chain(p, 0), head_chain(p, 1)]
            while gens:
                for g_ in list(gens):
                    if next(g_, "done") == "done":
                        gens.remove(g_)
        S.op("dve", lambda e: e.tensor_tensor(out=h3(o1s.t[0:C, :]), in0=h3(o1s.t[0:C, :]), in1=_bc(s_(40, 48), 2, [C, 8, 64]), op=ALU.mult), [o1s, sc], [o1s])
        S.op("dve", lambda e: e.tensor_tensor(out=self.ogdn.t[0:C, :], in0=o1s.t[0:C, :], in1=bank(0)[0:C, :], op=ALU.add), [o1s, pb[0]], [self.ogdn])
        S0g = bg(0, 2).rearrange("p (s v) -> p s v", s=NS)
        tg = bg(2, 2)[0:C, :].rearrange("p (s v) -> p s v", s=NS)
        tgT = bg(2, 2)[0:C, :].rearrange("p (s v) -> p v s", s=NS)
        Vb = bg(4, 2)[0:C, :].rearrange("p (s v) -> p s v", s=NS)
        ps25 = self.pst.t[:, 2:6, :].rearrange("p b (s v) -> p (b s) v", v=128)
        S0gT, tgB, VbB = [big[0], big[1]], [big[2], big[3]], [big[4], big[5]]
        pbs = [pb[2], pb[3], pb[4], pb[5]]
        for p in range(2):
            for h2 in range(2):
                S.dma(S0g[R(h2), :, :], self.sgla[:, 2 * p + h2, :, :].rearrange("s d v -> d s v"), [], S0gT)
            for h2 in range(2):
                h = 2 * p + h2
                cs = slice(h * 128, (h + 1) * 128)

                def mmq(e, p=p, h2=h2, h=h):
                    for i in range(4):
                        r = e.matmul(bank(2 + i)[0:C, :], lhsT=self.qeTm.t[:, h, 0:C], rhs=S0g[:, i * 4:(i + 1) * 4, :].rearrange("p s v -> p (s v)"), start=True, stop=True)
                    return r
                S.op("pe", mmq, [self.qeTm] + S0gT, pbs)
                S.op("dve", lambda e: e.tensor_tensor(out=tg, in0=ps25[0:C], in1=_bc(bm, 2, [C, NS, 128]), op=ALU.mult), pbs + [self.cmat], tgB)
                S.op("dve", lambda e, cs=cs: e.tensor_reduce(out=self.ogla.t[0:C, cs], in_=tgT, op=ALU.add, axis=AX.X), tgB, [self.ogla])
                S.op("pool", lambda e, cs=cs: e.tensor_tensor(out=Vb, in0=_bc(self.gv_tok.t[0:C, cs], 1, [C, NS, 128]), in1=_bc(bm, 2, [C, NS, 128]), op=ALU.mult),
                     [self.gv_tok, self.cmat], VbB)

                def mmv(e, p=p):
                    for i in range(4):
                        r = e.matmul(bank(2 + i)[:, :], lhsT=self.kd.t[0:C, p * 128:(p + 1) * 128], rhs=bg(4, 2)[0:C, i * 512:(i + 1) * 512], start=True, stop=True)
                    return r
                S.op("pe", mmv, [self.kd] + VbB, pbs)
                ebl = self.ebT.t[R(h2), p, :].rearrange("p (s l) -> p s l", l=LS)[:, :, LS - 1]
                S.op("pool", lambda e, h2=h2, ebl=ebl: e.tensor_tensor(out=S0g[R(h2), :, :], in0=S0g[R(h2), :, :], in1=_bc(ebl, 2, [64, NS, 128]), op=ALU.mult),
                     S0gT + [self.ebT], S0gT)
                S.op("dve", lambda e, h2=h2: e.tensor_tensor(out=S0g[R(h2), :, :], in0=S0g[R(h2), :, :], in1=ps25[R(h2)], op=ALU.add), S0gT + pbs, S0gT)
                S.dma(self.gla_s[:, h, :, :].rearrange("s d v -> d s v"), S0g[R(h2), :, :], S0gT, [], is_out=True)

        def mg2(e):
            for h in range(4):
                r = e.matmul(bank(6)[0:C, h * 128:(h + 1) * 128], lhsT=self.PTg.t[0:C, h, 0:C], rhs=self.gv_tok.t[0:C, h * 128:(h + 1) * 128], start=True, stop=True)
            return r
        S.op("pe", mg2, [self.PTg, self.gv_tok], [pb[6]])
        S.op("dve", lambda e: e.tensor_tensor(out=self.ogla.t[0:C, :], in0=self.ogla.t[0:C, :], in1=bank(6)[0:C, :], op=ALU.add), [self.ogla, pb[6]], [self.ogla])

    def post_mix(self, e0, C, smp):
        S, cm, pb, bank = self.S, self.cm, self.pb, self.bank
        sc, mix, xh = self.sc, self.mix, self.xh
        s_ = lambda a, b_: sc.t[0:C, a:b_]
        if not mix.parts:
            mix.parts = [S.alias("mixA", mix), S.alias("mixB", mix)]
            self.scn = [S.alias("scA", sc), S.alias("scB", sc)]

        def norm_chain(o, nh, dv, col0, gcol, sg, sco, sqb, mixp, scp):
            v = lambda ap: ap.rearrange("p (h d) -> p h d", h=nh)
            yield
            S.op("dve", lambda e, o=o: e.tensor_tensor(out=sqb.t[0:C, :], in0=o.t[0:C, :], in1=o.t[0:C, :], op=ALU.mult), [o], [sqb])
            yield
            S.op("dve", lambda e, v=v, sco=sco, nh=nh: e.tensor_reduce(out=s_(sco, sco + nh), in_=v(sqb.t[0:C, :]), op=ALU.add, axis=AX.X), [sqb], [scp])
            yield
            S.op("act", lambda e, sco=sco, nh=nh, dv=dv: e.activation(out=s_(sco, sco + nh), in_=s_(sco, sco + nh), func=AF.Ln,
                                                                      bias=self.epsc.t[0:C, 1:2], scale=1.0 / dv), [scp, self.epsc], [scp])
            yield
            S.op("act", lambda e, sco=sco, nh=nh: e.activation(out=s_(sco, sco + nh), in_=s_(sco, sco + nh), func=AF.Exp, scale=-0.5), [scp], [scp])
            mv_ = v(mix.t[0:C, col0:col0 + 512])
            yield
            S.op("dve", lambda e, o=o, v=v, mv_=mv_, sco=sco, nh=nh, dv=dv: e.tensor_tensor(out=mv_, in0=v(o.t[0:C, :]), in1=_bc(s_(sco, sco + nh), 2, [C, nh, dv]), op=ALU.mult),
                 [o, scp], [mixp])
            yield
            S.op("pool", lambda e, mv_=mv_, gcol=gcol, nh=nh, dv=dv: e.tensor_tensor(out=mv_, in0=mv_, in1=_bc(self.pvec.t[0:C, gcol:gcol + dv], 1, [C, nh, dv]), op=ALU.mult),
                 [mixp, self.pvec], [mixp])
            yield
            S.op("dve", lambda e, col0=col0, sg=sg: e.tensor_tensor(out=mix.t[0:C, col0:col0 + 512], in0=mix.t[0:C, col0:col0 + 512], in1=sg.t[0:C, :], op=ALU.mult),
                 [mixp, sg], [mixp])
        gens = [norm_chain(self.ogdn, 8, 64, 0, 16, self.sg_gdn, 72, self.otmp, mix.parts[0], self.scn[0]),
                norm_chain(self.ogla, 4, 128, 512, 80, self.sg_gla, 80, self.kdec, mix.parts[1], self.scn[1])]
        while gens:
            for g_ in list(gens):
                if next(g_, 'done') == 'done':
                    gens.remove(g_)

        for half in range(2):
            def tr(e, half=half):
                for j in range(4):
                    kc = half * 4 + j
                    r = e.transpose(bank(half)[:, j * 128:j * 128 + C], mix.t[0:C, kc * 128:(kc + 1) * 128], cm(C_ID, C, C))
                return r
            S.op("pe", tr, [mix, self.cmat], [pb[half]])
            S.op("act", lambda e, half=half: e.activation(out=self.mixT.t[:, half * 4:half * 4 + 4, 0:C],
                                                          in_=bank(half).rearrange("p (j c) -> p j c", j=4)[:, :, 0:C], func=AF.Copy), [pb[half]], [self.mixT])
        for half in range(2):
            def mmo(e, half=half):
                for kc in range(8):
                    r = e.matmul(bank(6 + half)[0:C, :], lhsT=self.mixT.t[:, kc, 0:C], rhs=self.w_out.t[:, kc, half * 512:(half + 1) * 512],
                                 start=(kc == 0), stop=(kc == 7))
                return r
            S.op("pe", mmo, [self.mixT, self.w_out], [pb[6 + half]])
            S.op("dve", lambda e, half=half: e.scalar_tensor_tensor(out=xh.t[0:C, half * 512:(half + 1) * 512], in0=xh.t[0:C, half * 512:(half + 1) * 512],
                                                                     scalar=ALPHA, in1=bank(6 + half)[0:C, :], op0=ALU.mult, op1=ALU.add), [xh, pb[6 + half]], [xh])
        self.layer_norm(xh, C, self.lnbc.t[:, 2, :], self.lnbc.t[:, 3, :], self.epsc.t[0:C, 0:1], "ln1")
        row0 = TP if smp else e0
        S.dma(self.h1s[row0:row0 + C, :], xh.t[0:C, :], [xh], [self.h1scr])

    def conv_state_out(self, C, smp):
        S, cm, pb, bank = self.S, self.cm, self.pb, self.bank
        if smp:
            n = NS * 3
            cst = self.bg(3)[:, 0:12 * n].rearrange("p (a c) -> p a c", a=12)
            S.op("pool", lambda e: e.tensor_copy(out=cst.rearrange("p a (s l) -> p a s l", l=3),
                                                 in_=self.qkvx.t[:, :, 0:NS * 11].rearrange("p a (s l) -> p a s l", l=11)[:, :, :, 8:11]),
                 [self.qkvx], [self.big[3]])
            src = lambda cc: cst[:, cc, :]
            dst = self.gconv_s
        else:
            n = 3
            src = lambda cc: self.qkvx.t[:, cc, 0:3]
            dst = self.gconv_p
        stage = self.bg(1, 2)[0:n, 0:1536]
        for g in range(3):
            def tr(e, g=g):
                for j in range(4):
                    r = e.transpose(bank(2 + g)[0:n, j * 128:(j + 1) * 128], src(g * 4 + j), cm(C_ID))
                return r
            S.op("pe", tr, [self.qkvx, self.big[3], self.cmat], [pb[2 + g]])
            S.op("act", lambda e, g=g: e.activation(out=stage[:, g * 512:(g + 1) * 512], in_=bank(2 + g)[0:n, :], func=AF.Copy), [pb[2 + g]], [self.big[1], self.big[2]])
        S.dma(dst, stage, [self.big[1], self.big[2]], [], is_out=True)

    def stage1(self):
        S = self.S
        self.stage1_alloc()
        self.stage1_setup()
        R = lambda h2: slice(64 * h2, 64 * h2 + 64)
        plist = []
        e0 = 0
        while e0 < TP:
            C = min(128, TP - e0)
            skip = (DBG.get("maxchunks") is not None and e0 // 128 >= DBG["maxchunks"] and C == 128) or (DBG.get("no16") and C == 16)
            if not skip:
                plist.append((e0, C, "p"))
            e0 += C
        plist = [(a, b, c, self.xhs[i % 2]) for i, (a, b, c) in enumerate(plist)]
        self.sample_item = (0, NS * LS, "s", self.xhs[len(plist) % 2])
        self.front0(*plist[0])
        for i, it in enumerate(plist):
            self.chunk(*it, plist[i + 1] if i + 1 < len(plist) else None)
        if DBG.get("stop_early"):
            return
        if DBG.get("stop_after_prompt"):
            return
        for h2 in range(2):
            S.dma(self.gdn_p.rearrange("(a two) d v -> two d a v", two=2)[h2], self.Sg.t[R(h2), :, :], [self.Sg], [], is_out=True)
            S.dma(self.gla_p.rearrange("(a two) d v -> two d a v", two=2)[h2], self.Sl.t[R(h2), :, :], [self.Sl], [], is_out=True)
        if not DBG.get("no_cso"):
            self.conv_state_out(16, False)
        if DBG.get("stop_after_outputs"):
            return
        stc = self.bg(3, 2)[0:NS * 3, 0:1536]
        S.dma(stc, self.sgconv, [], [self.big[3], self.big[4]])
        qs = self.qkvx.t[:, :, 0:NS * 11].rearrange("p a (s l) -> p a s l", l=11)
        for g in range(3):
            def tr(e, g=g):
                for j in range(4):
                    cc = g * 4 + j
                    r = e.transpose(self.bank(2 + g)[:, j * 128:j * 128 + NS * 3], stc[:, cc * 128:(cc + 1) * 128], self.cm(C_ID, NS * 3, NS * 3))
                return r
            S.op("pe", tr, [self.big[3], self.big[4], self.cmat], [self.pb[2 + g]])
            S.op("act", lambda e, g=g: e.activation(out=qs[:, g * 4:(g + 1) * 4, :, 0:3],
                                                    in_=self.bank(2 + g).rearrange("p (j c) -> p j c", j=4)[:, :, 0:NS * 3].rearrange("p j (s l) -> p j s l", l=3),
                                                    func=AF.Copy), [self.pb[2 + g]], [self.qkvx])
        self.front0(*self.sample_item)
        self.chunk(*self.sample_item, None)
        self.conv_state_out(NS * LS, True)

    def stage2(self):
        S, cm, pb, bank = self.S, self.cm, self.pb, self.bank
        self.lnbc = S.sb("lnbc2", [128, 2, D], F32)
        w_up = S.sb("w_up_sb", [128, 8, 2 * DFF], BF16)
        w_dn = S.sb("w_dn", [128, 22, D], BF16)
        cwf = S.sb("cwf_sb", [128, NFF, 4], F32)
        h1t = [S.sb(f"h1t{i}", [128, D], F32) for i in range(2)]
        h1T = S.sb("h1T", [128, 8, 512], BF16)
        ue = [S.sb(f"ue{i}", [128, 160], F32) for i in range(2)]
        accs = [[S.sb(f"acc{w}{i}", [128, 512], F32) for i in range(2)] for w in range(2)]
        sgt1 = S.sb("sgt", [128, 512], F32)
        sgt = [sgt1, sgt1]
        hc = S.sb("hc", [128, NFF, 4], F32)
        uh = [S.sb(f"uh{i}", [128, NFF, 2], F32) for i in range(2)]
        actT = S.sb("actT", [128, 22, 512], BF16)
        usave = S.sb("usave", [128, NFF, 32], F32)
        st32 = S.sb("st32", [128, 512], F32)
        for i in range(2):
            S.dma(self.lnbc.t[:, i, :], self.lnv_d[4 + i:5 + i, :].partition_broadcast(128), [], [self.lnbc])
        S.dma(cwf.t[:], self.cwf_d, [], [cwf])
        wu_ = self.w_up_d.rearrange("(kc p) n -> p kc n", p=128)
        wd_ = self.w_down_d.rearrange("(j p) n -> p j n", p=128)
        w_up_tb = {}
        wdma = []
        for g in range(3):
            for which in range(2):
                c0 = which * DFF + g * 1024
                c1 = min(c0 + 1024, (which + 1) * DFF)
                tb = S.alias(f"w_up_{which}_{g}", w_up)
                for gg in (2 * g, 2 * g + 1):
                    w_up_tb[(which, gg)] = tb
                for kc in range(0, 8, 4):
                    wdma.append((lambda kc=kc, c0=c0, c1=c1, tb=tb: S.dma(w_up.t[:, kc:kc + 4, c0:c1], wu_[:, kc:kc + 4, c0:c1], [], [tb], cast=True)))
        w_dn_tb = {}
        wddma = []
        for j in range(0, 22, 2):
            tb = S.alias(f"w_dn_{j}", w_dn)
            w_dn_tb[j] = tb
            w_dn_tb[j + 1] = tb
            wddma.append((lambda j=j, tb=tb: S.dma(w_dn.t[:, j:j + 2, :], wd_[:, j:j + 2, :], [], [tb], cast=True)))
        for _ in range(4):
            wdma.pop(0)()
        S.op("pool", lambda e: e.memset(uh[0].t[:], 0.0), [], [uh[0]])

        def conv_out(n, dst, src):
            for g in range(11):
                def tr(e, g=g):
                    for j in range(4):
                        r = e.transpose(bank(g % 4)[0:n, j * 128:(j + 1) * 128], src.t[:, g * 4 + j, 0:n], cm(C_ID))
                    return r
                S.op("pe", tr, [src, self.cmat], [pb[g % 4]])
                S.op("act", lambda e, g=g: e.activation(out=st32.t[0:n, :], in_=bank(g % 4)[0:n, :], func=AF.Copy), [pb[g % 4]], [st32])
                S.dma(dst[:, g * 512:(g + 1) * 512], st32.t[0:n, :], [st32], [], is_out=True)

        tiles = []
        e0 = 0
        while e0 < TP:
            W = min(512, TP - e0)
            tiles.append((e0, W, False))
            e0 += W
        tiles.append((TP, NS * LS, True))
        upb = 0
        for ti, (r0, W, smp) in enumerate(tiles):
            G, L = (NS, LS) if smp else (1, W)
            uprev, ucur = uh[ti % 2], uh[(ti + 1) % 2]
            if not smp and not DBG.get("nohc"):
                S.op("pool", lambda e, uprev=uprev: e.tensor_tensor(out=hc.t[:, :, 0:2], in0=uprev.t[:, :, :], in1=cwf.t[:, :, 0:1].to_broadcast([128, NFF, 2]), op=ALU.mult),
                     [uprev, cwf], [hc])
                S.op("pool", lambda e, uprev=uprev: e.tensor_tensor(out=hc.t[:, :, 2:3], in0=uprev.t[:, :, 1:2], in1=cwf.t[:, :, 1:2], op=ALU.mult),
                     [uprev, cwf, hc], [hc])
            if smp:
                for g in range(11):
                    S.dma(st32.t[0:NS * 2, :], self.sfconv[:, g * 512:(g + 1) * 512], [], [st32])

                    def tr(e, g=g):
                        for j in range(4):
                            r = e.transpose(bank(4 + g % 2)[:, j * 32:(j + 1) * 32], st32.t[0:NS * 2, j * 128:(j + 1) * 128], cm(C_ID, NS * 2, NS * 2))
                        return r
                    S.op("pe", tr, [st32, self.cmat], [pb[4 + g % 2]])
                    S.op("act", lambda e, g=g: e.activation(out=usave.t[:, g * 4:(g + 1) * 4, :],
                                                            in_=bank(4 + g % 2)[:, 0:128].rearrange("p (j c) -> p j c", j=4), func=AF.Copy), [pb[4 + g % 2]], [usave])
            nsub = (W + 127) // 128
            for j in range(nsub):
                C = min(128, W - j * 128)
                ht = h1t[j % 2]
                S.dma(ht.t[0:C, :], self.h1s[r0 + j * 128:r0 + j * 128 + C, :], [self.h1scr], [ht])
                for half in range(2):
                    def tr(e, half=half, ht=ht, C=C):
                        for q in range(4):
                            r = e.transpose(bank(6 + half)[:, q * 128:q * 128 + C], ht.t[0:C, (half * 4 + q) * 128:(half * 4 + q + 1) * 128], cm(C_ID, C, C))
                        return r
                    S.op("pe", tr, [ht, self.cmat], [pb[6 + half]])
                    S.op("act", lambda e, half=half, j=j, C=C: e.activation(out=h1T.t[:, half * 4:half * 4 + 4, j * 128:j * 128 + C],
                                                                            in_=bank(6 + half).rearrange("p (q c) -> p q c", q=4)[:, :, 0:C], func=AF.Copy),
                         [pb[6 + half]], [h1T])
            pend = None
            for jg in range(22):
                par = jg % 2
                if jg % 8 == 1:
                    for _ in range(4):
                        if wdma:
                            wdma.pop(0)()
                if wddma and jg % 2 == 0:
                    wddma.pop(0)()
                if pend is not None:
                    pend()
                for which in (0, 1):
                    acc = accs[which][par]
                    cc = jg + 22 * which
                    b = upb % 4
                    upb += 1
                    wtb = w_up_tb[(which, jg // 4)]

                    def mmu(e, cc=cc, b=b):
                        for kc in range(8):
                            r = e.matmul(bank(b)[:, 0:W], lhsT=w_up.t[:, kc, cc * 128:(cc + 1) * 128], rhs=h1T.t[:, kc, 0:W], start=(kc == 0), stop=(kc == 7))
                        return r
                    S.op("pe", mmu, [wtb, h1T], [pb[b]])
                    if smp:
                        u = ue[cc % 2]
                        uv = u.t[:, 0:G * (L + 2)].rearrange("p (g l) -> p g l", g=G)
                        S.op("act", lambda e, b=b, uv=uv: e.activation(out=uv[:, :, 2:2 + L], in_=bank(b)[:, 0:W].rearrange("p (g l) -> p g l", g=G), func=AF.Copy),
                             [pb[b]], [u])
                        hsrc = usave.t[:, cc, :].rearrange("p (s i) -> p s i", i=2)
                        S.op("pool", lambda e, uv=uv, hsrc=hsrc: e.tensor_copy(out=uv[:, :, 0:2], in_=hsrc), [usave, u], [u])
                        av = acc.t[:, 0:W].rearrange("p (g l) -> p g l", g=G)
                        S.op("act", lambda e, uv=uv, av=av, cc=cc: e.activation(out=av, in_=uv[:, :, 2:2 + L], func=AF.Identity,
                                                                                scale=cwf.t[:, cc, 2:3], bias=cwf.t[:, cc, 3:4]), [u, cwf], [acc])
                        S.op("dve", lambda e, uv=uv, av=av, cc=cc: e.scalar_tensor_tensor(out=av, in0=uv[:, :, 1:1 + L], scalar=cwf.t[:, cc, 1:2], in1=av,
                                                                                           op0=ALU.mult, op1=ALU.add), [u, cwf, acc], [acc])
                        S.op("dve", lambda e, uv=uv, av=av, cc=cc: e.scalar_tensor_tensor(out=av, in0=uv[:, :, 0:L], scalar=cwf.t[:, cc, 0:1], in1=av,
                                                                                           op0=ALU.mult, op1=ALU.add), [u, cwf, acc], [acc])
                        hdst = usave.t[:, cc, :].rearrange("p (s i) -> p s i", i=2)
                        S.op("pool", lambda e, uv=uv, hdst=hdst: e.tensor_copy(out=hdst, in_=uv[:, :, L:L + 2]), [u, usave], [usave])
                    else:
                        S.op("act", lambda e, cc=cc, b=b: e.activation(out=acc.t[:, 0:W], in_=bank(b)[:, 0:W], func=AF.Identity,
                                                                       scale=cwf.t[:, cc, 2:3], bias=cwf.t[:, cc, 3:4]), [pb[b], cwf], [acc])
                        if not DBG.get("noact2"):
                            S.op("dve", lambda e, cc=cc, b=b, ucur=ucur: e.tensor_copy(out=ucur.t[:, cc, 0:2], in_=bank(b)[:, W - 2:W]), [pb[b], acc], [ucur])
                        if not DBG.get("nostt"):
                            S.op("dve", lambda e, cc=cc, b=b, acc=acc: e.scalar_tensor_tensor(out=acc.t[:, 1:W], in0=bank(b)[:, 0:W - 1], scalar=cwf.t[:, cc, 1:2], in1=acc.t[:, 1:W],
                                                                                     op0=ALU.mult, op1=ALU.add), [pb[b], cwf, acc], [acc])
                            S.op("dve", lambda e, cc=cc, b=b, acc=acc: e.scalar_tensor_tensor(out=acc.t[:, 2:W], in0=bank(b)[:, 0:W - 2], scalar=cwf.t[:, cc, 0:1], in1=acc.t[:, 2:W],
                                                                                     op0=ALU.mult, op1=ALU.add), [pb[b], cwf, acc], [acc])
                        if not DBG.get("nopool"):
                            S.op("pool", lambda e, cc=cc, acc=acc: e.tensor_tensor(out=acc.t[:, 0:2], in0=acc.t[:, 0:2], in1=hc.t[:, cc, 0:2], op=ALU.add), [acc, hc], [acc])
                            S.op("pool", lambda e, cc=cc, acc=acc: e.tensor_tensor(out=acc.t[:, 0:1], in0=acc.t[:, 0:1], in1=hc.t[:, cc, 2:3], op=ALU.add), [acc, hc], [acc])
                def fin(jg=jg, par=par):
                    S.op("act", lambda e: e.activation(out=sgt[par].t[:, 0:W], in_=accs[0][par].t[:, 0:W], func=AF.Silu), [accs[0][par]], [sgt[par]])
                    S.op("pool", lambda e: e.tensor_tensor(out=actT.t[:, jg, 0:W], in0=sgt[par].t[:, 0:W], in1=accs[1][par].t[:, 0:W], op=ALU.mult),
                         [sgt[par], accs[1][par]], [actT])
                pend = fin
            pend()
            def reload(j):
                Cj = min(128, W - j * 128)
                S.dma(h1t[j % 2].t[0:Cj, :], self.h1s[r0 + j * 128:r0 + j * 128 + Cj, :], [self.h1scr], [h1t[j % 2]])
            reload(0)
            for j in range(nsub):
                C = min(128, W - j * 128)
                ht = h1t[j % 2]
                if j + 1 < nsub:
                    reload(j + 1)
                for half in range(2):
                    def mmd(e, half=half, j=j, C=C):
                        for jg in range(22):
                            r = e.matmul(bank(4 + 2 * (j % 2) + half)[0:C, :], lhsT=actT.t[:, jg, j * 128:j * 128 + C], rhs=w_dn.t[:, jg, half * 512:(half + 1) * 512],
                                         start=(jg == 0), stop=(jg == 21))
                        return r
                    S.op("pe", mmd, [actT] + [w_dn_tb[j_] for j_ in range(0, 22, 2)], [pb[4 + 2 * (j % 2) + half]])
                    S.op("dve", lambda e, half=half, ht=ht, C=C: e.scalar_tensor_tensor(out=ht.t[0:C, half * 512:(half + 1) * 512], in0=ht.t[0:C, half * 512:(half + 1) * 512],
                                                                                        scalar=ALPHA, in1=bank(4 + 2 * (j % 2) + half)[0:C, :], op0=ALU.mult, op1=ALU.add),
                         [ht, pb[4 + 2 * (j % 2) + half]], [ht])
                self.layer_norm(ht, C, self.lnbc.t[:, 0, :], self.lnbc.t[:, 1, :], self.epsc.t[0:C, 0:1], "ln2")
                if smp:
                    S.dma(self.y_s, ht.t[0:C, :], [ht], [], is_out=True)
                else:
                    e = r0 + j * 128
                    if e == 0:
                        S.dma(self.y_p[0:128 - NMETA, :], ht.t[NMETA:128, :], [ht], [], is_out=True)
                    else:
                        S.dma(self.y_p[e - NMETA:e - NMETA + C, :], ht.t[0:C, :], [ht], [], is_out=True)
            if (not smp) and r0 + W == TP:
                conv_out(2, self.fconv_p, ucur)
        conv_out(NS * 2, self.fconv_s, usave)

    def build(self):
        S = self.S
        with S:
            self.cmat = S.sb("cmat_sb", [128, NCM, 128], F32)
            self.epsc = S.sb("epsc", [128, 4], F32)
            self.ln_st = S.sb("ln_st", [128, 2, 6], F32)
            self.ln_mv = S.sb("ln_mv", [128, 4], F32)
            self.glb = S.sb("glb", [128, 128], F32)
            self.pst = S.ps("pst", [128, 8, 512], F32)
            self.pb = [S.alias(f"pb{i}", self.pst) for i in range(8)]
            self.h1scr = TB("h1scr", None)
            main_stack = S.stack
            S.stack = ExitStack()
            with S.stack:
                if not DBG.get("skip1"):
                    self.stage1()
                S.barrier()
            S.stack = ExitStack()
            with S.stack:
                if not DBG.get("skip2"):
                    self.stage2()
                S.finish()
            S.stack = main_stack
        return self.nc


_PROG = None


def _program():
    global _PROG
    if _PROG is None:
        _PROG = K().build()
    return _PROG


def kernel(x_prompt, x_sample, state_gdn, state_gdn_conv, state_gla, state_ffn_conv, meta_tokens,
           ln_in_g, ln_in_b, w_in, gdn_conv_w, gdn_A_log, gdn_dt_bias, gdn_norm_g, gla_wgk2,
           gla_bgk, gla_norm_g, w_out, ln1_g, ln1_b, w_up, ffn_conv_w, ffn_conv_b, w_down,
           ln2_g, ln2_b):
    f = lambda a: np.ascontiguousarray(np.asarray(a, dtype=np.float32))
    x_prompt, x_sample = f(x_prompt), f(x_sample)
    w_in0 = f(w_in)[0]
    fm_cols = np.r_[0:1536, 2064:2320, 2320:2576, 3088:3104]
    tm_cols = np.r_[1536:1552, 2320:2576, 1552:2064, 2576:3088, 3104:3616]
    w_in_r = np.ascontiguousarray(w_in0[:, np.r_[fm_cols, tm_cols]])
    lnv = np.stack([f(ln_in_g), f(ln_in_b), f(ln1_g)[0], f(ln1_b)[0], f(ln2_g)[0], f(ln2_b)[0]])
    cwg = np.ascontiguousarray(f(gdn_conv_w)[0].T.reshape(12, 128, 4).transpose(1, 0, 2))
    cwf4 = np.concatenate([f(ffn_conv_w)[0], f(ffn_conv_b)], axis=0)
    cwf = np.ascontiguousarray(cwf4.T.reshape(NFF, 128, 4).transpose(1, 0, 2))
    pvec = np.concatenate([f(gdn_A_log)[0], f(gdn_dt_bias)[0], f(gdn_norm_g)[0], f(gla_norm_g)[0], f(gla_bgk)[0]])[None, :]
    shared = dict(meta=f(meta_tokens), cmat=_const_mats(), w_in_r=w_in_r, w_out=f(w_out)[0], w_up=f(w_up)[0], w_down=f(w_down)[0],
                  lnv=np.ascontiguousarray(lnv), cwg=cwg, cwf=cwf, pvec=np.ascontiguousarray(pvec), wgk2=f(gla_wgk2)[0])
    sg, sgc, sl, sfc = f(state_gdn)[0], f(state_gdn_conv)[0], f(state_gla)[0], f(state_ffn_conv)[0]
    in_maps = []
    for c in range(8):
        sl_ = slice(c * NS, (c + 1) * NS)
        m = dict(shared)
        m.update(xp=x_prompt[c], xs=np.ascontiguousarray(x_sample[sl_].reshape(NS * LS, D)), sgdn=sg[sl_],
                 sgconv=np.ascontiguousarray(sgc[sl_].reshape(NS * 3, 1536)), sgla=sl[sl_],
                 sfconv=np.ascontiguousarray(sfc[sl_].reshape(NS * 2, 2 * DFF)))
        in_maps.append(m)
    ncr = DBG.get("ncores", 8)
    res = run_bass_kernel_spmd(_program(), in_maps[:ncr], core_ids=list(range(ncr)))
    r = res.results
    cat = lambda k: np.stack([np.asarray(r[min(c, ncr - 1)][k]) for c in range(8)])
    y_prompt = cat("y_p")
    y_sample = cat("y_s").reshape(128, LS, D)
    gdn_p = cat("gdn_p")[None]
    gconv_p = cat("gconv_p")[None]
    gla_p = cat("gla_p")[None]
    fconv_p = cat("fconv_p")[None]
    gdn_s = cat("gdn_s").reshape(1, 128, 8, 64, 64)
    gconv_s = cat("gconv_s").reshape(1, 128, 3, 1536)
    gla_s = cat("gla_s").reshape(1, 128, 4, 64, 128)
    fconv_s = cat("fconv_s").reshape(1, 128, 2, 2 * DFF)
    outs = (y_prompt, y_sample, gdn_p, gconv_p, gla_p, fconv_p, gdn_s, gconv_s, gla_s, fconv_s)
    return tuple(np.ascontiguousarray(o, dtype=np.float32) for o in outs)
```

```python
from contextlib import ExitStack

import numpy as np
import concourse.bass as bass
import concourse.mybir as mybir
from concourse.bass_utils import run_bass_kernel_spmd

F32 = mybir.dt.float32
BF16 = mybir.dt.bfloat16
F32R = mybir.dt.float32r
AF = mybir.ActivationFunctionType
ALU = mybir.AluOpType
AX = mybir.AxisListType


class TB:
    def __init__(self, name, t):
        self.name = name
        self.t = t
        self.last_w = None
        self.readers = {}
        self.parts = []


class Sched:
    def __init__(self, nc, nslots=40):
        self.nc = nc
        self.stack = ExitStack()
        self.engs = {"pe": nc.tensor, "dve": nc.vector, "act": nc.scalar, "pool": nc.gpsimd, "sp": nc.sync}
        self.nslots = nslots

    def __enter__(self):
        self.stack.__enter__()
        nc = self.nc
        self.sem = {k: self.stack.enter_context(nc.semaphore(f"s_{k}")) for k in self.engs}
        self.cnt = {k: 0 for k in self.engs}
        self.waited = {k: {} for k in self.engs}
        self.slot_sem = [self.stack.enter_context(nc.semaphore(f"d_{i}")) for i in range(self.nslots)]
        self.slot_cnt = [0] * self.nslots
        self.next_slot = {"sp": 0, "pool": 0}
        self.out_deps = []
        self.nbuf = 0
        return self

    def __exit__(self, *a):
        return self.stack.__exit__(*a)

    def sb(self, name, shape, dtype):
        t = self.stack.enter_context(self.nc.sbuf_tensor(name, list(shape), dtype))
        return TB(name, t)

    def ps(self, name, shape, dtype):
        t = self.stack.enter_context(self.nc.psum_tensor(name, list(shape), dtype))
        return TB(name, t)

    def alias(self, name, tb):
        return TB(name, tb.t)

    def _semof(self, key):
        if isinstance(key, tuple):
            return self.slot_sem[key[1]]
        return self.sem[key]

    def _wait(self, eng, dep):
        key, val = dep
        if eng == "pe" and key == "pe":
            return
        w = self.waited[eng]
        if w.get(key, 0) >= val:
            return
        self.engs[eng].wait_ge(self._semof(key), val)
        w[key] = val

    @staticmethod
    def _expand(bufs):
        out = []
        for b in bufs:
            out.append(b)
            out.extend(b.parts)
        return out

    def _deps(self, reads, writes):
        reads, writes = self._expand(reads), self._expand(writes)
        deps = set()
        for b in reads:
            if b.last_w is not None:
                deps.add(b.last_w)
        for b in writes:
            if b.last_w is not None:
                deps.add(b.last_w)
            for k, v in b.readers.items():
                deps.add((k, v))
        return deps

    def _commit(self, me, reads, writes):
        reads, writes = self._expand(reads), self._expand(writes)
        for b in writes:
            b.last_w = me
            b.readers = {}
        for b in reads:
            if b not in writes:
                b.readers[me[0]] = max(b.readers.get(me[0], 0), me[1])

    def op(self, eng, emit, reads=(), writes=()):
        for d in sorted(self._deps(reads, writes), key=str):
            self._wait(eng, d)
        inst = emit(self.engs[eng])
        self.cnt[eng] += 1
        inst.then_inc(self.sem[eng], 1)
        self._commit((eng, self.cnt[eng]), reads, writes)

    def dma(self, out, in_, reads=(), writes=(), cast=False, is_out=False, q=None):
        eng = q or ("pool" if cast else "sp")
        nsp = (self.nslots * 5) // 8
        lo, n = (0, nsp) if eng == "sp" else (nsp, self.nslots - nsp)
        i = lo + self.next_slot[eng]
        self.next_slot[eng] = (self.next_slot[eng] + 1) % n
        if self.slot_cnt[i] > 0:
            self._wait(eng, (("slot", i), 16 * self.slot_cnt[i]))
        for d in sorted(self._deps(reads, writes), key=str):
            self._wait(eng, d)
        inst = self.engs[eng].dma_start(out=out, in_=in_)
        inst.then_inc(self.slot_sem[i], 16)
        self.slot_cnt[i] += 1
        me = (("slot", i), 16 * self.slot_cnt[i])
        self._commit(me, reads, writes)
        if is_out:
            self.out_deps.append(me)

    def finish(self):
        for d in self.out_deps:
            self._wait("sp", d)
        for k in ("pe", "dve", "act", "pool"):
            if self.cnt[k] > 0:
                self._wait("sp", (k, self.cnt[k]))

    def barrier(self):
        deps = [(k, self.cnt[k]) for k in ("pe", "dve", "act", "pool") if self.cnt[k] > 0]
        deps += [(("slot", i), 16 * c) for i, c in enumerate(self.slot_cnt) if c > 0]
        for e in ("pe", "dve", "act", "pool", "sp"):
            for d in deps:
                if d[0] != e:
                    self._wait(e, d)


DBG = {}
D = 1024
SEQ = 2048
NMETA = 16
TP = SEQ + NMETA
NS = 16
LS = 8
DFF = 2816
NFF = 44
ALPHA = 2.0 ** 0.25
NFM = 2064
NTM = 1808
NEG = -30000.0

C_ID, C_ONE, C_BO64 = 0, 1, 2
C_PU, C_PSU, C_PMBT, C_PMBS, C_PM01T = 3, 4, 5, 6, 7
C_SU, C_SSU, C_SBO, C_SMBT, C_SMBS, C_SM01T, C_SBM = 8, 9, 10, 11, 12, 13, 14
NCM = 15


def _const_mats():
    m = np.zeros((NCM, 128, 128), np.float32)
    k = np.arange(128)[:, None]
    c = np.arange(128)[None, :]
    m[C_ID] = (k == c)
    m[C_ONE] = 1.0
    m[C_BO64] = (k // 64 == c // 64)
    m[C_PU] = (k <= c)
    m[C_PSU] = (k > c)
    m[C_PMBT] = np.where(c >= k, 0.0, NEG)
    m[C_PMBS] = np.where(c < k, 0.0, NEG)
    m[C_PM01T] = (c >= k)
    sb = (k // LS == c // LS)
    m[C_SU] = (k <= c) & sb
    m[C_SSU] = (k > c) & sb
    m[C_SBO] = sb
    m[C_SMBT] = np.where((c >= k) & sb, 0.0, NEG)
    m[C_SMBS] = np.where((c < k) & sb, 0.0, NEG)
    m[C_SM01T] = (c >= k) & sb
    m[C_SBM][:, :NS] = (k // LS == np.arange(NS)[None, :])
    return np.ascontiguousarray(m.transpose(1, 0, 2))


def _bc(ap, axis, shape):
    return ap.unsqueeze(axis).to_broadcast(list(shape))


class K:
    def __init__(self):
        nc = self.nc = bass.Bass("TRN2", target_bir_lowering=False)
        di = lambda n, s: nc.dram_tensor(n, list(s), F32, kind="ExternalInput").ap()
        do = lambda n, s: nc.dram_tensor(n, list(s), F32, kind="ExternalOutput").ap()
        self.xp = di("xp", [SEQ, D]); self.xs = di("xs", [NS * LS, D]); self.meta = di("meta", [NMETA, D])
        self.sgdn = di("sgdn", [NS, 8, 64, 64]); self.sgconv = di("sgconv", [NS * 3, 1536])
        self.sgla = di("sgla", [NS, 4, 64, 128]); self.sfconv = di("sfconv", [NS * 2, 2 * DFF])
        self.cmat_d = di("cmat", [128, NCM, 128])
        self.w_in_d = di("w_in_r", [D, NFM + NTM]); self.w_out_d = di("w_out", [D, D])
        self.w_up_d = di("w_up", [D, 2 * DFF]); self.w_down_d = di("w_down", [DFF, D])
        self.lnv_d = di("lnv", [6, D]); self.cwg_d = di("cwg", [128, 12, 4]); self.cwf_d = di("cwf", [128, NFF, 4])
        self.pvec_d = di("pvec", [1, 464]); self.wgk2_d = di("wgk2", [16, 256])
        self.y_p = do("y_p", [SEQ, D]); self.y_s = do("y_s", [NS * LS, D])
        self.gdn_p = do("gdn_p", [8, 64, 64]); self.gconv_p = do("gconv_p", [3, 1536])
        self.gla_p = do("gla_p", [4, 64, 128]); self.fconv_p = do("fconv_p", [2, 2 * DFF])
        self.gdn_s = do("gdn_s", [NS, 8, 64, 64]); self.gconv_s = do("gconv_s", [NS * 3, 1536])
        self.gla_s = do("gla_s", [NS, 4, 64, 128]); self.fconv_s = do("fconv_s", [NS * 2, 2 * DFF])
        self.h1s = nc.dram_tensor("h1s", [TP + NS * LS, D], F32, kind="Internal").ap()
        self.S = Sched(nc)

    def cm(self, idx, r=128, c=128):
        return self.cmat.t[0:r, idx, 0:c]

    def bank(self, b, n=1):
        if n == 1:
            return self.pst.t[:, b, :]
        return self.pst.t[:, b:b + n, :].rearrange("p b f -> p (b f)")

    def layer_norm(self, buf, C, g_ap, b_ap, eps_tile, tag):
        S = self.S
        st, mv = self.ln_st, self.ln_mv
        x = buf.t

        def stats(e):
            e.bn_stats(out=st.t[0:C, 0, :], in_=x[0:C, 0:512])
            return e.bn_stats(out=st.t[0:C, 1, :], in_=x[0:C, 512:1024])
        S.op("dve", stats, [buf], [st])
        S.op("dve", lambda e: e.bn_aggr(out=mv.t[0:C, 0:2], in_=st.t[0:C, :, :].rearrange("p a b -> p (a b)")), [st], [mv])
        S.op("act", lambda e: e.activation(out=mv.t[0:C, 2:3], in_=mv.t[0:C, 1:2], func=AF.Ln, bias=eps_tile, scale=1.0), [mv, self.epsc], [mv])
        S.op("act", lambda e: e.activation(out=mv.t[0:C, 3:4], in_=mv.t[0:C, 2:3], func=AF.Exp, scale=-0.5), [mv], [mv])
        S.op("dve", lambda e: e.scalar_tensor_tensor(out=x[0:C, :], in0=x[0:C, :], scalar=mv.t[0:C, 0:1], in1=g_ap[0:C, :],
                                                      op0=ALU.subtract, op1=ALU.mult), [buf, mv, self.lnbc], [buf])
        S.op("dve", lambda e: e.scalar_tensor_tensor(out=x[0:C, :], in0=x[0:C, :], scalar=mv.t[0:C, 3:4], in1=b_ap[0:C, :],
                                                      op0=ALU.mult, op1=ALU.add), [buf, mv, self.lnbc], [buf])

    def stage1_alloc(self):
        S = self.S
        self.lnbc = S.sb("lnbc", [128, 4, D], F32)
        self.pvec = S.sb("pvec_sb", [128, 464], F32)
        self.negA = S.sb("negA", [128, 8], F32)
        self.wgk2 = S.sb("wgk2_sb", [16, 256], F32)
        self.cwg = S.sb("cwg_sb", [128, 12, 4], F32)
        self.w_in = S.sb("w_in_sb", [128, 8, NFM + NTM], BF16)
        self.w_out = S.sb("w_out_sb", [128, 8, D], BF16)
        self.xhs = [S.sb(f"xh{i}", [128, D], F32) for i in range(2)]
        self.hT = S.sb("hT", [128, 8, 128], BF16)
        self.qkvx = S.sb("qkvx", [128, 12, 176], F32)
        self.fmx = S.sb("fmx", [128, 5, 128], F32)
        self.tm0 = S.sb("tm0", [128, 272], F32)
        self.gv_tok = S.sb("gv_tok", [128, 512], F32)
        self.sg_gdn = S.sb("sg_gdn", [128, 512], F32)
        self.sg_gla = S.sb("sg_gla", [128, 512], F32)
        self.bigs = S.sb("bigs", [128, 6, 1024], F32)
        self.big = [S.alias(f"big{i}", self.bigs) for i in range(6)]
        self.Pc = S.sb("Pc", [128, 1024], F32)
        self.PTc = S.sb("PTc", [128, 1024], F32)
        self.TTc = S.sb("TTc", [128, 1024], F32)
        for tb in (self.Pc, self.PTc, self.TTc):
            tb.parts = [S.alias(f"{tb.name}_hg{g}", tb) for g in range(2)]
        self.kq = S.sb("kq", [128, 4, 2, 128], F32)
        self.wkT = S.sb("wkT", [128, 8, 128], F32)
        self.KTm = self.wkT
        self.QTm = S.sb("QTm", [128, 8, 128], F32)
        self.keTm = S.sb("keTm", [128, 4, 128], F32)
        self.qeTm = S.sb("qeTm", [128, 4, 128], F32)
        self.wv = S.sb("wv", [128, 512], F32)
        self.RK = S.sb("RK", [128, 512], F32)
        self.RV = S.sb("RV", [128, 512], F32)
        self.kdec = S.sb("kdec", [128, 512], F32)
        self.U = self.RV
        self.ogdn = self.RK
        self.Sg = S.sb("Sg", [128, 4, 64], F32)
        self.sc = S.sb("sc", [128, 96], F32)
        self.lt = TB("lt", self.wv.t[:, 0:256])
        self.lt.parts = [self.wv]
        self.ebT = S.sb("ebT", [128, 2, 128], F32)
        self.enbT = S.sb("enbT", [128, 2, 128], F32)
        self.qeT = S.sb("qeT", [128, 2, 128], F32)
        self.PTg = S.sb("PTg", [128, 4, 128], F32)
        self.kd = TB("kd", self.enbT.t[:, :, :].rearrange("p a c -> p (a c)"))
        self.kd.parts = [self.enbT]
        self.ogla = S.sb("ogla", [128, 512], F32)
        self.Sl = S.sb("Sl", [128, 2, 128], F32)
        self.mix = S.sb("mix", [128, D], F32)
        self.mixT = S.sb("mixT", [128, 8, 128], BF16)
        self.otmp = TB("otmp", self.bigs.t[:, 3, 0:512])
        self.otmp.parts = [self.big[3]]

    def bg(self, i, n=1):
        if n == 1:
            return self.bigs.t[:, i, :]
        return self.bigs.t[:, i:i + n, :].rearrange("p b f -> p (b f)")

    def stage1_setup(self):
        S = self.S
        nc = self.nc
        S.dma(self.cmat.t[:], self.cmat_d, [], [self.cmat])
        for i in range(4):
            S.dma(self.lnbc.t[:, i, :], self.lnv_d[i:i + 1, :].partition_broadcast(128), [], [self.lnbc])
        S.dma(self.pvec.t[:], self.pvec_d[0:1, :].partition_broadcast(128), [], [self.pvec])
        S.dma(self.wgk2.t[:], self.wgk2_d, [], [self.wgk2])
        S.dma(self.cwg.t[:], self.cwg_d, [], [self.cwg])
        S.op("pool", lambda e: e.memset(self.epsc.t[:, 0:1], 1e-5), [], [self.epsc])
        S.op("pool", lambda e: e.memset(self.epsc.t[:, 1:2], 1e-6), [self.epsc], [self.epsc])
        S.op("pool", lambda e: e.memset(self.epsc.t[:, 2:3], 1.0), [self.epsc], [self.epsc])
        S.op("pool", lambda e: e.memset(self.epsc.t[:, 3:4], 0.0), [self.epsc], [self.epsc])
        wv_ = self.w_in_d.rearrange("(kc p) n -> p kc n", p=128)
        for kc in range(8):
            S.dma(self.w_in.t[:, kc, :], wv_[:, kc, :], [], [self.w_in], cast=True)
        wo_ = self.w_out_d.rearrange("(kc p) n -> p kc n", p=128)
        for kc in range(0, 8, 4):
            S.dma(self.w_out.t[:, kc:kc + 4, :], wo_[:, kc:kc + 4, :], [], [self.w_out], cast=True)
        S.op("act", lambda e: e.activation(out=self.negA.t[:], in_=self.pvec.t[:, 0:8], func=AF.Exp), [self.pvec], [self.negA])
        S.op("dve", lambda e: e.tensor_scalar(out=self.negA.t[:], in0=self.negA.t[:], scalar1=-1.0, scalar2=None, op0=ALU.mult),
             [self.negA], [self.negA])
        S.op("pool", lambda e: e.memset(self.Sg.t[:], 0.0), [], [self.Sg])
        S.op("pool", lambda e: e.memset(self.Sl.t[:], 0.0), [], [self.Sl])
        S.op("pool", lambda e: e.memset(self.qkvx.t[:], 0.0), [], [self.qkvx])
        for tb in (self.wkT, self.QTm, self.keTm, self.qeTm):
            S.op("pool", lambda e, tb=tb: e.memset(tb.t[:], 0.0), [], [tb])

    def front0(self, e0, C, kind, xh):
        self.front0_load(e0, C, kind, xh)
        self.front0_compute(e0, C, kind, xh)

    def front0_load(self, e0, C, kind, xh):
        S = self.S
        smp = kind == "s"
        if smp:
            S.dma(xh.t[0:C, :], self.xs, [], [xh])
        elif e0 == 0:
            S.dma(xh.t[0:NMETA, :], self.meta, [], [xh])
            S.dma(xh.t[NMETA:128, :], self.xp[0:128 - NMETA, :], [], [xh])
        else:
            S.dma(xh.t[0:C, :], self.xp[e0 - NMETA:e0 - NMETA + C, :], [], [xh])

    def front0_compute(self, e0, C, kind, xh):
        S, cm, pb, bank, hT = self.S, self.cm, self.pb, self.bank, self.hT
        self.layer_norm(xh, C, self.lnbc.t[:, 0, :], self.lnbc.t[:, 1, :], self.epsc.t[0:C, 0:1], "in")
        for half in range(2):
            def tr(e, half=half):
                for j in range(4):
                    kc = half * 4 + j
                    r = e.transpose(bank(half)[:, j * 128:j * 128 + C], xh.t[0:C, kc * 128:(kc + 1) * 128], cm(C_ID, C, C))
                return r
            S.op("pe", tr, [xh, self.cmat], [pb[half]])
            S.op("act", lambda e, half=half: e.activation(
                out=hT.t[:, half * 4:half * 4 + 4, 0:C],
                in_=bank(half).rearrange("p (j c) -> p j c", j=4)[:, :, 0:C], func=AF.Copy), [pb[half]], [hT])

    def chunk(self, e0, C, kind, xh, nxt):
        S = self.S
        self.xh = xh
        cm = self.cm
        pb = self.pb
        bank = self.bank
        big = self.big
        bg = self.bg
        smp = kind == "s"
        if smp:
            iU, iSU, iBO, iMBT, iMBS, iM01 = C_SU, C_SSU, C_SBO, C_SMBT, C_SMBS, C_SM01T
            G, L, nlev = NS, LS, 3
        else:
            iU, iSU, iBO, iMBT, iMBS, iM01 = C_PU, C_PSU, C_ONE, C_PMBT, C_PMBS, C_PM01T
            G, L, nlev = 1, C, {128: 7, 16: 4}[C]
        hT, qkvx, fmx, tm0, kq, sc = self.hT, self.qkvx, self.fmx, self.tm0, self.kq, self.sc
        ident = cm(C_ID)

        if DBG.get("step", 99) < 4:
            return
        qv = qkvx.t[:, :, 0:G * (L + 3)].rearrange("p a (g l) -> p a g l", g=G)
        for grp in range(5):
            b = 2 + (grp % 4)
            ccs = list(range(grp * 4, min(grp * 4 + 4, 17)))

            def mmf(e, ccs=ccs, b=b):
                for j, cc in enumerate(ccs):
                    M = 128 if cc < 16 else 16
                    for kc in range(8):
                        r = e.matmul(bank(b)[0:M, j * 128:j * 128 + C], lhsT=self.w_in.t[:, kc, cc * 128:cc * 128 + M],
                                     rhs=hT.t[:, kc, 0:C], start=(kc == 0), stop=(kc == 7))
                return r
            S.op("pe", mmf, [self.w_in, hT], [pb[b]])
            src = bank(b).rearrange("p (j c) -> p j c", j=4)
            if grp < 3:
                S.op("act", lambda e, grp=grp, src=src: e.activation(
                    out=qv[:, grp * 4:grp * 4 + 4, :, 3:3 + L],
                    in_=src[:, :, 0:C].rearrange("p j (g l) -> p j g l", g=G), func=AF.Copy), [pb[b]], [qkvx])
            elif grp == 3:
                S.op("act", lambda e, src=src: e.activation(out=fmx.t[:, 0:4, 0:C], in_=src[:, :, 0:C], func=AF.Copy), [pb[b]], [fmx])
            else:
                S.op("act", lambda e, src=src: e.activation(out=fmx.t[0:16, 4, 0:C], in_=src[0:16, 0, 0:C], func=AF.Copy), [pb[b]], [fmx])
        if DBG.get("step", 99) < 5:
            return
        tmoff = [NFM, NFM + 272, NFM + 784, NFM + 1296]
        tmn = [272, 512, 512, 512]
        for gi in range(4):
            b = 6 + (gi % 2)

            def mmt(e, gi=gi, b=b):
                for kc in range(8):
                    r = e.matmul(bank(b)[0:C, 0:tmn[gi]], lhsT=hT.t[:, kc, 0:C], rhs=self.w_in.t[:, kc, tmoff[gi]:tmoff[gi] + tmn[gi]],
                                 start=(kc == 0), stop=(kc == 7))
                return r
            S.op("pe", mmt, [self.w_in, hT], [pb[b]])
            if gi == 0:
                S.op("dve", lambda e, b=b: e.tensor_copy(out=tm0.t[0:C, :], in_=bank(b)[0:C, 0:272]), [pb[b]], [tm0])
            elif gi == 1:
                S.op("act", lambda e, b=b: e.activation(out=self.sg_gdn.t[0:C, :], in_=bank(b)[0:C, :], func=AF.Copy), [pb[b]], [self.sg_gdn])
            elif gi == 2:
                S.op("dve", lambda e, b=b: e.tensor_copy(out=self.gv_tok.t[0:C, :], in_=bank(b)[0:C, :]), [pb[b]], [self.gv_tok])
            else:
                S.op("act", lambda e, b=b: e.activation(out=self.sg_gla.t[0:C, :], in_=bank(b)[0:C, :], func=AF.Copy), [pb[b]], [self.sg_gla])
        if nxt is not None:
            self.front0_load(*nxt)
        if DBG.get("step", 99) < 6:
            return
        acc = bg(0, 2)[:, 0:12 * C].rearrange("p (a g l) -> p a g l", a=12, g=G)
        tmp = bg(2, 2)[:, 0:12 * C].rearrange("p (a g l) -> p a g l", a=12, g=G)
        accT, tmpT = [big[0], big[1]], [big[2], big[3]]

        def cwb(i):
            return self.cwg.t[:, :, i:i + 1].unsqueeze(3).to_broadcast([128, 12, G, L])
        S.op("dve", lambda e: e.tensor_tensor(out=acc, in0=qv[:, :, :, 0:L], in1=cwb(0), op=ALU.mult), [qkvx, self.cwg], accT)
        for i in range(1, 4):
            S.op("pool" if i == 1 else "dve", lambda e, i=i: e.tensor_tensor(out=tmp, in0=qv[:, :, :, i:i + L], in1=cwb(i), op=ALU.mult), [qkvx, self.cwg], tmpT)
            S.op("dve", lambda e: e.tensor_tensor(out=acc, in0=acc, in1=tmp, op=ALU.add), accT + tmpT, accT)
        qa = bg(0, 2)[:, 0:12 * C].rearrange("p (a c) -> p a c", a=12)
        S.op("act", lambda e: e.activation(out=qa, in_=qa, func=AF.Silu), accT, accT)
        if DBG.get("step", 99) < 7:
            return
        sq = bg(4)[:, 0:8 * C].rearrange("p (a c) -> p a c", a=8)
        rn = bg(5)[:, 0:8 * C].rearrange("p (a c) -> p a c", a=8)
        for sg in (self.sg_gdn, self.sg_gla):
            S.op("act", lambda e, sg=sg: e.activation(out=sg.t[0:C, :], in_=sg.t[0:C, :], func=AF.Silu), [sg], [sg])
        S.op("act", lambda e: e.activation(out=sq, in_=qa[:, 0:8, :], func=AF.Square), accT, [big[4]])
        S.op("act", lambda e: e.activation(out=sc.t[0:C, 16:24], in_=tm0.t[0:C, 8:16], func=AF.Sigmoid), [tm0], [sc])
        for half in range(2):
            S.op("pe", lambda e, half=half: e.matmul(bank(half)[:, 0:4 * C], lhsT=cm(C_BO64),
                                                     rhs=bg(4)[:, half * 4 * C:(half + 1) * 4 * C], start=True, stop=True),
                 [big[4], self.cmat], [pb[half]])
            S.op("act", lambda e, half=half: e.activation(out=bg(5)[:, half * 4 * C:(half + 1) * 4 * C], in_=bank(half)[:, 0:4 * C],
                                                          func=AF.Ln, bias=self.epsc.t[:, 1:2], scale=1.0), [pb[half], self.epsc], [big[5]])
        S.op("act", lambda e: e.activation(out=bg(5)[:, 0:8 * C], in_=bg(5)[:, 0:8 * C], func=AF.Exp, scale=-0.5), [big[5]], [big[5]])
        S.op("dve", lambda e: e.scalar_tensor_tensor(out=kq.t[:, :, 1, 0:C], in0=qa[:, 0:4, :], scalar=0.125, in1=rn[:, 0:4, :],
                                                      op0=ALU.mult, op1=ALU.mult), accT + [big[5]], [kq])
        S.op("pool", lambda e: e.tensor_tensor(out=kq.t[:, :, 0, 0:C], in0=qa[:, 4:8, :], in1=rn[:, 4:8, :], op=ALU.mult), accT + [big[5]], [kq])
        for h2 in range(2):
            rows = slice(64 * h2, 64 * h2 + 64)
            pad = lambda tb: tb.t[rows, :, 0:C].rearrange("p (a two) c -> p a two c", two=2)[:, :, h2, :]
            S.op("act", lambda e, rows=rows, pad=pad: e.activation(out=pad(self.KTm), in_=kq.t[rows, :, 0, 0:C], func=AF.Copy), [kq], [self.KTm])
            S.op("dve", lambda e, rows=rows, pad=pad: e.tensor_copy(out=pad(self.QTm), in_=kq.t[rows, :, 1, 0:C]), [kq], [self.QTm])
        if DBG.get("step", 99) < 8:
            return
        s_ = lambda a, b_: sc.t[0:C, a:b_]
        S.op("dve", lambda e: e.tensor_tensor(out=s_(0, 8), in0=tm0.t[0:C, 0:8], in1=self.pvec.t[0:C, 8:16], op=ALU.add), [tm0, self.pvec], [sc])
        S.op("act", lambda e: e.activation(out=s_(0, 8), in_=s_(0, 8), func=AF.Exp), [sc], [sc])
        S.op("act", lambda e: e.activation(out=s_(0, 8), in_=s_(0, 8), func=AF.Ln, bias=self.epsc.t[0:C, 2:3], scale=1.0), [sc, self.epsc], [sc])
        S.op("dve", lambda e: e.tensor_tensor(out=s_(8, 16), in0=s_(0, 8), in1=self.negA.t[0:C, :], op=ALU.mult), [sc, self.negA], [sc])

        def mmG(e):
            e.matmul(bank(0)[0:C, 0:8], lhsT=cm(iU, C, C), rhs=s_(8, 16), start=True, stop=True)
            return e.matmul(bank(0)[0:C, 8:16], lhsT=cm(iBO, C, C), rhs=s_(8, 16), start=True, stop=True)
        S.op("pe", mmG, [sc, self.cmat], [pb[0]])
        S.op("dve", lambda e: e.tensor_copy(out=s_(24, 40), in_=bank(0)[0:C, 0:16]), [pb[0]], [sc])
        S.op("act", lambda e: e.activation(out=s_(40, 48), in_=s_(24, 32), func=AF.Exp), [sc], [sc])
        S.op("dve", lambda e: e.tensor_tensor(out=s_(48, 56), in0=s_(32, 40), in1=s_(24, 32), op=ALU.subtract), [sc], [sc])
        S.op("act", lambda e: e.activation(out=s_(48, 56), in_=s_(48, 56), func=AF.Exp), [sc], [sc])
        S.op("act", lambda e: e.activation(out=s_(56, 64), in_=s_(32, 40), func=AF.Exp), [sc], [sc])
        S.op("dve", lambda e: e.tensor_tensor(out=s_(64, 72), in0=s_(16, 24), in1=s_(40, 48), op=ALU.mult), [sc], [sc])
        if DBG.get("step", 99) < 9:
            return
        def trk(e):
            for p in range(4):
                r = e.transpose(bank(6)[0:C, p * 128:(p + 1) * 128], kq.t[:, p, 0, 0:C], ident)
            return r
        S.op("pe", trk, [kq, self.cmat], [pb[6]])

        def trv(e):
            for p in range(4):
                r = e.transpose(bank(7)[0:C, p * 128:(p + 1) * 128], qa[:, 8 + p, :], ident)
            return r
        S.op("pe", trv, accT + [self.cmat], [pb[7]])
        h3 = lambda ap: ap.rearrange("p (h d) -> p h d", h=8)
        S.op("dve", lambda e: e.tensor_tensor(out=h3(self.RK.t[0:C, :]), in0=h3(bank(6)[0:C, :]), in1=_bc(s_(64, 72), 2, [C, 8, 64]), op=ALU.mult),
             [pb[6], sc], [self.RK])
        S.op("dve", lambda e: e.tensor_tensor(out=h3(self.kdec.t[0:C, :]), in0=h3(bank(6)[0:C, :]), in1=_bc(s_(48, 56), 2, [C, 8, 64]), op=ALU.mult),
             [pb[6], sc], [self.kdec])
        S.op("dve", lambda e: e.tensor_tensor(out=h3(self.RV.t[0:C, :]), in0=h3(bank(7)[0:C, :]), in1=_bc(s_(16, 24), 2, [C, 8, 64]), op=ALU.mult),
             [pb[7], sc], [self.RV])
        if DBG.get("step", 99) < 10:
            return
        v3 = lambda i: bg(i)[0:C, 0:8 * C].rearrange("p (h c) -> p h c", h=8)
        S.op("dve", lambda e: e.tensor_tensor(out=v3(2), in0=_bc(cm(iU, C, C), 1, [C, 8, C]), in1=_bc(s_(8, 16), 2, [C, 8, C]), op=ALU.mult),
             [self.cmat, sc], [big[2]])
        for half in range(2):
            S.op("pe", lambda e, half=half: e.matmul(bank(half)[0:C, 0:4 * C], lhsT=cm(C_ONE, C, C),
                                                     rhs=bg(2)[0:C, half * 4 * C:(half + 1) * 4 * C], start=True, stop=True),
                 [big[2], self.cmat], [pb[half]])
        gbc = bank(0, 2)

        def gview(r):
            return self.pst.t[0:r, 0:2, 0:4 * C].rearrange("p b (h c) -> p b h c", h=4)
        v4 = lambda i: bg(i)[0:C, 0:8 * C].rearrange("p (b h c) -> p b h c", b=2, h=4)
        S.op("pool", lambda e: e.tensor_tensor(out=v3(3), in0=_bc(cm(iMBT, C, C), 1, [C, 8, C]), in1=_bc(s_(24, 32), 2, [C, 8, C]), op=ALU.subtract),
             [self.cmat, sc], [big[3]])
        S.op("pool", lambda e: e.tensor_tensor(out=v3(4), in0=_bc(cm(iMBS, C, C), 1, [C, 8, C]), in1=_bc(s_(24, 32), 2, [C, 8, C]), op=ALU.add),
             [self.cmat, sc], [big[4]])
        S.op("dve", lambda e: e.tensor_tensor(out=v4(5), in0=gview(C), in1=v4(3), op=ALU.add), [pb[0], pb[1], big[3]], [big[5]])
        S.op("act", lambda e: e.activation(out=bg(5)[0:C, 0:8 * C], in_=bg(5)[0:C, 0:8 * C], func=AF.Exp), [big[5]], [big[5]])
        S.op("dve", lambda e: e.tensor_tensor(out=v4(1), in0=v4(4), in1=gview(C), op=ALU.subtract), [pb[0], pb[1], big[4]], [big[1]])
        S.op("act", lambda e: e.activation(out=bg(1)[0:C, 0:8 * C], in_=bg(1)[0:C, 0:8 * C], func=AF.Exp), [big[1]], [big[1]])
        if DBG.get("step", 99) < 11:
            return
        def mmkk(e):
            for h in range(8):
                p, h2 = h // 2, h % 2
                ov = bank(2 + h // 2).rearrange("p (hh two c) -> p hh two c", hh=2, two=2)
                if C == 128:
                    r = e.matmul(bank(2 + h // 2)[0:C, (h % 2) * 256:(h % 2) * 256 + 256], lhsT=self.KTm.t[:, h, 0:C],
                                 rhs=kq.t[:, p, :, :].rearrange("p a c -> p (a c)"), start=True, stop=True)
                else:
                    for two in range(2):
                        r = e.matmul(ov[0:C, h % 2, two, 0:C], lhsT=self.KTm.t[:, h, 0:C],
                                     rhs=kq.t[:, p, two, 0:C], start=True, stop=True)
            return r
        S.op("pe", mmkk, [kq, self.KTm], [pb[2], pb[3], pb[4], pb[5]])
        kkv = self.pst.t[0:C, 2:6, :].rearrange("p b (hh two c) -> p b hh two c", hh=2, two=2)
        v5 = lambda i: bg(i)[0:C, 0:8 * C].rearrange("p (b hh c) -> p b hh c", b=4, hh=2)
        S.op("dve", lambda e: e.tensor_tensor(out=v5(2), in0=kkv[:, :, :, 0, 0:C], in1=v5(1), op=ALU.mult), [pb[2], pb[3], pb[4], pb[5], big[1]], [big[2]])
        use_r = (C == 128) and bool(DBG.get("f32r"))
        ro = (lambda ap: ap.bitcast(F32R)) if use_r else (lambda ap: ap)
        ri = (lambda ap: ap.bitcast(F32R)) if use_r else (lambda ap: ap)
        Pc, PTc, TTc = self.Pc, self.PTc, self.TTc
        c3 = lambda tb: tb.t[0:C, 0:8 * C].rearrange("p (h c) -> p h c", h=8)
        c4 = lambda tb: tb.t[0:C, 0:8 * C].rearrange("p (b h c) -> p b h c", b=2, h=4)
        S.op("dve", lambda e: e.scalar_tensor_tensor(out=ro(c3(Pc)), in0=v3(2), scalar=-1.0, in1=_bc(s_(16, 24), 2, [C, 8, C]),
                                                      op0=ALU.mult, op1=ALU.mult), [big[2], sc], [Pc])
        S.op("dve", lambda e: e.tensor_tensor(out=v5(0), in0=kkv[:, :, :, 1, 0:C], in1=v5(5), op=ALU.mult), [pb[2], pb[3], pb[4], pb[5], big[5]] + accT, [big[0]])
        for half in range(2):
            def trn(e, half=half):
                for j in range(4):
                    r = e.transpose(bank(half)[0:C, j * C:(j + 1) * C], c3(Pc)[:, half * 4 + j, :], cm(C_ID, C, C))
                return r
            S.op("pe", trn, [Pc, self.cmat], [pb[half]])
        S.op("act", lambda e: e.activation(out=ro(c4(PTc)), in_=gview(C), func=AF.Copy), [pb[0], pb[1]], [PTc])
        S.op("dve", lambda e: e.tensor_tensor(out=ro(c3(TTc)), in0=c3(PTc), in1=_bc(cm(C_ID, C, C), 1, [C, 8, C]), op=ALU.add), [PTc, self.cmat], [TTc])
        if DBG.get("step", 99) < 12:
            return
        if nxt is not None:
            self.front0_compute(*nxt)
        gla_gen = self.gla_prep(C, iU, iSU, iM01)
        for lev in range(nlev):
            doA, doC, doB = lev >= 1, lev <= nlev - 2, lev <= nlev - 3
            for hg in range(2):
                bA, bB, bC = (2, 3, 4) if hg == 0 else (5, 6, 7)

                def mminv(e, hg=hg, bA=bA, bB=bB, bC=bC, doA=doA, doB=doB, doC=doC):
                    r = None
                    for j in range(4):
                        h = hg * 4 + j
                        o = lambda b_: bank(b_)[0:C, j * C:(j + 1) * C]
                        if doA:
                            r = e.matmul(o(bA), lhsT=ri(c3(Pc)[:, h, :]), rhs=ri(c3(TTc)[:, h, :]), start=True, stop=True)
                        if doC:
                            r = e.matmul(o(bC), lhsT=ri(c3(PTc)[:, h, :]), rhs=ri(c3(Pc)[:, h, :]), start=True, stop=True)
                        if doB:
                            r = e.matmul(o(bB), lhsT=ri(c3(Pc)[:, h, :]), rhs=ri(c3(PTc)[:, h, :]), start=True, stop=True)
                    return r
                wr = ([pb[bA]] if doA else []) + ([pb[bB]] if doB else []) + ([pb[bC]] if doC else [])
                S.op("pe", mminv, [Pc.parts[hg], PTc.parts[hg], TTc.parts[hg]], wr)
                hs = slice(hg * 4 * C, (hg + 1) * 4 * C)
                if doA:
                    S.op("dve", lambda e, bA=bA, hs=hs: e.tensor_tensor(out=ro(TTc.t[0:C, hs]), in0=bank(bA)[0:C, 0:4 * C], in1=TTc.t[0:C, hs], op=ALU.add),
                         [pb[bA], TTc.parts[hg]], [TTc.parts[hg]])
                if doC:
                    S.op("act", lambda e, bC=bC, hs=hs: e.activation(out=ro(Pc.t[0:C, hs]), in_=bank(bC)[0:C, 0:4 * C], func=AF.Copy), [pb[bC]], [Pc.parts[hg]])
                if doB:
                    S.op("act", lambda e, bB=bB, hs=hs: e.activation(out=ro(PTc.t[0:C, hs]), in_=bank(bB)[0:C, 0:4 * C], func=AF.Copy), [pb[bB]], [PTc.parts[hg]])
            for _ in range(4):
                next(gla_gen, None)
        for _ in gla_gen:
            pass
        if DBG.get("step", 99) < 13:
            return
        def mmwv(e):
            for h in range(8):
                r = e.matmul(bank(0)[0:C, h * 64:(h + 1) * 64], lhsT=c3(TTc)[:, h, :], rhs=self.RV.t[0:C, h * 64:(h + 1) * 64], start=True, stop=True)
            return r
        S.op("pe", mmwv, [TTc, self.RV], [pb[0]])
        S.op("act", lambda e: e.activation(out=self.wv.t[0:C, :], in_=bank(0)[0:C, :], func=AF.Copy), [pb[0]], [self.wv])

        def mmwk(e):
            for h in range(8):
                p = h // 2
                r = e.matmul(bank(2 + h // 4)[:, (h % 4) * 128:(h % 4) * 128 + C], lhsT=self.RK.t[0:C, p * 128:(p + 1) * 128], rhs=c3(TTc)[:, h, :],
                             start=True, stop=True)
            return r
        S.op("pe", mmwk, [TTc, self.RK], [pb[2], pb[3]])
        wkv = self.pst.t[:, 2:4, :].rearrange("p b (hh two c) -> p (b hh) two c", hh=2, two=2)
        for h2 in range(2):
            rows = slice(64 * h2, 64 * h2 + 64)
            S.op("dve" if h2 == 0 else "act",
                 (lambda e, rows=rows, h2=h2: e.tensor_copy(out=self.wkT.t[rows, :, 0:C].rearrange("p (a two) c -> p a two c", two=2)[:, :, h2, :], in_=wkv[rows, :, h2, 0:C])) if h2 == 0 else
                 (lambda e, rows=rows, h2=h2: e.activation(out=self.wkT.t[rows, :, 0:C].rearrange("p (a two) c -> p a two c", two=2)[:, :, h2, :], in_=wkv[rows, :, h2, 0:C], func=AF.Copy)),
                 [pb[2], pb[3]], [self.wkT])
        if DBG.get("step", 99) < 14:
            return
        for _ in gla_gen:
            pass
        if DBG.get("step", 99) < 15:
            return
        if smp:
            self.state_sample(C)
        else:
            self.state_prompt(C)
        if DBG.get("step", 99) < 16:
            return
        self.post_mix(e0, C, smp)
        if DBG.get("step", 99) < 17:
            return
        if not smp:
            S.op("pool", lambda e: e.tensor_copy(out=qkvx.t[:, :, 0:3], in_=qkvx.t[:, :, L:L + 3]), [qkvx], [qkvx])

    def gla_prep(self, C, iU, iSU, iM01):
        S, cm, pb, bank = self.S, self.cm, self.pb, self.bank
        fmx, lt = self.fmx, self.lt
        yield
        S.op("pe", lambda e: e.matmul(bank(0)[0:C, 0:256], lhsT=fmx.t[0:16, 4, 0:C], rhs=self.wgk2.t[:, :], start=True, stop=True),
             [fmx, self.wgk2], [pb[0]])
        yield
        S.op("dve", lambda e: e.tensor_tensor(out=lt.t[0:C, :], in0=bank(0)[0:C, 0:256], in1=self.pvec.t[0:C, 208:464], op=ALU.add), [pb[0], self.pvec], [lt])
        yield
        S.op("act", lambda e: e.activation(out=lt.t[0:C, :], in_=lt.t[0:C, :], func=AF.Exp, scale=-1.0), [lt], [lt])
        yield
        S.op("act", lambda e: e.activation(out=lt.t[0:C, :], in_=lt.t[0:C, :], func=AF.Ln, bias=self.epsc.t[0:C, 2:3], scale=1.0), [lt, self.epsc], [lt])

        yield
        def mmbc(e):
            for p in range(2):
                r = e.matmul(bank(1)[:, p * 128:p * 128 + C], lhsT=lt.t[0:C, p * 128:(p + 1) * 128], rhs=cm(iU, C, C), start=True, stop=True)
            return r
        yield
        S.op("pe", mmbc, [lt, self.cmat], [pb[1]])
        bcv = bank(1)[:, 0:256].rearrange("p (a c) -> p a c", a=2)[:, :, 0:C]
        yield
        S.op("act", lambda e: e.activation(out=self.ebT.t[:, :, 0:C], in_=bcv, func=AF.Exp, scale=-1.0 / 16.0), [pb[1]], [self.ebT])
        yield
        S.op("act", lambda e: e.activation(out=self.enbT.t[:, :, 0:C], in_=bcv, func=AF.Exp, scale=1.0 / 16.0), [pb[1]], [self.enbT])
        yield
        S.op("dve", lambda e: e.scalar_tensor_tensor(out=self.qeT.t[:, :, 0:C], in0=fmx.t[:, 0:2, 0:C], scalar=0.125, in1=self.ebT.t[:, :, 0:C],
                                                      op0=ALU.mult, op1=ALU.mult), [fmx, self.ebT], [self.qeT])
        yield
        for h2 in range(2):
            rows = slice(64 * h2, 64 * h2 + 64)
            pad = lambda tb: tb.t[rows, :, 0:C].rearrange("p (a two) c -> p a two c", two=2)[:, :, h2, :]
            S.op("pool", lambda e, rows=rows, pad=pad: e.tensor_tensor(out=pad(self.keTm), in0=fmx.t[rows, 2:4, 0:C], in1=self.enbT.t[rows, :, 0:C], op=ALU.mult),
                 [fmx, self.enbT], [self.keTm])
            S.op("act", lambda e, rows=rows, pad=pad: e.activation(out=pad(self.qeTm), in_=self.qeT.t[rows, :, 0:C], func=AF.Copy), [self.qeT], [self.qeTm])

        yield
        def mmA(e):
            for h in range(4):
                p, h2 = h // 2, h % 2
                rows = slice(64 * h2, 64 * h2 + 64)
                r = e.matmul(bank(0)[0:C, h * 128:h * 128 + C], lhsT=self.keTm.t[:, h, 0:C], rhs=self.qeT.t[:, p, 0:C], start=True, stop=True)
            return r
        yield
        S.op("pe", mmA, [self.keTm, self.qeT], [pb[0]])
        yield
        S.op("dve", lambda e: e.tensor_tensor(out=self.PTg.t[0:C, :, 0:C], in0=bank(0).rearrange("p (h c) -> p h c", h=4)[0:C, :, 0:C],
                                              in1=_bc(cm(iM01, C, C), 1, [C, 4, C]), op=ALU.mult), [pb[0], self.cmat], [self.PTg])
        yield
        S.op("pe", lambda e: e.matmul(bank(1)[0:C, 0:256], lhsT=cm(iSU, C, C), rhs=lt.t[0:C, :], start=True, stop=True), [lt, self.cmat], [pb[1]])
        yield
        S.op("act", lambda e: e.activation(out=self.kd.t[0:C, :], in_=bank(1)[0:C, 0:256], func=AF.Exp, scale=-1.0 / 16.0), [pb[1]], [self.kd])
        yield
        S.op("pool", lambda e: e.tensor_tensor(out=self.kd.t[0:C, :], in0=self.kd.t[0:C, :], in1=self.tm0.t[0:C, 16:272], op=ALU.mult), [self.kd, self.tm0], [self.kd])

    def state_prompt(self, C):
        S, cm, pb, bank, big, bg = self.S, self.cm, self.pb, self.bank, self.big, self.bg
        sc, kq = self.sc, self.kq
        s_ = lambda a, b_: sc.t[0:C, a:b_]
        v3 = lambda i: bg(i)[0:C, 0:8 * C].rearrange("p (h c) -> p h c", h=8)
        Sg, Sl, U = self.Sg, self.Sl, self.U
        R = lambda h2: slice(64 * h2, 64 * h2 + 64)
        h3 = lambda ap: ap.rearrange("p (h d) -> p h d", h=8)
        S.op("pe", lambda e: e.matmul(bank(1)[:, 0:8], lhsT=cm(C_ONE, C, 128), rhs=s_(8, 16), start=True, stop=True), [sc, self.cmat], [pb[1]])
        S.op("act", lambda e: e.activation(out=self.glb.t[:, 0:8], in_=bank(1)[:, 0:8], func=AF.Exp), [pb[1]], [self.glb])

        def mm1(e):
            for h in range(8):
                p, h2 = h // 2, h % 2
                r = e.matmul(bank(6)[0:C, h * 64:(h + 1) * 64], lhsT=self.wkT.t[:, h, 0:C], rhs=Sg.t[:, p, :], start=True, stop=True)
            return r
        S.op("pe", mm1, [self.wkT, Sg], [pb[6]])
        def mg1(e):
            for h in range(4):
                p, h2 = h // 2, h % 2
                r = e.matmul(bank(2)[0:C, h * 128:(h + 1) * 128], lhsT=self.qeTm.t[:, h, 0:C], rhs=Sl.t[:, p, :], start=True, stop=True)
            return r
        S.op("pe", mg1, [self.qeTm, Sl], [pb[2]])
        def mg2(e):
            for h in range(4):
                r = e.matmul(bank(3)[0:C, h * 128:(h + 1) * 128], lhsT=self.PTg.t[0:C, h, 0:C], rhs=self.gv_tok.t[0:C, h * 128:(h + 1) * 128], start=True, stop=True)
            return r
        S.op("pe", mg2, [self.PTg, self.gv_tok], [pb[3]])
        def mg3(e):
            for h in range(4):
                p = h // 2
                r = e.matmul(bank(4)[:, h * 128:(h + 1) * 128], lhsT=self.kd.t[0:C, p * 128:(p + 1) * 128], rhs=self.gv_tok.t[0:C, h * 128:(h + 1) * 128], start=True, stop=True)
            return r
        S.op("pe", mg3, [self.kd, self.gv_tok], [pb[4]])
        S.op("dve", lambda e: e.tensor_tensor(out=U.t[0:C, :], in0=self.wv.t[0:C, :], in1=bank(6)[0:C, :], op=ALU.subtract), [self.wv, pb[6]], [U])

        def mm2(e):
            for h in range(8):
                p, h2 = h // 2, h % 2
                r = e.matmul(bank(7)[0:C, h * 64:(h + 1) * 64], lhsT=self.QTm.t[:, h, 0:C], rhs=Sg.t[:, p, :], start=True, stop=True)
            return r
        S.op("pe", mm2, [self.QTm, Sg], [pb[7]])

        def mm3(e):
            for h in range(8):
                r = e.matmul(bank(0)[0:C, h * 64:(h + 1) * 64], lhsT=v3(0)[:, h, :], rhs=U.t[0:C, h * 64:(h + 1) * 64], start=True, stop=True)
            return r
        S.op("pe", mm3, [big[0], U], [pb[0]])
        S.op("act", lambda e: e.activation(out=self.ogla.t[0:C, :], in_=bank(2)[0:C, :], func=AF.Copy), [pb[2]], [self.ogla])
        S.op("dve", lambda e: e.tensor_tensor(out=self.ogla.t[0:C, :], in0=self.ogla.t[0:C, :], in1=bank(3)[0:C, :], op=ALU.add), [self.ogla, pb[3]], [self.ogla])
        S.op("dve", lambda e: e.tensor_tensor(out=h3(self.otmp.t[0:C, :]), in0=h3(bank(7)[0:C, :]), in1=_bc(s_(40, 48), 2, [C, 8, 64]), op=ALU.mult),
             [pb[7], sc], [self.otmp])
        S.op("dve", lambda e: e.tensor_tensor(out=self.ogdn.t[0:C, :], in0=self.otmp.t[0:C, :], in1=bank(0)[0:C, :], op=ALU.add), [self.otmp, pb[0]], [self.ogdn])

        def mm4(e):
            for h in range(8):
                p = h // 2
                r = e.matmul(bank(1)[:, h * 64:(h + 1) * 64], lhsT=self.kdec.t[0:C, p * 128:(p + 1) * 128], rhs=U.t[0:C, h * 64:(h + 1) * 64], start=True, stop=True)
            return r
        S.op("pe", mm4, [self.kdec, U, self.glb], [pb[1]])
        for h2 in range(2):
            glv = self.glb.t[R(h2), 0:8].rearrange("p (a two) -> p a two", two=2)[:, :, h2]
            psv = bank(1).rearrange("p (a two v) -> p a two v", two=2, v=64)[R(h2), :, h2, :]
            S.op("pool", lambda e, h2=h2, glv=glv: e.tensor_tensor(out=Sg.t[R(h2), :, :], in0=Sg.t[R(h2), :, :], in1=_bc(glv, 2, [64, 4, 64]), op=ALU.mult),
                 [Sg, self.glb], [Sg])
            S.op("dve", lambda e, h2=h2, psv=psv: e.tensor_tensor(out=Sg.t[R(h2), :, :], in0=Sg.t[R(h2), :, :], in1=psv, op=ALU.add), [Sg, pb[1]], [Sg])


        for h in range(4):
            p, h2 = h // 2, h % 2
            S.op("dve", lambda e, p=p, h2=h2, h=h: e.scalar_tensor_tensor(
                out=Sl.t[R(h2), p, :], in0=Sl.t[R(h2), p, :], scalar=self.ebT.t[R(h2), p, C - 1:C], in1=bank(4)[R(h2), h * 128:(h + 1) * 128],
                op0=ALU.mult, op1=ALU.add), [Sl, self.ebT, pb[4]], [Sl])

    def state_sample(self, C):
        S, cm, pb, bank, big, bg = self.S, self.cm, self.pb, self.bank, self.big, self.bg
        sc, kq = self.sc, self.kq
        s_ = lambda a, b_: sc.t[0:C, a:b_]
        v3 = lambda i: bg(i)[0:C, 0:8 * C].rearrange("p (h c) -> p h c", h=8)
        U = self.U
        R = lambda h2: slice(64 * h2, 64 * h2 + 64)
        h3 = lambda ap: ap.rearrange("p (h d) -> p h d", h=8)
        bm = cm(C_SBM, C, NS)
        gsel = bg(5)[0:C, 0:128].rearrange("p (h s) -> p h s", h=8)
        S.op("pool", lambda e: e.tensor_tensor(out=gsel, in0=_bc(s_(8, 16), 2, [C, 8, NS]), in1=_bc(bm, 1, [C, 8, NS]), op=ALU.mult), [sc, self.cmat], [big[5]])
        S.op("pe", lambda e: e.matmul(bank(1)[:, 0:128], lhsT=cm(C_ONE), rhs=bg(5)[0:C, 0:128], start=True, stop=True), [big[5], self.cmat], [pb[1]])
        S.op("act", lambda e: e.activation(out=self.glb.t[:, 0:128], in_=bank(1)[:, 0:128], func=AF.Exp), [pb[1]], [self.glb])
        glbs = self.glb.t[:, 0:128].rearrange("p (h s) -> p h s", h=8)
        S0 = bg(4).rearrange("p (s v) -> p s v", s=NS)
        tmp = bg(5)[0:C, :].rearrange("p (s v) -> p s v", s=NS)
        tmpT = bg(5)[0:C, :].rearrange("p (s v) -> p v s", s=NS)
        Ub = bg(1)[0:C, :].rearrange("p (s v) -> p s v", s=NS)
        ps67 = self.pst.t[:, 6:8, :].rearrange("p b (s v) -> p (b s) v", v=64)
        wks, o1s = self.otmp, self.ogdn
        S0v = lambda ap: ap.rearrange("p (s v) -> p s v", s=NS)
        S0s = [(S0v(bg(4)), big[4]), (S0v(bg(2)), big[2])]
        scr = [dict(tmp=S0v(bg(5)[0:C, :]), tmpT=bg(5)[0:C, :].rearrange("p (s v) -> p v s", s=NS), tmpB=big[5],
                    Ub=S0v(bg(1)[0:C, :]), Ubf=bg(1), UbB=big[1], bk=6),
               dict(tmp=S0v(self.Pc.t[0:C, :]), tmpT=self.Pc.t[0:C, :].rearrange("p (s v) -> p v s", s=NS), tmpB=self.Pc,
                    Ub=S0v(self.PTc.t[0:C, :]), Ubf=self.PTc.t, UbB=self.PTc, bk=4)]

        def load(p):
            S0, S0t = S0s[p % 2]
            for h2 in range(2):
                S.dma(S0[R(h2), :, :], self.sgdn[:, 2 * p + h2, :, :].rearrange("s d v -> d s v"), [], [S0t])

        def head_chain(p, h2):
            h = 2 * p + h2
            S0, S0t = S0s[p % 2]
            q = scr[h2]
            bk = q["bk"]
            psx = self.pst.t[:, bk:bk + 2, :].rearrange("p b (s v) -> p (b s) v", v=64)
            for (lhs, lhsb, dst) in ((self.wkT.t[:, h, 0:C], self.wkT, wks), (self.QTm.t[:, h, 0:C], self.QTm, o1s)):
                def mma(e, lhs=lhs):
                    for i in range(2):
                        r = e.matmul(bank(bk + i)[0:C, :], lhsT=lhs, rhs=S0[:, i * 8:(i + 1) * 8, :].rearrange("p s v -> p (s v)"), start=True, stop=True)
                    return r
                S.op("pe", mma, [lhsb, S0t], [pb[bk], pb[bk + 1]])
                yield
                S.op("dve", lambda e: e.tensor_tensor(out=q["tmp"], in0=psx[0:C], in1=_bc(bm, 2, [C, NS, 64]), op=ALU.mult), [pb[bk], pb[bk + 1], self.cmat], [q["tmpB"]])
                yield
                S.op("dve", lambda e, dst=dst: e.tensor_reduce(out=dst.t[0:C, h * 64:(h + 1) * 64], in_=q["tmpT"], op=ALU.add, axis=AX.X), [q["tmpB"]], [dst])
                yield
            cs = slice(h * 64, (h + 1) * 64)
            S.op("dve", lambda e: e.tensor_tensor(out=U.t[0:C, cs], in0=self.wv.t[0:C, cs], in1=wks.t[0:C, cs], op=ALU.subtract), [self.wv, wks], [U])
            yield
            S.op("pe", lambda e: e.matmul(bank(0)[0:C, cs], lhsT=v3(0)[:, h, :], rhs=U.t[0:C, cs], start=True, stop=True), [big[0], U], [pb[0]])
            yield
            S.op("pool", lambda e: e.tensor_tensor(out=q["Ub"], in0=_bc(U.t[0:C, cs], 1, [C, NS, 64]), in1=_bc(bm, 2, [C, NS, 64]), op=ALU.mult),
                 [U, self.cmat], [q["UbB"]])
            yield

            def mmb(e):
                for i in range(2):
                    r = e.matmul(bank(bk + i)[:, :], lhsT=self.kdec.t[0:C, p * 128:(p + 1) * 128], rhs=q["Ubf"][0:C, i * 512:(i + 1) * 512], start=True, stop=True)
                return r
            S.op("pe", mmb, [self.kdec, q["UbB"]], [pb[bk], pb[bk + 1]])
            yield
            S.op("pool", lambda e: e.tensor_tensor(out=S0[R(h2), :, :], in0=S0[R(h2), :, :], in1=_bc(glbs[R(h2), h, :], 2, [64, NS, 64]), op=ALU.mult),
                 [S0t, self.glb], [S0t])
            yield
            S.op("dve", lambda e: e.tensor_tensor(out=S0[R(h2), :, :], in0=S0[R(h2), :, :], in1=psx[R(h2)], op=ALU.add), [S0t, pb[bk], pb[bk + 1]], [S0t])
            yield
            S.dma(self.gdn_s[:, h, :, :].rearrange("s d v -> d s v"), S0[R(h2), :, :], [S0t], [], is_out=True)

        load(0)
        for p in range(4):
            if p + 1 < 4:
                load(p + 1)
            gens = [head_chain(p, 0), head_chain(p, 1)]
            while gens:
                for g_ in list(gens):
                    if next(g_, "done") == "done":
                        gens.remove(g_)
        S.op("dve", lambda e: e.tensor_tensor(out=h3(o1s.t[0:C, :]), in0=h3(o1s.t[0:C, :]), in1=_bc(s_(40, 48), 2, [C, 8, 64]), op=ALU.mult), [o1s, sc], [o1s])
        S.op("dve", lambda e: e.tensor_tensor(out=self.ogdn.t[0:C, :], in0=o1s.t[0:C, :], in1=bank(0)[0:C, :], op=ALU.add), [o1s, pb[0]], [self.ogdn])
        S0g = bg(0, 2).rearrange("p (s v) -> p s v", s=NS)
        tg = bg(2, 2)[0:C, :].rearrange("p (s v) -> p s v", s=NS)
        tgT = bg(2, 2)[0:C, :].rearrange("p (s v) -> p v s", s=NS)
        Vb = bg(4, 2)[0:C, :].rearrange("p (s v) -> p s v", s=NS)
        ps25 = self.pst.t[:, 2:6, :].rearrange("p b (s v) -> p (b s) v", v=128)
        S0gT, tgB, VbB = [big[0], big[1]], [big[2], big[3]], [big[4], big[5]]
        pbs = [pb[2], pb[3], pb[4], pb[5]]
        for p in range(2):
            for h2 in range(2):
                S.dma(S0g[R(h2), :, :], self.sgla[:, 2 * p + h2, :, :].rearrange("s d v -> d s v"), [], S0gT)
            for h2 in range(2):
                h = 2 * p + h2
                cs = slice(h * 128, (h + 1) * 128)

                def mmq(e, p=p, h2=h2, h=h):
                    for i in range(4):
                        r = e.matmul(bank(2 + i)[0:C, :], lhsT=self.qeTm.t[:, h, 0:C], rhs=S0g[:, i * 4:(i + 1) * 4, :].rearrange("p s v -> p (s v)"), start=True, stop=True)
                    return r
                S.op("pe", mmq, [self.qeTm] + S0gT, pbs)
                S.op("dve", lambda e: e.tensor_tensor(out=tg, in0=ps25[0:C], in1=_bc(bm, 2, [C, NS, 128]), op=ALU.mult), pbs + [self.cmat], tgB)
                S.op("dve", lambda e, cs=cs: e.tensor_reduce(out=self.ogla.t[0:C, cs], in_=tgT, op=ALU.add, axis=AX.X), tgB, [self.ogla])
                S.op("pool", lambda e, cs=cs: e.tensor_tensor(out=Vb, in0=_bc(self.gv_tok.t[0:C, cs], 1, [C, NS, 128]), in1=_bc(bm, 2, [C, NS, 128]), op=ALU.mult),
                     [self.gv_tok, self.cmat], VbB)

                def mmv(e, p=p):
                    for i in range(4):
                        r = e.matmul(bank(2 + i)[:, :], lhsT=self.kd.t[0:C, p * 128:(p + 1) * 128], rhs=bg(4, 2)[0:C, i * 512:(i + 1) * 512], start=True, stop=True)
                    return r
                S.op("pe", mmv, [self.kd] + VbB, pbs)
                ebl = self.ebT.t[R(h2), p, :].rearrange("p (s l) -> p s l", l=LS)[:, :, LS - 1]
                S.op("pool", lambda e, h2=h2, ebl=ebl: e.tensor_tensor(out=S0g[R(h2), :, :], in0=S0g[R(h2), :, :], in1=_bc(ebl, 2, [64, NS, 128]), op=ALU.mult),
                     S0gT + [self.ebT], S0gT)
                S.op("dve", lambda e, h2=h2: e.tensor_tensor(out=S0g[R(h2), :, :], in0=S0g[R(h2), :, :], in1=ps25[R(h2)], op=ALU.add), S0gT + pbs, S0gT)
                S.dma(self.gla_s[:, h, :, :].rearrange("s d v -> d s v"), S0g[R(h2), :, :], S0gT, [], is_out=True)

        def mg2(e):
            for h in range(4):
                r = e.matmul(bank(6)[0:C, h * 128:(h + 1) * 128], lhsT=self.PTg.t[0:C, h, 0:C], rhs=self.gv_tok.t[0:C, h * 128:(h + 1) * 128], start=True, stop=True)
            return r
        S.op("pe", mg2, [self.PTg, self.gv_tok], [pb[6]])
        S.op("dve", lambda e: e.tensor_tensor(out=self.ogla.t[0:C, :], in0=self.ogla.t[0:C, :], in1=bank(6)[0:C, :], op=ALU.add), [self.ogla, pb[6]], [self.ogla])

    def post_mix(self, e0, C, smp):
        S, cm, pb, bank = self.S, self.cm, self.pb, self.bank
        sc, mix, xh = self.sc, self.mix, self.xh
        s_ = lambda a, b_: sc.t[0:C, a:b_]
        if not mix.parts:
            mix.parts = [S.alias("mixA", mix), S.alias("mixB", mix)]
            self.scn = [S.alias("scA", sc), S.alias("scB", sc)]

        def norm_chain(o, nh, dv, col0, gcol, sg, sco, sqb, mixp, scp):
            v = lambda ap: ap.rearrange("p (h d) -> p h d", h=nh)
            yield
            S.op("dve", lambda e, o=o: e.tensor_tensor(out=sqb.t[0:C, :], in0=o.t[0:C, :], in1=o.t[0:C, :], op=ALU.mult), [o], [sqb])
            yield
            S.op("dve", lambda e, v=v, sco=sco, nh=nh: e.tensor_reduce(out=s_(sco, sco + nh), in_=v(sqb.t[0:C, :]), op=ALU.add, axis=AX.X), [sqb], [scp])
            yield
            S.op("act", lambda e, sco=sco, nh=nh, dv=dv: e.activation(out=s_(sco, sco + nh), in_=s_(sco, sco + nh), func=AF.Ln,
                                                                      bias=self.epsc.t[0:C, 1:2], scale=1.0 / dv), [scp, self.epsc], [scp])
            yield
            S.op("act", lambda e, sco=sco, nh=nh: e.activation(out=s_(sco, sco + nh), in_=s_(sco, sco + nh), func=AF.Exp, scale=-0.5), [scp], [scp])
            mv_ = v(mix.t[0:C, col0:col0 + 512])
            yield
            S.op("dve", lambda e, o=o, v=v, mv_=mv_, sco=sco, nh=nh, dv=dv: e.tensor_tensor(out=mv_, in0=v(o.t[0:C, :]), in1=_bc(s_(sco, sco + nh), 2, [C, nh, dv]), op=ALU.mult),
                 [o, scp], [mixp])
            yield
            S.op("pool", lambda e, mv_=mv_, gcol=gcol, nh=nh, dv=dv: e.tensor_tensor(out=mv_, in0=mv_, in1=_bc(self.pvec.t[0:C, gcol:gcol + dv], 1, [C, nh, dv]), op=ALU.mult),
                 [mixp, self.pvec], [mixp])
            yield
            S.op("dve", lambda e, col0=col0, sg=sg: e.tensor_tensor(out=mix.t[0:C, col0:col0 + 512], in0=mix.t[0:C, col0:col0 + 512], in1=sg.t[0:C, :], op=ALU.mult),
                 [mixp, sg], [mixp])
        gens = [norm_chain(self.ogdn, 8, 64, 0, 16, self.sg_gdn, 72, self.otmp, mix.parts[0], self.scn[0]),
                norm_chain(self.ogla, 4, 128, 512, 80, self.sg_gla, 80, self.kdec, mix.parts[1], self.scn[1])]
        while gens:
            for g_ in list(gens):
                if next(g_, 'done') == 'done':
                    gens.remove(g_)

        for half in range(2):
            def tr(e, half=half):
                for j in range(4):
                    kc = half * 4 + j
                    r = e.transpose(bank(half)[:, j * 128:j * 128 + C], mix.t[0:C, kc * 128:(kc + 1) * 128], cm(C_ID, C, C))
                return r
            S.op("pe", tr, [mix, self.cmat], [pb[half]])
            S.op("act", lambda e, half=half: e.activation(out=self.mixT.t[:, half * 4:half * 4 + 4, 0:C],
                                                          in_=bank(half).rearrange("p (j c) -> p j c", j=4)[:, :, 0:C], func=AF.Copy), [pb[half]], [self.mixT])
        for half in range(2):
            def mmo(e, half=half):
                for kc in range(8):
                    r = e.matmul(bank(6 + half)[0:C, :], lhsT=self.mixT.t[:, kc, 0:C], rhs=self.w_out.t[:, kc, half * 512:(half + 1) * 512],
                                 start=(kc == 0), stop=(kc == 7))
                return r
            S.op("pe", mmo, [self.mixT, self.w_out], [pb[6 + half]])
            S.op("dve", lambda e, half=half: e.scalar_tensor_tensor(out=xh.t[0:C, half * 512:(half + 1) * 512], in0=xh.t[0:C, half * 512:(half + 1) * 512],
                                                                     scalar=ALPHA, in1=bank(6 + half)[0:C, :], op0=ALU.mult, op1=ALU.add), [xh, pb[6 + half]], [xh])
        self.layer_norm(xh, C, self.lnbc.t[:, 2, :], self.lnbc.t[:, 3, :], self.epsc.t[0:C, 0:1], "ln1")
        row0 = TP if smp else e0
        S.dma(self.h1s[row0:row0 + C, :], xh.t[0:C, :], [xh], [self.h1scr])

    def conv_state_out(self, C, smp):
        S, cm, pb, bank = self.S, self.cm, self.pb, self.bank
        if smp:
            n = NS * 3
            cst = self.bg(3)[:, 0:12 * n].rearrange("p (a c) -> p a c", a=12)
            S.op("pool", lambda e: e.tensor_copy(out=cst.rearrange("p a (s l) -> p a s l", l=3),
                                                 in_=self.qkvx.t[:, :, 0:NS * 11].rearrange("p a (s l) -> p a s l", l=11)[:, :, :, 8:11]),
                 [self.qkvx], [self.big[3]])
            src = lambda cc: cst[:, cc, :]
            dst = self.gconv_s
        else:
            n = 3
            src = lambda cc: self.qkvx.t[:, cc, 0:3]
            dst = self.gconv_p
        stage = self.bg(1, 2)[0:n, 0:1536]
        for g in range(3):
            def tr(e, g=g):
                for j in range(4):
                    r = e.transpose(bank(2 + g)[0:n, j * 128:(j + 1) * 128], src(g * 4 + j), cm(C_ID))
                return r
            S.op("pe", tr, [self.qkvx, self.big[3], self.cmat], [pb[2 + g]])
            S.op("act", lambda e, g=g: e.activation(out=stage[:, g * 512:(g + 1) * 512], in_=bank(2 + g)[0:n, :], func=AF.Copy), [pb[2 + g]], [self.big[1], self.big[2]])
        S.dma(dst, stage, [self.big[1], self.big[2]], [], is_out=True)

    def stage1(self):
        S = self.S
        self.stage1_alloc()
        self.stage1_setup()
        R = lambda h2: slice(64 * h2, 64 * h2 + 64)
        plist = []
        e0 = 0
        while e0 < TP:
            C = min(128, TP - e0)
            skip = (DBG.get("maxchunks") is not None and e0 // 128 >= DBG["maxchunks"] and C == 128) or (DBG.get("no16") and C == 16)
            if not skip:
                plist.append((e0, C, "p"))
            e0 += C
        plist = [(a, b, c, self.xhs[i % 2]) for i, (a, b, c) in enumerate(plist)]
        self.sample_item = (0, NS * LS, "s", self.xhs[len(plist) % 2])
        self.front0(*plist[0])
        for i, it in enumerate(plist):
            self.chunk(*it, plist[i + 1] if i + 1 < len(plist) else None)
        if DBG.get("stop_early"):
            return
        if DBG.get("stop_after_prompt"):
            return
        for h2 in range(2):
            S.dma(self.gdn_p.rearrange("(a two) d v -> two d a v", two=2)[h2], self.Sg.t[R(h2), :, :], [self.Sg], [], is_out=True)
            S.dma(self.gla_p.rearrange("(a two) d v -> two d a v", two=2)[h2], self.Sl.t[R(h2), :, :], [self.Sl], [], is_out=True)
        if not DBG.get("no_cso"):
            self.conv_state_out(16, False)
        if DBG.get("stop_after_outputs"):
            return
        stc = self.bg(3, 2)[0:NS * 3, 0:1536]
        S.dma(stc, self.sgconv, [], [self.big[3], self.big[4]])
        qs = self.qkvx.t[:, :, 0:NS * 11].rearrange("p a (s l) -> p a s l", l=11)
        for g in range(3):
            def tr(e, g=g):
                for j in range(4):
                    cc = g * 4 + j
                    r = e.transpose(self.bank(2 + g)[:, j * 128:j * 128 + NS * 3], stc[:, cc * 128:(cc + 1) * 128], self.cm(C_ID, NS * 3, NS * 3))
                return r
            S.op("pe", tr, [self.big[3], self.big[4], self.cmat], [self.pb[2 + g]])
            S.op("act", lambda e, g=g: e.activation(out=qs[:, g * 4:(g + 1) * 4, :, 0:3],
                                                    in_=self.bank(2 + g).rearrange("p (j c) -> p j c", j=4)[:, :, 0:NS * 3].rearrange("p j (s l) -> p j s l", l=3),
                                                    func=AF.Copy), [self.pb[2 + g]], [self.qkvx])
        self.front0(*self.sample_item)
        self.chunk(*self.sample_item, None)
        self.conv_state_out(NS * LS, True)

    def stage2(self):
        S, cm, pb, bank = self.S, self.cm, self.pb, self.bank
        self.lnbc = S.sb("lnbc2", [128, 2, D], F32)
        w_up = S.sb("w_up_sb", [128, 8, 2 * DFF], BF16)
        w_dn = S.sb("w_dn", [128, 22, D], BF16)
        cwf = S.sb("cwf_sb", [128, NFF, 4], F32)
        h1t = [S.sb(f"h1t{i}", [128, D], F32) for i in range(2)]
        h1T = S.sb("h1T", [128, 8, 512], BF16)
        ue = [S.sb(f"ue{i}", [128, 160], F32) for i in range(2)]
        accs = [[S.sb(f"acc{w}{i}", [128, 512], F32) for i in range(2)] for w in range(2)]
        sgt1 = S.sb("sgt", [128, 512], F32)
        sgt = [sgt1, sgt1]
        hc = S.sb("hc", [128, NFF, 4], F32)
        uh = [S.sb(f"uh{i}", [128, NFF, 2], F32) for i in range(2)]
        actT = S.sb("actT", [128, 22, 512], BF16)
        usave = S.sb("usave", [128, NFF, 32], F32)
        st32 = S.sb("st32", [128, 512], F32)
        for i in range(2):
            S.dma(self.lnbc.t[:, i, :], self.lnv_d[4 + i:5 + i, :].partition_broadcast(128), [], [self.lnbc])
        S.dma(cwf.t[:], self.cwf_d, [], [cwf])
        wu_ = self.w_up_d.rearrange("(kc p) n -> p kc n", p=128)
        wd_ = self.w_down_d.rearrange("(j p) n -> p j n", p=128)
        w_up_tb = {}
        wdma = []
        for g in range(3):
            for which in range(2):
                c0 = which * DFF + g * 1024
                c1 = min(c0 + 1024, (which + 1) * DFF)
                tb = S.alias(f"w_up_{which}_{g}", w_up)
                for gg in (2 * g, 2 * g + 1):
                    w_up_tb[(which, gg)] = tb
                for kc in range(0, 8, 4):
                    wdma.append((lambda kc=kc, c0=c0, c1=c1, tb=tb: S.dma(w_up.t[:, kc:kc + 4, c0:c1], wu_[:, kc:kc + 4, c0:c1], [], [tb], cast=True)))
        w_dn_tb = {}
        wddma = []
        for j in range(0, 22, 2):
            tb = S.alias(f"w_dn_{j}", w_dn)
            w_dn_tb[j] = tb
            w_dn_tb[j + 1] = tb
            wddma.append((lambda j=j, tb=tb: S.dma(w_dn.t[:, j:j + 2, :], wd_[:, j:j + 2, :], [], [tb], cast=True)))
        for _ in range(4):
            wdma.pop(0)()
        S.op("pool", lambda e: e.memset(uh[0].t[:], 0.0), [], [uh[0]])

        def conv_out(n, dst, src):
            for g in range(11):
                def tr(e, g=g):
                    for j in range(4):
                        r = e.transpose(bank(g % 4)[0:n, j * 128:(j + 1) * 128], src.t[:, g * 4 + j, 0:n], cm(C_ID))
                    return r
                S.op("pe", tr, [src, self.cmat], [pb[g % 4]])
                S.op("act", lambda e, g=g: e.activation(out=st32.t[0:n, :], in_=bank(g % 4)[0:n, :], func=AF.Copy), [pb[g % 4]], [st32])
                S.dma(dst[:, g * 512:(g + 1) * 512], st32.t[0:n, :], [st32], [], is_out=True)

        tiles = []
        e0 = 0
        while e0 < TP:
            W = min(512, TP - e0)
            tiles.append((e0, W, False))
            e0 += W
        tiles.append((TP, NS * LS, True))
        upb = 0
        for ti, (r0, W, smp) in enumerate(tiles):
            G, L = (NS, LS) if smp else (1, W)
            uprev, ucur = uh[ti % 2], uh[(ti + 1) % 2]
            if not smp and not DBG.get("nohc"):
                S.op("pool", lambda e, uprev=uprev: e.tensor_tensor(out=hc.t[:, :, 0:2], in0=uprev.t[:, :, :], in1=cwf.t[:, :, 0:1].to_broadcast([128, NFF, 2]), op=ALU.mult),
                     [uprev, cwf], [hc])
                S.op("pool", lambda e, uprev=uprev: e.tensor_tensor(out=hc.t[:, :, 2:3], in0=uprev.t[:, :, 1:2], in1=cwf.t[:, :, 1:2], op=ALU.mult),
                     [uprev, cwf, hc], [hc])
            if smp:
                for g in range(11):
                    S.dma(st32.t[0:NS * 2, :], self.sfconv[:, g * 512:(g + 1) * 512], [], [st32])

                    def tr(e, g=g):
                        for j in range(4):
                            r = e.transpose(bank(4 + g % 2)[:, j * 32:(j + 1) * 32], st32.t[0:NS * 2, j * 128:(j + 1) * 128], cm(C_ID, NS * 2, NS * 2))
                        return r
                    S.op("pe", tr, [st32, self.cmat], [pb[4 + g % 2]])
                    S.op("act", lambda e, g=g: e.activation(out=usave.t[:, g * 4:(g + 1) * 4, :],
                                                            in_=bank(4 + g % 2)[:, 0:128].rearrange("p (j c) -> p j c", j=4), func=AF.Copy), [pb[4 + g % 2]], [usave])
            nsub = (W + 127) // 128
            for j in range(nsub):
                C = min(128, W - j * 128)
                ht = h1t[j % 2]
                S.dma(ht.t[0:C, :], self.h1s[r0 + j * 128:r0 + j * 128 + C, :], [self.h1scr], [ht])
                for half in range(2):
                    def tr(e, half=half, ht=ht, C=C):
                        for q in range(4):
                            r = e.transpose(bank(6 + half)[:, q * 128:q * 128 + C], ht.t[0:C, (half * 4 + q) * 128:(half * 4 + q + 1) * 128], cm(C_ID, C, C))
                        return r
                    S.op("pe", tr, [ht, self.cmat], [pb[6 + half]])
                    S.op("act", lambda e, half=half, j=j, C=C: e.activation(out=h1T.t[:, half * 4:half * 4 + 4, j * 128:j * 128 + C],
                                                                            in_=bank(6 + half).rearrange("p (q c) -> p q c", q=4)[:, :, 0:C], func=AF.Copy),
                         [pb[6 + half]], [h1T])
            pend = None
            for jg in range(22):
                par = jg % 2
                if jg % 8 == 1:
                    for _ in range(4):
                        if wdma:
                            wdma.pop(0)()
                if wddma and jg % 2 == 0:
                    wddma.pop(0)()
                if pend is not None:
                    pend()
                for which in (0, 1):
                    acc = accs[which][par]
                    cc = jg + 22 * which
                    b = upb % 4
                    upb += 1
                    wtb = w_up_tb[(which, jg // 4)]

                    def mmu(e, cc=cc, b=b):
                        for kc in range(8):
                            r = e.matmul(bank(b)[:, 0:W], lhsT=w_up.t[:, kc, cc * 128:(cc + 1) * 128], rhs=h1T.t[:, kc, 0:W], start=(kc == 0), stop=(kc == 7))
                        return r
                    S.op("pe", mmu, [wtb, h1T], [pb[b]])
                    if smp:
                        u = ue[cc % 2]
                        uv = u.t[:, 0:G * (L + 2)].rearrange("p (g l) -> p g l", g=G)
                        S.op("act", lambda e, b=b, uv=uv: e.activation(out=uv[:, :, 2:2 + L], in_=bank(b)[:, 0:W].rearrange("p (g l) -> p g l", g=G), func=AF.Copy),
                             [pb[b]], [u])
                        hsrc = usave.t[:, cc, :].rearrange("p (s i) -> p s i", i=2)
                        S.op("pool", lambda e, uv=uv, hsrc=hsrc: e.tensor_copy(out=uv[:, :, 0:2], in_=hsrc), [usave, u], [u])
                        av = acc.t[:, 0:W].rearrange("p (g l) -> p g l", g=G)
                        S.op("act", lambda e, uv=uv, av=av, cc=cc: e.activation(out=av, in_=uv[:, :, 2:2 + L], func=AF.Identity,
                                                                                scale=cwf.t[:, cc, 2:3], bias=cwf.t[:, cc, 3:4]), [u, cwf], [acc])
                        S.op("dve", lambda e, uv=uv, av=av, cc=cc: e.scalar_tensor_tensor(out=av, in0=uv[:, :, 1:1 + L], scalar=cwf.t[:, cc, 1:2], in1=av,
                                                                                           op0=ALU.mult, op1=ALU.add), [u, cwf, acc], [acc])
                        S.op("dve", lambda e, uv=uv, av=av, cc=cc: e.scalar_tensor_tensor(out=av, in0=uv[:, :, 0:L], scalar=cwf.t[:, cc, 0:1], in1=av,
                                                                                           op0=ALU.mult, op1=ALU.add), [u, cwf, acc], [acc])
                        hdst = usave.t[:, cc, :].rearrange("p (s i) -> p s i", i=2)
                        S.op("pool", lambda e, uv=uv, hdst=hdst: e.tensor_copy(out=hdst, in_=uv[:, :, L:L + 2]), [u, usave], [usave])
                    else:
                        S.op("act", lambda e, cc=cc, b=b: e.activation(out=acc.t[:, 0:W], in_=bank(b)[:, 0:W], func=AF.Identity,
                                                                       scale=cwf.t[:, cc, 2:3], bias=cwf.t[:, cc, 3:4]), [pb[b], cwf], [acc])
                        if not DBG.get("noact2"):
                            S.op("dve", lambda e, cc=cc, b=b, ucur=ucur: e.tensor_copy(out=ucur.t[:, cc, 0:2], in_=bank(b)[:, W - 2:W]), [pb[b], acc], [ucur])
                        if not DBG.get("nostt"):
                            S.op("dve", lambda e, cc=cc, b=b, acc=acc: e.scalar_tensor_tensor(out=acc.t[:, 1:W], in0=bank(b)[:, 0:W - 1], scalar=cwf.t[:, cc, 1:2], in1=acc.t[:, 1:W],
                                                                                     op0=ALU.mult, op1=ALU.add), [pb[b], cwf, acc], [acc])
                            S.op("dve", lambda e, cc=cc, b=b, acc=acc: e.scalar_tensor_tensor(out=acc.t[:, 2:W], in0=bank(b)[:, 0:W - 2], scalar=cwf.t[:, cc, 0:1], in1=acc.t[:, 2:W],
                                                                                     op0=ALU.mult, op1=ALU.add), [pb[b], cwf, acc], [acc])
                        if not DBG.get("nopool"):
                            S.op("pool", lambda e, cc=cc, acc=acc: e.tensor_tensor(out=acc.t[:, 0:2], in0=acc.t[:, 0:2], in1=hc.t[:, cc, 0:2], op=ALU.add), [acc, hc], [acc])
                            S.op("pool", lambda e, cc=cc, acc=acc: e.tensor_tensor(out=acc.t[:, 0:1], in0=acc.t[:, 0:1], in1=hc.t[:, cc, 2:3], op=ALU.add), [acc, hc], [acc])
                def fin(jg=jg, par=par):
                    S.op("act", lambda e: e.activation(out=sgt[par].t[:, 0:W], in_=accs[0][par].t[:, 0:W], func=AF.Silu), [accs[0][par]], [sgt[par]])
                    S.op("pool", lambda e: e.tensor_tensor(out=actT.t[:, jg, 0:W], in0=sgt[par].t[:, 0:W], in1=accs[1][par].t[:, 0:W], op=ALU.mult),
                         [sgt[par], accs[1][par]], [actT])
                pend = fin
            pend()
            def reload(j):
                Cj = min(128, W - j * 128)
                S.dma(h1t[j % 2].t[0:Cj, :], self.h1s[r0 + j * 128:r0 + j * 128 + Cj, :], [self.h1scr], [h1t[j % 2]])
            reload(0)
            for j in range(nsub):
                C = min(128, W - j * 128)
                ht = h1t[j % 2]
                if j + 1 < nsub:
                    reload(j + 1)
                for half in range(2):
                    def mmd(e, half=half, j=j, C=C):
                        for jg in range(22):
                            r = e.matmul(bank(4 + 2 * (j % 2) + half)[0:C, :], lhsT=actT.t[:, jg, j * 128:j * 128 + C], rhs=w_dn.t[:, jg, half * 512:(half + 1) * 512],
                                         start=(jg == 0), stop=(jg == 21))
                        return r
                    S.op("pe", mmd, [actT] + [w_dn_tb[j_] for j_ in range(0, 22, 2)], [pb[4 + 2 * (j % 2) + half]])
                    S.op("dve", lambda e, half=half, ht=ht, C=C: e.scalar_tensor_tensor(out=ht.t[0:C, half * 512:(half + 1) * 512], in0=ht.t[0:C, half * 512:(half + 1) * 512],
                                                                                        scalar=ALPHA, in1=bank(4 + 2 * (j % 2) + half)[0:C, :], op0=ALU.mult, op1=ALU.add),
                         [ht, pb[4 + 2 * (j % 2) + half]], [ht])
                self.layer_norm(ht, C, self.lnbc.t[:, 0, :], self.lnbc.t[:, 1, :], self.epsc.t[0:C, 0:1], "ln2")
                if smp:
                    S.dma(self.y_s, ht.t[0:C, :], [ht], [], is_out=True)
                else:
                    e = r0 + j * 128
                    if e == 0:
                        S.dma(self.y_p[0:128 - NMETA, :], ht.t[NMETA:128, :], [ht], [], is_out=True)
                    else:
                        S.dma(self.y_p[e - NMETA:e - NMETA + C, :], ht.t[0:C, :], [ht], [], is_out=True)
            if (not smp) and r0 + W == TP:
                conv_out(2, self.fconv_p, ucur)
        conv_out(NS * 2, self.fconv_s, usave)

    def build(self):
        S = self.S
        with S:
            self.cmat = S.sb("cmat_sb", [128, NCM, 128], F32)
            self.epsc = S.sb("epsc", [128, 4], F32)
            self.ln_st = S.sb("ln_st", [128, 2, 6], F32)
            self.ln_mv = S.sb("ln_mv", [128, 4], F32)
            self.glb = S.sb("glb", [128, 128], F32)
            self.pst = S.ps("pst", [128, 8, 512], F32)
            self.pb = [S.alias(f"pb{i}", self.pst) for i in range(8)]
            self.h1scr = TB("h1scr", None)
            main_stack = S.stack
            S.stack = ExitStack()
            with S.stack:
                if not DBG.get("skip1"):
                    self.stage1()
                S.barrier()
            S.stack = ExitStack()
            with S.stack:
                if not DBG.get("skip2"):
                    self.stage2()
                S.finish()
            S.stack = main_stack
        return self.nc


_PROG = None


def _program():
    global _PROG
    if _PROG is None:
        _PROG = K().build()
    return _PROG


def kernel(x_prompt, x_sample, state_gdn, state_gdn_conv, state_gla, state_ffn_conv, meta_tokens,
           ln_in_g, ln_in_b, w_in, gdn_conv_w, gdn_A_log, gdn_dt_bias, gdn_norm_g, gla_wgk2,
           gla_bgk, gla_norm_g, w_out, ln1_g, ln1_b, w_up, ffn_conv_w, ffn_conv_b, w_down,
           ln2_g, ln2_b):
    f = lambda a: np.ascontiguousarray(np.asarray(a, dtype=np.float32))
    x_prompt, x_sample = f(x_prompt), f(x_sample)
    w_in0 = f(w_in)[0]
    fm_cols = np.r_[0:1536, 2064:2320, 2320:2576, 3088:3104]
    tm_cols = np.r_[1536:1552, 2320:2576, 1552:2064, 2576:3088, 3104:3616]
    w_in_r = np.ascontiguousarray(w_in0[:, np.r_[fm_cols, tm_cols]])
    lnv = np.stack([f(ln_in_g), f(ln_in_b), f(ln1_g)[0], f(ln1_b)[0], f(ln2_g)[0], f(ln2_b)[0]])
    cwg = np.ascontiguousarray(f(gdn_conv_w)[0].T.reshape(12, 128, 4).transpose(1, 0, 2))
    cwf4 = np.concatenate([f(ffn_conv_w)[0], f(ffn_conv_b)], axis=0)
    cwf = np.ascontiguousarray(cwf4.T.reshape(NFF, 128, 4).transpose(1, 0, 2))
    pvec = np.concatenate([f(gdn_A_log)[0], f(gdn_dt_bias)[0], f(gdn_norm_g)[0], f(gla_norm_g)[0], f(gla_bgk)[0]])[None, :]
    shared = dict(meta=f(meta_tokens), cmat=_const_mats(), w_in_r=w_in_r, w_out=f(w_out)[0], w_up=f(w_up)[0], w_down=f(w_down)[0],
                  lnv=np.ascontiguousarray(lnv), cwg=cwg, cwf=cwf, pvec=np.ascontiguousarray(pvec), wgk2=f(gla_wgk2)[0])
    sg, sgc, sl, sfc = f(state_gdn)[0], f(state_gdn_conv)[0], f(state_gla)[0], f(state_ffn_conv)[0]
    in_maps = []
    for c in range(8):
        sl_ = slice(c * NS, (c + 1) * NS)
        m = dict(shared)
        m.update(xp=x_prompt[c], xs=np.ascontiguousarray(x_sample[sl_].reshape(NS * LS, D)), sgdn=sg[sl_],
                 sgconv=np.ascontiguousarray(sgc[sl_].reshape(NS * 3, 1536)), sgla=sl[sl_],
                 sfconv=np.ascontiguousarray(sfc[sl_].reshape(NS * 2, 2 * DFF)))
        in_maps.append(m)
    ncr = DBG.get("ncores", 8)
    res = run_bass_kernel_spmd(_program(), in_maps[:ncr], core_ids=list(range(ncr)))
    r = res.results
    cat = lambda k: np.stack([np.asarray(r[min(c, ncr - 1)][k]) for c in range(8)])
    y_prompt = cat("y_p")
    y_sample = cat("y_s").reshape(128, LS, D)
    gdn_p = cat("gdn_p")[None]
    gconv_p = cat("gconv_p")[None]
    gla_p = cat("gla_p")[None]
    fconv_p = cat("fconv_p")[None]
    gdn_s = cat("gdn_s").reshape(1, 128, 8, 64, 64)
    gconv_s = cat("gconv_s").reshape(1, 128, 3, 1536)
    gla_s = cat("gla_s").reshape(1, 128, 4, 64, 128)
    fconv_s = cat("fconv_s").reshape(1, 128, 2, 2 * DFF)
    outs = (y_prompt, y_sample, gdn_p, gconv_p, gla_p, fconv_p, gdn_s, gconv_s, gla_s, fconv_s)
    return tuple(np.ascontiguousarray(o, dtype=np.float32) for o in outs)
```

```python
from contextlib import ExitStack

import numpy as np
import concourse.bass as bass
import concourse.mybir as mybir
from concourse.bass_utils import run_bass_kernel_spmd

F32 = mybir.dt.float32
BF16 = mybir.dt.bfloat16
F32R = mybir.dt.float32r
AF = mybir.ActivationFunctionType
ALU = mybir.AluOpType
AX = mybir.AxisListType


class TB:
    def __init__(self, name, t):
        self.name = name
        self.t = t
        self.last_w = None
        self.readers = {}
        self.parts = []


class Sched:
    def __init__(self, nc, nslots=40):
        self.nc = nc
        self.stack = ExitStack()
        self.engs = {"pe": nc.tensor, "dve": nc.vector, "act": nc.scalar, "pool": nc.gpsimd, "sp": nc.sync}
        self.nslots = nslots

    def __enter__(self):
        self.stack.__enter__()
        nc = self.nc
        self.sem = {k: self.stack.enter_context(nc.semaphore(f"s_{k}")) for k in self.engs}
        self.cnt = {k: 0 for k in self.engs}
        self.waited = {k: {} for k in self.engs}
        self.slot_sem = [self.stack.enter_context(nc.semaphore(f"d_{i}")) for i in range(self.nslots)]
        self.slot_cnt = [0] * self.nslots
        self.next_slot = {"sp": 0, "pool": 0}
        self.out_deps = []
        self.nbuf = 0
        return self

    def __exit__(self, *a):
        return self.stack.__exit__(*a)

    def sb(self, name, shape, dtype):
        t = self.stack.enter_context(self.nc.sbuf_tensor(name, list(shape), dtype))
        return TB(name, t)

    def ps(self, name, shape, dtype):
        t = self.stack.enter_context(self.nc.psum_tensor(name, list(shape), dtype))
        return TB(name, t)

    def alias(self, name, tb):
        return TB(name, tb.t)

    def _semof(self, key):
        if isinstance(key, tuple):
            return self.slot_sem[key[1]]
        return self.sem[key]

    def _wait(self, eng, dep):
        key, val = dep
        if eng == "pe" and key == "pe":
            return
        w = self.waited[eng]
        if w.get(key, 0) >= val:
            return
        self.engs[eng].wait_ge(self._semof(key), val)
        w[key] = val

    @staticmethod
    def _expand(bufs):
        out = []
        for b in bufs:
            out.append(b)
            out.extend(b.parts)
        return out

    def _deps(self, reads, writes):
        reads, writes = self._expand(reads), self._expand(writes)
        deps = set()
        for b in reads:
            if b.last_w is not None:
                deps.add(b.last_w)
        for b in writes:
            if b.last_w is not None:
                deps.add(b.last_w)
            for k, v in b.readers.items():
                deps.add((k, v))
        return deps

    def _commit(self, me, reads, writes):
        reads, writes = self._expand(reads), self._expand(writes)
        for b in writes:
            b.last_w = me
            b.readers = {}
        for b in reads:
            if b not in writes:
                b.readers[me[0]] = max(b.readers.get(me[0], 0), me[1])

    def op(self, eng, emit, reads=(), writes=()):
        for d in sorted(self._deps(reads, writes), key=str):
            self._wait(eng, d)
        inst = emit(self.engs[eng])
        self.cnt[eng] += 1
        inst.then_inc(self.sem[eng], 1)
        self._commit((eng, self.cnt[eng]), reads, writes)

    def dma(self, out, in_, reads=(), writes=(), cast=False, is_out=False, q=None):
        eng = q or ("pool" if cast else "sp")
        nsp = (self.nslots * 5) // 8
        lo, n = (0, nsp) if eng == "sp" else (nsp, self.nslots - nsp)
        i = lo + self.next_slot[eng]
        self.next_slot[eng] = (self.next_slot[eng] + 1) % n
        if self.slot_cnt[i] > 0:
            self._wait(eng, (("slot", i), 16 * self.slot_cnt[i]))
        for d in sorted(self._deps(reads, writes), key=str):
            self._wait(eng, d)
        inst = self.engs[eng].dma_start(out=out, in_=in_)
        inst.then_inc(self.slot_sem[i], 16)
        self.slot_cnt[i] += 1
        me = (("slot", i), 16 * self.slot_cnt[i])
        self._commit(me, reads, writes)
        if is_out:
            self.out_deps.append(me)

    def finish(self):
        for d in self.out_deps:
            self._wait("sp", d)
        for k in ("pe", "dve", "act", "pool"):
            if self.cnt[k] > 0:
                self._wait("sp", (k, self.cnt[k]))

    def barrier(self):
        deps = [(k, self.cnt[k]) for k in ("pe", "dve", "act", "pool") if self.cnt[k] > 0]
        deps += [(("slot", i), 16 * c) for i, c in enumerate(self.slot_cnt) if c > 0]
        for e in ("pe", "dve", "act", "pool", "sp"):
            for d in deps:
                if d[0] != e:
                    self._wait(e, d)


DBG = {}
D = 1024
SEQ = 2048
NMETA = 16
TP = SEQ + NMETA
NS = 16
LS = 8
DFF = 2816
NFF = 44
ALPHA = 2.0 ** 0.25
NFM = 2064
NTM = 1808
NEG = -30000.0

C_ID, C_ONE, C_BO64 = 0, 1, 2
C_PU, C_PSU, C_PMBT, C_PMBS, C_PM01T = 3, 4, 5, 6, 7
C_SU, C_SSU, C_SBO, C_SMBT, C_SMBS, C_SM01T, C_SBM = 8, 9, 10, 11, 12, 13, 14
NCM = 15


def _const_mats():
    m = np.zeros((NCM, 128, 128), np.float32)
    k = np.arange(128)[:, None]
    c = np.arange(128)[None, :]
    m[C_ID] = (k == c)
    m[C_ONE] = 1.0
    m[C_BO64] = (k // 64 == c // 64)
    m[C_PU] = (k <= c)
    m[C_PSU] = (k > c)
    m[C_PMBT] = np.where(c >= k, 0.0, NEG)
    m[C_PMBS] = np.where(c < k, 0.0, NEG)
    m[C_PM01T] = (c >= k)
    sb = (k // LS == c // LS)
    m[C_SU] = (k <= c) & sb
    m[C_SSU] = (k > c) & sb
    m[C_SBO] = sb
    m[C_SMBT] = np.where((c >= k) & sb, 0.0, NEG)
    m[C_SMBS] = np.where((c < k) & sb, 0.0, NEG)
    m[C_SM01T] = (c >= k) & sb
    m[C_SBM][:, :NS] = (k // LS == np.arange(NS)[None, :])
    return np.ascontiguousarray(m.transpose(1, 0, 2))


def _bc(ap, axis, shape):
    return ap.unsqueeze(axis).to_broadcast(list(shape))


class K:
    def __init__(self):
        nc = self.nc = bass.Bass("TRN2", target_bir_lowering=False)
        di = lambda n, s: nc.dram_tensor(n, list(s), F32, kind="ExternalInput").ap()
        do = lambda n, s: nc.dram_tensor(n, list(s), F32, kind="ExternalOutput").ap()
        self.xp = di("xp", [SEQ, D]); self.xs = di("xs", [NS * LS, D]); self.meta = di("meta", [NMETA, D])
        self.sgdn = di("sgdn", [NS, 8, 64, 64]); self.sgconv = di("sgconv", [NS * 3, 1536])
        self.sgla = di("sgla", [NS, 4, 64, 128]); self.sfconv = di("sfconv", [NS * 2, 2 * DFF])
        self.cmat_d = di("cmat", [128, NCM, 128])
        self.w_in_d = di("w_in_r", [D, NFM + NTM]); self.w_out_d = di("w_out", [D, D])
        self.w_up_d = di("w_up", [D, 2 * DFF]); self.w_down_d = di("w_down", [DFF, D])
        self.lnv_d = di("lnv", [6, D]); self.cwg_d = di("cwg", [128, 12, 4]); self.cwf_d = di("cwf", [128, NFF, 4])
        self.pvec_d = di("pvec", [1, 464]); self.wgk2_d = di("wgk2", [16, 256])
        self.y_p = do("y_p", [SEQ, D]); self.y_s = do("y_s", [NS * LS, D])
        self.gdn_p = do("gdn_p", [8, 64, 64]); self.gconv_p = do("gconv_p", [3, 1536])
        self.gla_p = do("gla_p", [4, 64, 128]); self.fconv_p = do("fconv_p", [2, 2 * DFF])
        self.gdn_s = do("gdn_s", [NS, 8, 64, 64]); self.gconv_s = do("gconv_s", [NS * 3, 1536])
        self.gla_s = do("gla_s", [NS, 4, 64, 128]); self.fconv_s = do("fconv_s", [NS * 2, 2 * DFF])
        self.h1s = nc.dram_tensor("h1s", [TP + NS * LS, D], F32, kind="Internal").ap()
        self.S = Sched(nc)

    def cm(self, idx, r=128, c=128):
        return self.cmat.t[0:r, idx, 0:c]

    def bank(self, b, n=1):
        if n == 1:
            return self.pst.t[:, b, :]
        return self.pst.t[:, b:b + n, :].rearrange("p b f -> p (b f)")

    def layer_norm(self, buf, C, g_ap, b_ap, eps_tile, tag):
        S = self.S
        st, mv = self.ln_st, self.ln_mv
        x = buf.t

        def stats(e):
            e.bn_stats(out=st.t[0:C, 0, :], in_=x[0:C, 0:512])
            return e.bn_stats(out=st.t[0:C, 1, :], in_=x[0:C, 512:1024])
        S.op("dve", stats, [buf], [st])
        S.op("dve", lambda e: e.bn_aggr(out=mv.t[0:C, 0:2], in_=st.t[0:C, :, :].rearrange("p a b -> p (a b)")), [st], [mv])
        S.op("act", lambda e: e.activation(out=mv.t[0:C, 2:3], in_=mv.t[0:C, 1:2], func=AF.Ln, bias=eps_tile, scale=1.0), [mv, self.epsc], [mv])
        S.op("act", lambda e: e.activation(out=mv.t[0:C, 3:4], in_=mv.t[0:C, 2:3], func=AF.Exp, scale=-0.5), [mv], [mv])
        S.op("dve", lambda e: e.scalar_tensor_tensor(out=x[0:C, :], in0=x[0:C, :], scalar=mv.t[0:C, 0:1], in1=g_ap[0:C, :],
                                                      op0=ALU.subtract, op1=ALU.mult), [buf, mv, self.lnbc], [buf])
        S.op("dve", lambda e: e.scalar_tensor_tensor(out=x[0:C, :], in0=x[0:C, :], scalar=mv.t[0:C, 3:4], in1=b_ap[0:C, :],
                                                      op0=ALU.mult, op1=ALU.add), [buf, mv, self.lnbc], [buf])

    def stage1_alloc(self):
        S = self.S
        self.lnbc = S.sb("lnbc", [128, 4, D], F32)
        self.pvec = S.sb("pvec_sb", [128, 464], F32)
        self.negA = S.sb("negA", [128, 8], F32)
        self.wgk2 = S.sb("wgk2_sb", [16, 256], F32)
        self.cwg = S.sb("cwg_sb", [128, 12, 4], F32)
        self.w_in = S.sb("w_in_sb", [128, 8, NFM + NTM], BF16)
        self.w_out = S.sb("w_out_sb", [128, 8, D], BF16)
        self.xhs = [S.sb(f"xh{i}", [128, D], F32) for i in range(2)]
        self.hT = S.sb("hT", [128, 8, 128], BF16)
        self.qkvx = S.sb("qkvx", [128, 12, 176], F32)
        self.fmx = S.sb("fmx", [128, 5, 128], F32)
        self.tm0 = S.sb("tm0", [128, 272], F32)
        self.gv_tok = S.sb("gv_tok", [128, 512], F32)
        self.sg_gdn = S.sb("sg_gdn", [128, 512], F32)
        self.sg_gla = S.sb("sg_gla", [128, 512], F32)
        self.bigs = S.sb("bigs", [128, 6, 1024], F32)
        self.big = [S.alias(f"big{i}", self.bigs) for i in range(6)]
        self.Pc = S.sb("Pc", [128, 1024], F32)
        self.PTc = S.sb("PTc", [128, 1024], F32)
        self.TTc = S.sb("TTc", [128, 1024], F32)
        for tb in (self.Pc, self.PTc, self.TTc):
            tb.parts = [S.alias(f"{tb.name}_hg{g}", tb) for g in range(2)]
        self.kq = S.sb("kq", [128, 4, 2, 128], F32)
        self.wkT = S.sb("wkT", [128, 8, 128], F32)
        self.KTm = self.wkT
        self.QTm = S.sb("QTm", [128, 8, 128], F32)
        self.keTm = S.sb("keTm", [128, 4, 128], F32)
        self.qeTm = S.sb("qeTm", [128, 4, 128], F32)
        self.wv = S.sb("wv", [128, 512], F32)
        self.RK = S.sb("RK", [128, 512], F32)
        self.RV = S.sb("RV", [128, 512], F32)
        self.kdec = S.sb("kdec", [128, 512], F32)
        self.U = self.RV
        self.ogdn = self.RK
        self.Sg = S.sb("Sg", [128, 4, 64], F32)
        self.sc = S.sb("sc", [128, 96], F32)
        self.lt = TB("lt", self.wv.t[:, 0:256])
        self.lt.parts = [self.wv]
        self.ebT = S.sb("ebT", [128, 2, 128], F32)
        self.enbT = S.sb("enbT", [128, 2, 128], F32)
        self.qeT = S.sb("qeT", [128, 2, 128], F32)
        self.PTg = S.sb("PTg", [128, 4, 128], F32)
        self.kd = TB("kd", self.enbT.t[:, :, :].rearrange("p a c -> p (a c)"))
        self.kd.parts = [self.enbT]
        self.ogla = S.sb("ogla", [128, 512], F32)
        self.Sl = S.sb("Sl", [128, 2, 128], F32)
        self.mix = S.sb("mix", [128, D], F32)
        self.mixT = S.sb("mixT", [128, 8, 128], BF16)
        self.otmp = TB("otmp", self.bigs.t[:, 3, 0:512])
        self.otmp.parts = [self.big[3]]

    def bg(self, i, n=1):
        if n == 1:
            return self.bigs.t[:, i, :]
        return self.bigs.t[:, i:i + n, :].rearrange("p b f -> p (b f)")

    def stage1_setup(self):
        S = self.S
        nc = self.nc
        S.dma(self.cmat.t[:], self.cmat_d, [], [self.cmat])
        for i in range(4):
            S.dma(self.lnbc.t[:, i, :], self.lnv_d[i:i + 1, :].partition_broadcast(128), [], [self.lnbc])
        S.dma(self.pvec.t[:], self.pvec_d[0:1, :].partition_broadcast(128), [], [self.pvec])
        S.dma(self.wgk2.t[:], self.wgk2_d, [], [self.wgk2])
        S.dma(self.cwg.t[:], self.cwg_d, [], [self.cwg])
        S.op("pool", lambda e: e.memset(self.epsc.t[:, 0:1], 1e-5), [], [self.epsc])
        S.op("pool", lambda e: e.memset(self.epsc.t[:, 1:2], 1e-6), [self.epsc], [self.epsc])
        S.op("pool", lambda e: e.memset(self.epsc.t[:, 2:3], 1.0), [self.epsc], [self.epsc])
        S.op("pool", lambda e: e.memset(self.epsc.t[:, 3:4], 0.0), [self.epsc], [self.epsc])
        wv_ = self.w_in_d.rearrange("(kc p) n -> p kc n", p=128)
        for kc in range(8):
            S.dma(self.w_in.t[:, kc, :], wv_[:, kc, :], [], [self.w_in], cast=True)
        wo_ = self.w_out_d.rearrange("(kc p) n -> p kc n", p=128)
        for kc in range(0, 8, 4):
            S.dma(self.w_out.t[:, kc:kc + 4, :], wo_[:, kc:kc + 4, :], [], [self.w_out], cast=True)
        S.op("act", lambda e: e.activation(out=self.negA.t[:], in_=self.pvec.t[:, 0:8], func=AF.Exp), [self.pvec], [self.negA])
        S.op("dve", lambda e: e.tensor_scalar(out=self.negA.t[:], in0=self.negA.t[:], scalar1=-1.0, scalar2=None, op0=ALU.mult),
             [self.negA], [self.negA])
        S.op("pool", lambda e: e.memset(self.Sg.t[:], 0.0), [], [self.Sg])
        S.op("pool", lambda e: e.memset(self.Sl.t[:], 0.0), [], [self.Sl])
        S.op("pool", lambda e: e.memset(self.qkvx.t[:], 0.0), [], [self.qkvx])
        for tb in (self.wkT, self.QTm, self.keTm, self.qeTm):
            S.op("pool", lambda e, tb=tb: e.memset(tb.t[:], 0.0), [], [tb])

    def front0(self, e0, C, kind, xh):
        self.front0_load(e0, C, kind, xh)
        self.front0_compute(e0, C, kind, xh)

    def front0_load(self, e0, C, kind, xh):
        S = self.S
        smp = kind == "s"
        if smp:
            S.dma(xh.t[0:C, :], self.xs, [], [xh])
        elif e0 == 0:
            S.dma(xh.t[0:NMETA, :], self.meta, [], [xh])
            S.dma(xh.t[NMETA:128, :], self.xp[0:128 - NMETA, :], [], [xh])
        else:
            S.dma(xh.t[0:C, :], self.xp[e0 - NMETA:e0 - NMETA + C, :], [], [xh])

    def front0_compute(self, e0, C, kind, xh):
        S, cm, pb, bank, hT = self.S, self.cm, self.pb, self.bank, self.hT
        self.layer_norm(xh, C, self.lnbc.t[:, 0, :], self.lnbc.t[:, 1, :], self.epsc.t[0:C, 0:1], "in")
        for half in range(2):
            def tr(e, half=half):
                for j in range(4):
                    kc = half * 4 + j
                    r = e.transpose(bank(half)[:, j * 128:j * 128 + C], xh.t[0:C, kc * 128:(kc + 1) * 128], cm(C_ID, C, C))
                return r
            S.op("pe", tr, [xh, self.cmat], [pb[half]])
            S.op("act", lambda e, half=half: e.activation(
                out=hT.t[:, half * 4:half * 4 + 4, 0:C],
                in_=bank(half).rearrange("p (j c) -> p j c", j=4)[:, :, 0:C], func=AF.Copy), [pb[half]], [hT])

    def chunk(self, e0, C, kind, xh, nxt):
        S = self.S
        self.xh = xh
        cm = self.cm
        pb = self.pb
        bank = self.bank
        big = self.big
        bg = self.bg
        smp = kind == "s"
        if smp:
            iU, iSU, iBO, iMBT, iMBS, iM01 = C_SU, C_SSU, C_SBO, C_SMBT, C_SMBS, C_SM01T
            G, L, nlev = NS, LS, 3
        else:
            iU, iSU, iBO, iMBT, iMBS, iM01 = C_PU, C_PSU, C_ONE, C_PMBT, C_PMBS, C_PM01T
            G, L, nlev = 1, C, {128: 7, 16: 4}[C]
        hT, qkvx, fmx, tm0, kq, sc = self.hT, self.qkvx, self.fmx, self.tm0, self.kq, self.sc
        ident = cm(C_ID)

        if DBG.get("step", 99) < 4:
            return
        qv = qkvx.t[:, :, 0:G * (L + 3)].rearrange("p a (g l) -> p a g l", g=G)
        for grp in range(5):
            b = 2 + (grp % 4)
            ccs = list(range(grp * 4, min(grp * 4 + 4, 17)))

            def mmf(e, ccs=ccs, b=b):
                for j, cc in enumerate(ccs):
                    M = 128 if cc < 16 else 16
                    for kc in range(8):
                        r = e.matmul(bank(b)[0:M, j * 128:j * 128 + C], lhsT=self.w_in.t[:, kc, cc * 128:cc * 128 + M],
                                     rhs=hT.t[:, kc, 0:C], start=(kc == 0), stop=(kc == 7))
                return r
            S.op("pe", mmf, [self.w_in, hT], [pb[b]])
            src = bank(b).rearrange("p (j c) -> p j c", j=4)
            if grp < 3:
                S.op("act", lambda e, grp=grp, src=src: e.activation(
                    out=qv[:, grp * 4:grp * 4 + 4, :, 3:3 + L],
                    in_=src[:, :, 0:C].rearrange("p j (g l) -> p j g l", g=G), func=AF.Copy), [pb[b]], [qkvx])
            elif grp == 3:
                S.op("act", lambda e, src=src: e.activation(out=fmx.t[:, 0:4, 0:C], in_=src[:, :, 0:C], func=AF.Copy), [pb[b]], [fmx])
            else:
                S.op("act", lambda e, src=src: e.activation(out=fmx.t[0:16, 4, 0:C], in_=src[0:16, 0, 0:C], func=AF.Copy), [pb[b]], [fmx])
        if DBG.get("step", 99) < 5:
            return
        tmoff = [NFM, NFM + 272, NFM + 784, NFM + 1296]
        tmn = [272, 512, 512, 512]
        for gi in range(4):
            b = 6 + (gi % 2)

            def mmt(e, gi=gi, b=b):
                for kc in range(8):
                    r = e.matmul(bank(b)[0:C, 0:tmn[gi]], lhsT=hT.t[:, kc, 0:C], rhs=self.w_in.t[:, kc, tmoff[gi]:tmoff[gi] + tmn[gi]],
                                 start=(kc == 0), stop=(kc == 7))
                return r
            S.op("pe", mmt, [self.w_in, hT], [pb[b]])
            if gi == 0:
                S.op("dve", lambda e, b=b: e.tensor_copy(out=tm0.t[0:C, :], in_=bank(b)[0:C, 0:272]), [pb[b]], [tm0])
            elif gi == 1:
                S.op("act", lambda e, b=b: e.activation(out=self.sg_gdn.t[0:C, :], in_=bank(b)[0:C, :], func=AF.Copy), [pb[b]], [self.sg_gdn])
            elif gi == 2:
                S.op("dve", lambda e, b=b: e.tensor_copy(out=self.gv_tok.t[0:C, :], in_=bank(b)[0:C, :]), [pb[b]], [self.gv_tok])
            else:
                S.op("act", lambda e, b=b: e.activation(out=self.sg_gla.t[0:C, :], in_=bank(b)[0:C, :], func=AF.Copy), [pb[b]], [self.sg_gla])
        if nxt is not None:
            self.front0_load(*nxt)
        if DBG.get("step", 99) < 6:
            return
        acc = bg(0, 2)[:, 0:12 * C].rearrange("p (a g l) -> p a g l", a=12, g=G)
        tmp = bg(2, 2)[:, 0:12 * C].rearrange("p (a g l) -> p a g l", a=12, g=G)
        accT, tmpT = [big[0], big[1]], [big[2], big[3]]

        def cwb(i):
            return self.cwg.t[:, :, i:i + 1].unsqueeze(3).to_broadcast([128, 12, G, L])
        S.op("dve", lambda e: e.tensor_tensor(out=acc, in0=qv[:, :, :, 0:L], in1=cwb(0), op=ALU.mult), [qkvx, self.cwg], accT)
        for i in range(1, 4):
            S.op("pool" if i == 1 else "dve", lambda e, i=i: e.tensor_tensor(out=tmp, in0=qv[:, :, :, i:i + L], in1=cwb(i), op=ALU.mult), [qkvx, self.cwg], tmpT)
            S.op("dve", lambda e: e.tensor_tensor(out=acc, in0=acc, in1=tmp, op=ALU.add), accT + tmpT, accT)
        qa = bg(0, 2)[:, 0:12 * C].rearrange("p (a c) -> p a c", a=12)
        S.op("act", lambda e: e.activation(out=qa, in_=qa, func=AF.Silu), accT, accT)
        if DBG.get("step", 99) < 7:
            return
        sq = bg(4)[:, 0:8 * C].rearrange("p (a c) -> p a c", a=8)
        rn = bg(5)[:, 0:8 * C].rearrange("p (a c) -> p a c", a=8)
        for sg in (self.sg_gdn, self.sg_gla):
            S.op("act", lambda e, sg=sg: e.activation(out=sg.t[0:C, :], in_=sg.t[0:C, :], func=AF.Silu), [sg], [sg])
        S.op("act", lambda e: e.activation(out=sq, in_=qa[:, 0:8, :], func=AF.Square), accT, [big[4]])
        S.op("act", lambda e: e.activation(out=sc.t[0:C, 16:24], in_=tm0.t[0:C, 8:16], func=AF.Sigmoid), [tm0], [sc])
        for half in range(2):
            S.op("pe", lambda e, half=half: e.matmul(bank(half)[:, 0:4 * C], lhsT=cm(C_BO64),
                                                     rhs=bg(4)[:, half * 4 * C:(half + 1) * 4 * C], start=True, stop=True),
                 [big[4], self.cmat], [pb[half]])
            S.op("act", lambda e, half=half: e.activation(out=bg(5)[:, half * 4 * C:(half + 1) * 4 * C], in_=bank(half)[:, 0:4 * C],
                                                          func=AF.Ln, bias=self.epsc.t[:, 1:2], scale=1.0), [pb[half], self.epsc], [big[5]])
        S.op("act", lambda e: e.activation(out=bg(5)[:, 0:8 * C], in_=bg(5)[:, 0:8 * C], func=AF.Exp, scale=-0.5), [big[5]], [big[5]])
        S.op("dve", lambda e: e.scalar_tensor_tensor(out=kq.t[:, :, 1, 0:C], in0=qa[:, 0:4, :], scalar=0.125, in1=rn[:, 0:4, :],
                                                      op0=ALU.mult, op1=ALU.mult), accT + [big[5]], [kq])
        S.op("pool", lambda e: e.tensor_tensor(out=kq.t[:, :, 0, 0:C], in0=qa[:, 4:8, :], in1=rn[:, 4:8, :], op=ALU.mult), accT + [big[5]], [kq])
        for h2 in range(2):
            rows = slice(64 * h2, 64 * h2 + 64)
            pad = lambda tb: tb.t[rows, :, 0:C].rearrange("p (a two) c -> p a two c", two=2)[:, :, h2, :]
            S.op("act", lambda e, rows=rows, pad=pad: e.activation(out=pad(self.KTm), in_=kq.t[rows, :, 0, 0:C], func=AF.Copy), [kq], [self.KTm])
            S.op("dve", lambda e, rows=rows, pad=pad: e.tensor_copy(out=pad(self.QTm), in_=kq.t[rows, :, 1, 0:C]), [kq], [self.QTm])
        if DBG.get("step", 99) < 8:
            return
        s_ = lambda a, b_: sc.t[0:C, a:b_]
        S.op("dve", lambda e: e.tensor_tensor(out=s_(0, 8), in0=tm0.t[0:C, 0:8], in1=self.pvec.t[0:C, 8:16], op=ALU.add), [tm0, self.pvec], [sc])
        S.op("act", lambda e: e.activation(out=s_(0, 8), in_=s_(0, 8), func=AF.Exp), [sc], [sc])
        S.op("act", lambda e: e.activation(out=s_(0, 8), in_=s_(0, 8), func=AF.Ln, bias=self.epsc.t[0:C, 2:3], scale=1.0), [sc, self.epsc], [sc])
        S.op("dve", lambda e: e.tensor_tensor(out=s_(8, 16), in0=s_(0, 8), in1=self.negA.t[0:C, :], op=ALU.mult), [sc, self.negA], [sc])

        def mmG(e):
            e.matmul(bank(0)[0:C, 0:8], lhsT=cm(iU, C, C), rhs=s_(8, 16), start=True, stop=True)
            return e.matmul(bank(0)[0:C, 8:16], lhsT=cm(iBO, C, C), rhs=s_(8, 16), start=True, stop=True)
        S.op("pe", mmG, [sc, self.cmat], [pb[0]])
        S.op("dve", lambda e: e.tensor_copy(out=s_(24, 40), in_=bank(0)[0:C, 0:16]), [pb[0]], [sc])
        S.op("act", lambda e: e.activation(out=s_(40, 48), in_=s_(24, 32), func=AF.Exp), [sc], [sc])
        S.op("dve", lambda e: e.tensor_tensor(out=s_(48, 56), in0=s_(32, 40), in1=s_(24, 32), op=ALU.subtract), [sc], [sc])
        S.op("act", lambda e: e.activation(out=s_(48, 56), in_=s_(48, 56), func=AF.Exp), [sc], [sc])
        S.op("act", lambda e: e.activation(out=s_(56, 64), in_=s_(32, 40), func=AF.Exp), [sc], [sc])
        S.op("dve", lambda e: e.tensor_tensor(out=s_(64, 72), in0=s_(16, 24), in1=s_(40, 48), op=ALU.mult), [sc], [sc])
        if DBG.get("step", 99) < 9:
            return
        def trk(e):
            for p in range(4):
                r = e.transpose(bank(6)[0:C, p * 128:(p + 1) * 128], kq.t[:, p, 0, 0:C], ident)
            return r
        S.op("pe", trk, [kq, self.cmat], [pb[6]])

        def trv(e):
            for p in range(4):
                r = e.transpose(bank(7)[0:C, p * 128:(p + 1) * 128], qa[:, 8 + p, :], ident)
            return r
        S.op("pe", trv, accT + [self.cmat], [pb[7]])
        h3 = lambda ap: ap.rearrange("p (h d) -> p h d", h=8)
        S.op("dve", lambda e: e.tensor_tensor(out=h3(self.RK.t[0:C, :]), in0=h3(bank(6)[0:C, :]), in1=_bc(s_(64, 72), 2, [C, 8, 64]), op=ALU.mult),
             [pb[6], sc], [self.RK])
        S.op("dve", lambda e: e.tensor_tensor(out=h3(self.kdec.t[0:C, :]), in0=h3(bank(6)[0:C, :]), in1=_bc(s_(48, 56), 2, [C, 8, 64]), op=ALU.mult),
             [pb[6], sc], [self.kdec])
        S.op("dve", lambda e: e.tensor_tensor(out=h3(self.RV.t[0:C, :]), in0=h3(bank(7)[0:C, :]), in1=_bc(s_(16, 24), 2, [C, 8, 64]), op=ALU.mult),
             [pb[7], sc], [self.RV])
        if DBG.get("step", 99) < 10:
            return
        v3 = lambda i: bg(i)[0:C, 0:8 * C].rearrange("p (h c) -> p h c", h=8)
        S.op("dve", lambda e: e.tensor_tensor(out=v3(2), in0=_bc(cm(iU, C, C), 1, [C, 8, C]), in1=_bc(s_(8, 16), 2, [C, 8, C]), op=ALU.mult),
             [self.cmat, sc], [big[2]])
        for half in range(2):
            S.op("pe", lambda e, half=half: e.matmul(bank(half)[0:C, 0:4 * C], lhsT=cm(C_ONE, C, C),
                                                     rhs=bg(2)[0:C, half * 4 * C:(half + 1) * 4 * C], start=True, stop=True),
                 [big[2], self.cmat], [pb[half]])
        gbc = bank(0, 2)

        def gview(r):
            return self.pst.t[0:r, 0:2, 0:4 * C].rearrange("p b (h c) -> p b h c", h=4)
        v4 = lambda i: bg(i)[0:C, 0:8 * C].rearrange("p (b h c) -> p b h c", b=2, h=4)
        S.op("pool", lambda e: e.tensor_tensor(out=v3(3), in0=_bc(cm(iMBT, C, C), 1, [C, 8, C]), in1=_bc(s_(24, 32), 2, [C, 8, C]), op=ALU.subtract),
             [self.cmat, sc], [big[3]])
        S.op("pool", lambda e: e.tensor_tensor(out=v3(4), in0=_bc(cm(iMBS, C, C), 1, [C, 8, C]), in1=_bc(s_(24, 32), 2, [C, 8, C]), op=ALU.add),
             [self.cmat, sc], [big[4]])
        S.op("dve", lambda e: e.tensor_tensor(out=v4(5), in0=gview(C), in1=v4(3), op=ALU.add), [pb[0], pb[1], big[3]], [big[5]])
        S.op("act", lambda e: e.activation(out=bg(5)[0:C, 0:8 * C], in_=bg(5)[0:C, 0:8 * C], func=AF.Exp), [big[5]], [big[5]])
        S.op("dve", lambda e: e.tensor_tensor(out=v4(1), in0=v4(4), in1=gview(C), op=ALU.subtract), [pb[0], pb[1], big[4]], [big[1]])
        S.op("act", lambda e: e.activation(out=bg(1)[0:C, 0:8 * C], in_=bg(1)[0:C, 0:8 * C], func=AF.Exp), [big[1]], [big[1]])
        if DBG.get("step", 99) < 11:
            return
        def mmkk(e):
            for h in range(8):
                p, h2 = h // 2, h % 2
                ov = bank(2 + h // 2).rearrange("p (hh two c) -> p hh two c", hh=2, two=2)
                if C == 128:
                    r = e.matmul(bank(2 + h // 2)[0:C, (h % 2) * 256:(h % 2) * 256 + 256], lhsT=self.KTm.t[:, h, 0:C],
                                 rhs=kq.t[:, p, :, :].rearrange("p a c -> p (a c)"), start=True, stop=True)
                else:
                    for two in range(2):
                        r = e.matmul(ov[0:C, h % 2, two, 0:C], lhsT=self.KTm.t[:, h, 0:C],
                                     rhs=kq.t[:, p, two, 0:C], start=True, stop=True)
            return r
        S.op("pe", mmkk, [kq, self.KTm], [pb[2], pb[3], pb[4], pb[5]])
        kkv = self.pst.t[0:C, 2:6, :].rearrange("p b (hh two c) -> p b hh two c", hh=2, two=2)
        v5 = lambda i: bg(i)[0:C, 0:8 * C].rearrange("p (b hh c) -> p b hh c", b=4, hh=2)
        S.op("dve", lambda e: e.tensor_tensor(out=v5(2), in0=kkv[:, :, :, 0, 0:C], in1=v5(1), op=ALU.mult), [pb[2], pb[3], pb[4], pb[5], big[1]], [big[2]])
        use_r = (C == 128) and bool(DBG.get("f32r"))
        ro = (lambda ap: ap.bitcast(F32R)) if use_r else (lambda ap: ap)
        ri = (lambda ap: ap.bitcast(F32R)) if use_r else (lambda ap: ap)
        Pc, PTc, TTc = self.Pc, self.PTc, self.TTc
        c3 = lambda tb: tb.t[0:C, 0:8 * C].rearrange("p (h c) -> p h c", h=8)
        c4 = lambda tb: tb.t[0:C, 0:8 * C].rearrange("p (b h c) -> p b h c", b=2, h=4)
        S.op("dve", lambda e: e.scalar_tensor_tensor(out=ro(c3(Pc)), in0=v3(2), scalar=-1.0, in1=_bc(s_(16, 24), 2, [C, 8, C]),
                                                      op0=ALU.mult, op1=ALU.mult), [big[2], sc], [Pc])
        S.op("dve", lambda e: e.tensor_tensor(out=v5(0), in0=kkv[:, :, :, 1, 0:C], in1=v5(5), op=ALU.mult), [pb[2], pb[3], pb[4], pb[5], big[5]] + accT, [big[0]])
        for half in range(2):
            def trn(e, half=half):
                for j in range(4):
                    r = e.transpose(bank(half)[0:C, j * C:(j + 1) * C], c3(Pc)[:, half * 4 + j, :], cm(C_ID, C, C))
                return r
            S.op("pe", trn, [Pc, self.cmat], [pb[half]])
        S.op("act", lambda e: e.activation(out=ro(c4(PTc)), in_=gview(C), func=AF.Copy), [pb[0], pb[1]], [PTc])
        S.op("dve", lambda e: e.tensor_tensor(out=ro(c3(TTc)), in0=c3(PTc), in1=_bc(cm(C_ID, C, C), 1, [C, 8, C]), op=ALU.add), [PTc, self.cmat], [TTc])
        if DBG.get("step", 99) < 12:
            return
        if nxt is not None:
            self.front0_compute(*nxt)
        gla_gen = self.gla_prep(C, iU, iSU, iM01)
        for lev in range(nlev):
            doA, doC, doB = lev >= 1, lev <= nlev - 2, lev <= nlev - 3
            for hg in range(2):
                bA, bB, bC = (2, 3, 4) if hg == 0 else (5, 6, 7)

                def mminv(e, hg=hg, bA=bA, bB=bB, bC=bC, doA=doA, doB=doB, doC=doC):
                    r = None
                    for j in range(4):
                        h = hg * 4 + j
                        o = lambda b_: bank(b_)[0:C, j * C:(j + 1) * C]
                        if doA:
                            r = e.matmul(o(bA), lhsT=ri(c3(Pc)[:, h, :]), rhs=ri(c3(TTc)[:, h, :]), start=True, stop=True)
                        if doC:
                            r = e.matmul(o(bC), lhsT=ri(c3(PTc)[:, h, :]), rhs=ri(c3(Pc)[:, h, :]), start=True, stop=True)
                        if doB:
                            r = e.matmul(o(bB), lhsT=ri(c3(Pc)[:, h, :]), rhs=ri(c3(PTc)[:, h, :]), start=True, stop=True)
                    return r
                wr = ([pb[bA]] if doA else []) + ([pb[bB]] if doB else []) + ([pb[bC]] if doC else [])
                S.op("pe", mminv, [Pc.parts[hg], PTc.parts[hg], TTc.parts[hg]], wr)
                hs = slice(hg * 4 * C, (hg + 1) * 4 * C)
                if doA:
                    S.op("dve", lambda e, bA=bA, hs=hs: e.tensor_tensor(out=ro(TTc.t[0:C, hs]), in0=bank(bA)[0:C, 0:4 * C], in1=TTc.t[0:C, hs], op=ALU.add),
                         [pb[bA], TTc.parts[hg]], [TTc.parts[hg]])
                if doC:
                    S.op("act", lambda e, bC=bC, hs=hs: e.activation(out=ro(Pc.t[0:C, hs]), in_=bank(bC)[0:C, 0:4 * C], func=AF.Copy), [pb[bC]], [Pc.parts[hg]])
                if doB:
                    S.op("act", lambda e, bB=bB, hs=hs: e.activation(out=ro(PTc.t[0:C, hs]), in_=bank(bB)[0:C, 0:4 * C], func=AF.Copy), [pb[bB]], [PTc.parts[hg]])
            for _ in range(4):
                next(gla_gen, None)
        for _ in gla_gen:
            pass
        if DBG.get("step", 99) < 13:
            return
        def mmwv(e):
            for h in range(8):
                r = e.matmul(bank(0)[0:C, h * 64:(h + 1) * 64], lhsT=c3(TTc)[:, h, :], rhs=self.RV.t[0:C, h * 64:(h + 1) * 64], start=True, stop=True)
            return r
        S.op("pe", mmwv, [TTc, self.RV], [pb[0]])
        S.op("act", lambda e: e.activation(out=self.wv.t[0:C, :], in_=bank(0)[0:C, :], func=AF.Copy), [pb[0]], [self.wv])

        def mmwk(e):
            for h in range(8):
                p = h // 2
                r = e.matmul(bank(2 + h // 4)[:, (h % 4) * 128:(h % 4) * 128 + C], lhsT=self.RK.t[0:C, p * 128:(p + 1) * 128], rhs=c3(TTc)[:, h, :],
                             start=True, stop=True)
            return r
        S.op("pe", mmwk, [TTc, self.RK], [pb[2], pb[3]])
        wkv = self.pst.t[:, 2:4, :].rearrange("p b (hh two c) -> p (b hh) two c", hh=2, two=2)
        for h2 in range(2):
            rows = slice(64 * h2, 64 * h2 + 64)
            S.op("dve" if h2 == 0 else "act",
                 (lambda e, rows=rows, h2=h2: e.tensor_copy(out=self.wkT.t[rows, :, 0:C].rearrange("p (a two) c -> p a two c", two=2)[:, :, h2, :], in_=wkv[rows, :, h2, 0:C])) if h2 == 0 else
                 (lambda e, rows=rows, h2=h2: e.activation(out=self.wkT.t[rows, :, 0:C].rearrange("p (a two) c -> p a two c", two=2)[:, :, h2, :], in_=wkv[rows, :, h2, 0:C], func=AF.Copy)),
                 [pb[2], pb[3]], [self.wkT])
        if DBG.get("step", 99) < 14:
            return
        for _ in gla_gen:
            pass
        if DBG.get("step", 99) < 15:
            return
        if smp:
            self.state_sample(C)
        else:
            self.state_prompt(C)
        if DBG.get("step", 99) < 16:
            return
        self.post_mix(e0, C, smp)
        if DBG.get("step", 99) < 17:
            return
        if not smp:
            S.op("pool", lambda e: e.tensor_copy(out=qkvx.t[:, :, 0:3], in_=qkvx.t[:, :, L:L + 3]), [qkvx], [qkvx])

    def gla_prep(self, C, iU, iSU, iM01):
        S, cm, pb, bank = self.S, self.cm, self.pb, self.bank
        fmx, lt = self.fmx, self.lt
        yield
        S.op("pe", lambda e: e.matmul(bank(0)[0:C, 0:256], lhsT=fmx.t[0:16, 4, 0:C], rhs=self.wgk2.t[:, :], start=True, stop=True),
             [fmx, self.wgk2], [pb[0]])
        yield
        S.op("dve", lambda e: e.tensor_tensor(out=lt.t[0:C, :], in0=bank(0)[0:C, 0:256], in1=self.pvec.t[0:C, 208:464], op=ALU.add), [pb[0], self.pvec], [lt])
        yield
        S.op("act", lambda e: e.activation(out=lt.t[0:C, :], in_=lt.t[0:C, :], func=AF.Exp, scale=-1.0), [lt], [lt])
        yield
        S.op("act", lambda e: e.activation(out=lt.t[0:C, :], in_=lt.t[0:C, :], func=AF.Ln, bias=self.epsc.t[0:C, 2:3], scale=1.0), [lt, self.epsc], [lt])

        yield
        def mmbc(e):
            for p in range(2):
                r = e.matmul(bank(1)[:, p * 128:p * 128 + C], lhsT=lt.t[0:C, p * 128:(p + 1) * 128], rhs=cm(iU, C, C), start=True, stop=True)
            return r
        yield
        S.op("pe", mmbc, [lt, self.cmat], [pb[1]])
        bcv = bank(1)[:, 0:256].rearrange("p (a c) -> p a c", a=2)[:, :, 0:C]
        yield
        S.op("act", lambda e: e.activation(out=self.ebT.t[:, :, 0:C], in_=bcv, func=AF.Exp, scale=-1.0 / 16.0), [pb[1]], [self.ebT])
        yield
        S.op("act", lambda e: e.activation(out=self.enbT.t[:, :, 0:C], in_=bcv, func=AF.Exp, scale=1.0 / 16.0), [pb[1]], [self.enbT])
        yield
        S.op("dve", lambda e: e.scalar_tensor_tensor(out=self.qeT.t[:, :, 0:C], in0=fmx.t[:, 0:2, 0:C], scalar=0.125, in1=self.ebT.t[:, :, 0:C],
                                                      op0=ALU.mult, op1=ALU.mult), [fmx, self.ebT], [self.qeT])
        yield
        for h2 in range(2):
            rows = slice(64 * h2, 64 * h2 + 64)
            pad = lambda tb: tb.t[rows, :, 0:C].rearrange("p (a two) c -> p a two c", two=2)[:, :, h2, :]
            S.op("pool", lambda e, rows=rows, pad=pad: e.tensor_tensor(out=pad(self.keTm), in0=fmx.t[rows, 2:4, 0:C], in1=self.enbT.t[rows, :, 0:C], op=ALU.mult),
                 [fmx, self.enbT], [self.keTm])
            S.op("act", lambda e, rows=rows, pad=pad: e.activation(out=pad(self.qeTm), in_=self.qeT.t[rows, :, 0:C], func=AF.Copy), [self.qeT], [self.qeTm])

        yield
        def mmA(e):
            for h in range(4):
                p, h2 = h // 2, h % 2
                rows = slice(64 * h2, 64 * h2 + 64)
                r = e.matmul(bank(0)[0:C, h * 128:h * 128 + C], lhsT=self.keTm.t[:, h, 0:C], rhs=self.qeT.t[:, p, 0:C], start=True, stop=True)
            return r
        yield
        S.op("pe", mmA, [self.keTm, self.qeT], [pb[0]])
        yield
        S.op("dve", lambda e: e.tensor_tensor(out=self.PTg.t[0:C, :, 0:C], in0=bank(0).rearrange("p (h c) -> p h c", h=4)[0:C, :, 0:C],
                                              in1=_bc(cm(iM01, C, C), 1, [C, 4, C]), op=ALU.mult), [pb[0], self.cmat], [self.PTg])
        yield
        S.op("pe", lambda e: e.matmul(bank(1)[0:C, 0:256], lhsT=cm(iSU, C, C), rhs=lt.t[0:C, :], start=True, stop=True), [lt, self.cmat], [pb[1]])
        yield
        S.op("act", lambda e: e.activation(out=self.kd.t[0:C, :], in_=bank(1)[0:C, 0:256], func=AF.Exp, scale=-1.0 / 16.0), [pb[1]], [self.kd])
        yield
        S.op("pool", lambda e: e.tensor_tensor(out=self.kd.t[0:C, :], in0=self.kd.t[0:C, :], in1=self.tm0.t[0:C, 16:272], op=ALU.mult), [self.kd, self.tm0], [self.kd])

    def state_prompt(self, C):
        S, cm, pb, bank, big, bg = self.S, self.cm, self.pb, self.bank, self.big, self.bg
        sc, kq = self.sc, self.kq
        s_ = lambda a, b_: sc.t[0:C, a:b_]
        v3 = lambda i: bg(i)[0:C, 0:8 * C].rearrange("p (h c) -> p h c", h=8)
        Sg, Sl, U = self.Sg, self.Sl, self.U
        R = lambda h2: slice(64 * h2, 64 * h2 + 64)
        h3 = lambda ap: ap.rearrange("p (h d) -> p h d", h=8)
        S.op("pe", lambda e: e.matmul(bank(1)[:, 0:8], lhsT=cm(C_ONE, C, 128), rhs=s_(8, 16), start=True, stop=True), [sc, self.cmat], [pb[1]])
        S.op("act", lambda e: e.activation(out=self.glb.t[:, 0:8], in_=bank(1)[:, 0:8], func=AF.Exp), [pb[1]], [self.glb])

        def mm1(e):
            for h in range(8):
                p, h2 = h // 2, h % 2
                r = e.matmul(bank(6)[0:C, h * 64:(h + 1) * 64], lhsT=self.wkT.t[:, h, 0:C], rhs=Sg.t[:, p, :], start=True, stop=True)
            return r
        S.op("pe", mm1, [self.wkT, Sg], [pb[6]])
        def mg1(e):
            for h in range(4):
                p, h2 = h // 2, h % 2
                r = e.matmul(bank(2)[0:C, h * 128:(h + 1) * 128], lhsT=self.qeTm.t[:, h, 0:C], rhs=Sl.t[:, p, :], start=True, stop=True)
            return r
        S.op("pe", mg1, [self.qeTm, Sl], [pb[2]])
        def mg2(e):
            for h in range(4):
                r = e.matmul(bank(3)[0:C, h * 128:(h + 1) * 128], lhsT=self.PTg.t[0:C, h, 0:C], rhs=self.gv_tok.t[0:C, h * 128:(h + 1) * 128], start=True, stop=True)
            return r
        S.op("pe", mg2, [self.PTg, self.gv_tok], [pb[3]])
        def mg3(e):
            for h in range(4):
                p = h // 2
                r = e.matmul(bank(4)[:, h * 128:(h + 1) * 128], lhsT=self.kd.t[0:C, p * 128:(p + 1) * 128], rhs=self.gv_tok.t[0:C, h * 128:(h + 1) * 128], start=True, stop=True)
            return r
        S.op("pe", mg3, [self.kd, self.gv_tok], [pb[4]])
        S.op("dve", lambda e: e.tensor_tensor(out=U.t[0:C, :], in0=self.wv.t[0:C, :], in1=bank(6)[0:C, :], op=ALU.subtract), [self.wv, pb[6]], [U])

        def mm2(e):
            for h in range(8):
                p, h2 = h // 2, h % 2
                r = e.matmul(bank(7)[0:C, h * 64:(h + 1) * 64], lhsT=self.QTm.t[:, h, 0:C], rhs=Sg.t[:, p, :], start=True, stop=True)
            return r
        S.op("pe", mm2, [self.QTm, Sg], [pb[7]])

        def mm3(e):
            for h in range(8):
                r = e.matmul(bank(0)[0:C, h * 64:(h + 1) * 64], lhsT=v3(0)[:, h, :], rhs=U.t[0:C, h * 64:(h + 1) * 64], start=True, stop=True)
            return r
        S.op("pe", mm3, [big[0], U], [pb[0]])
        S.op("act", lambda e: e.activation(out=self.ogla.t[0:C, :], in_=bank(2)[0:C, :], func=AF.Copy), [pb[2]], [self.ogla])
        S.op("dve", lambda e: e.tensor_tensor(out=self.ogla.t[0:C, :], in0=self.ogla.t[0:C, :], in1=bank(3)[0:C, :], op=ALU.add), [self.ogla, pb[3]], [self.ogla])
        S.op("dve", lambda e: e.tensor_tensor(out=h3(self.otmp.t[0:C, :]), in0=h3(bank(7)[0:C, :]), in1=_bc(s_(40, 48), 2, [C, 8, 64]), op=ALU.mult),
             [pb[7], sc], [self.otmp])
        S.op("dve", lambda e: e.tensor_tensor(out=self.ogdn.t[0:C, :], in0=self.otmp.t[0:C, :], in1=bank(0)[0:C, :], op=ALU.add), [self.otmp, pb[0]], [self.ogdn])

        def mm4(e):
            for h in range(8):
                p = h // 2
                r = e.matmul(bank(1)[:, h * 64:(h + 1) * 64], lhsT=self.kdec.t[0:C, p * 128:(p + 1) * 128], rhs=U.t[0:C, h * 64:(h + 1) * 64], start=True, stop=True)
            return r
        S.op("pe", mm4, [self.kdec, U, self.glb], [pb[1]])
        for h2 in range(2):
            glv = self.glb.t[R(h2), 0:8].rearrange("p (a two) -> p a two", two=2)[:, :, h2]
            psv = bank(1).rearrange("p (a two v) -> p a two v", two=2, v=64)[R(h2), :, h2, :]
            S.op("pool", lambda e, h2=h2, glv=glv: e.tensor_tensor(out=Sg.t[R(h2), :, :], in0=Sg.t[R(h2), :, :], in1=_bc(glv, 2, [64, 4, 64]), op=ALU.mult),
                 [Sg, self.glb], [Sg])
            S.op("dve", lambda e, h2=h2, psv=psv: e.tensor_tensor(out=Sg.t[R(h2), :, :], in0=Sg.t[R(h2), :, :], in1=psv, op=ALU.add), [Sg, pb[1]], [Sg])


        for h in range(4):
            p, h2 = h // 2, h % 2
            S.op("dve", lambda e, p=p, h2=h2, h=h: e.scalar_tensor_tensor(
                out=Sl.t[R(h2), p, :], in0=Sl.t[R(h2), p, :], scalar=self.ebT.t[R(h2), p, C - 1:C], in1=bank(4)[R(h2), h * 128:(h + 1) * 128],
                op0=ALU.mult, op1=ALU.add), [Sl, self.ebT, pb[4]], [Sl])

    def state_sample(self, C):
        S, cm, pb, bank, big, bg = self.S, self.cm, self.pb, self.bank, self.big, self.bg
        sc, kq = self.sc, self.kq
        s_ = lambda a, b_: sc.t[0:C, a:b_]
        v3 = lambda i: bg(i)[0:C, 0:8 * C].rearrange("p (h c) -> p h c", h=8)
        U = self.U
        R = lambda h2: slice(64 * h2, 64 * h2 + 64)
        h3 = lambda ap: ap.rearrange("p (h d) -> p h d", h=8)
        bm = cm(C_SBM, C, NS)
        gsel = bg(5)[0:C, 0:128].rearrange("p (h s) -> p h s", h=8)
        S.op("pool", lambda e: e.tensor_tensor(out=gsel, in0=_bc(s_(8, 16), 2, [C, 8, NS]), in1=_bc(bm, 1, [C, 8, NS]), op=ALU.mult), [sc, self.cmat], [big[5]])
        S.op("pe", lambda e: e.matmul(bank(1)[:, 0:128], lhsT=cm(C_ONE), rhs=bg(5)[0:C, 0:128], start=True, stop=True), [big[5], self.cmat], [pb[1]])
        S.op("act", lambda e: e.activation(out=self.glb.t[:, 0:128], in_=bank(1)[:, 0:128], func=AF.Exp), [pb[1]], [self.glb])
        glbs = self.glb.t[:, 0:128].rearrange("p (h s) -> p h s", h=8)
        S0 = bg(4).rearrange("p (s v) -> p s v", s=NS)
        tmp = bg(5)[0:C, :].rearrange("p (s v) -> p s v", s=NS)
        tmpT = bg(5)[0:C, :].rearrange("p (s v) -> p v s", s=NS)
        Ub = bg(1)[0:C, :].rearrange("p (s v) -> p s v", s=NS)
        ps67 = self.pst.t[:, 6:8, :].rearrange("p b (s v) -> p (b s) v", v=64)
        wks, o1s = self.otmp, self.ogdn
        S0v = lambda ap: ap.rearrange("p (s v) -> p s v", s=NS)
        S0s = [(S0v(bg(4)), big[4]), (S0v(bg(2)), big[2])]
        scr = [dict(tmp=S0v(bg(5)[0:C, :]), tmpT=bg(5)[0:C, :].rearrange("p (s v) -> p v s", s=NS), tmpB=big[5],
                    Ub=S0v(bg(1)[0:C, :]), Ubf=bg(1), UbB=big[1], bk=6),
               dict(tmp=S0v(self.Pc.t[0:C, :]), tmpT=self.Pc.t[0:C, :].rearrange("p (s v) -> p v s", s=NS), tmpB=self.Pc,
                    Ub=S0v(self.PTc.t[0:C, :]), Ubf=self.PTc.t, UbB=self.PTc, bk=4)]

        def load(p):
            S0, S0t = S0s[p % 2]
            for h2 in range(2):
                S.dma(S0[R(h2), :, :], self.sgdn[:, 2 * p + h2, :, :].rearrange("s d v -> d s v"), [], [S0t])

        def head_chain(p, h2):
            h = 2 * p + h2
            S0, S0t = S0s[p % 2]
            q = scr[h2]
            bk = q["bk"]
            psx = self.pst.t[:, bk:bk + 2, :].rearrange("p b (s v) -> p (b s) v", v=64)
            for (lhs, lhsb, dst) in ((self.wkT.t[:, h, 0:C], self.wkT, wks), (self.QTm.t[:, h, 0:C], self.QTm, o1s)):
                def mma(e, lhs=lhs):
                    for i in range(2):
                        r = e.matmul(bank(bk + i)[0:C, :], lhsT=lhs, rhs=S0[:, i * 8:(i + 1) * 8, :].rearrange("p s v -> p (s v)"), start=True, stop=True)
                    return r
                S.op("pe", mma, [lhsb, S0t], [pb[bk], pb[bk + 1]])
                yield
                S.op("dve", lambda e: e.tensor_tensor(out=q["tmp"], in0=psx[0:C], in1=_bc(bm, 2, [C, NS, 64]), op=ALU.mult), [pb[bk], pb[bk + 1], self.cmat], [q["tmpB"]])
                yield
                S.op("dve", lambda e, dst=dst: e.tensor_reduce(out=dst.t[0:C, h * 64:(h + 1) * 64], in_=q["tmpT"], op=ALU.add, axis=AX.X), [q["tmpB"]], [dst])
                yield
            cs = slice(h * 64, (h + 1) * 64)
            S.op("dve", lambda e: e.tensor_tensor(out=U.t[0:C, cs], in0=self.wv.t[0:C, cs], in1=wks.t[0:C, cs], op=ALU.subtract), [self.wv, wks], [U])
            yield
            S.op("pe", lambda e: e.matmul(bank(0)[0:C, cs], lhsT=v3(0)[:, h, :], rhs=U.t[0:C, cs], start=True, stop=True), [big[0], U], [pb[0]])
            yield
            S.op("pool", lambda e: e.tensor_tensor(out=q["Ub"], in0=_bc(U.t[0:C, cs], 1, [C, NS, 64]), in1=_bc(bm, 2, [C, NS, 64]), op=ALU.mult),
                 [U, self.cmat], [q["UbB"]])
            yield

            def mmb(e):
                for i in range(2):
                    r = e.matmul(bank(bk + i)[:, :], lhsT=self.kdec.t[0:C, p * 128:(p + 1) * 128], rhs=q["Ubf"][0:C, i * 512:(i + 1) * 512], start=True, stop=True)
                return r
            S.op("pe", mmb, [self.kdec, q["UbB"]], [pb[bk], pb[bk + 1]])
            yield
            S.op("pool", lambda e: e.tensor_tensor(out=S0[R(h2), :, :], in0=S0[R(h2), :, :], in1=_bc(glbs[R(h2), h, :], 2, [64, NS, 64]), op=ALU.mult),
                 [S0t, self.glb], [S0t])
            yield
            S.op("dve", lambda e: e.tensor_tensor(out=S0[R(h2), :, :], in0=S0[R(h2), :, :], in1=psx[R(h2)], op=ALU.add), [S0t, pb[bk], pb[bk + 1]], [S0t])
            yield
            S.dma(self.gdn_s[:, h, :, :].rearrange("s d v -> d s v"), S0[R(h2), :, :], [S0t], [], is_out=True)

        load(0)
        for p in range(4):
            if p + 1 < 4:
                load(p + 1)
            gens = [head_chain(p, 0), head_chain(p, 1)]
            while gens:
                for g_ in list(gens):
                    if next(g_, "done") == "done":
                        gens.remove(g_)
        S.op("dve", lambda e: e.tensor_tensor(out=h3(o1s.t[0:C, :]), in0=h3(o1s.t[0:C, :]), in1=_bc(s_(40, 48), 2, [C, 8, 64]), op=ALU.mult), [o1s, sc], [o1s])
        S.op("dve", lambda e: e.tensor_tensor(out=self.ogdn.t[0:C, :], in0=o1s.t[0:C, :], in1=bank(0)[0:C, :], op=ALU.add), [o1s, pb[0]], [self.ogdn])
        S0g = bg(0, 2).rearrange("p (s v) -> p s v", s=NS)
        tg = bg(2, 2)[0:C, :].rearrange("p (s v) -> p s v", s=NS)
        tgT = bg(2, 2)[0:C, :].rearrange("p (s v) -> p v s", s=NS)
        Vb = bg(4, 2)[0:C, :].rearrange("p (s v) -> p s v", s=NS)
        ps25 = self.pst.t[:, 2:6, :].rearrange("p b (s v) -> p (b s) v", v=128)
        S0gT, tgB, VbB = [big[0], big[1]], [big[2], big[3]], [big[4], big[5]]
        pbs = [pb[2], pb[3], pb[4], pb[5]]
        for p in range(2):
            for h2 in range(2):
                S.dma(S0g[R(h2), :, :], self.sgla[:, 2 * p + h2, :, :].rearrange("s d v -> d s v"), [], S0gT)
            for h2 in range(2):
                h = 2 * p + h2
                cs = slice(h * 128, (h + 1) * 128)

                def mmq(e, p=p, h2=h2, h=h):
                    for i in range(4):
                        r = e.matmul(bank(2 + i)[0:C, :], lhsT=self.qeTm.t[:, h, 0:C], rhs=S0g[:, i * 4:(i + 1) * 4, :].rearrange("p s v -> p (s v)"), start=True, stop=True)
                    return r
                S.op("pe", mmq, [self.qeTm] + S0gT, pbs)
                S.op("dve", lambda e: e.tensor_tensor(out=tg, in0=ps25[0:C], in1=_bc(bm, 2, [C, NS, 128]), op=ALU.mult), pbs + [self.cmat], tgB)
                S.op("dve", lambda e, cs=cs: e.tensor_reduce(out=self.ogla.t[0:C, cs], in_=tgT, op=ALU.add, axis=AX.X), tgB, [self.ogla])
                S.op("pool", lambda e, cs=cs: e.tensor_tensor(out=Vb, in0=_bc(self.gv_tok.t[0:C, cs], 1, [C, NS, 128]), in1=_bc(bm, 2, [C, NS, 128]), op=ALU.mult),
                     [self.gv_tok, self.cmat], VbB)

                def mmv(e, p=p):
                    for i in range(4):
                        r = e.matmul(bank(2 + i)[:, :], lhsT=self.kd.t[0:C, p * 128:(p + 1) * 128], rhs=bg(4, 2)[0:C, i * 512:(i + 1) * 512], start=True, stop=True)
                    return r
                S.op("pe", mmv, [self.kd] + VbB, pbs)
                ebl = self.ebT.t[R(h2), p, :].rearrange("p (s l) -> p s l", l=LS)[:, :, LS - 1]
                S.op("pool", lambda e, h2=h2, ebl=ebl: e.tensor_tensor(out=S0g[R(h2), :, :], in0=S0g[R(h2), :, :], in1=_bc(ebl, 2, [64, NS, 128]), op=ALU.mult),
                     S0gT + [self.ebT], S0gT)
                S.op("dve", lambda e, h2=h2: e.tensor_tensor(out=S0g[R(h2), :, :], in0=S0g[R(h2), :, :], in1=ps25[R(h2)], op=ALU.add), S0gT + pbs, S0gT)
                S.dma(self.gla_s[:, h, :, :].rearrange("s d v -> d s v"), S0g[R(h2), :, :], S0gT, [], is_out=True)

        def mg2(e):
            for h in range(4):
                r = e.matmul(bank(6)[0:C, h * 128:(h + 1) * 128], lhsT=self.PTg.t[0:C, h, 0:C], rhs=self.gv_tok.t[0:C, h * 128:(h + 1) * 128], start=True, stop=True)
            return r
        S.op("pe", mg2, [self.PTg, self.gv_tok], [pb[6]])
        S.op("dve", lambda e: e.tensor_tensor(out=self.ogla.t[0:C, :], in0=self.ogla.t[0:C, :], in1=bank(6)[0:C, :], op=ALU.add), [self.ogla, pb[6]], [self.ogla])

    def post_mix(self, e0, C, smp):
        S, cm, pb, bank = self.S, self.cm, self.pb, self.bank
        sc, mix, xh = self.sc, self.mix, self.xh
        s_ = lambda a, b_: sc.t[0:C, a:b_]
        if not mix.parts:
            mix.parts = [S.alias("mixA", mix), S.alias("mixB", mix)]
            self.scn = [S.alias("scA", sc), S.alias("scB", sc)]

        def norm_chain(o, nh, dv, col0, gcol, sg, sco, sqb, mixp, scp):
            v = lambda ap: ap.rearrange("p (h d) -> p h d", h=nh)
            yield
            S.op("dve", lambda e, o=o: e.tensor_tensor(out=sqb.t[0:C, :], in0=o.t[0:C, :], in1=o.t[0:C, :], op=ALU.mult), [o], [sqb])
            yield
            S.op("dve", lambda e, v=v, sco=sco, nh=nh: e.tensor_reduce(out=s_(sco, sco + nh), in_=v(sqb.t[0:C, :]), op=ALU.add, axis=AX.X), [sqb], [scp])
            yield
            S.op("act", lambda e, sco=sco, nh=nh, dv=dv: e.activation(out=s_(sco, sco + nh), in_=s_(sco, sco + nh), func=AF.Ln,
                                                                      bias=self.epsc.t[0:C, 1:2], scale=1.0 / dv), [scp, self.epsc], [scp])
            yield
            S.op("act", lambda e, sco=sco, nh=nh: e.activation(out=s_(sco, sco + nh), in_=s_(sco, sco + nh), func=AF.Exp, scale=-0.5), [scp], [scp])
            mv_ = v(mix.t[0:C, col0:col0 + 512])
            yield
            S.op("dve", lambda e, o=o, v=v, mv_=mv_, sco=sco, nh=nh, dv=dv: e.tensor_tensor(out=mv_, in0=v(o.t[0:C, :]), in1=_bc(s_(sco, sco + nh), 2, [C, nh, dv]), op=ALU.mult),
                 [o, scp], [mixp])
            yield
            S.op("dve", lambda e, mv_=mv_, gcol=gcol, nh=nh, dv=dv: e.tensor_tensor(out=mv_, in0=mv_, in1=_bc(self.pvec.t[0:C, gcol:gcol + dv], 1, [C, nh, dv]), op=ALU.mult),
                 [mixp, self.pvec], [mixp])
            yield
            S.op("dve", lambda e, col0=col0, sg=sg: e.tensor_tensor(out=mix.t[0:C, col0:col0 + 512], in0=mix.t[0:C, col0:col0 + 512], in1=sg.t[0:C, :], op=ALU.mult),
                 [mixp, sg], [mixp])
        gens = [norm_chain(self.ogdn, 8, 64, 0, 16, self.sg_gdn, 72, self.otmp, mix.parts[0], self.scn[0]),
                norm_chain(self.ogla, 4, 128, 512, 80, self.sg_gla, 80, self.kdec, mix.parts[1], self.scn[1])]
        while gens:
            for g_ in list(gens):
                if next(g_, 'done') == 'done':
                    gens.remove(g_)

        for half in range(2):
            def tr(e, half=half):
                for j in range(4):
                    kc = half * 4 + j
                    r = e.transpose(bank(half)[:, j * 128:j * 128 + C], mix.t[0:C, kc * 128:(kc + 1) * 128], cm(C_ID, C, C))
                return r
            S.op("pe", tr, [mix, self.cmat], [pb[half]])
            S.op("act", lambda e, half=half: e.activation(out=self.mixT.t[:, half * 4:half * 4 + 4, 0:C],
                                                          in_=bank(half).rearrange("p (j c) -> p j c", j=4)[:, :, 0:C], func=AF.Copy), [pb[half]], [self.mixT])
        for half in range(2):
            def mmo(e, half=half):
                for kc in range(8):
                    r = e.matmul(bank(6 + half)[0:C, :], lhsT=self.mixT.t[:, kc, 0:C], rhs=self.w_out.t[:, kc, half * 512:(half + 1) * 512],
                                 start=(kc == 0), stop=(kc == 7))
                return r
            S.op("pe", mmo, [self.mixT, self.w_out], [pb[6 + half]])
            S.op("dve", lambda e, half=half: e.scalar_tensor_tensor(out=xh.t[0:C, half * 512:(half + 1) * 512], in0=xh.t[0:C, half * 512:(half + 1) * 512],
                                                                     scalar=ALPHA, in1=bank(6 + half)[0:C, :], op0=ALU.mult, op1=ALU.add), [xh, pb[6 + half]], [xh])
        self.layer_norm(xh, C, self.lnbc.t[:, 2, :], self.lnbc.t[:, 3, :], self.epsc.t[0:C, 0:1], "ln1")
        row0 = TP if smp else e0
        S.dma(self.h1s[row0:row0 + C, :], xh.t[0:C, :], [xh], [self.h1scr])

    def conv_state_out(self, C, smp):
        S, cm, pb, bank = self.S, self.cm, self.pb, self.bank
        if smp:
            n = NS * 3
            cst = self.bg(3)[:, 0:12 * n].rearrange("p (a c) -> p a c", a=12)
            S.op("pool", lambda e: e.tensor_copy(out=cst.rearrange("p a (s l) -> p a s l", l=3),
                                                 in_=self.qkvx.t[:, :, 0:NS * 11].rearrange("p a (s l) -> p a s l", l=11)[:, :, :, 8:11]),
                 [self.qkvx], [self.big[3]])
            src = lambda cc: cst[:, cc, :]
            dst = self.gconv_s
        else:
            n = 3
            src = lambda cc: self.qkvx.t[:, cc, 0:3]
            dst = self.gconv_p
        stage = self.bg(1, 2)[0:n, 0:1536]
        for g in range(3):
            def tr(e, g=g):
                for j in range(4):
                    r = e.transpose(bank(2 + g)[0:n, j * 128:(j + 1) * 128], src(g * 4 + j), cm(C_ID))
                return r
            S.op("pe", tr, [self.qkvx, self.big[3], self.cmat], [pb[2 + g]])
            S.op("act", lambda e, g=g: e.activation(out=stage[:, g * 512:(g + 1) * 512], in_=bank(2 + g)[0:n, :], func=AF.Copy), [pb[2 + g]], [self.big[1], self.big[2]])
        S.dma(dst, stage, [self.big[1], self.big[2]], [], is_out=True)

    def stage1(self):
        S = self.S
        self.stage1_alloc()
        self.stage1_setup()
        R = lambda h2: slice(64 * h2, 64 * h2 + 64)
        plist = []
        e0 = 0
        while e0 < TP:
            C = min(128, TP - e0)
            skip = (DBG.get("maxchunks") is not None and e0 // 128 >= DBG["maxchunks"] and C == 128) or (DBG.get("no16") and C == 16)
            if not skip:
                plist.append((e0, C, "p"))
            e0 += C
        plist = [(a, b, c, self.xhs[i % 2]) for i, (a, b, c) in enumerate(plist)]
        self.sample_item = (0, NS * LS, "s", self.xhs[len(plist) % 2])
        self.front0(*plist[0])
        for i, it in enumerate(plist):
            self.chunk(*it, plist[i + 1] if i + 1 < len(plist) else None)
        if DBG.get("stop_early"):
            return
        if DBG.get("stop_after_prompt"):
            return
        for h2 in range(2):
            S.dma(self.gdn_p.rearrange("(a two) d v -> two d a v", two=2)[h2], self.Sg.t[R(h2), :, :], [self.Sg], [], is_out=True)
            S.dma(self.gla_p.rearrange("(a two) d v -> two d a v", two=2)[h2], self.Sl.t[R(h2), :, :], [self.Sl], [], is_out=True)
        if not DBG.get("no_cso"):
            self.conv_state_out(16, False)
        if DBG.get("stop_after_outputs"):
            return
        stc = self.bg(3, 2)[0:NS * 3, 0:1536]
        S.dma(stc, self.sgconv, [], [self.big[3], self.big[4]])
        qs = self.qkvx.t[:, :, 0:NS * 11].rearrange("p a (s l) -> p a s l", l=11)
        for g in range(3):
            def tr(e, g=g):
                for j in range(4):
                    cc = g * 4 + j
                    r = e.transpose(self.bank(2 + g)[:, j * 128:j * 128 + NS * 3], stc[:, cc * 128:(cc + 1) * 128], self.cm(C_ID, NS * 3, NS * 3))
                return r
            S.op("pe", tr, [self.big[3], self.big[4], self.cmat], [self.pb[2 + g]])
            S.op("act", lambda e, g=g: e.activation(out=qs[:, g * 4:(g + 1) * 4, :, 0:3],
                                                    in_=self.bank(2 + g).rearrange("p (j c) -> p j c", j=4)[:, :, 0:NS * 3].rearrange("p j (s l) -> p j s l", l=3),
                                                    func=AF.Copy), [self.pb[2 + g]], [self.qkvx])
        self.front0(*self.sample_item)
        self.chunk(*self.sample_item, None)
        self.conv_state_out(NS * LS, True)

    def stage2(self):
        S, cm, pb, bank = self.S, self.cm, self.pb, self.bank
        self.lnbc = S.sb("lnbc2", [128, 2, D], F32)
        w_up = S.sb("w_up_sb", [128, 8, 2 * DFF], BF16)
        w_dn = S.sb("w_dn", [128, 22, D], BF16)
        cwf = S.sb("cwf_sb", [128, NFF, 4], F32)
        h1t = [S.sb(f"h1t{i}", [128, D], F32) for i in range(2)]
        h1T = S.sb("h1T", [128, 8, 512], BF16)
        ue = [S.sb(f"ue{i}", [128, 160], F32) for i in range(2)]
        accs = [[S.sb(f"acc{w}{i}", [128, 512], F32) for i in range(2)] for w in range(2)]
        sgt1 = S.sb("sgt", [128, 512], F32)
        sgt = [sgt1, sgt1]
        hc = S.sb("hc", [128, NFF, 4], F32)
        uh = [S.sb(f"uh{i}", [128, NFF, 2], F32) for i in range(2)]
        actT = S.sb("actT", [128, 22, 512], BF16)
        usave = S.sb("usave", [128, NFF, 32], F32)
        st32 = S.sb("st32", [128, 512], F32)
        for i in range(2):
            S.dma(self.lnbc.t[:, i, :], self.lnv_d[4 + i:5 + i, :].partition_broadcast(128), [], [self.lnbc])
        S.dma(cwf.t[:], self.cwf_d, [], [cwf])
        wu_ = self.w_up_d.rearrange("(kc p) n -> p kc n", p=128)
        wd_ = self.w_down_d.rearrange("(j p) n -> p j n", p=128)
        w_up_tb = {}
        wdma = []
        for g in range(3):
            for which in range(2):
                c0 = which * DFF + g * 1024
                c1 = min(c0 + 1024, (which + 1) * DFF)
                tb = S.alias(f"w_up_{which}_{g}", w_up)
                for gg in (2 * g, 2 * g + 1):
                    w_up_tb[(which, gg)] = tb
                for kc in range(0, 8, 4):
                    wdma.append((lambda kc=kc, c0=c0, c1=c1, tb=tb: S.dma(w_up.t[:, kc:kc + 4, c0:c1], wu_[:, kc:kc + 4, c0:c1], [], [tb], cast=True)))
        w_dn_tb = {}
        wddma = []
        for j in range(0, 22, 2):
            tb = S.alias(f"w_dn_{j}", w_dn)
            w_dn_tb[j] = tb
            w_dn_tb[j + 1] = tb
            wddma.append((lambda j=j, tb=tb: S.dma(w_dn.t[:, j:j + 2, :], wd_[:, j:j + 2, :], [], [tb], cast=True)))
        for _ in range(4):
            wdma.pop(0)()
        S.op("pool", lambda e: e.memset(uh[0].t[:], 0.0), [], [uh[0]])

        def conv_out(n, dst, src):
            for g in range(11):
                def tr(e, g=g):
                    for j in range(4):
                        r = e.transpose(bank(g % 4)[0:n, j * 128:(j + 1) * 128], src.t[:, g * 4 + j, 0:n], cm(C_ID))
                    return r
                S.op("pe", tr, [src, self.cmat], [pb[g % 4]])
                S.op("act", lambda e, g=g: e.activation(out=st32.t[0:n, :], in_=bank(g % 4)[0:n, :], func=AF.Copy), [pb[g % 4]], [st32])
                S.dma(dst[:, g * 512:(g + 1) * 512], st32.t[0:n, :], [st32], [], is_out=True)

        tiles = []
        e0 = 0
        while e0 < TP:
            W = min(512, TP - e0)
            tiles.append((e0, W, False))
            e0 += W
        tiles.append((TP, NS * LS, True))
        upb = 0
        for ti, (r0, W, smp) in enumerate(tiles):
            G, L = (NS, LS) if smp else (1, W)
            uprev, ucur = uh[ti % 2], uh[(ti + 1) % 2]
            if not smp and not DBG.get("nohc"):
                S.op("pool", lambda e, uprev=uprev: e.tensor_tensor(out=hc.t[:, :, 0:2], in0=uprev.t[:, :, :], in1=cwf.t[:, :, 0:1].to_broadcast([128, NFF, 2]), op=ALU.mult),
                     [uprev, cwf], [hc])
                S.op("pool", lambda e, uprev=uprev: e.tensor_tensor(out=hc.t[:, :, 2:3], in0=uprev.t[:, :, 1:2], in1=cwf.t[:, :, 1:2], op=ALU.mult),
                     [uprev, cwf, hc], [hc])
            if smp:
                for g in range(11):
                    S.dma(st32.t[0:NS * 2, :], self.sfconv[:, g * 512:(g + 1) * 512], [], [st32])

                    def tr(e, g=g):
                        for j in range(4):
                            r = e.transpose(bank(4 + g % 2)[:, j * 32:(j + 1) * 32], st32.t[0:NS * 2, j * 128:(j + 1) * 128], cm(C_ID, NS * 2, NS * 2))
                        return r
                    S.op("pe", tr, [st32, self.cmat], [pb[4 + g % 2]])
                    S.op("act", lambda e, g=g: e.activation(out=usave.t[:, g * 4:(g + 1) * 4, :],
                                                            in_=bank(4 + g % 2)[:, 0:128].rearrange("p (j c) -> p j c", j=4), func=AF.Copy), [pb[4 + g % 2]], [usave])
            nsub = (W + 127) // 128
            for j in range(nsub):
                C = min(128, W - j * 128)
                ht = h1t[j % 2]
                S.dma(ht.t[0:C, :], self.h1s[r0 + j * 128:r0 + j * 128 + C, :], [self.h1scr], [ht])
                for half in range(2):
                    def tr(e, half=half, ht=ht, C=C):
                        for q in range(4):
                            r = e.transpose(bank(6 + half)[:, q * 128:q * 128 + C], ht.t[0:C, (half * 4 + q) * 128:(half * 4 + q + 1) * 128], cm(C_ID, C, C))
                        return r
                    S.op("pe", tr, [ht, self.cmat], [pb[6 + half]])
                    S.op("act", lambda e, half=half, j=j, C=C: e.activation(out=h1T.t[:, half * 4:half * 4 + 4, j * 128:j * 128 + C],
                                                                            in_=bank(6 + half).rearrange("p (q c) -> p q c", q=4)[:, :, 0:C], func=AF.Copy),
                         [pb[6 + half]], [h1T])
            pend = None
            for jg in range(22):
                par = jg % 2
                if jg % 8 == 1:
                    for _ in range(4):
                        if wdma:
                            wdma.pop(0)()
                if wddma and jg % 2 == 0:
                    wddma.pop(0)()
                if pend is not None:
                    pend()
                for which in (0, 1):
                    acc = accs[which][par]
                    cc = jg + 22 * which
                    b = upb % 4
                    upb += 1
                    wtb = w_up_tb[(which, jg // 4)]

                    def mmu(e, cc=cc, b=b):
                        for kc in range(8):
                            r = e.matmul(bank(b)[:, 0:W], lhsT=w_up.t[:, kc, cc * 128:(cc + 1) * 128], rhs=h1T.t[:, kc, 0:W], start=(kc == 0), stop=(kc == 7))
                        return r
                    S.op("pe", mmu, [wtb, h1T], [pb[b]])
                    if smp:
                        u = ue[cc % 2]
                        uv = u.t[:, 0:G * (L + 2)].rearrange("p (g l) -> p g l", g=G)
                        S.op("act", lambda e, b=b, uv=uv: e.activation(out=uv[:, :, 2:2 + L], in_=bank(b)[:, 0:W].rearrange("p (g l) -> p g l", g=G), func=AF.Copy),
                             [pb[b]], [u])
                        hsrc = usave.t[:, cc, :].rearrange("p (s i) -> p s i", i=2)
                        S.op("pool", lambda e, uv=uv, hsrc=hsrc: e.tensor_copy(out=uv[:, :, 0:2], in_=hsrc), [usave, u], [u])
                        av = acc.t[:, 0:W].rearrange("p (g l) -> p g l", g=G)
                        S.op("act", lambda e, uv=uv, av=av, cc=cc: e.activation(out=av, in_=uv[:, :, 2:2 + L], func=AF.Identity,
                                                                                scale=cwf.t[:, cc, 2:3], bias=cwf.t[:, cc, 3:4]), [u, cwf], [acc])
                        S.op("dve", lambda e, uv=uv, av=av, cc=cc: e.scalar_tensor_tensor(out=av, in0=uv[:, :, 1:1 + L], scalar=cwf.t[:, cc, 1:2], in1=av,
                                                                                           op0=ALU.mult, op1=ALU.add), [u, cwf, acc], [acc])
                        S.op("dve", lambda e, uv=uv, av=av, cc=cc: e.scalar_tensor_tensor(out=av, in0=uv[:, :, 0:L], scalar=cwf.t[:, cc, 0:1], in1=av,
                                                                                           op0=ALU.mult, op1=ALU.add), [u, cwf, acc], [acc])
                        hdst = usave.t[:, cc, :].rearrange("p (s i) -> p s i", i=2)
                        S.op("pool", lambda e, uv=uv, hdst=hdst: e.tensor_copy(out=hdst, in_=uv[:, :, L:L + 2]), [u, usave], [usave])
                    else:
                        S.op("act", lambda e, cc=cc, b=b: e.activation(out=acc.t[:, 0:W], in_=bank(b)[:, 0:W], func=AF.Identity,
                                                                       scale=cwf.t[:, cc, 2:3], bias=cwf.t[:, cc, 3:4]), [pb[b], cwf], [acc])
                        if not DBG.get("noact2"):
                            S.op("dve", lambda e, cc=cc, b=b, ucur=ucur: e.tensor_copy(out=ucur.t[:, cc, 0:2], in_=bank(b)[:, W - 2:W]), [pb[b], acc], [ucur])
                        if not DBG.get("nostt"):
                            S.op("dve", lambda e, cc=cc, b=b, acc=acc: e.scalar_tensor_tensor(out=acc.t[:, 1:W], in0=bank(b)[:, 0:W - 1], scalar=cwf.t[:, cc, 1:2], in1=acc.t[:, 1:W],
                                                                                     op0=ALU.mult, op1=ALU.add), [pb[b], cwf, acc], [acc])
                            S.op("dve", lambda e, cc=cc, b=b, acc=acc: e.scalar_tensor_tensor(out=acc.t[:, 2:W], in0=bank(b)[:, 0:W - 2], scalar=cwf.t[:, cc, 0:1], in1=acc.t[:, 2:W],
                                                                                     op0=ALU.mult, op1=ALU.add), [pb[b], cwf, acc], [acc])
                        if not DBG.get("nopool"):
                            S.op("pool", lambda e, cc=cc, acc=acc: e.tensor_tensor(out=acc.t[:, 0:2], in0=acc.t[:, 0:2], in1=hc.t[:, cc, 0:2], op=ALU.add), [acc, hc], [acc])
                            S.op("pool", lambda e, cc=cc, acc=acc: e.tensor_tensor(out=acc.t[:, 0:1], in0=acc.t[:, 0:1], in1=hc.t[:, cc, 2:3], op=ALU.add), [acc, hc], [acc])
                def fin(jg=jg, par=par):
                    S.op("act", lambda e: e.activation(out=sgt[par].t[:, 0:W], in_=accs[0][par].t[:, 0:W], func=AF.Silu), [accs[0][par]], [sgt[par]])
                    S.op("pool", lambda e: e.tensor_tensor(out=actT.t[:, jg, 0:W], in0=sgt[par].t[:, 0:W], in1=accs[1][par].t[:, 0:W], op=ALU.mult),
                         [sgt[par], accs[1][par]], [actT])
                pend = fin
            pend()
            def reload(j):
                Cj = min(128, W - j * 128)
                S.dma(h1t[j % 2].t[0:Cj, :], self.h1s[r0 + j * 128:r0 + j * 128 + Cj, :], [self.h1scr], [h1t[j % 2]])
            reload(0)
            for j in range(nsub):
                C = min(128, W - j * 128)
                ht = h1t[j % 2]
                if j + 1 < nsub:
                    reload(j + 1)
                for half in range(2):
                    def mmd(e, half=half, j=j, C=C):
                        for jg in range(22):
                            r = e.matmul(bank(4 + 2 * (j % 2) + half)[0:C, :], lhsT=actT.t[:, jg, j * 128:j * 128 + C], rhs=w_dn.t[:, jg, half * 512:(half + 1) * 512],
                                         start=(jg == 0), stop=(jg == 21))
                        return r
                    S.op("pe", mmd, [actT] + [w_dn_tb[j_] for j_ in range(0, 22, 2)], [pb[4 + 2 * (j % 2) + half]])
                    S.op("dve", lambda e, half=half, ht=ht, C=C: e.scalar_tensor_tensor(out=ht.t[0:C, half * 512:(half + 1) * 512], in0=ht.t[0:C, half * 512:(half + 1) * 512],
                                                                                        scalar=ALPHA, in1=bank(4 + 2 * (j % 2) + half)[0:C, :], op0=ALU.mult, op1=ALU.add),
                         [ht, pb[4 + 2 * (j % 2) + half]], [ht])
                self.layer_norm(ht, C, self.lnbc.t[:, 0, :], self.lnbc.t[:, 1, :], self.epsc.t[0:C, 0:1], "ln2")
                if smp:
                    S.dma(self.y_s, ht.t[0:C, :], [ht], [], is_out=True)
                else:
                    e = r0 + j * 128
                    if e == 0:
                        S.dma(self.y_p[0:128 - NMETA, :], ht.t[NMETA:128, :], [ht], [], is_out=True)
                    else:
                        S.dma(self.y_p[e - NMETA:e - NMETA + C, :], ht.t[0:C, :], [ht], [], is_out=True)
            if (not smp) and r0 + W == TP:
                conv_out(2, self.fconv_p, ucur)
        conv_out(NS * 2, self.fconv_s, usave)

    def build(self):
        S = self.S
        with S:
            self.cmat = S.sb("cmat_sb", [128, NCM, 128], F32)
            self.epsc = S.sb("epsc", [128, 4], F32)
            self.ln_st = S.sb("ln_st", [128, 2, 6], F32)
            self.ln_mv = S.sb("ln_mv", [128, 4], F32)
            self.glb = S.sb("glb", [128, 128], F32)
            self.pst = S.ps("pst", [128, 8, 512], F32)
            self.pb = [S.alias(f"pb{i}", self.pst) for i in range(8)]
            self.h1scr = TB("h1scr", None)
            main_stack = S.stack
            S.stack = ExitStack()
            with S.stack:
                if not DBG.get("skip1"):
                    self.stage1()
                S.barrier()
            S.stack = ExitStack()
            with S.stack:
                if not DBG.get("skip2"):
                    self.stage2()
                S.finish()
            S.stack = main_stack
        return self.nc


_PROG = None


def _program():
    global _PROG
    if _PROG is None:
        _PROG = K().build()
    return _PROG


def kernel(x_prompt, x_sample, state_gdn, state_gdn_conv, state_gla, state_ffn_conv, meta_tokens,
           ln_in_g, ln_in_b, w_in, gdn_conv_w, gdn_A_log, gdn_dt_bias, gdn_norm_g, gla_wgk2,
           gla_bgk, gla_norm_g, w_out, ln1_g, ln1_b, w_up, ffn_conv_w, ffn_conv_b, w_down,
           ln2_g, ln2_b):
    f = lambda a: np.ascontiguousarray(np.asarray(a, dtype=np.float32))
    x_prompt, x_sample = f(x_prompt), f(x_sample)
    w_in0 = f(w_in)[0]
    fm_cols = np.r_[0:1536, 2064:2320, 2320:2576, 3088:3104]
    tm_cols = np.r_[1536:1552, 2320:2576, 1552:2064, 2576:3088, 3104:3616]
    w_in_r = np.ascontiguousarray(w_in0[:, np.r_[fm_cols, tm_cols]])
    lnv = np.stack([f(ln_in_g), f(ln_in_b), f(ln1_g)[0], f(ln1_b)[0], f(ln2_g)[0], f(ln2_b)[0]])
    cwg = np.ascontiguousarray(f(gdn_conv_w)[0].T.reshape(12, 128, 4).transpose(1, 0, 2))
    cwf4 = np.concatenate([f(ffn_conv_w)[0], f(ffn_conv_b)], axis=0)
    cwf = np.ascontiguousarray(cwf4.T.reshape(NFF, 128, 4).transpose(1, 0, 2))
    pvec = np.concatenate([f(gdn_A_log)[0], f(gdn_dt_bias)[0], f(gdn_norm_g)[0], f(gla_norm_g)[0], f(gla_bgk)[0]])[None, :]
    shared = dict(meta=f(meta_tokens), cmat=_const_mats(), w_in_r=w_in_r, w_out=f(w_out)[0], w_up=f(w_up)[0], w_down=f(w_down)[0],
                  lnv=np.ascontiguousarray(lnv), cwg=cwg, cwf=cwf, pvec=np.ascontiguousarray(pvec), wgk2=f(gla_wgk2)[0])
    sg, sgc, sl, sfc = f(state_gdn)[0], f(state_gdn_conv)[0], f(state_gla)[0], f(state_ffn_conv)[0]
    in_maps = []
    for c in range(8):
        sl_ = slice(c * NS, (c + 1) * NS)
        m = dict(shared)
        m.update(xp=x_prompt[c], xs=np.ascontiguousarray(x_sample[sl_].reshape(NS * LS, D)), sgdn=sg[sl_],
                 sgconv=np.ascontiguousarray(sgc[sl_].reshape(NS * 3, 1536)), sgla=sl[sl_],
                 sfconv=np.ascontiguousarray(sfc[sl_].reshape(NS * 2, 2 * DFF)))
        in_maps.append(m)
    ncr = DBG.get("ncores", 8)
    res = run_bass_kernel_spmd(_program(), in_maps[:ncr], core_ids=list(range(ncr)))
    r = res.results
    cat = lambda k: np.stack([np.asarray(r[min(c, ncr - 1)][k]) for c in range(8)])
    y_prompt = cat("y_p")
    y_sample = cat("y_s").reshape(128, LS, D)
    gdn_p = cat("gdn_p")[None]
    gconv_p = cat("gconv_p")[None]
    gla_p = cat("gla_p")[None]
    fconv_p = cat("fconv_p")[None]
    gdn_s = cat("gdn_s").reshape(1, 128, 8, 64, 64)
    gconv_s = cat("gconv_s").reshape(1, 128, 3, 1536)
    gla_s = cat("gla_s").reshape(1, 128, 4, 64, 128)
    fconv_s = cat("fconv_s").reshape(1, 128, 2, 2 * DFF)
    outs = (y_prompt, y_sample, gdn_p, gconv_p, gla_p, fconv_p, gdn_s, gconv_s, gla_s, fconv_s)
    return tuple(np.ascontiguousarray(o, dtype=np.float32) for o in outs)
```

```python
from contextlib import ExitStack

import numpy as np
import concourse.bass as bass
import concourse.mybir as mybir
from concourse.bass_utils import run_bass_kernel_spmd

F32 = mybir.dt.float32
BF16 = mybir.dt.bfloat16
F32R = mybir.dt.float32r
AF = mybir.ActivationFunctionType
ALU = mybir.AluOpType
AX = mybir.AxisListType


class TB:
    def __init__(self, name, t):
        self.name = name
        self.t = t
        self.last_w = None
        self.readers = {}
        self.parts = []


class Sched:
    def __init__(self, nc, nslots=40):
        self.nc = nc
        self.stack = ExitStack()
        self.engs = {"pe": nc.tensor, "dve": nc.vector, "act": nc.scalar, "pool": nc.gpsimd, "sp": nc.sync}
        self.nslots = nslots

    def __enter__(self):
        self.stack.__enter__()
        nc = self.nc
        self.sem = {k: self.stack.enter_context(nc.semaphore(f"s_{k}")) for k in self.engs}
        self.cnt = {k: 0 for k in self.engs}
        self.waited = {k: {} for k in self.engs}
        self.slot_sem = [self.stack.enter_context(nc.semaphore(f"d_{i}")) for i in range(self.nslots)]
        self.slot_cnt = [0] * self.nslots
        self.next_slot = {"sp": 0, "pool": 0}
        self.out_deps = []
        self.nbuf = 0
        return self

    def __exit__(self, *a):
        return self.stack.__exit__(*a)

    def sb(self, name, shape, dtype):
        t = self.stack.enter_context(self.nc.sbuf_tensor(name, list(shape), dtype))
        return TB(name, t)

    def ps(self, name, shape, dtype):
        t = self.stack.enter_context(self.nc.psum_tensor(name, list(shape), dtype))
        return TB(name, t)

    def alias(self, name, tb):
        return TB(name, tb.t)

    def _semof(self, key):
        if isinstance(key, tuple):
            return self.slot_sem[key[1]]
        return self.sem[key]

    def _wait(self, eng, dep):
        key, val = dep
        if eng == "pe" and key == "pe":
            return
        w = self.waited[eng]
        if w.get(key, 0) >= val:
            return
        self.engs[eng].wait_ge(self._semof(key), val)
        w[key] = val

    @staticmethod
    def _expand(bufs):
        out = []
        for b in bufs:
            out.append(b)
            out.extend(b.parts)
        return out

    def _deps(self, reads, writes):
        reads, writes = self._expand(reads), self._expand(writes)
        deps = set()
        for b in reads:
            if b.last_w is not None:
                deps.add(b.last_w)
        for b in writes:
            if b.last_w is not None:
                deps.add(b.last_w)
            for k, v in b.readers.items():
                deps.add((k, v))
        return deps

    def _commit(self, me, reads, writes):
        reads, writes = self._expand(reads), self._expand(writes)
        for b in writes:
            b.last_w = me
            b.readers = {}
        for b in reads:
            if b not in writes:
                b.readers[me[0]] = max(b.readers.get(me[0], 0), me[1])

    def op(self, eng, emit, reads=(), writes=()):
        for d in sorted(self._deps(reads, writes), key=str):
            self._wait(eng, d)
        inst = emit(self.engs[eng])
        self.cnt[eng] += 1
        inst.then_inc(self.sem[eng], 1)
        self._commit((eng, self.cnt[eng]), reads, writes)

    def dma(self, out, in_, reads=(), writes=(), cast=False, is_out=False, q=None):
        eng = q or ("pool" if cast else "sp")
        nsp = (self.nslots * 5) // 8
        lo, n = (0, nsp) if eng == "sp" else (nsp, self.nslots - nsp)
        i = lo + self.next_slot[eng]
        self.next_slot[eng] = (self.next_slot[eng] + 1) % n
        if self.slot_cnt[i] > 0:
            self._wait(eng, (("slot", i), 16 * self.slot_cnt[i]))
        for d in sorted(self._deps(reads, writes), key=str):
            self._wait(eng, d)
        inst = self.engs[eng].dma_start(out=out, in_=in_)
        inst.then_inc(self.slot_sem[i], 16)
        self.slot_cnt[i] += 1
        me = (("slot", i), 16 * self.slot_cnt[i])
        self._commit(me, reads, writes)
        if is_out:
            self.out_deps.append(me)

    def finish(self):
        for d in self.out_deps:
            self._wait("sp", d)
        for k in ("pe", "dve", "act", "pool"):
            if self.cnt[k] > 0:
                self._wait("sp", (k, self.cnt[k]))

    def barrier(self):
        deps = [(k, self.cnt[k]) for k in ("pe", "dve", "act", "pool") if self.cnt[k] > 0]
        deps += [(("slot", i), 16 * c) for i, c in enumerate(self.slot_cnt) if c > 0]
        for e in ("pe", "dve", "act", "pool", "sp"):
            for d in deps:
                if d[0] != e:
                    self._wait(e, d)


DBG = {}
D = 1024
SEQ = 2048
NMETA = 16
TP = SEQ + NMETA
NS = 16
LS = 8
DFF = 2816
NFF = 44
ALPHA = 2.0 ** 0.25
NFM = 2064
NTM = 1808
NEG = -30000.0

C_ID, C_ONE, C_BO64 = 0, 1, 2
C_PU, C_PSU, C_PMBT, C_PMBS, C_PM01T = 3, 4, 5, 6, 7
C_SU, C_SSU, C_SBO, C_SMBT, C_SMBS, C_SM01T, C_SBM = 8, 9, 10, 11, 12, 13, 14
NCM = 15


def _const_mats():
    m = np.zeros((NCM, 128, 128), np.float32)
    k = np.arange(128)[:, None]
    c = np.arange(128)[None, :]
    m[C_ID] = (k == c)
    m[C_ONE] = 1.0
    m[C_BO64] = (k // 64 == c // 64)
    m[C_PU] = (k <= c)
    m[C_PSU] = (k > c)
    m[C_PMBT] = np.where(c >= k, 0.0, NEG)
    m[C_PMBS] = np.where(c < k, 0.0, NEG)
    m[C_PM01T] = (c >= k)
    sb = (k // LS == c // LS)
    m[C_SU] = (k <= c) & sb
    m[C_SSU] = (k > c) & sb
    m[C_SBO] = sb
    m[C_SMBT] = np.where((c >= k) & sb, 0.0, NEG)
    m[C_SMBS] = np.where((c < k) & sb, 0.0, NEG)
    m[C_SM01T] = (c >= k) & sb
    m[C_SBM][:, :NS] = (k // LS == np.arange(NS)[None, :])
    return np.ascontiguousarray(m.transpose(1, 0, 2))


def _bc(ap, axis, shape):
    return ap.unsqueeze(axis).to_broadcast(list(shape))


class K:
    def __init__(self):
        nc = self.nc = bass.Bass("TRN2", target_bir_lowering=False)
        di = lambda n, s: nc.dram_tensor(n, list(s), F32, kind="ExternalInput").ap()
        do = lambda n, s: nc.dram_tensor(n, list(s), F32, kind="ExternalOutput").ap()
        self.xp = di("xp", [SEQ, D]); self.xs = di("xs", [NS * LS, D]); self.meta = di("meta", [NMETA, D])
        self.sgdn = di("sgdn", [NS, 8, 64, 64]); self.sgconv = di("sgconv", [NS * 3, 1536])
        self.sgla = di("sgla", [NS, 4, 64, 128]); self.sfconv = di("sfconv", [NS * 2, 2 * DFF])
        self.cmat_d = di("cmat", [128, NCM, 128])
        self.w_in_d = di("w_in_r", [D, NFM + NTM]); self.w_out_d = di("w_out", [D, D])
        self.w_up_d = di("w_up", [D, 2 * DFF]); self.w_down_d = di("w_down", [DFF, D])
        self.lnv_d = di("lnv", [6, D]); self.cwg_d = di("cwg", [128, 12, 4]); self.cwf_d = di("cwf", [128, NFF, 4])
        self.pvec_d = di("pvec", [1, 464]); self.wgk2_d = di("wgk2", [16, 256])
        self.y_p = do("y_p", [SEQ, D]); self.y_s = do("y_s", [NS * LS, D])
        self.gdn_p = do("gdn_p", [8, 64, 64]); self.gconv_p = do("gconv_p", [3, 1536])
        self.gla_p = do("gla_p", [4, 64, 128]); self.fconv_p = do("fconv_p", [2, 2 * DFF])
        self.gdn_s = do("gdn_s", [NS, 8, 64, 64]); self.gconv_s = do("gconv_s", [NS * 3, 1536])
        self.gla_s = do("gla_s", [NS, 4, 64, 128]); self.fconv_s = do("fconv_s", [NS * 2, 2 * DFF])
        self.h1s = nc.dram_tensor("h1s", [TP + NS * LS, D], F32, kind="Internal").ap()
        self.S = Sched(nc)

    def cm(self, idx, r=128, c=128):
        return self.cmat.t[0:r, idx, 0:c]

    def bank(self, b, n=1):
        if n == 1:
            return self.pst.t[:, b, :]
        return self.pst.t[:, b:b + n, :].rearrange("p b f -> p (b f)")

    def layer_norm(self, buf, C, g_ap, b_ap, eps_tile, tag):
        S = self.S
        st, mv = self.ln_st, self.ln_mv
        x = buf.t

        def stats(e):
            e.bn_stats(out=st.t[0:C, 0, :], in_=x[0:C, 0:512])
            return e.bn_stats(out=st.t[0:C, 1, :], in_=x[0:C, 512:1024])
        S.op("dve", stats, [buf], [st])
        S.op("dve", lambda e: e.bn_aggr(out=mv.t[0:C, 0:2], in_=st.t[0:C, :, :].rearrange("p a b -> p (a b)")), [st], [mv])
        S.op("act", lambda e: e.activation(out=mv.t[0:C, 2:3], in_=mv.t[0:C, 1:2], func=AF.Ln, bias=eps_tile, scale=1.0), [mv, self.epsc], [mv])
        S.op("act", lambda e: e.activation(out=mv.t[0:C, 3:4], in_=mv.t[0:C, 2:3], func=AF.Exp, scale=-0.5), [mv], [mv])
        S.op("dve", lambda e: e.scalar_tensor_tensor(out=x[0:C, :], in0=x[0:C, :], scalar=mv.t[0:C, 0:1], in1=g_ap[0:C, :],
                                                      op0=ALU.subtract, op1=ALU.mult), [buf, mv, self.lnbc], [buf])
        S.op("dve", lambda e: e.scalar_tensor_tensor(out=x[0:C, :], in0=x[0:C, :], scalar=mv.t[0:C, 3:4], in1=b_ap[0:C, :],
                                                      op0=ALU.mult, op1=ALU.add), [buf, mv, self.lnbc], [buf])

    def stage1_alloc(self):
        S = self.S
        self.lnbc = S.sb("lnbc", [128, 4, D], F32)
        self.pvec = S.sb("pvec_sb", [128, 464], F32)
        self.negA = S.sb("negA", [128, 8], F32)
        self.wgk2 = S.sb("wgk2_sb", [16, 256], F32)
        self.cwg = S.sb("cwg_sb", [128, 12, 4], F32)
        self.w_in = S.sb("w_in_sb", [128, 8, NFM + NTM], BF16)
        self.w_out = S.sb("w_out_sb", [128, 8, D], BF16)
        self.xhs = [S.sb(f"xh{i}", [128, D], F32) for i in range(2)]
        self.hT = S.sb("hT", [128, 8, 128], BF16)
        self.qkvx = S.sb("qkvx", [128, 12, 176], F32)
        self.fmx = S.sb("fmx", [128, 5, 128], F32)
        self.tm0 = S.sb("tm0", [128, 272], F32)
        self.gv_tok = S.sb("gv_tok", [128, 512], F32)
        self.sg_gdn = S.sb("sg_gdn", [128, 512], F32)
        self.sg_gla = S.sb("sg_gla", [128, 512], F32)
        self.bigs = S.sb("bigs", [128, 6, 1024], F32)
        self.big = [S.alias(f"big{i}", self.bigs) for i in range(6)]
        self.Pc = S.sb("Pc", [128, 1024], F32)
        self.PTc = S.sb("PTc", [128, 1024], F32)
        self.TTc = S.sb("TTc", [128, 1024], F32)
        for tb in (self.Pc, self.PTc, self.TTc):
            tb.parts = [S.alias(f"{tb.name}_hg{g}", tb) for g in range(2)]
        self.kq = S.sb("kq", [128, 4, 2, 128], F32)
        self.wkT = S.sb("wkT", [128, 8, 128], F32)
        self.KTm = self.wkT
        self.QTm = S.sb("QTm", [128, 8, 128], F32)
        self.keTm = S.sb("keTm", [128, 4, 128], F32)
        self.qeTm = S.sb("qeTm", [128, 4, 128], F32)
        self.wv = S.sb("wv", [128, 512], F32)
        self.RK = S.sb("RK", [128, 512], F32)
        self.RV = S.sb("RV", [128, 512], F32)
        self.kdec = S.sb("kdec", [128, 512], F32)
        self.U = self.RV
        self.ogdn = self.RK
        self.Sg = S.sb("Sg", [128, 4, 64], F32)
        self.sc = S.sb("sc", [128, 96], F32)
        self.lt = TB("lt", self.wv.t[:, 0:256])
        self.lt.parts = [self.wv]
        self.ebT = S.sb("ebT", [128, 2, 128], F32)
        self.enbT = S.sb("enbT", [128, 2, 128], F32)
        self.qeT = S.sb("qeT", [128, 2, 128], F32)
        self.PTg = S.sb("PTg", [128, 4, 128], F32)
        self.kd = TB("kd", self.enbT.t[:, :, :].rearrange("p a c -> p (a c)"))
        self.kd.parts = [self.enbT]
        self.ogla = S.sb("ogla", [128, 512], F32)
        self.Sl = S.sb("Sl", [128, 2, 128], F32)
        self.mix = S.sb("mix", [128, D], F32)
        self.mixT = S.sb("mixT", [128, 8, 128], BF16)
        self.otmp = TB("otmp", self.bigs.t[:, 3, 0:512])
        self.otmp.parts = [self.big[3]]

    def bg(self, i, n=1):
        if n == 1:
            return self.bigs.t[:, i, :]
        return self.bigs.t[:, i:i + n, :].rearrange("p b f -> p (b f)")

    def stage1_setup(self):
        S = self.S
        nc = self.nc
        S.dma(self.cmat.t[:], self.cmat_d, [], [self.cmat])
        for i in range(4):
            S.dma(self.lnbc.t[:, i, :], self.lnv_d[i:i + 1, :].partition_broadcast(128), [], [self.lnbc])
        S.dma(self.pvec.t[:], self.pvec_d[0:1, :].partition_broadcast(128), [], [self.pvec])
        S.dma(self.wgk2.t[:], self.wgk2_d, [], [self.wgk2])
        S.dma(self.cwg.t[:], self.cwg_d, [], [self.cwg])
        S.op("pool", lambda e: e.memset(self.epsc.t[:, 0:1], 1e-5), [], [self.epsc])
        S.op("pool", lambda e: e.memset(self.epsc.t[:, 1:2], 1e-6), [self.epsc], [self.epsc])
        S.op("pool", lambda e: e.memset(self.epsc.t[:, 2:3], 1.0), [self.epsc], [self.epsc])
        S.op("pool", lambda e: e.memset(self.epsc.t[:, 3:4], 0.0), [self.epsc], [self.epsc])
        wv_ = self.w_in_d.rearrange("(kc p) n -> p kc n", p=128)
        for kc in range(8):
            S.dma(self.w_in.t[:, kc, :], wv_[:, kc, :], [], [self.w_in], cast=True)
        wo_ = self.w_out_d.rearrange("(kc p) n -> p kc n", p=128)
        for kc in range(0, 8, 4):
            S.dma(self.w_out.t[:, kc:kc + 4, :], wo_[:, kc:kc + 4, :], [], [self.w_out], cast=True)
        S.op("act", lambda e: e.activation(out=self.negA.t[:], in_=self.pvec.t[:, 0:8], func=AF.Exp), [self.pvec], [self.negA])
        S.op("dve", lambda e: e.tensor_scalar(out=self.negA.t[:], in0=self.negA.t[:], scalar1=-1.0, scalar2=None, op0=ALU.mult),
             [self.negA], [self.negA])
        S.op("pool", lambda e: e.memset(self.Sg.t[:], 0.0), [], [self.Sg])
        S.op("pool", lambda e: e.memset(self.Sl.t[:], 0.0), [], [self.Sl])
        S.op("pool", lambda e: e.memset(self.qkvx.t[:], 0.0), [], [self.qkvx])
        for tb in (self.wkT, self.QTm, self.keTm, self.qeTm):
            S.op("pool", lambda e, tb=tb: e.memset(tb.t[:], 0.0), [], [tb])

    def front0(self, e0, C, kind, xh):
        self.front0_load(e0, C, kind, xh)
        self.front0_compute(e0, C, kind, xh)

    def front0_load(self, e0, C, kind, xh):
        S = self.S
        smp = kind == "s"
        if smp:
            S.dma(xh.t[0:C, :], self.xs, [], [xh])
        elif e0 == 0:
            S.dma(xh.t[0:NMETA, :], self.meta, [], [xh])
            S.dma(xh.t[NMETA:128, :], self.xp[0:128 - NMETA, :], [], [xh])
        else:
            S.dma(xh.t[0:C, :], self.xp[e0 - NMETA:e0 - NMETA + C, :], [], [xh])

    def front0_compute(self, e0, C, kind, xh):
        S, cm, pb, bank, hT = self.S, self.cm, self.pb, self.bank, self.hT
        self.layer_norm(xh, C, self.lnbc.t[:, 0, :], self.lnbc.t[:, 1, :], self.epsc.t[0:C, 0:1], "in")
        for half in range(2):
            def tr(e, half=half):
                for j in range(4):
                    kc = half * 4 + j
                    r = e.transpose(bank(half)[:, j * 128:j * 128 + C], xh.t[0:C, kc * 128:(kc + 1) * 128], cm(C_ID, C, C))
                return r
            S.op("pe", tr, [xh, self.cmat], [pb[half]])
            S.op("act", lambda e, half=half: e.activation(
                out=hT.t[:, half * 4:half * 4 + 4, 0:C],
                in_=bank(half).rearrange("p (j c) -> p j c", j=4)[:, :, 0:C], func=AF.Copy), [pb[half]], [hT])

    def chunk(self, e0, C, kind, xh, nxt):
        S = self.S
        self.xh = xh
        cm = self.cm
        pb = self.pb
        bank = self.bank
        big = self.big
        bg = self.bg
        smp = kind == "s"
        if smp:
            iU, iSU, iBO, iMBT, iMBS, iM01 = C_SU, C_SSU, C_SBO, C_SMBT, C_SMBS, C_SM01T
            G, L, nlev = NS, LS, 3
        else:
            iU, iSU, iBO, iMBT, iMBS, iM01 = C_PU, C_PSU, C_ONE, C_PMBT, C_PMBS, C_PM01T
            G, L, nlev = 1, C, {128: 7, 16: 4}[C]
        hT, qkvx, fmx, tm0, kq, sc = self.hT, self.qkvx, self.fmx, self.tm0, self.kq, self.sc
        ident = cm(C_ID)

        if DBG.get("step", 99) < 4:
            return
        qv = qkvx.t[:, :, 0:G * (L + 3)].rearrange("p a (g l) -> p a g l", g=G)
        for grp in range(5):
            b = 2 + (grp % 4)
            ccs = list(range(grp * 4, min(grp * 4 + 4, 17)))

            def mmf(e, ccs=ccs, b=b):
                for j, cc in enumerate(ccs):
                    M = 128 if cc < 16 else 16
                    for kc in range(8):
                        r = e.matmul(bank(b)[0:M, j * 128:j * 128 + C], lhsT=self.w_in.t[:, kc, cc * 128:cc * 128 + M],
                                     rhs=hT.t[:, kc, 0:C], start=(kc == 0), stop=(kc == 7))
                return r
            S.op("pe", mmf, [self.w_in, hT], [pb[b]])
            src = bank(b).rearrange("p (j c) -> p j c", j=4)
            if grp < 3:
                S.op("act", lambda e, grp=grp, src=src: e.activation(
                    out=qv[:, grp * 4:grp * 4 + 4, :, 3:3 + L],
                    in_=src[:, :, 0:C].rearrange("p j (g l) -> p j g l", g=G), func=AF.Copy), [pb[b]], [qkvx])
            elif grp == 3:
                S.op("act", lambda e, src=src: e.activation(out=fmx.t[:, 0:4, 0:C], in_=src[:, :, 0:C], func=AF.Copy), [pb[b]], [fmx])
            else:
                S.op("act", lambda e, src=src: e.activation(out=fmx.t[0:16, 4, 0:C], in_=src[0:16, 0, 0:C], func=AF.Copy), [pb[b]], [fmx])
        if DBG.get("step", 99) < 5:
            return
        tmoff = [NFM, NFM + 272, NFM + 784, NFM + 1296]
        tmn = [272, 512, 512, 512]
        for gi in range(4):
            b = 6 + (gi % 2)

            def mmt(e, gi=gi, b=b):
                for kc in range(8):
                    r = e.matmul(bank(b)[0:C, 0:tmn[gi]], lhsT=hT.t[:, kc, 0:C], rhs=self.w_in.t[:, kc, tmoff[gi]:tmoff[gi] + tmn[gi]],
                                 start=(kc == 0), stop=(kc == 7))
                return r
            S.op("pe", mmt, [self.w_in, hT], [pb[b]])
            if gi == 0:
                S.op("dve", lambda e, b=b: e.tensor_copy(out=tm0.t[0:C, :], in_=bank(b)[0:C, 0:272]), [pb[b]], [tm0])
            elif gi == 1:
                S.op("act", lambda e, b=b: e.activation(out=self.sg_gdn.t[0:C, :], in_=bank(b)[0:C, :], func=AF.Copy), [pb[b]], [self.sg_gdn])
            elif gi == 2:
                S.op("dve", lambda e, b=b: e.tensor_copy(out=self.gv_tok.t[0:C, :], in_=bank(b)[0:C, :]), [pb[b]], [self.gv_tok])
            else:
                S.op("act", lambda e, b=b: e.activation(out=self.sg_gla.t[0:C, :], in_=bank(b)[0:C, :], func=AF.Copy), [pb[b]], [self.sg_gla])
        if nxt is not None:
            self.front0_load(*nxt)
        if DBG.get("step", 99) < 6:
            return
        acc = bg(0, 2)[:, 0:12 * C].rearrange("p (a g l) -> p a g l", a=12, g=G)
        tmp = bg(2, 2)[:, 0:12 * C].rearrange("p (a g l) -> p a g l", a=12, g=G)
        accT, tmpT = [big[0], big[1]], [big[2], big[3]]

        def cwb(i):
            return self.cwg.t[:, :, i:i + 1].unsqueeze(3).to_broadcast([128, 12, G, L])
        S.op("dve", lambda e: e.tensor_tensor(out=acc, in0=qv[:, :, :, 0:L], in1=cwb(0), op=ALU.mult), [qkvx, self.cwg], accT)
        for i in range(1, 4):
            S.op("pool" if i == 1 else "dve", lambda e, i=i: e.tensor_tensor(out=tmp, in0=qv[:, :, :, i:i + L], in1=cwb(i), op=ALU.mult), [qkvx, self.cwg], tmpT)
            S.op("dve", lambda e: e.tensor_tensor(out=acc, in0=acc, in1=tmp, op=ALU.add), accT + tmpT, accT)
        qa = bg(0, 2)[:, 0:12 * C].rearrange("p (a c) -> p a c", a=12)
        S.op("act", lambda e: e.activation(out=qa, in_=qa, func=AF.Silu), accT, accT)
        if DBG.get("step", 99) < 7:
            return
        sq = bg(4)[:, 0:8 * C].rearrange("p (a c) -> p a c", a=8)
        rn = bg(5)[:, 0:8 * C].rearrange("p (a c) -> p a c", a=8)
        for sg in (self.sg_gdn, self.sg_gla):
            S.op("act", lambda e, sg=sg: e.activation(out=sg.t[0:C, :], in_=sg.t[0:C, :], func=AF.Silu), [sg], [sg])
        S.op("act", lambda e: e.activation(out=sq, in_=qa[:, 0:8, :], func=AF.Square), accT, [big[4]])
        S.op("act", lambda e: e.activation(out=sc.t[0:C, 16:24], in_=tm0.t[0:C, 8:16], func=AF.Sigmoid), [tm0], [sc])
        for half in range(2):
            S.op("pe", lambda e, half=half: e.matmul(bank(half)[:, 0:4 * C], lhsT=cm(C_BO64),
                                                     rhs=bg(4)[:, half * 4 * C:(half + 1) * 4 * C], start=True, stop=True),
                 [big[4], self.cmat], [pb[half]])
            S.op("act", lambda e, half=half: e.activation(out=bg(5)[:, half * 4 * C:(half + 1) * 4 * C], in_=bank(half)[:, 0:4 * C],
                                                          func=AF.Ln, bias=self.epsc.t[:, 1:2], scale=1.0), [pb[half], self.epsc], [big[5]])
        S.op("act", lambda e: e.activation(out=bg(5)[:, 0:8 * C], in_=bg(5)[:, 0:8 * C], func=AF.Exp, scale=-0.5), [big[5]], [big[5]])
        S.op("dve", lambda e: e.scalar_tensor_tensor(out=kq.t[:, :, 1, 0:C], in0=qa[:, 0:4, :], scalar=0.125, in1=rn[:, 0:4, :],
                                                      op0=ALU.mult, op1=ALU.mult), accT + [big[5]], [kq])
        S.op("pool", lambda e: e.tensor_tensor(out=kq.t[:, :, 0, 0:C], in0=qa[:, 4:8, :], in1=rn[:, 4:8, :], op=ALU.mult), accT + [big[5]], [kq])
        for h2 in range(2):
            rows = slice(64 * h2, 64 * h2 + 64)
            pad = lambda tb: tb.t[rows, :, 0:C].rearrange("p (a two) c -> p a two c", two=2)[:, :, h2, :]
            S.op("act", lambda e, rows=rows, pad=pad: e.activation(out=pad(self.KTm), in_=kq.t[rows, :, 0, 0:C], func=AF.Copy), [kq], [self.KTm])
            S.op("dve", lambda e, rows=rows, pad=pad: e.tensor_copy(out=pad(self.QTm), in_=kq.t[rows, :, 1, 0:C]), [kq], [self.QTm])
        if DBG.get("step", 99) < 8:
            return
        s_ = lambda a, b_: sc.t[0:C, a:b_]
        S.op("dve", lambda e: e.tensor_tensor(out=s_(0, 8), in0=tm0.t[0:C, 0:8], in1=self.pvec.t[0:C, 8:16], op=ALU.add), [tm0, self.pvec], [sc])
        S.op("act", lambda e: e.activation(out=s_(0, 8), in_=s_(0, 8), func=AF.Exp), [sc], [sc])
        S.op("act", lambda e: e.activation(out=s_(0, 8), in_=s_(0, 8), func=AF.Ln, bias=self.epsc.t[0:C, 2:3], scale=1.0), [sc, self.epsc], [sc])
        S.op("dve", lambda e: e.tensor_tensor(out=s_(8, 16), in0=s_(0, 8), in1=self.negA.t[0:C, :], op=ALU.mult), [sc, self.negA], [sc])

        def mmG(e):
            e.matmul(bank(0)[0:C, 0:8], lhsT=cm(iU, C, C), rhs=s_(8, 16), start=True, stop=True)
            return e.matmul(bank(0)[0:C, 8:16], lhsT=cm(iBO, C, C), rhs=s_(8, 16), start=True, stop=True)
        S.op("pe", mmG, [sc, self.cmat], [pb[0]])
        S.op("dve", lambda e: e.tensor_copy(out=s_(24, 40), in_=bank(0)[0:C, 0:16]), [pb[0]], [sc])
        S.op("act", lambda e: e.activation(out=s_(40, 48), in_=s_(24, 32), func=AF.Exp), [sc], [sc])
        S.op("dve", lambda e: e.tensor_tensor(out=s_(48, 56), in0=s_(32, 40), in1=s_(24, 32), op=ALU.subtract), [sc], [sc])
        S.op("act", lambda e: e.activation(out=s_(48, 56), in_=s_(48, 56), func=AF.Exp), [sc], [sc])
        S.op("act", lambda e: e.activation(out=s_(56, 64), in_=s_(32, 40), func=AF.Exp), [sc], [sc])
        S.op("dve", lambda e: e.tensor_tensor(out=s_(64, 72), in0=s_(16, 24), in1=s_(40, 48), op=ALU.mult), [sc], [sc])
        if DBG.get("step", 99) < 9:
            return
        def trk(e):
            for p in range(4):
                r = e.transpose(bank(6)[0:C, p * 128:(p + 1) * 128], kq.t[:, p, 0, 0:C], ident)
            return r
        S.op("pe", trk, [kq, self.cmat], [pb[6]])

        def trv(e):
            for p in range(4):
                r = e.transpose(bank(7)[0:C, p * 128:(p + 1) * 128], qa[:, 8 + p, :], ident)
            return r
        S.op("pe", trv, accT + [self.cmat], [pb[7]])
        h3 = lambda ap: ap.rearrange("p (h d) -> p h d", h=8)
        S.op("dve", lambda e: e.tensor_tensor(out=h3(self.RK.t[0:C, :]), in0=h3(bank(6)[0:C, :]), in1=_bc(s_(64, 72), 2, [C, 8, 64]), op=ALU.mult),
             [pb[6], sc], [self.RK])
        S.op("dve", lambda e: e.tensor_tensor(out=h3(self.kdec.t[0:C, :]), in0=h3(bank(6)[0:C, :]), in1=_bc(s_(48, 56), 2, [C, 8, 64]), op=ALU.mult),
             [pb[6], sc], [self.kdec])
        S.op("dve", lambda e: e.tensor_tensor(out=h3(self.RV.t[0:C, :]), in0=h3(bank(7)[0:C, :]), in1=_bc(s_(16, 24), 2, [C, 8, 64]), op=ALU.mult),
             [pb[7], sc], [self.RV])
        if DBG.get("step", 99) < 10:
            return
        v3 = lambda i: bg(i)[0:C, 0:8 * C].rearrange("p (h c) -> p h c", h=8)
        S.op("dve", lambda e: e.tensor_tensor(out=v3(2), in0=_bc(cm(iU, C, C), 1, [C, 8, C]), in1=_bc(s_(8, 16), 2, [C, 8, C]), op=ALU.mult),
             [self.cmat, sc], [big[2]])
        for half in range(2):
            S.op("pe", lambda e, half=half: e.matmul(bank(half)[0:C, 0:4 * C], lhsT=cm(C_ONE, C, C),
                                                     rhs=bg(2)[0:C, half * 4 * C:(half + 1) * 4 * C], start=True, stop=True),
                 [big[2], self.cmat], [pb[half]])
        gbc = bank(0, 2)

        def gview(r):
            return self.pst.t[0:r, 0:2, 0:4 * C].rearrange("p b (h c) -> p b h c", h=4)
        v4 = lambda i: bg(i)[0:C, 0:8 * C].rearrange("p (b h c) -> p b h c", b=2, h=4)
        S.op("pool", lambda e: e.tensor_tensor(out=v3(3), in0=_bc(cm(iMBT, C, C), 1, [C, 8, C]), in1=_bc(s_(24, 32), 2, [C, 8, C]), op=ALU.subtract),
             [self.cmat, sc], [big[3]])
        S.op("dve", lambda e: e.tensor_tensor(out=v3(4), in0=_bc(cm(iMBS, C, C), 1, [C, 8, C]), in1=_bc(s_(24, 32), 2, [C, 8, C]), op=ALU.add),
             [self.cmat, sc], [big[4]])
        S.op("dve", lambda e: e.tensor_tensor(out=v4(5), in0=gview(C), in1=v4(3), op=ALU.add), [pb[0], pb[1], big[3]], [big[5]])
        S.op("act", lambda e: e.activation(out=bg(5)[0:C, 0:8 * C], in_=bg(5)[0:C, 0:8 * C], func=AF.Exp), [big[5]], [big[5]])
        S.op("dve", lambda e: e.tensor_tensor(out=v4(1), in0=v4(4), in1=gview(C), op=ALU.subtract), [pb[0], pb[1], big[4]], [big[1]])
        S.op("act", lambda e: e.activation(out=bg(1)[0:C, 0:8 * C], in_=bg(1)[0:C, 0:8 * C], func=AF.Exp), [big[1]], [big[1]])
        if DBG.get("step", 99) < 11:
            return
        def mmkk(e):
            for h in range(8):
                p, h2 = h // 2, h % 2
                ov = bank(2 + h // 2).rearrange("p (hh two c) -> p hh two c", hh=2, two=2)
                if C == 128:
                    r = e.matmul(bank(2 + h // 2)[0:C, (h % 2) * 256:(h % 2) * 256 + 256], lhsT=self.KTm.t[:, h, 0:C],
                                 rhs=kq.t[:, p, :, :].rearrange("p a c -> p (a c)"), start=True, stop=True)
                else:
                    for two in range(2):
                        r = e.matmul(ov[0:C, h % 2, two, 0:C], lhsT=self.KTm.t[:, h, 0:C],
                                     rhs=kq.t[:, p, two, 0:C], start=True, stop=True)
            return r
        S.op("pe", mmkk, [kq, self.KTm], [pb[2], pb[3], pb[4], pb[5]])
        kkv = self.pst.t[0:C, 2:6, :].rearrange("p b (hh two c) -> p b hh two c", hh=2, two=2)
        v5 = lambda i: bg(i)[0:C, 0:8 * C].rearrange("p (b hh c) -> p b hh c", b=4, hh=2)
        S.op("dve", lambda e: e.tensor_tensor(out=v5(2), in0=kkv[:, :, :, 0, 0:C], in1=v5(1), op=ALU.mult), [pb[2], pb[3], pb[4], pb[5], big[1]], [big[2]])
        use_r = (C == 128) and bool(DBG.get("f32r"))
        ro = (lambda ap: ap.bitcast(F32R)) if use_r else (lambda ap: ap)
        ri = (lambda ap: ap.bitcast(F32R)) if use_r else (lambda ap: ap)
        Pc, PTc, TTc = self.Pc, self.PTc, self.TTc
        c3 = lambda tb: tb.t[0:C, 0:8 * C].rearrange("p (h c) -> p h c", h=8)
        c4 = lambda tb: tb.t[0:C, 0:8 * C].rearrange("p (b h c) -> p b h c", b=2, h=4)
        S.op("dve", lambda e: e.scalar_tensor_tensor(out=ro(c3(Pc)), in0=v3(2), scalar=-1.0, in1=_bc(s_(16, 24), 2, [C, 8, C]),
                                                      op0=ALU.mult, op1=ALU.mult), [big[2], sc], [Pc])
        S.op("dve", lambda e: e.tensor_tensor(out=v5(0), in0=kkv[:, :, :, 1, 0:C], in1=v5(5), op=ALU.mult), [pb[2], pb[3], pb[4], pb[5], big[5]] + accT, [big[0]])
        for half in range(2):
            def trn(e, half=half):
                for j in range(4):
                    r = e.transpose(bank(half)[0:C, j * C:(j + 1) * C], c3(Pc)[:, half * 4 + j, :], cm(C_ID, C, C))
                return r
            S.op("pe", trn, [Pc, self.cmat], [pb[half]])
        S.op("act", lambda e: e.activation(out=ro(c4(PTc)), in_=gview(C), func=AF.Copy), [pb[0], pb[1]], [PTc])
        S.op("dve", lambda e: e.tensor_tensor(out=ro(c3(TTc)), in0=c3(PTc), in1=_bc(cm(C_ID, C, C), 1, [C, 8, C]), op=ALU.add), [PTc, self.cmat], [TTc])
        if DBG.get("step", 99) < 12:
            return
        if nxt is not None:
            self.front0_compute(*nxt)
        gla_gen = self.gla_prep(C, iU, iSU, iM01)
        for lev in range(nlev):
            doA, doC, doB = lev >= 1, lev <= nlev - 2, lev <= nlev - 3
            for hg in range(2):
                bA, bB, bC = (2, 3, 4) if hg == 0 else (5, 6, 7)

                def mminv(e, hg=hg, bA=bA, bB=bB, bC=bC, doA=doA, doB=doB, doC=doC):
                    r = None
                    for j in range(4):
                        h = hg * 4 + j
                        o = lambda b_: bank(b_)[0:C, j * C:(j + 1) * C]
                        if doA:
                            r = e.matmul(o(bA), lhsT=ri(c3(Pc)[:, h, :]), rhs=ri(c3(TTc)[:, h, :]), start=True, stop=True)
                        if doC:
                            r = e.matmul(o(bC), lhsT=ri(c3(PTc)[:, h, :]), rhs=ri(c3(Pc)[:, h, :]), start=True, stop=True)
                        if doB:
                            r = e.matmul(o(bB), lhsT=ri(c3(Pc)[:, h, :]), rhs=ri(c3(PTc)[:, h, :]), start=True, stop=True)
                    return r
                wr = ([pb[bA]] if doA else []) + ([pb[bB]] if doB else []) + ([pb[bC]] if doC else [])
                S.op("pe", mminv, [Pc.parts[hg], PTc.parts[hg], TTc.parts[hg]], wr)
                hs = slice(hg * 4 * C, (hg + 1) * 4 * C)
                if doA:
                    S.op("dve", lambda e, bA=bA, hs=hs: e.tensor_tensor(out=ro(TTc.t[0:C, hs]), in0=bank(bA)[0:C, 0:4 * C], in1=TTc.t[0:C, hs], op=ALU.add),
                         [pb[bA], TTc.parts[hg]], [TTc.parts[hg]])
                if doC:
                    S.op("act", lambda e, bC=bC, hs=hs: e.activation(out=ro(Pc.t[0:C, hs]), in_=bank(bC)[0:C, 0:4 * C], func=AF.Copy), [pb[bC]], [Pc.parts[hg]])
                if doB:
                    S.op("act", lambda e, bB=bB, hs=hs: e.activation(out=ro(PTc.t[0:C, hs]), in_=bank(bB)[0:C, 0:4 * C], func=AF.Copy), [pb[bB]], [PTc.parts[hg]])
            for _ in range(4):
                next(gla_gen, None)
        for _ in gla_gen:
            pass
        if DBG.get("step", 99) < 13:
            return
        def mmwv(e):
            for h in range(8):
                r = e.matmul(bank(0)[0:C, h * 64:(h + 1) * 64], lhsT=c3(TTc)[:, h, :], rhs=self.RV.t[0:C, h * 64:(h + 1) * 64], start=True, stop=True)
            return r
        S.op("pe", mmwv, [TTc, self.RV], [pb[0]])
        S.op("act", lambda e: e.activation(out=self.wv.t[0:C, :], in_=bank(0)[0:C, :], func=AF.Copy), [pb[0]], [self.wv])

        def mmwk(e):
            for h in range(8):
                p = h // 2
                r = e.matmul(bank(2 + h // 4)[:, (h % 4) * 128:(h % 4) * 128 + C], lhsT=self.RK.t[0:C, p * 128:(p + 1) * 128], rhs=c3(TTc)[:, h, :],
                             start=True, stop=True)
            return r
        S.op("pe", mmwk, [TTc, self.RK], [pb[2], pb[3]])
        wkv = self.pst.t[:, 2:4, :].rearrange("p b (hh two c) -> p (b hh) two c", hh=2, two=2)
        for h2 in range(2):
            rows = slice(64 * h2, 64 * h2 + 64)
            S.op("dve" if h2 == 0 else "act",
                 (lambda e, rows=rows, h2=h2: e.tensor_copy(out=self.wkT.t[rows, :, 0:C].rearrange("p (a two) c -> p a two c", two=2)[:, :, h2, :], in_=wkv[rows, :, h2, 0:C])) if h2 == 0 else
                 (lambda e, rows=rows, h2=h2: e.activation(out=self.wkT.t[rows, :, 0:C].rearrange("p (a two) c -> p a two c", two=2)[:, :, h2, :], in_=wkv[rows, :, h2, 0:C], func=AF.Copy)),
                 [pb[2], pb[3]], [self.wkT])
        if DBG.get("step", 99) < 14:
            return
        for _ in gla_gen:
            pass
        if DBG.get("step", 99) < 15:
            return
        if smp:
            self.state_sample(C)
        else:
            self.state_prompt(C)
        if DBG.get("step", 99) < 16:
            return
        self.post_mix(e0, C, smp)
        if DBG.get("step", 99) < 17:
            return
        if not smp:
            S.op("pool", lambda e: e.tensor_copy(out=qkvx.t[:, :, 0:3], in_=qkvx.t[:, :, L:L + 3]), [qkvx], [qkvx])

    def gla_prep(self, C, iU, iSU, iM01):
        S, cm, pb, bank = self.S, self.cm, self.pb, self.bank
        fmx, lt = self.fmx, self.lt
        yield
        S.op("pe", lambda e: e.matmul(bank(0)[0:C, 0:256], lhsT=fmx.t[0:16, 4, 0:C], rhs=self.wgk2.t[:, :], start=True, stop=True),
             [fmx, self.wgk2], [pb[0]])
        yield
        S.op("dve", lambda e: e.tensor_tensor(out=lt.t[0:C, :], in0=bank(0)[0:C, 0:256], in1=self.pvec.t[0:C, 208:464], op=ALU.add), [pb[0], self.pvec], [lt])
        yield
        S.op("act", lambda e: e.activation(out=lt.t[0:C, :], in_=lt.t[0:C, :], func=AF.Exp, scale=-1.0), [lt], [lt])
        yield
        S.op("act", lambda e: e.activation(out=lt.t[0:C, :], in_=lt.t[0:C, :], func=AF.Ln, bias=self.epsc.t[0:C, 2:3], scale=1.0), [lt, self.epsc], [lt])

        yield
        def mmbc(e):
            for p in range(2):
                r = e.matmul(bank(1)[:, p * 128:p * 128 + C], lhsT=lt.t[0:C, p * 128:(p + 1) * 128], rhs=cm(iU, C, C), start=True, stop=True)
            return r
        yield
        S.op("pe", mmbc, [lt, self.cmat], [pb[1]])
        bcv = bank(1)[:, 0:256].rearrange("p (a c) -> p a c", a=2)[:, :, 0:C]
        yield
        S.op("act", lambda e: e.activation(out=self.ebT.t[:, :, 0:C], in_=bcv, func=AF.Exp, scale=-1.0 / 16.0), [pb[1]], [self.ebT])
        yield
        S.op("act", lambda e: e.activation(out=self.enbT.t[:, :, 0:C], in_=bcv, func=AF.Exp, scale=1.0 / 16.0), [pb[1]], [self.enbT])
        yield
        S.op("dve", lambda e: e.scalar_tensor_tensor(out=self.qeT.t[:, :, 0:C], in0=fmx.t[:, 0:2, 0:C], scalar=0.125, in1=self.ebT.t[:, :, 0:C],
                                                      op0=ALU.mult, op1=ALU.mult), [fmx, self.ebT], [self.qeT])
        yield
        for h2 in range(2):
            rows = slice(64 * h2, 64 * h2 + 64)
            pad = lambda tb: tb.t[rows, :, 0:C].rearrange("p (a two) c -> p a two c", two=2)[:, :, h2, :]
            S.op("pool", lambda e, rows=rows, pad=pad: e.tensor_tensor(out=pad(self.keTm), in0=fmx.t[rows, 2:4, 0:C], in1=self.enbT.t[rows, :, 0:C], op=ALU.mult),
                 [fmx, self.enbT], [self.keTm])
            S.op("act", lambda e, rows=rows, pad=pad: e.activation(out=pad(self.qeTm), in_=self.qeT.t[rows, :, 0:C], func=AF.Copy), [self.qeT], [self.qeTm])

        yield
        def mmA(e):
            for h in range(4):
                p, h2 = h // 2, h % 2
                rows = slice(64 * h2, 64 * h2 + 64)
                r = e.matmul(bank(0)[0:C, h * 128:h * 128 + C], lhsT=self.keTm.t[:, h, 0:C], rhs=self.qeT.t[:, p, 0:C], start=True, stop=True)
            return r
        yield
        S.op("pe", mmA, [self.keTm, self.qeT], [pb[0]])
        yield
        S.op("dve", lambda e: e.tensor_tensor(out=self.PTg.t[0:C, :, 0:C], in0=bank(0).rearrange("p (h c) -> p h c", h=4)[0:C, :, 0:C],
                                              in1=_bc(cm(iM01, C, C), 1, [C, 4, C]), op=ALU.mult), [pb[0], self.cmat], [self.PTg])
        yield
        S.op("pe", lambda e: e.matmul(bank(1)[0:C, 0:256], lhsT=cm(iSU, C, C), rhs=lt.t[0:C, :], start=True, stop=True), [lt, self.cmat], [pb[1]])
        yield
        S.op("act", lambda e: e.activation(out=self.kd.t[0:C, :], in_=bank(1)[0:C, 0:256], func=AF.Exp, scale=-1.0 / 16.0), [pb[1]], [self.kd])
        yield
        S.op("pool", lambda e: e.tensor_tensor(out=self.kd.t[0:C, :], in0=self.kd.t[0:C, :], in1=self.tm0.t[0:C, 16:272], op=ALU.mult), [self.kd, self.tm0], [self.kd])

    def state_prompt(self, C):
        S, cm, pb, bank, big, bg = self.S, self.cm, self.pb, self.bank, self.big, self.bg
        sc, kq = self.sc, self.kq
        s_ = lambda a, b_: sc.t[0:C, a:b_]
        v3 = lambda i: bg(i)[0:C, 0:8 * C].rearrange("p (h c) -> p h c", h=8)
        Sg, Sl, U = self.Sg, self.Sl, self.U
        R = lambda h2: slice(64 * h2, 64 * h2 + 64)
        h3 = lambda ap: ap.rearrange("p (h d) -> p h d", h=8)
        S.op("pe", lambda e: e.matmul(bank(1)[:, 0:8], lhsT=cm(C_ONE, C, 128), rhs=s_(8, 16), start=True, stop=True), [sc, self.cmat], [pb[1]])
        S.op("act", lambda e: e.activation(out=self.glb.t[:, 0:8], in_=bank(1)[:, 0:8], func=AF.Exp), [pb[1]], [self.glb])

        def mm1(e):
            for h in range(8):
                p, h2 = h // 2, h % 2
                r = e.matmul(bank(6)[0:C, h * 64:(h + 1) * 64], lhsT=self.wkT.t[:, h, 0:C], rhs=Sg.t[:, p, :], start=True, stop=True)
            return r
        S.op("pe", mm1, [self.wkT, Sg], [pb[6]])
        def mg1(e):
            for h in range(4):
                p, h2 = h // 2, h % 2
                r = e.matmul(bank(2)[0:C, h * 128:(h + 1) * 128], lhsT=self.qeTm.t[:, h, 0:C], rhs=Sl.t[:, p, :], start=True, stop=True)
            return r
        S.op("pe", mg1, [self.qeTm, Sl], [pb[2]])
        def mg2(e):
            for h in range(4):
                r = e.matmul(bank(3)[0:C, h * 128:(h + 1) * 128], lhsT=self.PTg.t[0:C, h, 0:C], rhs=self.gv_tok.t[0:C, h * 128:(h + 1) * 128], start=True, stop=True)
            return r
        S.op("pe", mg2, [self.PTg, self.gv_tok], [pb[3]])
        def mg3(e):
            for h in range(4):
                p = h // 2
                r = e.matmul(bank(4)[:, h * 128:(h + 1) * 128], lhsT=self.kd.t[0:C, p * 128:(p + 1) * 128], rhs=self.gv_tok.t[0:C, h * 128:(h + 1) * 128], start=True, stop=True)
            return r
        S.op("pe", mg3, [self.kd, self.gv_tok], [pb[4]])
        S.op("dve", lambda e: e.tensor_tensor(out=U.t[0:C, :], in0=self.wv.t[0:C, :], in1=bank(6)[0:C, :], op=ALU.subtract), [self.wv, pb[6]], [U])

        def mm2(e):
            for h in range(8):
                p, h2 = h // 2, h % 2
                r = e.matmul(bank(7)[0:C, h * 64:(h + 1) * 64], lhsT=self.QTm.t[:, h, 0:C], rhs=Sg.t[:, p, :], start=True, stop=True)
            return r
        S.op("pe", mm2, [self.QTm, Sg], [pb[7]])

        def mm3(e):
            for h in range(8):
                r = e.matmul(bank(0)[0:C, h * 64:(h + 1) * 64], lhsT=v3(0)[:, h, :], rhs=U.t[0:C, h * 64:(h + 1) * 64], start=True, stop=True)
            return r
        S.op("pe", mm3, [big[0], U], [pb[0]])
        S.op("act", lambda e: e.activation(out=self.ogla.t[0:C, :], in_=bank(2)[0:C, :], func=AF.Copy), [pb[2]], [self.ogla])
        S.op("dve", lambda e: e.tensor_tensor(out=self.ogla.t[0:C, :], in0=self.ogla.t[0:C, :], in1=bank(3)[0:C, :], op=ALU.add), [self.ogla, pb[3]], [self.ogla])
        S.op("dve", lambda e: e.tensor_tensor(out=h3(self.otmp.t[0:C, :]), in0=h3(bank(7)[0:C, :]), in1=_bc(s_(40, 48), 2, [C, 8, 64]), op=ALU.mult),
             [pb[7], sc], [self.otmp])
        S.op("dve", lambda e: e.tensor_tensor(out=self.ogdn.t[0:C, :], in0=self.otmp.t[0:C, :], in1=bank(0)[0:C, :], op=ALU.add), [self.otmp, pb[0]], [self.ogdn])

        def mm4(e):
            for h in range(8):
                p = h // 2
                r = e.matmul(bank(1)[:, h * 64:(h + 1) * 64], lhsT=self.kdec.t[0:C, p * 128:(p + 1) * 128], rhs=U.t[0:C, h * 64:(h + 1) * 64], start=True, stop=True)
            return r
        S.op("pe", mm4, [self.kdec, U, self.glb], [pb[1]])
        for h2 in range(2):
            glv = self.glb.t[R(h2), 0:8].rearrange("p (a two) -> p a two", two=2)[:, :, h2]
            psv = bank(1).rearrange("p (a two v) -> p a two v", two=2, v=64)[R(h2), :, h2, :]
            S.op("dve", lambda e, h2=h2, glv=glv: e.tensor_tensor(out=Sg.t[R(h2), :, :], in0=Sg.t[R(h2), :, :], in1=_bc(glv, 2, [64, 4, 64]), op=ALU.mult),
                 [Sg, self.glb], [Sg])
            S.op("dve", lambda e, h2=h2, psv=psv: e.tensor_tensor(out=Sg.t[R(h2), :, :], in0=Sg.t[R(h2), :, :], in1=psv, op=ALU.add), [Sg, pb[1]], [Sg])


        for h in range(4):
            p, h2 = h // 2, h % 2
            S.op("dve", lambda e, p=p, h2=h2, h=h: e.scalar_tensor_tensor(
                out=Sl.t[R(h2), p, :], in0=Sl.t[R(h2), p, :], scalar=self.ebT.t[R(h2), p, C - 1:C], in1=bank(4)[R(h2), h * 128:(h + 1) * 128],
                op0=ALU.mult, op1=ALU.add), [Sl, self.ebT, pb[4]], [Sl])

    def state_sample(self, C):
        S, cm, pb, bank, big, bg = self.S, self.cm, self.pb, self.bank, self.big, self.bg
        sc, kq = self.sc, self.kq
        s_ = lambda a, b_: sc.t[0:C, a:b_]
        v3 = lambda i: bg(i)[0:C, 0:8 * C].rearrange("p (h c) -> p h c", h=8)
        U = self.U
        R = lambda h2: slice(64 * h2, 64 * h2 + 64)
        h3 = lambda ap: ap.rearrange("p (h d) -> p h d", h=8)
        bm = cm(C_SBM, C, NS)
        gsel = bg(5)[0:C, 0:128].rearrange("p (h s) -> p h s", h=8)
        S.op("pool", lambda e: e.tensor_tensor(out=gsel, in0=_bc(s_(8, 16), 2, [C, 8, NS]), in1=_bc(bm, 1, [C, 8, NS]), op=ALU.mult), [sc, self.cmat], [big[5]])
        S.op("pe", lambda e: e.matmul(bank(1)[:, 0:128], lhsT=cm(C_ONE), rhs=bg(5)[0:C, 0:128], start=True, stop=True), [big[5], self.cmat], [pb[1]])
        S.op("act", lambda e: e.activation(out=self.glb.t[:, 0:128], in_=bank(1)[:, 0:128], func=AF.Exp), [pb[1]], [self.glb])
        glbs = self.glb.t[:, 0:128].rearrange("p (h s) -> p h s", h=8)
        S0 = bg(4).rearrange("p (s v) -> p s v", s=NS)
        tmp = bg(5)[0:C, :].rearrange("p (s v) -> p s v", s=NS)
        tmpT = bg(5)[0:C, :].rearrange("p (s v) -> p v s", s=NS)
        Ub = bg(1)[0:C, :].rearrange("p (s v) -> p s v", s=NS)
        ps67 = self.pst.t[:, 6:8, :].rearrange("p b (s v) -> p (b s) v", v=64)
        wks, o1s = self.otmp, self.ogdn
        S0v = lambda ap: ap.rearrange("p (s v) -> p s v", s=NS)
        S0s = [(S0v(bg(4)), big[4]), (S0v(bg(2)), big[2])]
        scr = [dict(tmp=S0v(bg(5)[0:C, :]), tmpT=bg(5)[0:C, :].rearrange("p (s v) -> p v s", s=NS), tmpB=big[5],
                    Ub=S0v(bg(1)[0:C, :]), Ubf=bg(1), UbB=big[1], bk=6),
               dict(tmp=S0v(self.Pc.t[0:C, :]), tmpT=self.Pc.t[0:C, :].rearrange("p (s v) -> p v s", s=NS), tmpB=self.Pc,
                    Ub=S0v(self.PTc.t[0:C, :]), Ubf=self.PTc.t, UbB=self.PTc, bk=4)]

        def load(p):
            S0, S0t = S0s[p % 2]
            for h2 in range(2):
                S.dma(S0[R(h2), :, :], self.sgdn[:, 2 * p + h2, :, :].rearrange("s d v -> d s v"), [], [S0t])

        def head_chain(p, h2):
            h = 2 * p + h2
            S0, S0t = S0s[p % 2]
            q = scr[h2]
            bk = q["bk"]
            psx = self.pst.t[:, bk:bk + 2, :].rearrange("p b (s v) -> p (b s) v", v=64)
            for (lhs, lhsb, dst) in ((self.wkT.t[:, h, 0:C], self.wkT, wks), (self.QTm.t[:, h, 0:C], self.QTm, o1s)):
                def mma(e, lhs=lhs):
                    for i in range(2):
                        r = e.matmul(bank(bk + i)[0:C, :], lhsT=lhs, rhs=S0[:, i * 8:(i + 1) * 8, :].rearrange("p s v -> p (s v)"), start=True, stop=True)
                    return r
                S.op("pe", mma, [lhsb, S0t], [pb[bk], pb[bk + 1]])
                yield
                S.op("dve", lambda e: e.tensor_tensor(out=q["tmp"], in0=psx[0:C], in1=_bc(bm, 2, [C, NS, 64]), op=ALU.mult), [pb[bk], pb[bk + 1], self.cmat], [q["tmpB"]])
                yield
                S.op("dve", lambda e, dst=dst: e.tensor_reduce(out=dst.t[0:C, h * 64:(h + 1) * 64], in_=q["tmpT"], op=ALU.add, axis=AX.X), [q["tmpB"]], [dst])
                yield
            cs = slice(h * 64, (h + 1) * 64)
            S.op("dve", lambda e: e.tensor_tensor(out=U.t[0:C, cs], in0=self.wv.t[0:C, cs], in1=wks.t[0:C, cs], op=ALU.subtract), [self.wv, wks], [U])
            yield
            S.op("pe", lambda e: e.matmul(bank(0)[0:C, cs], lhsT=v3(0)[:, h, :], rhs=U.t[0:C, cs], start=True, stop=True), [big[0], U], [pb[0]])
            yield
            S.op("pool", lambda e: e.tensor_tensor(out=q["Ub"], in0=_bc(U.t[0:C, cs], 1, [C, NS, 64]), in1=_bc(bm, 2, [C, NS, 64]), op=ALU.mult),
                 [U, self.cmat], [q["UbB"]])
            yield

            def mmb(e):
                for i in range(2):
                    r = e.matmul(bank(bk + i)[:, :], lhsT=self.kdec.t[0:C, p * 128:(p + 1) * 128], rhs=q["Ubf"][0:C, i * 512:(i + 1) * 512], start=True, stop=True)
                return r
            S.op("pe", mmb, [self.kdec, q["UbB"]], [pb[bk], pb[bk + 1]])
            yield
            S.op("dve", lambda e: e.tensor_tensor(out=S0[R(h2), :, :], in0=S0[R(h2), :, :], in1=_bc(glbs[R(h2), h, :], 2, [64, NS, 64]), op=ALU.mult),
                 [S0t, self.glb], [S0t])
            yield
            S.op("dve", lambda e: e.tensor_tensor(out=S0[R(h2), :, :], in0=S0[R(h2), :, :], in1=psx[R(h2)], op=ALU.add), [S0t, pb[bk], pb[bk + 1]], [S0t])
            yield
            S.dma(self.gdn_s[:, h, :, :].rearrange("s d v -> d s v"), S0[R(h2), :, :], [S0t], [], is_out=True)

        load(0)
        for p in range(4):
            if p + 1 < 4:
                load(p + 1)
            gens = [head_chain(p, 0), head_chain(p, 1)]
            while gens:
                for g_ in list(gens):
                    if next(g_, "done") == "done":
                        gens.remove(g_)
        S.op("dve", lambda e: e.tensor_tensor(out=h3(o1s.t[0:C, :]), in0=h3(o1s.t[0:C, :]), in1=_bc(s_(40, 48), 2, [C, 8, 64]), op=ALU.mult), [o1s, sc], [o1s])
        S.op("dve", lambda e: e.tensor_tensor(out=self.ogdn.t[0:C, :], in0=o1s.t[0:C, :], in1=bank(0)[0:C, :], op=ALU.add), [o1s, pb[0]], [self.ogdn])
        S0g = bg(0, 2).rearrange("p (s v) -> p s v", s=NS)
        tg = bg(2, 2)[0:C, :].rearrange("p (s v) -> p s v", s=NS)
        tgT = bg(2, 2)[0:C, :].rearrange("p (s v) -> p v s", s=NS)
        Vb = bg(4, 2)[0:C, :].rearrange("p (s v) -> p s v", s=NS)
        ps25 = self.pst.t[:, 2:6, :].rearrange("p b (s v) -> p (b s) v", v=128)
        S0gT, tgB, VbB = [big[0], big[1]], [big[2], big[3]], [big[4], big[5]]
        pbs = [pb[2], pb[3], pb[4], pb[5]]
        for p in range(2):
            for h2 in range(2):
                S.dma(S0g[R(h2), :, :], self.sgla[:, 2 * p + h2, :, :].rearrange("s d v -> d s v"), [], S0gT)
            for h2 in range(2):
                h = 2 * p + h2
                cs = slice(h * 128, (h + 1) * 128)

                def mmq(e, p=p, h2=h2, h=h):
                    for i in range(4):
                        r = e.matmul(bank(2 + i)[0:C, :], lhsT=self.qeTm.t[:, h, 0:C], rhs=S0g[:, i * 4:(i + 1) * 4, :].rearrange("p s v -> p (s v)"), start=True, stop=True)
                    return r
                S.op("pe", mmq, [self.qeTm] + S0gT, pbs)
                S.op("dve", lambda e: e.tensor_tensor(out=tg, in0=ps25[0:C], in1=_bc(bm, 2, [C, NS, 128]), op=ALU.mult), pbs + [self.cmat], tgB)
                S.op("dve", lambda e, cs=cs: e.tensor_reduce(out=self.ogla.t[0:C, cs], in_=tgT, op=ALU.add, axis=AX.X), tgB, [self.ogla])
                S.op("pool", lambda e, cs=cs: e.tensor_tensor(out=Vb, in0=_bc(self.gv_tok.t[0:C, cs], 1, [C, NS, 128]), in1=_bc(bm, 2, [C, NS, 128]), op=ALU.mult),
                     [self.gv_tok, self.cmat], VbB)

                def mmv(e, p=p):
                    for i in range(4):
                        r = e.matmul(bank(2 + i)[:, :], lhsT=self.kd.t[0:C, p * 128:(p + 1) * 128], rhs=bg(4, 2)[0:C, i * 512:(i + 1) * 512], start=True, stop=True)
                    return r
                S.op("pe", mmv, [self.kd] + VbB, pbs)
                ebl = self.ebT.t[R(h2), p, :].rearrange("p (s l) -> p s l", l=LS)[:, :, LS - 1]
                S.op("dve", lambda e, h2=h2, ebl=ebl: e.tensor_tensor(out=S0g[R(h2), :, :], in0=S0g[R(h2), :, :], in1=_bc(ebl, 2, [64, NS, 128]), op=ALU.mult),
                     S0gT + [self.ebT], S0gT)
                S.op("dve", lambda e, h2=h2: e.tensor_tensor(out=S0g[R(h2), :, :], in0=S0g[R(h2), :, :], in1=ps25[R(h2)], op=ALU.add), S0gT + pbs, S0gT)
                S.dma(self.gla_s[:, h, :, :].rearrange("s d v -> d s v"), S0g[R(h2), :, :], S0gT, [], is_out=True)

        def mg2(e):
            for h in range(4):
                r = e.matmul(bank(6)[0:C, h * 128:(h + 1) * 128], lhsT=self.PTg.t[0:C, h, 0:C], rhs=self.gv_tok.t[0:C, h * 128:(h + 1) * 128], start=True, stop=True)
            return r
        S.op("pe", mg2, [self.PTg, self.gv_tok], [pb[6]])
        S.op("dve", lambda e: e.tensor_tensor(out=self.ogla.t[0:C, :], in0=self.ogla.t[0:C, :], in1=bank(6)[0:C, :], op=ALU.add), [self.ogla, pb[6]], [self.ogla])

    def post_mix(self, e0, C, smp):
        S, cm, pb, bank = self.S, self.cm, self.pb, self.bank
        sc, mix, xh = self.sc, self.mix, self.xh
        s_ = lambda a, b_: sc.t[0:C, a:b_]
        if not mix.parts:
            mix.parts = [S.alias("mixA", mix), S.alias("mixB", mix)]
            self.scn = [S.alias("scA", sc), S.alias("scB", sc)]

        def norm_chain(o, nh, dv, col0, gcol, sg, sco, sqb, mixp, scp):
            v = lambda ap: ap.rearrange("p (h d) -> p h d", h=nh)
            yield
            S.op("dve", lambda e, o=o: e.tensor_tensor(out=sqb.t[0:C, :], in0=o.t[0:C, :], in1=o.t[0:C, :], op=ALU.mult), [o], [sqb])
            yield
            S.op("dve", lambda e, v=v, sco=sco, nh=nh: e.tensor_reduce(out=s_(sco, sco + nh), in_=v(sqb.t[0:C, :]), op=ALU.add, axis=AX.X), [sqb], [scp])
            yield
            S.op("act", lambda e, sco=sco, nh=nh, dv=dv: e.activation(out=s_(sco, sco + nh), in_=s_(sco, sco + nh), func=AF.Ln,
                                                                      bias=self.epsc.t[0:C, 1:2], scale=1.0 / dv), [scp, self.epsc], [scp])
            yield
            S.op("act", lambda e, sco=sco, nh=nh: e.activation(out=s_(sco, sco + nh), in_=s_(sco, sco + nh), func=AF.Exp, scale=-0.5), [scp], [scp])
            mv_ = v(mix.t[0:C, col0:col0 + 512])
            yield
            S.op("dve", lambda e, o=o, v=v, mv_=mv_, sco=sco, nh=nh, dv=dv: e.tensor_tensor(out=mv_, in0=v(o.t[0:C, :]), in1=_bc(s_(sco, sco + nh), 2, [C, nh, dv]), op=ALU.mult),
                 [o, scp], [mixp])
            yield
            S.op("dve", lambda e, mv_=mv_, gcol=gcol, nh=nh, dv=dv: e.tensor_tensor(out=mv_, in0=mv_, in1=_bc(self.pvec.t[0:C, gcol:gcol + dv], 1, [C, nh, dv]), op=ALU.mult),
                 [mixp, self.pvec], [mixp])
            yield
            S.op("dve", lambda e, col0=col0, sg=sg: e.tensor_tensor(out=mix.t[0:C, col0:col0 + 512], in0=mix.t[0:C, col0:col0 + 512], in1=sg.t[0:C, :], op=ALU.mult),
                 [mixp, sg], [mixp])
        gens = [norm_chain(self.ogdn, 8, 64, 0, 16, self.sg_gdn, 72, self.otmp, mix.parts[0], self.scn[0]),
                norm_chain(self.ogla, 4, 128, 512, 80, self.sg_gla, 80, self.kdec, mix.parts[1], self.scn[1])]
        while gens:
            for g_ in list(gens):
                if next(g_, 'done') == 'done':
                    gens.remove(g_)

        for half in range(2):
            def tr(e, half=half):
                for j in range(4):
                    kc = half * 4 + j
                    r = e.transpose(bank(half)[:, j * 128:j * 128 + C], mix.t[0:C, kc * 128:(kc + 1) * 128], cm(C_ID, C, C))
                return r
            S.op("pe", tr, [mix, self.cmat], [pb[half]])
            S.op("act", lambda e, half=half: e.activation(out=self.mixT.t[:, half * 4:half * 4 + 4, 0:C],
                                                          in_=bank(half).rearrange("p (j c) -> p j c", j=4)[:, :, 0:C], func=AF.Copy), [pb[half]], [self.mixT])
        for half in range(2):
            def mmo(e, half=half):
                for kc in range(8):
                    r = e.matmul(bank(6 + half)[0:C, :], lhsT=self.mixT.t[:, kc, 0:C], rhs=self.w_out.t[:, kc, half * 512:(half + 1) * 512],
                                 start=(kc == 0), stop=(kc == 7))
                return r
            S.op("pe", mmo, [self.mixT, self.w_out], [pb[6 + half]])
            S.op("dve", lambda e, half=half: e.scalar_tensor_tensor(out=xh.t[0:C, half * 512:(half + 1) * 512], in0=xh.t[0:C, half * 512:(half + 1) * 512],
                                                                     scalar=ALPHA, in1=bank(6 + half)[0:C, :], op0=ALU.mult, op1=ALU.add), [xh, pb[6 + half]], [xh])
        self.layer_norm(xh, C, self.lnbc.t[:, 2, :], self.lnbc.t[:, 3, :], self.epsc.t[0:C, 0:1], "ln1")
        row0 = TP if smp else e0
        S.dma(self.h1s[row0:row0 + C, :], xh.t[0:C, :], [xh], [self.h1scr])

    def conv_state_out(self, C, smp):
        S, cm, pb, bank = self.S, self.cm, self.pb, self.bank
        if smp:
            n = NS * 3
            cst = self.bg(3)[:, 0:12 * n].rearrange("p (a c) -> p a c", a=12)
            S.op("pool", lambda e: e.tensor_copy(out=cst.rearrange("p a (s l) -> p a s l", l=3),
                                                 in_=self.qkvx.t[:, :, 0:NS * 11].rearrange("p a (s l) -> p a s l", l=11)[:, :, :, 8:11]),
                 [self.qkvx], [self.big[3]])
            src = lambda cc: cst[:, cc, :]
            dst = self.gconv_s
        else:
            n = 3
            src = lambda cc: self.qkvx.t[:, cc, 0:3]
            dst = self.gconv_p
        stage = self.bg(1, 2)[0:n, 0:1536]
        for g in range(3):
            def tr(e, g=g):
                for j in range(4):
                    r = e.transpose(bank(2 + g)[0:n, j * 128:(j + 1) * 128], src(g * 4 + j), cm(C_ID))
                return r
            S.op("pe", tr, [self.qkvx, self.big[3], self.cmat], [pb[2 + g]])
            S.op("act", lambda e, g=g: e.activation(out=stage[:, g * 512:(g + 1) * 512], in_=bank(2 + g)[0:n, :], func=AF.Copy), [pb[2 + g]], [self.big[1], self.big[2]])
        S.dma(dst, stage, [self.big[1], self.big[2]], [], is_out=True)

    def stage1(self):
        S = self.S
        self.stage1_alloc()
        self.stage1_setup()
        R = lambda h2: slice(64 * h2, 64 * h2 + 64)
        plist = []
        e0 = 0
        while e0 < TP:
            C = min(128, TP - e0)
            skip = (DBG.get("maxchunks") is not None and e0 // 128 >= DBG["maxchunks"] and C == 128) or (DBG.get("no16") and C == 16)
            if not skip:
                plist.append((e0, C, "p"))
            e0 += C
        plist = [(a, b, c, self.xhs[i % 2]) for i, (a, b, c) in enumerate(plist)]
        self.sample_item = (0, NS * LS, "s", self.xhs[len(plist) % 2])
        self.front0(*plist[0])
        for i, it in enumerate(plist):
            self.chunk(*it, plist[i + 1] if i + 1 < len(plist) else None)
        if DBG.get("stop_early"):
            return
        if DBG.get("stop_after_prompt"):
            return
        for h2 in range(2):
            S.dma(self.gdn_p.rearrange("(a two) d v -> two d a v", two=2)[h2], self.Sg.t[R(h2), :, :], [self.Sg], [], is_out=True)
            S.dma(self.gla_p.rearrange("(a two) d v -> two d a v", two=2)[h2], self.Sl.t[R(h2), :, :], [self.Sl], [], is_out=True)
        if not DBG.get("no_cso"):
            self.conv_state_out(16, False)
        if DBG.get("stop_after_outputs"):
            return
        stc = self.bg(3, 2)[0:NS * 3, 0:1536]
        S.dma(stc, self.sgconv, [], [self.big[3], self.big[4]])
        qs = self.qkvx.t[:, :, 0:NS * 11].rearrange("p a (s l) -> p a s l", l=11)
        for g in range(3):
            def tr(e, g=g):
                for j in range(4):
                    cc = g * 4 + j
                    r = e.transpose(self.bank(2 + g)[:, j * 128:j * 128 + NS * 3], stc[:, cc * 128:(cc + 1) * 128], self.cm(C_ID, NS * 3, NS * 3))
                return r
            S.op("pe", tr, [self.big[3], self.big[4], self.cmat], [self.pb[2 + g]])
            S.op("act", lambda e, g=g: e.activation(out=qs[:, g * 4:(g + 1) * 4, :, 0:3],
                                                    in_=self.bank(2 + g).rearrange("p (j c) -> p j c", j=4)[:, :, 0:NS * 3].rearrange("p j (s l) -> p j s l", l=3),
                                                    func=AF.Copy), [self.pb[2 + g]], [self.qkvx])
        self.front0(*self.sample_item)
        self.chunk(*self.sample_item, None)
        self.conv_state_out(NS * LS, True)

    def stage2(self):
        S, cm, pb, bank = self.S, self.cm, self.pb, self.bank
        self.lnbc = S.sb("lnbc2", [128, 2, D], F32)
        w_up = S.sb("w_up_sb", [128, 8, 2 * DFF], BF16)
        w_dn = S.sb("w_dn", [128, 22, D], BF16)
        cwf = S.sb("cwf_sb", [128, NFF, 4], F32)
        h1t = [S.sb(f"h1t{i}", [128, D], F32) for i in range(2)]
        h1T = S.sb("h1T", [128, 8, 512], BF16)
        ue = [S.sb(f"ue{i}", [128, 160], F32) for i in range(2)]
        accs = [[S.sb(f"acc{w}{i}", [128, 512], F32) for i in range(2)] for w in range(2)]
        sgt1 = S.sb("sgt", [128, 512], F32)
        sgt = [sgt1, sgt1]
        hc = S.sb("hc", [128, NFF, 4], F32)
        uh = [S.sb(f"uh{i}", [128, NFF, 2], F32) for i in range(2)]
        actT = S.sb("actT", [128, 22, 512], BF16)
        usave = S.sb("usave", [128, NFF, 32], F32)
        st32 = S.sb("st32", [128, 512], F32)
        for i in range(2):
            S.dma(self.lnbc.t[:, i, :], self.lnv_d[4 + i:5 + i, :].partition_broadcast(128), [], [self.lnbc])
        S.dma(cwf.t[:], self.cwf_d, [], [cwf])
        wu_ = self.w_up_d.rearrange("(kc p) n -> p kc n", p=128)
        wd_ = self.w_down_d.rearrange("(j p) n -> p j n", p=128)
        w_up_tb = {}
        wdma = []
        for g in range(3):
            for which in range(2):
                c0 = which * DFF + g * 1024
                c1 = min(c0 + 1024, (which + 1) * DFF)
                tb = S.alias(f"w_up_{which}_{g}", w_up)
                for gg in (2 * g, 2 * g + 1):
                    w_up_tb[(which, gg)] = tb
                for kc in range(0, 8, 4):
                    wdma.append((lambda kc=kc, c0=c0, c1=c1, tb=tb: S.dma(w_up.t[:, kc:kc + 4, c0:c1], wu_[:, kc:kc + 4, c0:c1], [], [tb], cast=True)))
        w_dn_tb = {}
        wddma = []
        for j in range(0, 22, 2):
            tb = S.alias(f"w_dn_{j}", w_dn)
            w_dn_tb[j] = tb
            w_dn_tb[j + 1] = tb
            wddma.append((lambda j=j, tb=tb: S.dma(w_dn.t[:, j:j + 2, :], wd_[:, j:j + 2, :], [], [tb], cast=True)))
        for _ in range(4):
            wdma.pop(0)()
        S.op("pool", lambda e: e.memset(uh[0].t[:], 0.0), [], [uh[0]])

        def conv_out(n, dst, src):
            for g in range(11):
                def tr(e, g=g):
                    for j in range(4):
                        r = e.transpose(bank(g % 4)[0:n, j * 128:(j + 1) * 128], src.t[:, g * 4 + j, 0:n], cm(C_ID))
                    return r
                S.op("pe", tr, [src, self.cmat], [pb[g % 4]])
                S.op("act", lambda e, g=g: e.activation(out=st32.t[0:n, :], in_=bank(g % 4)[0:n, :], func=AF.Copy), [pb[g % 4]], [st32])
                S.dma(dst[:, g * 512:(g + 1) * 512], st32.t[0:n, :], [st32], [], is_out=True)

        tiles = []
        e0 = 0
        while e0 < TP:
            W = min(512, TP - e0)
            tiles.append((e0, W, False))
            e0 += W
        tiles.append((TP, NS * LS, True))
        upb = 0
        for ti, (r0, W, smp) in enumerate(tiles):
            G, L = (NS, LS) if smp else (1, W)
            uprev, ucur = uh[ti % 2], uh[(ti + 1) % 2]
            if not smp and not DBG.get("nohc"):
                S.op("pool", lambda e, uprev=uprev: e.tensor_tensor(out=hc.t[:, :, 0:2], in0=uprev.t[:, :, :], in1=cwf.t[:, :, 0:1].to_broadcast([128, NFF, 2]), op=ALU.mult),
                     [uprev, cwf], [hc])
                S.op("pool", lambda e, uprev=uprev: e.tensor_tensor(out=hc.t[:, :, 2:3], in0=uprev.t[:, :, 1:2], in1=cwf.t[:, :, 1:2], op=ALU.mult),
                     [uprev, cwf, hc], [hc])
            if smp:
                for g in range(11):
                    S.dma(st32.t[0:NS * 2, :], self.sfconv[:, g * 512:(g + 1) * 512], [], [st32])

                    def tr(e, g=g):
                        for j in range(4):
                            r = e.transpose(bank(4 + g % 2)[:, j * 32:(j + 1) * 32], st32.t[0:NS * 2, j * 128:(j + 1) * 128], cm(C_ID, NS * 2, NS * 2))
                        return r
                    S.op("pe", tr, [st32, self.cmat], [pb[4 + g % 2]])
                    S.op("act", lambda e, g=g: e.activation(out=usave.t[:, g * 4:(g + 1) * 4, :],
                                                            in_=bank(4 + g % 2)[:, 0:128].rearrange("p (j c) -> p j c", j=4), func=AF.Copy), [pb[4 + g % 2]], [usave])
            nsub = (W + 127) // 128
            for j in range(nsub):
                C = min(128, W - j * 128)
                ht = h1t[j % 2]
                S.dma(ht.t[0:C, :], self.h1s[r0 + j * 128:r0 + j * 128 + C, :], [self.h1scr], [ht])
                for half in range(2):
                    def tr(e, half=half, ht=ht, C=C):
                        for q in range(4):
                            r = e.transpose(bank(6 + half)[:, q * 128:q * 128 + C], ht.t[0:C, (half * 4 + q) * 128:(half * 4 + q + 1) * 128], cm(C_ID, C, C))
                        return r
                    S.op("pe", tr, [ht, self.cmat], [pb[6 + half]])
                    S.op("act", lambda e, half=half, j=j, C=C: e.activation(out=h1T.t[:, half * 4:half * 4 + 4, j * 128:j * 128 + C],
                                                                            in_=bank(6 + half).rearrange("p (q c) -> p q c", q=4)[:, :, 0:C], func=AF.Copy),
                         [pb[6 + half]], [h1T])
            pend = None
            for jg in range(22):
                par = jg % 2
                if jg % 8 == 1:
                    for _ in range(4):
                        if wdma:
                            wdma.pop(0)()
                if wddma and jg % 2 == 0:
                    wddma.pop(0)()
                if pend is not None:
                    pend()
                for which in (0, 1):
                    acc = accs[which][par]
                    cc = jg + 22 * which
                    b = upb % 4
                    upb += 1
                    wtb = w_up_tb[(which, jg // 4)]

                    def mmu(e, cc=cc, b=b):
                        for kc in range(8):
                            r = e.matmul(bank(b)[:, 0:W], lhsT=w_up.t[:, kc, cc * 128:(cc + 1) * 128], rhs=h1T.t[:, kc, 0:W], start=(kc == 0), stop=(kc == 7))
                        return r
                    S.op("pe", mmu, [wtb, h1T], [pb[b]])
                    if smp:
                        u = ue[cc % 2]
                        uv = u.t[:, 0:G * (L + 2)].rearrange("p (g l) -> p g l", g=G)
                        S.op("act", lambda e, b=b, uv=uv: e.activation(out=uv[:, :, 2:2 + L], in_=bank(b)[:, 0:W].rearrange("p (g l) -> p g l", g=G), func=AF.Copy),
                             [pb[b]], [u])
                        hsrc = usave.t[:, cc, :].rearrange("p (s i) -> p s i", i=2)
                        S.op("pool", lambda e, uv=uv, hsrc=hsrc: e.tensor_copy(out=uv[:, :, 0:2], in_=hsrc), [usave, u], [u])
                        av = acc.t[:, 0:W].rearrange("p (g l) -> p g l", g=G)
                        S.op("act", lambda e, uv=uv, av=av, cc=cc: e.activation(out=av, in_=uv[:, :, 2:2 + L], func=AF.Identity,
                                                                                scale=cwf.t[:, cc, 2:3], bias=cwf.t[:, cc, 3:4]), [u, cwf], [acc])
                        S.op("dve", lambda e, uv=uv, av=av, cc=cc: e.scalar_tensor_tensor(out=av, in0=uv[:, :, 1:1 + L], scalar=cwf.t[:, cc, 1:2], in1=av,
                                                                                           op0=ALU.mult, op1=ALU.add), [u, cwf, acc], [acc])
                        S.op("dve", lambda e, uv=uv, av=av, cc=cc: e.scalar_tensor_tensor(out=av, in0=uv[:, :, 0:L], scalar=cwf.t[:, cc, 0:1], in1=av,
                                                                                           op0=ALU.mult, op1=ALU.add), [u, cwf, acc], [acc])
                        hdst = usave.t[:, cc, :].rearrange("p (s i) -> p s i", i=2)
                        S.op("pool", lambda e, uv=uv, hdst=hdst: e.tensor_copy(out=hdst, in_=uv[:, :, L:L + 2]), [u, usave], [usave])
                    else:
                        S.op("act", lambda e, cc=cc, b=b: e.activation(out=acc.t[:, 0:W], in_=bank(b)[:, 0:W], func=AF.Identity,
                                                                       scale=cwf.t[:, cc, 2:3], bias=cwf.t[:, cc, 3:4]), [pb[b], cwf], [acc])
                        if not DBG.get("noact2"):
                            S.op("dve", lambda e, cc=cc, b=b, ucur=ucur: e.tensor_copy(out=ucur.t[:, cc, 0:2], in_=bank(b)[:, W - 2:W]), [pb[b], acc], [ucur])
                        if not DBG.get("nostt"):
                            S.op("dve", lambda e, cc=cc, b=b, acc=acc: e.scalar_tensor_tensor(out=acc.t[:, 1:W], in0=bank(b)[:, 0:W - 1], scalar=cwf.t[:, cc, 1:2], in1=acc.t[:, 1:W],
                                                                                     op0=ALU.mult, op1=ALU.add), [pb[b], cwf, acc], [acc])
                            S.op("dve", lambda e, cc=cc, b=b, acc=acc: e.scalar_tensor_tensor(out=acc.t[:, 2:W], in0=bank(b)[:, 0:W - 2], scalar=cwf.t[:, cc, 0:1], in1=acc.t[:, 2:W],
                                                                                     op0=ALU.mult, op1=ALU.add), [pb[b], cwf, acc], [acc])
                        if not DBG.get("nopool"):
                            S.op("pool", lambda e, cc=cc, acc=acc: e.tensor_tensor(out=acc.t[:, 0:2], in0=acc.t[:, 0:2], in1=hc.t[:, cc, 0:2], op=ALU.add), [acc, hc], [acc])
                            S.op("pool", lambda e, cc=cc, acc=acc: e.tensor_tensor(out=acc.t[:, 0:1], in0=acc.t[:, 0:1], in1=hc.t[:, cc, 2:3], op=ALU.add), [acc, hc], [acc])
                def fin(jg=jg, par=par):
                    S.op("act", lambda e: e.activation(out=sgt[par].t[:, 0:W], in_=accs[0][par].t[:, 0:W], func=AF.Silu), [accs[0][par]], [sgt[par]])
                    S.op("pool", lambda e: e.tensor_tensor(out=actT.t[:, jg, 0:W], in0=sgt[par].t[:, 0:W], in1=accs[1][par].t[:, 0:W], op=ALU.mult),
                         [sgt[par], accs[1][par]], [actT])
                pend = fin
            pend()
            def reload(j):
                Cj = min(128, W - j * 128)
                S.dma(h1t[j % 2].t[0:Cj, :], self.h1s[r0 + j * 128:r0 + j * 128 + Cj, :], [self.h1scr], [h1t[j % 2]])
            reload(0)
            for j in range(nsub):
                C = min(128, W - j * 128)
                ht = h1t[j % 2]
                if j + 1 < nsub:
                    reload(j + 1)
                for half in range(2):
                    def mmd(e, half=half, j=j, C=C):
                        for jg in range(22):
                            r = e.matmul(bank(4 + 2 * (j % 2) + half)[0:C, :], lhsT=actT.t[:, jg, j * 128:j * 128 + C], rhs=w_dn.t[:, jg, half * 512:(half + 1) * 512],
                                         start=(jg == 0), stop=(jg == 21))
                        return r
                    S.op("pe", mmd, [actT] + [w_dn_tb[j_] for j_ in range(0, 22, 2)], [pb[4 + 2 * (j % 2) + half]])
                    S.op("dve", lambda e, half=half, ht=ht, C=C: e.scalar_tensor_tensor(out=ht.t[0:C, half * 512:(half + 1) * 512], in0=ht.t[0:C, half * 512:(half + 1) * 512],
                                                                                        scalar=ALPHA, in1=bank(4 + 2 * (j % 2) + half)[0:C, :], op0=ALU.mult, op1=ALU.add),
                         [ht, pb[4 + 2 * (j % 2) + half]], [ht])
                self.layer_norm(ht, C, self.lnbc.t[:, 0, :], self.lnbc.t[:, 1, :], self.epsc.t[0:C, 0:1], "ln2")
                if smp:
                    S.dma(self.y_s, ht.t[0:C, :], [ht], [], is_out=True)
                else:
                    e = r0 + j * 128
                    if e == 0:
                        S.dma(self.y_p[0:128 - NMETA, :], ht.t[NMETA:128, :], [ht], [], is_out=True)
                    else:
                        S.dma(self.y_p[e - NMETA:e - NMETA + C, :], ht.t[0:C, :], [ht], [], is_out=True)
            if (not smp) and r0 + W == TP:
                conv_out(2, self.fconv_p, ucur)
        conv_out(NS * 2, self.fconv_s, usave)

    def build(self):
        S = self.S
        with S:
            self.cmat = S.sb("cmat_sb", [128, NCM, 128], F32)
            self.epsc = S.sb("epsc", [128, 4], F32)
            self.ln_st = S.sb("ln_st", [128, 2, 6], F32)
            self.ln_mv = S.sb("ln_mv", [128, 4], F32)
            self.glb = S.sb("glb", [128, 128], F32)
            self.pst = S.ps("pst", [128, 8, 512], F32)
            self.pb = [S.alias(f"pb{i}", self.pst) for i in range(8)]
            self.h1scr = TB("h1scr", None)
            main_stack = S.stack
            S.stack = ExitStack()
            with S.stack:
                if not DBG.get("skip1"):
                    self.stage1()
                S.barrier()
            S.stack = ExitStack()
            with S.stack:
                if not DBG.get("skip2"):
                    self.stage2()
                S.finish()
            S.stack = main_stack
        return self.nc


_PROG = None


def _program():
    global _PROG
    if _PROG is None:
        _PROG = K().build()
    return _PROG


def kernel(x_prompt, x_sample, state_gdn, state_gdn_conv, state_gla, state_ffn_conv, meta_tokens,
           ln_in_g, ln_in_b, w_in, gdn_conv_w, gdn_A_log, gdn_dt_bias, gdn_norm_g, gla_wgk2,
           gla_bgk, gla_norm_g, w_out, ln1_g, ln1_b, w_up, ffn_conv_w, ffn_conv_b, w_down,
           ln2_g, ln2_b):
    f = lambda a: np.ascontiguousarray(np.asarray(a, dtype=np.float32))
    x_prompt, x_sample = f(x_prompt), f(x_sample)
    w_in0 = f(w_in)[0]
    fm_cols = np.r_[0:1536, 2064:2320, 2320:2576, 3088:3104]
    tm_cols = np.r_[1536:1552, 2320:2576, 1552:2064, 2576:3088, 3104:3616]
    w_in_r = np.ascontiguousarray(w_in0[:, np.r_[fm_cols, tm_cols]])
    lnv = np.stack([f(ln_in_g), f(ln_in_b), f(ln1_g)[0], f(ln1_b)[0], f(ln2_g)[0], f(ln2_b)[0]])
    cwg = np.ascontiguousarray(f(gdn_conv_w)[0].T.reshape(12, 128, 4).transpose(1, 0, 2))
    cwf4 = np.concatenate([f(ffn_conv_w)[0], f(ffn_conv_b)], axis=0)
    cwf = np.ascontiguousarray(cwf4.T.reshape(NFF, 128, 4).transpose(1, 0, 2))
    pvec = np.concatenate([f(gdn_A_log)[0], f(gdn_dt_bias)[0], f(gdn_norm_g)[0], f(gla_norm_g)[0], f(gla_bgk)[0]])[None, :]
    shared = dict(meta=f(meta_tokens), cmat=_const_mats(), w_in_r=w_in_r, w_out=f(w_out)[0], w_up=f(w_up)[0], w_down=f(w_down)[0],
                  lnv=np.ascontiguousarray(lnv), cwg=cwg, cwf=cwf, pvec=np.ascontiguousarray(pvec), wgk2=f(gla_wgk2)[0])
    sg, sgc, sl, sfc = f(state_gdn)[0], f(state_gdn_conv)[0], f(state_gla)[0], f(state_ffn_conv)[0]
    in_maps = []
    for c in range(8):
        sl_ = slice(c * NS, (c + 1) * NS)
        m = dict(shared)
        m.update(xp=x_prompt[c], xs=np.ascontiguousarray(x_sample[sl_].reshape(NS * LS, D)), sgdn=sg[sl_],
                 sgconv=np.ascontiguousarray(sgc[sl_].reshape(NS * 3, 1536)), sgla=sl[sl_],
                 sfconv=np.ascontiguousarray(sfc[sl_].reshape(NS * 2, 2 * DFF)))
        in_maps.append(m)
    ncr = DBG.get("ncores", 8)
    res = run_bass_kernel_spmd(_program(), in_maps[:ncr], core_ids=list(range(ncr)))
    r = res.results
    cat = lambda k: np.stack([np.asarray(r[min(c, ncr - 1)][k]) for c in range(8)])
    y_prompt = cat("y_p")
    y_sample = cat("y_s").reshape(128, LS, D)
    gdn_p = cat("gdn_p")[None]
    gconv_p = cat("gconv_p")[None]
    gla_p = cat("gla_p")[None]
    fconv_p = cat("fconv_p")[None]
    gdn_s = cat("gdn_s").reshape(1, 128, 8, 64, 64)
    gconv_s = cat("gconv_s").reshape(1, 128, 3, 1536)
    gla_s = cat("gla_s").reshape(1, 128, 4, 64, 128)
    fconv_s = cat("fconv_s").reshape(1, 128, 2, 2 * DFF)
    outs = (y_prompt, y_sample, gdn_p, gconv_p, gla_p, fconv_p, gdn_s, gconv_s, gla_s, fconv_s)
    return tuple(np.ascontiguousarray(o, dtype=np.float32) for o in outs)
```
